# Optimizing a Trainium2 kernel written in Bass

```python
import jax, jax.numpy as jnp
from jax import lax
import numpy as np

D_MODEL = 1024
BATCH = 8
SEQ = 8192
DEPTH = 4
DEC_BATCH = 8
DEC_SEQ = 32
PAST_LEN = 1024

CHUNK = 64
N_MIXERS = 2
N_ATTN_LAYERS = (DEPTH + 1) // 2
N_REC_LAYERS = DEPTH // 2
FOX_HEADS = 16
FOX_HEAD_DIM = D_MODEL // FOX_HEADS
FOX_SCALE = FOX_HEAD_DIM ** -0.5
HGRN_HEADS = 8
HGRN_KEY_DIM = D_MODEL // HGRN_HEADS
HGRN_VAL_DIM = D_MODEL // HGRN_HEADS
D_FF = 4 * D_MODEL
Q_BLOCK = 128
EPS = 1e-6
FOX_FORGET_BIAS = 2.0

kernel_name = "fox_hgrn2_streaming_step"


def rmsnorm(x, g):
    xf = x.astype(jnp.float32)
    y = xf * lax.rsqrt(jnp.mean(xf * xf, axis=-1, keepdims=True) + EPS)
    return (y * g.astype(jnp.float32)).astype(x.dtype)


def fox_attend(q, k, v, cq, ck, qpos, kpos):
    logits = jnp.einsum('bqhd,bkhd->bhqk', q, k).astype(jnp.float32) * FOX_SCALE
    bias = jnp.swapaxes(cq, 1, 2)[..., :, None] - jnp.swapaxes(ck, 1, 2)[..., None, :]
    mask = kpos[None, :] <= qpos[:, None]
    logits = jnp.where(mask, logits + bias, -jnp.inf)
    p = jax.nn.softmax(logits, axis=-1).astype(v.dtype)
    return jnp.einsum('bhqk,bkhd->bqhd', p, v)


def fox_block_sweep(q, k, v, c):
    B, T, H, Dh = q.shape
    nb = T // Q_BLOCK
    qb = q.reshape(B, nb, Q_BLOCK, H, Dh).transpose(1, 0, 2, 3, 4)
    cb = c.reshape(B, nb, Q_BLOCK, H).transpose(1, 0, 2, 3)
    pb = jnp.arange(T).reshape(nb, Q_BLOCK)
    kpos = jnp.arange(T)

    def one_block(args):
        qi, ci, pi = args
        return fox_attend(qi, k, v, ci, c, pi, kpos)

    out = lax.map(one_block, (qb, cb, pb))
    return out.transpose(1, 0, 2, 3, 4).reshape(B, T, H, Dh)


def fox_mixer(h, w_in, b_f, g_q, g_k, w_out, past_k, past_v, past_logf):
    B, T, _ = h.shape
    proj = h @ w_in
    q = proj[..., :D_MODEL].reshape(B, T, FOX_HEADS, FOX_HEAD_DIM)
    k = proj[..., D_MODEL:2 * D_MODEL].reshape(B, T, FOX_HEADS, FOX_HEAD_DIM)
    v = proj[..., 2 * D_MODEL:3 * D_MODEL].reshape(B, T, FOX_HEADS, FOX_HEAD_DIM)
    gate = proj[..., 3 * D_MODEL:4 * D_MODEL]
    fz = proj[..., 4 * D_MODEL:]
    q = rmsnorm(q, g_q)
    k = rmsnorm(k, g_k)
    logf = jax.nn.log_sigmoid((fz + b_f).astype(jnp.float32))
    if past_k is None:
        c = jnp.cumsum(logf, axis=1)
        o = fox_block_sweep(q, k, v, c)
    else:
        P = past_k.shape[1]
        k_all = jnp.concatenate([past_k, k], axis=1)
        v_all = jnp.concatenate([past_v, v], axis=1)
        c = jnp.cumsum(jnp.concatenate([past_logf.astype(jnp.float32), logf], axis=1), axis=1)
        o = fox_attend(q, k_all, v_all, c[:, P:], c, P + jnp.arange(T), jnp.arange(P + T))
    o = o.reshape(B, T, D_MODEL) * jax.nn.sigmoid(gate)
    return o @ w_out, k, v, logf.astype(h.dtype)


def hgrn2_chunkwise(q, k, v, logf, s0):
    B, T, H, DK = q.shape
    DV = v.shape[-1]
    L = CHUNK if T % CHUNK == 0 else T
    n = T // L

    def to_chunks(a):
        return a.reshape(B, n, L, H, a.shape[-1]).transpose(1, 0, 3, 2, 4)

    causal = jnp.tril(jnp.ones((L, L), dtype=bool))

    def step(S, inp):
        qc, kc, vc, gc = inp
        b = jnp.cumsum(gc, axis=2)
        o_inter = jnp.einsum('bhtk,bhkv->bhtv', qc * jnp.exp(b), S)
        diff = b[:, :, :, None, :] - b[:, :, None, :, :]
        decay = jnp.exp(jnp.where(causal[:, :, None], diff, -jnp.inf))
        A = jnp.einsum('bhtk,bhsk,bhtsk->bhts', qc, kc, decay)
        o_intra = jnp.einsum('bhts,bhsv->bhtv', A, vc)
        bl = b[:, :, -1, :]
        S_new = jnp.exp(bl)[..., None] * S + jnp.einsum(
            'bhsk,bhsv->bhkv', kc * jnp.exp(bl[:, :, None, :] - b), vc)
        return S_new, o_inter + o_intra

    S, o = lax.scan(step, s0, (to_chunks(q), to_chunks(k), to_chunks(v), to_chunks(logf)))
    o = o.transpose(1, 0, 3, 2, 4).reshape(B, T, H, DV)
    return o, S


def hgrn2_mixer(h, w_in, lb, g_out, w_out, s0):
    B, T, _ = h.shape
    proj = (h @ w_in).astype(jnp.float32)
    shp_k = (B, T, HGRN_HEADS, HGRN_KEY_DIM)
    q = jax.nn.silu(proj[..., :D_MODEL]).reshape(shp_k)
    f = lb + (1.0 - lb) * jax.nn.sigmoid(proj[..., D_MODEL:2 * D_MODEL])
    f = f.reshape(shp_k)
    i = proj[..., 2 * D_MODEL:3 * D_MODEL].reshape(B, T, HGRN_HEADS, HGRN_VAL_DIM)
    gate = proj[..., 3 * D_MODEL:]
    o, S = hgrn2_chunkwise(q, 1.0 - f, i, jnp.log(f), s0.astype(jnp.float32))
    o = rmsnorm(o.reshape(B, T, D_MODEL), g_out) * jax.nn.silu(gate)
    return o.astype(h.dtype) @ w_out, S.astype(h.dtype)


def trunk(x, past_k, past_v, past_logf, past_s,
          fox_w_in, fox_b_f, fox_q_norm, fox_k_norm, fox_w_out,
          hgrn_w_in, hgrn_lb_logits, hgrn_out_norm, hgrn_w_out,
          pre_mix_norm, post_mix_norm, pre_ffn_norm, post_ffn_norm, ffn_w_up, ffn_w_down):
    B = x.shape[0]
    sm = jax.nn.softmax(hgrn_lb_logits.astype(jnp.float32), axis=0)
    lower_bounds = jnp.cumsum(sm, axis=0) - sm[0]
    new_k, new_v, new_logf, new_s = [], [], [], []
    for layer in range(DEPTH):
        j = layer // N_MIXERS
        h = rmsnorm(x, pre_mix_norm[layer])
        if layer % N_MIXERS == 0:
            pk = None if past_k is None else past_k[j]
            pv = None if past_v is None else past_v[j]
            pl = None if past_logf is None else past_logf[j]
            out, k, v, lf = fox_mixer(h, fox_w_in[j], fox_b_f[j], fox_q_norm[j], fox_k_norm[j],
                                      fox_w_out[j], pk, pv, pl)
            new_k.append(k)
            new_v.append(v)
            new_logf.append(lf)
        else:
            s0 = (jnp.zeros((B, HGRN_HEADS, HGRN_KEY_DIM, HGRN_VAL_DIM), jnp.float32)
                  if past_s is None else past_s[j])
            out, s = hgrn2_mixer(h, hgrn_w_in[j], lower_bounds[layer], hgrn_out_norm[j],
                                 hgrn_w_out[j], s0)
            new_s.append(s)
        x = x + rmsnorm(out, post_mix_norm[layer])
        h = rmsnorm(x, pre_ffn_norm[layer])
        u = jax.nn.relu(h @ ffn_w_up[layer])
        x = x + rmsnorm((u * u) @ ffn_w_down[layer], post_ffn_norm[layer])
    return x, jnp.stack(new_k), jnp.stack(new_v), jnp.stack(new_logf), jnp.stack(new_s)


def setup_inputs(seed: int = 0) -> dict:
    key = jax.random.key(seed)
    ks = jax.random.split(key, 24)
    f32 = jnp.float32
    nrm = lambda k, shp, s: jax.random.normal(k, shp, f32) * s
    gain = lambda k, shp: 1.0 + 0.05 * jax.random.normal(k, shp, f32)
    d = D_MODEL
    return {
        "x_prompt": nrm(ks[0], (BATCH, SEQ, d), 1.0),
        "x_sample": nrm(ks[1], (DEC_BATCH, DEC_SEQ, d), 1.0),
        "cache_k": nrm(ks[2], (N_ATTN_LAYERS, DEC_BATCH, PAST_LEN, FOX_HEADS, FOX_HEAD_DIM), 1.0),
        "cache_v": nrm(ks[3], (N_ATTN_LAYERS, DEC_BATCH, PAST_LEN, FOX_HEADS, FOX_HEAD_DIM), 1.0),
        "cache_logf": jax.nn.log_sigmoid(FOX_FORGET_BIAS + nrm(ks[4], (N_ATTN_LAYERS, DEC_BATCH, PAST_LEN, FOX_HEADS), 0.5)),
        "state_s": nrm(ks[5], (N_REC_LAYERS, DEC_BATCH, HGRN_HEADS, HGRN_KEY_DIM, HGRN_VAL_DIM), 0.5),
        "fox_w_in": nrm(ks[6], (N_ATTN_LAYERS, d, 4 * d + FOX_HEADS), d ** -0.5),
        "fox_b_f": FOX_FORGET_BIAS + nrm(ks[7], (N_ATTN_LAYERS, FOX_HEADS), 0.5),
        "fox_q_norm": gain(ks[8], (N_ATTN_LAYERS, FOX_HEAD_DIM)),
        "fox_k_norm": gain(ks[9], (N_ATTN_LAYERS, FOX_HEAD_DIM)),
        "fox_w_out": nrm(ks[10], (N_ATTN_LAYERS, d, d), d ** -0.5),
        "hgrn_w_in": nrm(ks[11], (N_REC_LAYERS, d, 4 * d), d ** -0.5),
        "hgrn_lb_logits": nrm(ks[12], (DEPTH, d), 0.5),
        "hgrn_out_norm": gain(ks[13], (N_REC_LAYERS, d)),
        "hgrn_w_out": nrm(ks[14], (N_REC_LAYERS, d, d), d ** -0.5),
        "pre_mix_norm": gain(ks[15], (DEPTH, d)),
        "post_mix_norm": gain(ks[16], (DEPTH, d)),
        "pre_ffn_norm": gain(ks[17], (DEPTH, d)),
        "post_ffn_norm": gain(ks[18], (DEPTH, d)),
        "ffn_w_up": nrm(ks[19], (DEPTH, d, D_FF), d ** -0.5),
        "ffn_w_down": nrm(ks[20], (DEPTH, D_FF, d), D_FF ** -0.5),
    }


def reference(x_prompt, x_sample, cache_k, cache_v, cache_logf, state_s,
              fox_w_in, fox_b_f, fox_q_norm, fox_k_norm, fox_w_out,
              hgrn_w_in, hgrn_lb_logits, hgrn_out_norm, hgrn_w_out,
              pre_mix_norm, post_mix_norm, pre_ffn_norm, post_ffn_norm, ffn_w_up, ffn_w_down):
    y_prompt, k_p, v_p, lf_p, s_p = trunk(
        x_prompt, None, None, None, None,
        fox_w_in, fox_b_f, fox_q_norm, fox_k_norm, fox_w_out,
        hgrn_w_in, hgrn_lb_logits, hgrn_out_norm, hgrn_w_out,
        pre_mix_norm, post_mix_norm, pre_ffn_norm, post_ffn_norm, ffn_w_up, ffn_w_down)
    y_sample, k_s, v_s, lf_s, s_s = trunk(
        x_sample, cache_k, cache_v, cache_logf, state_s,
        fox_w_in, fox_b_f, fox_q_norm, fox_k_norm, fox_w_out,
        hgrn_w_in, hgrn_lb_logits, hgrn_out_norm, hgrn_w_out,
        pre_mix_norm, post_mix_norm, pre_ffn_norm, post_ffn_norm, ffn_w_up, ffn_w_down)
    return (y_prompt, y_sample, k_p, v_p, lf_p, s_p, k_s, v_s, lf_s, s_s)
```

```python
import contextlib
import numpy as np
import concourse.bass as bass
import concourse.mybir as mybir
from concourse.bass_utils import run_bass_kernel_spmd

F32 = mybir.dt.float32
BF16 = mybir.dt.bfloat16
AF = mybir.ActivationFunctionType
ALU = mybir.AluOpType
AX = mybir.AxisListType

D = 1024
NH = 16
DH = 64
HH = 8
DFF = 4096
EPS = 1e-6
DS = 32
FW = 4 * D + NH


class Buf:
    __slots__ = ("name", "w", "r")

    def __init__(self, name=""):
        self.name = name
        self.w = None
        self.r = []


class Op:
    __slots__ = ("eng", "fn", "deps", "sig", "sigval", "dma")

    def __init__(self, eng, fn, dma):
        self.eng = eng
        self.fn = fn
        self.deps = []
        self.sig = False
        self.sigval = None
        self.dma = dma


class Sched:
    CE = ("pe", "act", "dve", "pool")
    ENGS = ("pe", "act", "dve", "pool", "sp")

    def __init__(self, nc):
        self.nc = nc
        self.ops = {e: [] for e in self.ENGS}
        self.streams = {}
        self.last_dma = {}

    def op(self, eng, fn, reads=(), writes=(), dma=None, extra=()):
        o = Op(eng, fn, dma)
        deps = []
        seen = set()

        def add(d):
            if d is None or d is o or id(d) in seen:
                return
            seen.add(id(d))
            deps.append(d)

        for b in reads:
            add(b.w)
        for b in writes:
            add(b.w)
            for r in b.r:
                add(r)
        for d in extra:
            add(d)
        if dma is not None:
            add(self.last_dma.get(dma))
            self.last_dma[dma] = o
            n = self.streams.setdefault(dma, [0])
            n[0] += 1
            o.sigval = 16 * n[0]
        for d in deps:
            if d.dma is None and d.eng == eng and eng == "pe":
                continue
            o.deps.append(d)
            if d.dma is None:
                d.sig = True
        for b in writes:
            b.w = o
            b.r = []
        for b in reads:
            if b.w is not o:
                b.r.append(o)
        self.ops[eng].append(o)
        return o

    def barrier(self):
        firsts = []
        alld = list(self.last_dma.values())
        for e in self.CE:
            ex = list(alld)
            if self.ops[e]:
                ex.append(self.ops[e][-1])
            o = Op(e, lambda eh: eh.nop(), None)
            for d in ex:
                if d.dma is None and d.eng == e and e == "pe":
                    pass
                o.deps.append(d)
                if d.dma is None:
                    d.sig = True
            self.ops[e].append(o)
            firsts.append(o)
        for e in self.ENGS:
            o = Op(e, lambda eh: eh.nop(), None)
            for d in firsts:
                if d.eng == e:
                    continue
                o.deps.append(d)
                d.sig = True
            self.ops[e].append(o)

    def emit(self):
        nc = self.nc
        with contextlib.ExitStack() as es:
            esem = {e: es.enter_context(nc.semaphore("s_" + e)) for e in self.CE}
            ssem = {k: es.enter_context(nc.semaphore("d%d" % i)) for i, k in enumerate(self.streams)}
            for e in self.CE:
                c = 0
                for o in self.ops[e]:
                    if o.dma is None and o.sig:
                        c += 1
                        o.sigval = c
            block = es.enter_context(nc.Block())

            def run(ename, eh):
                waited = {}
                for o in self.ops[ename]:
                    for d in o.deps:
                        sem = ssem[d.dma] if d.dma is not None else esem[d.eng]
                        key = id(sem)
                        if waited.get(key, 0) >= d.sigval:
                            continue
                        waited[key] = d.sigval
                        eh.wait_ge(sem, d.sigval)
                    ins = o.fn(eh)
                    if o.dma is not None:
                        ins.then_inc(ssem[o.dma], 16)
                    elif o.sig:
                        ins.then_inc(esem[ename], 1)
                if ename == "sp":
                    for k, n in self.streams.items():
                        eh.wait_ge(ssem[k], 16 * n[0])

            @block.tensor
            def _(e):
                run("pe", e)

            @block.scalar
            def _(e):
                run("act", e)

            @block.vector
            def _(e):
                run("dve", e)

            @block.gpsimd
            def _(e):
                run("pool", e)

            @block.sync
            def _(e):
                run("sp", e)


class Tl:
    __slots__ = ("ap", "b")

    def __init__(self, ap, name=""):
        self.ap = ap
        self.b = Buf(name)


class Arena:
    def __init__(self, ap, size):
        self.ap = ap
        self.size = size
        self.off = 0

    def alloc(self, n, dt=BF16, name=""):
        w = n * 2 if dt == F32 else n
        off = self.off
        self.off += (w + 31) // 32 * 32
        assert self.off <= self.size, ("SBUF arena overflow", name, self.off, self.size)
        v = self.ap[:, off:off + w]
        if dt == F32:
            v = v.bitcast(F32)
        return Tl(v, name)


class Ring:
    def __init__(self, tiles):
        self.t = tiles
        self.i = -1

    def next(self):
        self.i += 1
        return self.t[self.i % len(self.t)]


def make_consts():
    s = np.arange(128)[:, None]
    t = np.arange(128)[None, :]
    c = {}
    c["ident"] = (s == t).astype(np.float32)
    c["tri"] = (s <= t).astype(np.float32)
    c["maskneg"] = np.where(s > t, -30000.0, 0.0).astype(np.float32)
    c["d1p"] = ((s <= t).astype(np.float32) - (s <= 63).astype(np.float32) * np.ones_like(t, dtype=np.float32))
    vs = (s < DS) & (t < DS)
    c["d1s"] = np.where(vs, (s <= t).astype(np.float32) - (s <= 15).astype(np.float32), 0.0).astype(np.float32)
    sel = np.zeros((128, 8), np.float32)
    sv = np.arange(128)
    sel[:, 0] = sv <= 63
    sel[:, 1] = 1.0
    sel[:, 2] = sv > 63
    sel[:, 4] = sv <= 15
    sel[:, 5] = sv < DS
    sel[:, 6] = (sv > 15) & (sv < DS)
    c["sel"] = sel
    c["him"] = np.where(s <= t, 3.0e38, 0.0).astype(np.float32)
    c["lom"] = np.where(s <= t, -3.0e38, 0.0).astype(np.float32)
    return np.concatenate([c["ident"], c["tri"], c["maskneg"], c["him"], c["lom"], c["d1p"], c["d1s"], c["sel"]], axis=1).astype(np.float32)


NCONST = 7 * 128 + 8


def build(T, PAST, NL=4):
    NT = T // 128
    NP = PAST // 128
    NTS = NT + 1
    TS = T + 128
    KC = T + PAST + 128
    NKS = KC // 128
    NA = (NL + 1) // 2
    NR = NL // 2

    nc = bass.Bass("TRN2", target_bir_lowering=False)

    def din(name, shape):
        return nc.dram_tensor(name, list(shape), F32, kind="ExternalInput").ap()

    def dout(name, shape):
        return nc.dram_tensor(name, list(shape), F32, kind="ExternalOutput").ap()

    def dscr(name, shape, dt):
        return nc.dram_tensor(name, list(shape), dt, kind="Internal").ap()

    x_p = din("x_p", [T, D])
    x_s = din("x_s", [DS, D])
    ck = din("ck", [NA, PAST, D])
    cv = din("cv", [NA, PAST, D])
    clf = din("clf", [NA, PAST, NH])
    st = din("st", [NR, HH, 128, 128])
    fox_w_in = din("fox_w_in", [NA, D, FW])
    fox_b_f = din("fox_b_f", [NA, NH])
    fox_q_norm = din("fox_q_norm", [NA, DH])
    fox_k_norm = din("fox_k_norm", [NA, DH])
    fox_w_out = din("fox_w_out", [NA, D, D])
    hgrn_w_in = din("hgrn_w_in", [NR, D, 4 * D])
    hgrn_lb = din("hgrn_lb", [4, D])
    hgrn_out_norm = din("hgrn_out_norm", [NR, D])
    hgrn_w_out = din("hgrn_w_out", [NR, D, D])
    pre_mix = din("pre_mix", [NL, D])
    post_mix = din("post_mix", [NL, D])
    pre_ffn = din("pre_ffn", [NL, D])
    post_ffn = din("post_ffn", [NL, D])
    ffn_up = din("ffn_up", [NL, D, DFF])
    ffn_down = din("ffn_down", [NL, DFF, D])
    consts = din("consts", [128, NCONST])

    y_p = dout("y_p", [T, D])
    y_s = dout("y_s", [DS, D])
    k_p = dout("k_p", [NA, T, D])
    v_p = dout("v_p", [NA, T, D])
    lf_p = dout("lf_p", [NA, T, NH])
    s_p = dout("s_p", [NR, HH, 128, 128])
    k_s = dout("k_s", [NA, DS, D])
    v_s = dout("v_s", [NA, DS, D])
    lf_s = dout("lf_s", [NA, DS, NH])
    s_s = dout("s_s", [NR, HH, 128, 128])

    xres = dscr("xres", [NTS, 128, D], F32)
    wb_fin = dscr("wb_fin", [NA, D, FW], BF16)
    wb_fout = dscr("wb_fout", [NA, D, D], BF16)
    wb_hin = dscr("wb_hin", [NR, D, 4 * D], BF16)
    wb_hout = dscr("wb_hout", [NR, D, D], BF16)
    wb_up = dscr("wb_up", [NL, 32, 128, 8, 128], BF16)
    wb_down = dscr("wb_down", [NL, DFF, D], BF16)
    QT = dscr("QT", [D, TS], BF16)
    KT = dscr("KT", [D, KC], BF16)
    VT = dscr("VT", [KC, D], BF16)
    GT = dscr("GT", [D, TS], BF16)
    CTs = dscr("CTs", [NH, 6, KC], BF16)
    OGT = dscr("OGT", [D, TS], BF16)
    H2T = dscr("H2T", [D, TS], BF16)

    db = {}

    def dbuf(*key):
        if key not in db:
            db[key] = Buf(str(key))
        return db[key]

    ARENA = 106000
    with contextlib.ExitStack() as es:
        arena_t = es.enter_context(nc.sbuf_tensor("arena", [128, ARENA], BF16))
        psum = es.enter_context(nc.psum_tensor("psum", [128, 8, 512], F32))
        A = Arena(arena_t, ARENA)
        S = Sched(nc)

        bank = [Tl(psum[:, k, :], "bank%d" % k) for k in range(8)]

        def slot2(k):
            return Tl(psum[:, k:k + 2, :].rearrange("p a b -> p (a b)"), "slot%d" % k)

        cst = A.alloc(NCONST, F32, "consts")
        S.op("sp", lambda e: e.dma_start(out=cst.ap, in_=consts[:, :]), writes=[cst.b], dma="cst")
        identf = cst.ap[:, 0:128]
        trif = cst.ap[:, 128:256]
        masknegf = cst.ap[:, 256:384]
        d1p = cst.ap[:, 640:768]
        d1s = cst.ap[:, 768:896]
        selp = cst.ap[:, 896:899]
        sels = cst.ap[:, 900:903]
        cb = A.alloc(5 * 128, BF16, "constb")
        S.op("dve", lambda e: e.tensor_copy(out=cb.ap, in_=cst.ap[:, 0:640]), reads=[cst.b], writes=[cb.b])
        idb = cb.ap[:, 0:128]
        mask01b = cb.ap[:, 128:256]
        masknegb = cb.ap[:, 256:384]
        himb = cb.ap[:, 384:512]
        lomb = cb.ap[:, 512:640]
        CONSTB = [cst.b, cb.b]

        oml = {}
        if NR > 0:
            for layer in range(1, NL, 2):
                oml[layer] = A.alloc(D, F32, "oml%d" % layer)
            keep = A.off
            L = [A.alloc(D, F32, "lbl%d" % i) for i in range(4)]
            mx = A.alloc(D, F32, "lbmx")
            sm = A.alloc(D, F32, "lbsum")

            def ld_l(i):
                S.op("sp", lambda e: e.dma_start(out=L[i].ap, in_=hgrn_lb[i:i + 1, :].partition_broadcast(128)), writes=[L[i].b], dma="lb%d" % i)
            for i in range(4):
                ld_l(i)

            def tt_(out, a_, b_, op):
                S.op("dve", lambda e: e.tensor_tensor(out=out.ap, in0=a_.ap, in1=b_.ap, op=op), reads=[a_.b, b_.b], writes=[out.b])
            tt_(mx, L[0], L[1], ALU.max)
            tt_(mx, mx, L[2], ALU.max)
            tt_(mx, mx, L[3], ALU.max)

            def ex_(i):
                tt_(L[i], L[i], mx, ALU.subtract)
                S.op("act", lambda e: e.activation(out=L[i].ap, in_=L[i].ap, func=AF.Exp), reads=[L[i].b], writes=[L[i].b])
            for i in range(4):
                ex_(i)
            tt_(sm, L[0], L[1], ALU.add)
            tt_(sm, sm, L[2], ALU.add)
            tt_(sm, sm, L[3], ALU.add)
            S.op("dve", lambda e: e.reciprocal(out=sm.ap, in_=sm.ap), reads=[sm.b], writes=[sm.b])

            def mk_oml(layer):
                o_ = oml[layer]
                S.op("dve", lambda e: e.tensor_copy(out=o_.ap, in_=L[1].ap), reads=[L[1].b], writes=[o_.b])
                for i in range(2, layer + 1):
                    tt_(o_, o_, L[i], ALU.add)
                tt_(o_, o_, sm, ALU.mult)
                S.op("dve", lambda e: e.tensor_scalar(out=o_.ap, in0=o_.ap, scalar1=-1.0, scalar2=1.0, op0=ALU.mult, op1=ALU.add),
                     reads=[o_.b], writes=[o_.b])
            for layer in range(1, NL, 2):
                mk_oml(layer)
            S.barrier()
            A.off = keep

        PERS = A.off

        def stats(src_ap, src_b, n, junk, ss):
            S.op("pool", lambda e: e.memset(ss.ap[:, 0:1], 0.0), writes=[ss.b])
            S.op("act", lambda e: e.activation(out=junk.ap[:, 0:n], in_=src_ap, func=AF.Square, accum_out=ss.ap[:, 0:1]),
                 reads=[src_b, ss.b], writes=[ss.b])
            S.op("act", lambda e: e.activation(out=ss.ap[:, 1:2], in_=ss.ap[:, 0:1], func=AF.Sqrt, scale=1.0 / n, bias=EPS),
                 reads=[ss.b], writes=[ss.b])
            S.op("dve", lambda e: e.reciprocal(out=ss.ap[:, 2:3], in_=ss.ap[:, 1:2]), reads=[ss.b], writes=[ss.b])

        def transposes(src, dstT, tb, copy_eng):
            tv = tb.ap.bitcast(BF16)

            def f(e):
                ins = None
                for c in range(8):
                    ins = e.transpose(out=tv[:, c * 128:(c + 1) * 128], in_=src.ap[:, c * 128:(c + 1) * 128], identity=idb)
                return ins
            S.op("pe", f, reads=[src.b] + CONSTB, writes=[tb.b])
            if copy_eng == "act":
                S.op("act", lambda e: e.activation(out=dstT.ap, in_=tv, func=AF.Copy), reads=[tb.b], writes=[dstT.b])
            else:
                S.op("dve", lambda e: e.tensor_copy(out=dstT.ap, in_=tv), reads=[tb.b], writes=[dstT.b])

        def proj_tok(hT, w, wcol0, sl):
            def f(e):
                ins = None
                for half in range(2):
                    for kc in range(8):
                        ins = e.matmul(sl.ap[:, half * 512:(half + 1) * 512], lhsT=hT.ap[:, kc * 128:(kc + 1) * 128],
                                       rhs=w.ap[:, kc, wcol0 + half * 512: wcol0 + (half + 1) * 512],
                                       start=(kc == 0), stop=(kc == 7))
                return ins
            S.op("pe", f, reads=[hT.b, w.b], writes=[sl.b])

        def load_bcast(dst, src_row, key):
            S.op("sp", lambda e: e.dma_start(out=dst.ap, in_=src_row.partition_broadcast(128)), writes=[dst.b], dma=key)

        def xsrc(layer, t):
            if layer == 0 and t < NT:
                return x_p[t * 128:(t + 1) * 128, :], []
            return xres[t], [dbuf("xres", t)]

        def qcol(t):
            return t * 128

        def kcol(t):
            return t * 128 if t < NT else T + PAST

        def phase_wcast():
            mark = A.off
            CH = 4096
            NWB = 4
            fb = [A.alloc(CH, F32, "wf%d" % i) for i in range(NWB)]
            bb = [A.alloc(CH, BF16, "wb%d" % i) for i in range(NWB)]
            jobs = []

            def flat(ap2d):
                return ap2d.rearrange("r c -> (r c)").rearrange("(p n) -> p n", p=128)

            for j in range(NA):
                jobs.append((flat(fox_w_in[j]), flat(wb_fin[j]), dbuf("wb_fin", j)))
                jobs.append((flat(fox_w_out[j]), flat(wb_fout[j]), dbuf("wb_fout", j)))
            for j in range(NR):
                jobs.append((flat(hgrn_w_in[j]), flat(wb_hin[j]), dbuf("wb_hin", j)))
                jobs.append((flat(hgrn_w_out[j]), flat(wb_hout[j]), dbuf("wb_hout", j)))
            for l in range(NL):
                jobs.append((flat(ffn_down[l]), flat(wb_down[l]), dbuf("wb_down", l)))
            step = 0
            engs = ("dve", "pool", "act")
            for src, dst, tok in jobs:
                n = src.shape[1]
                for c0 in range(0, n, CH):
                    w_ = min(CH, n - c0)
                    f_, b_ = fb[step % NWB], bb[step % NWB]
                    S.op("sp", lambda e, f_=f_, src=src, c0=c0, w_=w_: e.dma_start(out=f_.ap[:, 0:w_], in_=src[:, c0:c0 + w_]),
                         writes=[f_.b], dma="wl%d" % (step % NWB))
                    eng = engs[step % 3]
                    if eng == "act":
                        S.op("act", lambda e, f_=f_, b_=b_, w_=w_: e.activation(out=b_.ap[:, 0:w_], in_=f_.ap[:, 0:w_], func=AF.Copy),
                             reads=[f_.b], writes=[b_.b])
                    else:
                        S.op(eng, lambda e, f_=f_, b_=b_, w_=w_: e.tensor_copy(out=b_.ap[:, 0:w_], in_=f_.ap[:, 0:w_]),
                             reads=[f_.b], writes=[b_.b])
                    S.op("sp", lambda e, b_=b_, dst=dst, c0=c0, w_=w_: e.dma_start(out=dst[:, c0:c0 + w_], in_=b_.ap[:, 0:w_]),
                         reads=[b_.b], writes=[tok], dma="ws%d" % (step % NWB))
                    step += 1
            for l in range(NL):
                for c in range(8):
                    f_, b_ = fb[step % NWB], bb[step % NWB]
                    S.op("sp", lambda e, f_=f_, l=l, c=c: e.dma_start(out=f_.ap[:, 0:DFF], in_=ffn_up[l, c * 128:(c + 1) * 128, :]),
                         writes=[f_.b], dma="wl%d" % (step % NWB))
                    eng = engs[step % 3]
                    if eng == "act":
                        S.op("act", lambda e, f_=f_, b_=b_: e.activation(out=b_.ap[:, 0:DFF], in_=f_.ap[:, 0:DFF], func=AF.Copy),
                             reads=[f_.b], writes=[b_.b])
                    else:
                        S.op(eng, lambda e, f_=f_, b_=b_: e.tensor_copy(out=b_.ap[:, 0:DFF], in_=f_.ap[:, 0:DFF]),
                             reads=[f_.b], writes=[b_.b])
                    S.op("sp", lambda e, b_=b_, l=l, c=c: e.dma_start(
                        out=wb_up[l, :, :, c, :].rearrange("j p n -> p j n"),
                        in_=b_.ap[:, 0:DFF].rearrange("p (j n) -> p j n", n=128)),
                        reads=[b_.b], writes=[dbuf("wb_up", l)], dma="ws%d" % (step % NWB))
                    step += 1
            xt = fb[step % NWB]
            S.op("pool", lambda e: e.memset(xt.ap[:, 0:D], 0.0), writes=[xt.b])
            S.op("sp", lambda e: e.dma_start(out=xt.ap[0:DS, 0:D], in_=x_s[:, :]), reads=[xt.b], writes=[xt.b], dma="wl%d" % (step % NWB))
            S.op("sp", lambda e: e.dma_start(out=xres[NT], in_=xt.ap[:, 0:D]), reads=[xt.b], writes=[dbuf("xres", NT)], dma="ws%d" % (step % NWB))
            S.barrier()
            A.off = mark

        def make_tail_tiles():
            d = {}
            d["junk"] = A.alloc(D, BF16, "tjunk")
            d["ss"] = [A.alloc(8, F32, "tss%d" % i) for i in range(4)]
            d["tmp"] = A.alloc(D, F32, "ttmp")
            d["xn"] = Ring([A.alloc(D, F32, "txn%d" % i) for i in range(2)])
            d["h2"] = A.alloc(D, BF16, "th2")
            d["h2T"] = Ring([A.alloc(D, BF16, "th2T%d" % i) for i in range(2)])
            return d

        def tail(layer, t, sl, x, tt, gpost, gpre2, tb):
            ssA = tt["ss"][(2 * t) % 4]
            ssB = tt["ss"][(2 * t + 1) % 4]
            stats(sl.ap, sl.b, D, tt["junk"], ssA)
            tmp = tt["tmp"]
            S.op("dve", lambda e: e.scalar_tensor_tensor(out=tmp.ap, in0=sl.ap, scalar=ssA.ap[:, 2:3], in1=gpost.ap, op0=ALU.mult, op1=ALU.mult),
                 reads=[sl.b, ssA.b, gpost.b], writes=[tmp.b])
            xn = tt["xn"].next()
            S.op("pool", lambda e: e.tensor_tensor(out=xn.ap, in0=x.ap, in1=tmp.ap, op=ALU.add), reads=[x.b, tmp.b], writes=[xn.b])
            nrow = 128 if t < NT else DS
            S.op("sp", lambda e: e.dma_start(out=xres[t, 0:nrow, :], in_=xn.ap[0:nrow, :]), reads=[xn.b], writes=[dbuf("xres", t)], dma="st_x%d" % (t % 2))
            stats(xn.ap, xn.b, D, tt["junk"], ssB)
            h2 = tt["h2"]
            S.op("dve", lambda e: e.scalar_tensor_tensor(out=h2.ap, in0=xn.ap, scalar=ssB.ap[:, 2:3], in1=gpre2.ap, op0=ALU.mult, op1=ALU.mult),
                 reads=[xn.b, ssB.b, gpre2.b], writes=[h2.b])
            h2T = tt["h2T"].next()
            transposes(h2, h2T, tb, "act")
            S.op("sp", lambda e: e.dma_start(out=H2T[:, qcol(t):qcol(t) + 128].rearrange("(c p) n -> p c n", p=128),
                                             in_=h2T.ap.rearrange("p (c n) -> p c n", c=8)),
                 reads=[h2T.b], writes=[dbuf("H2T", t)], dma="st_h%d" % (t % 2))

        def phase_fox_a(layer):
            j = layer // 2
            LF = A.alloc(NKS * NH, F32, "LF")
            after_lf = A.off
            w = A.alloc(8 * FW, BF16, "fw")
            w.ap = w.ap.rearrange("p (c n) -> p c n", c=8)

            def ld_w(c):
                S.op("sp", lambda e: e.dma_start(out=w.ap[:, c, :], in_=wb_fin[j, c * 128:(c + 1) * 128, :]),
                     reads=[dbuf("wb_fin", j)], writes=[w.b], dma="wld%d" % (c % 2))
            for c in range(8):
                ld_w(c)
            gpre = A.alloc(D, F32, "gpre")
            load_bcast(gpre, pre_mix[layer:layer + 1, :], "g0")
            gq = A.alloc(DH, F32, "gq")
            load_bcast(gq, fox_q_norm[j:j + 1, :], "g1")
            gk = A.alloc(DH, F32, "gk")
            load_bcast(gk, fox_k_norm[j:j + 1, :], "g2")
            bfb = A.alloc(NH, F32, "bfb")
            load_bcast(bfb, fox_b_f[j:j + 1, :], "g3")
            xr = Ring([A.alloc(D, F32, "x%d" % i) for i in range(2)])
            junk = A.alloc(D, BF16, "junk")
            ss = [A.alloc(8, F32, "ss%d" % i) for i in range(2)]
            h = A.alloc(D, BF16, "h")
            hT = A.alloc(D, BF16, "hT")
            sq = A.alloc(D, F32, "sq")
            sq2 = A.alloc(D, F32, "sq2")
            ssq = [A.alloc(64, F32, "ssq%d" % i) for i in range(2)]
            qn = A.alloc(D, BF16, "qn")
            kf = Ring([A.alloc(D, F32, "kf%d" % i) for i in range(2)])
            kb = A.alloc(D, BF16, "kb")
            vf = Ring([A.alloc(D, F32, "vf%d" % i) for i in range(2)])
            vb = Ring([A.alloc(D, BF16, "vb%d" % i) for i in range(2)])
            QTs = Ring([A.alloc(D, BF16, "QTs%d" % i) for i in range(2)])
            KTs = Ring([A.alloc(D, BF16, "KTs%d" % i) for i in range(2)])
            GTs = Ring([A.alloc(D, BF16, "GTs%d" % i) for i in range(2)])
            f1 = [A.alloc(64, F32, "f1_%d" % i) for i in range(2)]
            slots = Ring([slot2(1), slot2(3), slot2(5)])
            tb = bank[0]
            fzb = bank[7]

            def k_to_scratch(kbt, col):
                KTt = KTs.next()
                transposes(kbt, KTt, tb, "dve")
                S.op("sp", lambda e: e.dma_start(out=KT[:, col:col + 128].rearrange("(c p) n -> p c n", p=128),
                                                 in_=KTt.ap.rearrange("p (c n) -> p c n", c=8)),
                     reads=[KTt.b], writes=[dbuf("KT", col)], dma="st_kt%d" % (KTs.i % 2))

            def v_to_scratch(vbt, col, par):
                S.op("sp", lambda e: e.dma_start(out=VT[col:col + 128, :], in_=vbt.ap), reads=[vbt.b], writes=[dbuf("VT", col)],
                     dma="st_vt%d" % par)

            def do_past(jt):
                col = T + jt * 128
                kft = kf.next()
                S.op("sp", lambda e: e.dma_start(out=kft.ap, in_=ck[j, jt * 128:(jt + 1) * 128, :]), writes=[kft.b], dma="ldk%d" % (kf.i % 2))
                S.op("pool", lambda e: e.tensor_copy(out=kb.ap, in_=kft.ap), reads=[kft.b], writes=[kb.b])
                k_to_scratch(kb, col)
                vft = vf.next()
                S.op("sp", lambda e: e.dma_start(out=vft.ap, in_=cv[j, jt * 128:(jt + 1) * 128, :]), writes=[vft.b], dma="ldv%d" % (vf.i % 2))
                vbt = vb.next()
                S.op("pool", lambda e: e.tensor_copy(out=vbt.ap, in_=vft.ap), reads=[vft.b], writes=[vbt.b])
                v_to_scratch(vbt, col, vb.i % 2)
                slot_ = NT + jt
                S.op("sp", lambda e: e.dma_start(out=LF.ap[:, slot_ * NH:(slot_ + 1) * NH], in_=clf[j, jt * 128:(jt + 1) * 128, :]),
                     writes=[LF.b], dma="ldlf")
            for jt in range(NP):
                do_past(jt)

            def load_x(t):
                xt = xr.next()
                src, rd = xsrc(layer, t)
                S.op("sp", lambda e: e.dma_start(out=xt.ap, in_=src), reads=rd, writes=[xt.b], dma="ldx%d" % (t % 2))
                return xt

            def headnorm(which, sl_, t):
                ssq_ = ssq[t % 2]
                nv = 128 if t < NT else DS
                sqt = sq if which == "q" else sq2
                o0 = 0 if which == "q" else 32
                sq3 = sqt.ap.rearrange("p (h d) -> p h d", h=NH)
                S.op("act", lambda e: e.activation(out=sqt.ap, in_=sl_.ap, func=AF.Square), reads=[sl_.b], writes=[sqt.b])
                S.op("dve", lambda e: e.tensor_reduce(out=ssq_.ap[:, o0:o0 + NH], in_=sq3, axis=AX.X, op=ALU.add), reads=[sqt.b], writes=[ssq_.b])
                S.op("act", lambda e: e.activation(out=ssq_.ap[:, o0 + NH:o0 + 2 * NH], in_=ssq_.ap[:, o0:o0 + NH], func=AF.Sqrt,
                                                   scale=1.0 / DH, bias=EPS), reads=[ssq_.b], writes=[ssq_.b])
                S.op("dve", lambda e: e.reciprocal(out=ssq_.ap[:, o0:o0 + NH], in_=ssq_.ap[:, o0 + NH:o0 + 2 * NH]),
                     reads=[ssq_.b], writes=[ssq_.b])
                if which == "q":
                    S.op("dve", lambda e: e.tensor_scalar(out=ssq_.ap[:, o0:o0 + NH], in0=ssq_.ap[:, o0:o0 + NH], scalar1=DH ** -0.5, scalar2=None, op0=ALU.mult),
                         reads=[ssq_.b], writes=[ssq_.b])
                S.op("dve", lambda e: e.tensor_tensor(
                    out=sq3, in0=sl_.ap.rearrange("p (h d) -> p h d", h=NH),
                    in1=ssq_.ap[:, o0:o0 + NH].unsqueeze(2).to_broadcast([128, NH, DH]), op=ALU.mult),
                    reads=[sl_.b, ssq_.b], writes=[sqt.b])
                if which == "q":
                    S.op("pool", lambda e: e.tensor_tensor(
                        out=qn.ap.rearrange("p (h d) -> p h d", h=NH), in0=sq3,
                        in1=gq.ap.unsqueeze(1).to_broadcast([128, NH, DH]), op=ALU.mult), reads=[sqt.b, gq.b], writes=[qn.b])
                else:
                    kft = kf.next()
                    S.op("pool", lambda e: e.tensor_tensor(
                        out=kft.ap.rearrange("p (h d) -> p h d", h=NH), in0=sq3,
                        in1=gk.ap.unsqueeze(1).to_broadcast([128, NH, DH]), op=ALU.mult), reads=[sqt.b, gk.b], writes=[kft.b])
                    S.op("pool", lambda e: e.tensor_copy(out=kb.ap, in_=kft.ap), reads=[kft.b], writes=[kb.b])
                    kdst = k_p[j, t * 128:(t + 1) * 128, :] if t < NT else k_s[j, :, :]
                    S.op("sp", lambda e: e.dma_start(out=kdst, in_=kft.ap[0:nv, :]), reads=[kft.b], dma="ldk%d" % (kf.i % 2))

            def do_tile(t, x):
                nv = 128 if t < NT else DS
                ss_ = ss[t % 2]
                stats(x.ap, x.b, D, junk, ss_)
                S.op("dve", lambda e: e.scalar_tensor_tensor(out=h.ap, in0=x.ap, scalar=ss_.ap[:, 2:3], in1=gpre.ap, op0=ALU.mult, op1=ALU.mult),
                     reads=[x.b, ss_.b, gpre.b], writes=[h.b])
                transposes(h, hT, tb, "act")
                slq = slots.next()
                proj_tok(hT, w, 0, slq)
                slk = slots.next()
                proj_tok(hT, w, D, slk)
                slv = slots.next()
                proj_tok(hT, w, 2 * D, slv)

                def f_fz(e):
                    ins = None
                    for kc in range(8):
                        ins = e.matmul(fzb.ap[:, 0:NH], lhsT=hT.ap[:, kc * 128:(kc + 1) * 128], rhs=w.ap[:, kc, 4 * D:4 * D + NH],
                                       start=(kc == 0), stop=(kc == 7))
                    return ins
                S.op("pe", f_fz, reads=[hT.b, w.b], writes=[fzb.b])
                headnorm("q", slq, t)
                headnorm("k", slk, t)
                vft = vf.next()
                S.op("act", lambda e: e.activation(out=vft.ap, in_=slv.ap, func=AF.Copy), reads=[slv.b], writes=[vft.b])
                vbt = vb.next()
                S.op("pool", lambda e: e.tensor_copy(out=vbt.ap, in_=vft.ap), reads=[vft.b], writes=[vbt.b])
                vdst = v_p[j, t * 128:(t + 1) * 128, :] if t < NT else v_s[j, :, :]
                S.op("sp", lambda e: e.dma_start(out=vdst, in_=vft.ap[0:nv, :]), reads=[vft.b], dma="ldv%d" % (vf.i % 2))
                v_to_scratch(vbt, kcol(t), vb.i % 2)
                slg = slots.next()

                def f_g(e):
                    ins = None
                    for c in range(8):
                        for kc in range(8):
                            ins = e.matmul(slg.ap[:, c * 128:(c + 1) * 128], lhsT=w.ap[:, kc, 3 * D + c * 128:3 * D + (c + 1) * 128],
                                           rhs=hT.ap[:, kc * 128:(kc + 1) * 128], start=(kc == 0), stop=(kc == 7))
                    return ins
                S.op("pe", f_g, reads=[hT.b, w.b], writes=[slg.b])
                GTt = GTs.next()
                S.op("act", lambda e: e.activation(out=GTt.ap, in_=slg.ap, func=AF.Sigmoid), reads=[slg.b], writes=[GTt.b])
                S.op("sp", lambda e: e.dma_start(out=GT[:, qcol(t):qcol(t) + 128].rearrange("(c p) n -> p c n", p=128),
                                                 in_=GTt.ap.rearrange("p (c n) -> p c n", c=8)),
                     reads=[GTt.b], writes=[dbuf("GT", t)], dma="st_gt%d" % (t % 2))
                f1_ = f1[t % 2]
                slot_ = t if t < NT else NT + NP
                lfv = LF.ap[:, slot_ * NH:(slot_ + 1) * NH]
                S.op("dve", lambda e: e.tensor_tensor(out=f1_.ap[:, 0:NH], in0=fzb.ap[:, 0:NH], in1=bfb.ap, op=ALU.add),
                     reads=[fzb.b, bfb.b], writes=[f1_.b])
                S.op("act", lambda e: e.activation(out=f1_.ap[:, 16:32], in_=f1_.ap[:, 0:NH], func=AF.Exp, scale=-1.0), reads=[f1_.b], writes=[f1_.b])
                S.op("act", lambda e: e.activation(out=f1_.ap[:, 32:48], in_=f1_.ap[:, 16:32], func=AF.Ln, bias=1.0), reads=[f1_.b], writes=[f1_.b])
                S.op("dve", lambda e: e.tensor_scalar(out=lfv, in0=f1_.ap[:, 32:48], scalar1=-1.0, scalar2=None, op0=ALU.mult),
                     reads=[f1_.b], writes=[LF.b])
                ldst = lf_p[j, t * 128:(t + 1) * 128, :] if t < NT else lf_s[j, :, :]
                S.op("sp", lambda e: e.dma_start(out=ldst, in_=lfv[0:nv, :]), reads=[LF.b], dma="st_lf")
                QTt = QTs.next()
                transposes(qn, QTt, tb, "act")
                S.op("sp", lambda e: e.dma_start(out=QT[:, qcol(t):qcol(t) + 128].rearrange("(c p) n -> p c n", p=128),
                                                 in_=QTt.ap.rearrange("p (c n) -> p c n", c=8)),
                     reads=[QTt.b], writes=[dbuf("QT", t)], dma="st_qt%d" % (t % 2))
                k_to_scratch(kb, kcol(t))

            xt_next = load_x(0)
            for t in range(NTS):
                x = xt_next
                if t + 1 < NTS:
                    xt_next = load_x(t + 1)
                do_tile(t, x)
            S.barrier()
            A.off = after_lf
            return LF

        def phase_fox_a2(layer, LF):
            CT = A.alloc(KC, F32, "CT")
            psb = Ring([bank[1], bank[2], bank[3], bank[4]])
            ngrp = (NKS + 3) // 4

            def do_grp(g):
                s0 = g * 4
                ns = min(4, NKS - s0)
                pb = psb.next()

                def f(e):
                    ins = None
                    for i in range(ns):
                        s_ = s0 + i
                        ins = e.matmul(pb.ap[0:NH, i * 128:(i + 1) * 128], lhsT=LF.ap[:, s_ * NH:(s_ + 1) * NH], rhs=trif, start=True, stop=True)
                    return ins
                S.op("pe", f, reads=[LF.b] + CONSTB, writes=[pb.b])
                S.op("act", lambda e: e.activation(out=CT.ap[0:NH, s0 * 128:(s0 + ns) * 128], in_=pb.ap[0:NH, 0:ns * 128], func=AF.Copy),
                     reads=[pb.b], writes=[CT.b])
            for g in range(ngrp):
                do_grp(g)

            def fix(s_):
                S.op("dve", lambda e: e.tensor_scalar(out=CT.ap[0:NH, s_ * 128:(s_ + 1) * 128], in0=CT.ap[0:NH, s_ * 128:(s_ + 1) * 128],
                                                      scalar1=CT.ap[0:NH, s_ * 128 - 1:s_ * 128], scalar2=None, op0=ALU.add),
                     reads=[CT.b], writes=[CT.b])
            for s_ in range(1, NKS):
                if s_ == NT:
                    continue
                fix(s_)
            CW = 1024
            r1 = A.alloc(CW, F32, "r1")
            r2 = A.alloc(CW, F32, "r2")
            o6 = Ring([A.alloc(6 * CW, BF16, "o6_%d" % i) for i in range(2)])

            def do_chunk(c0):
                w_ = min(CW, KC - c0)
                o_ = o6.next()
                ov = o_.ap.rearrange("p (a n) -> p a n", a=6)
                cs = CT.ap[0:NH, c0:c0 + w_]
                S.op("act", lambda e: e.activation(out=ov[0:NH, 0, 0:w_], in_=cs, func=AF.Copy), reads=[CT.b], writes=[o_.b])
                S.op("dve", lambda e: e.tensor_tensor(out=r1.ap[0:NH, 0:w_], in0=cs, in1=ov[0:NH, 0, 0:w_], op=ALU.subtract),
                     reads=[CT.b, o_.b], writes=[r1.b])
                S.op("act", lambda e: e.activation(out=ov[0:NH, 1, 0:w_], in_=r1.ap[0:NH, 0:w_], func=AF.Copy), reads=[r1.b], writes=[o_.b])
                S.op("dve", lambda e: e.tensor_tensor(out=r2.ap[0:NH, 0:w_], in0=r1.ap[0:NH, 0:w_], in1=ov[0:NH, 1, 0:w_], op=ALU.subtract),
                     reads=[r1.b, o_.b], writes=[r2.b])
                S.op("act", lambda e: e.activation(out=ov[0:NH, 2, 0:w_], in_=r2.ap[0:NH, 0:w_], func=AF.Copy), reads=[r2.b], writes=[o_.b])
                S.op("pool", lambda e: e.tensor_scalar(out=ov[0:NH, 3:6, 0:w_], in0=ov[0:NH, 0:3, 0:w_], scalar1=-1.0, scalar2=None, op0=ALU.mult),
                     reads=[o_.b], writes=[o_.b])
                S.op("sp", lambda e: e.dma_start(out=CTs[:, :, c0:c0 + w_], in_=ov[0:NH, :, 0:w_]),
                     reads=[o_.b], writes=[dbuf("CTs", c0)], dma="st_ct%d" % (o6.i % 2))
            for c0 in range(0, KC, CW):
                do_chunk(c0)
            S.barrier()
            A.off = PERS

        def phase_fox_b(layer):
            NQB = T // 512
            sets = []
            for i in range(2):
                d = {}
                d["QA"] = A.alloc(TS, BF16, "QA%d" % i)
                d["KA"] = A.alloc(KC, BF16, "KA%d" % i)
                d["VA"] = A.alloc(NKS * 128, BF16, "VA%d" % i)
                d["G"] = A.alloc(TS, BF16, "G%d" % i)
                d["VAv"] = d["VA"].ap.rearrange("p (s n) -> p s n", n=128)
                S.op("pool", lambda e, d=d: e.memset(d["QA"].ap[64:70, :], 1.0), writes=[d["QA"].b])
                S.op("pool", lambda e, d=d: e.memset(d["KA"].ap[64:70, :], 1.0), writes=[d["KA"].b])
                S.op("pool", lambda e, d=d: e.memset(d["VA"].ap, 1.0), writes=[d["VA"].b])
                sets.append(d)
            PT = Ring([A.alloc(512, BF16, "PT%d" % i) for i in range(6)])
            rc = Ring([A.alloc(512, F32, "rc%d" % i) for i in range(2)])
            tmp = Ring([A.alloc(512, F32, "otmp%d" % i) for i in range(2)])
            og = Ring([A.alloc(512, BF16, "og%d" % i) for i in range(2)])
            psS = Ring([bank[0], bank[1], bank[2], bank[3], bank[6], bank[7]])
            psO = Ring([bank[4], bank[5]])
            allQT = [dbuf("QT", t) for t in range(NTS)]
            allGT = [dbuf("GT", t) for t in range(NTS)]
            allKT = [dbuf("KT", c) for c in [t * 128 for t in range(NT)] + [T + i * 128 for i in range(NP + 1)]]
            allVT = [dbuf("VT", c) for c in [t * 128 for t in range(NT)] + [T + i * 128 for i in range(NP + 1)]]
            allCT = [dbuf("CTs", c0) for c0 in range(0, KC, 1024)]

            def load_head(hd):
                d = sets[hd % 2]
                p = hd % 2
                r0 = hd * DH
                S.op("sp", lambda e: e.dma_start(out=d["QA"].ap[0:64, :], in_=QT[r0:r0 + DH, :]), reads=allQT, writes=[d["QA"].b], dma="la%d" % p)
                S.op("sp", lambda e: e.dma_start(out=d["QA"].ap[64:67, 0:T], in_=CTs[hd, 0:3, 0:T]), reads=allCT, writes=[d["QA"].b], dma="lb%d" % p)
                S.op("sp", lambda e: e.dma_start(out=d["QA"].ap[64:67, T:TS], in_=CTs[hd, 0:3, T + PAST:KC]), reads=allCT, writes=[d["QA"].b], dma="lc%d" % p)
                S.op("sp", lambda e: e.dma_start(out=d["KA"].ap[0:64, :], in_=KT[r0:r0 + DH, :]), reads=allKT, writes=[d["KA"].b], dma="ld%d" % p)
                S.op("sp", lambda e: e.dma_start(out=d["KA"].ap[67:70, :], in_=CTs[hd, 3:6, :]), reads=allCT, writes=[d["KA"].b], dma="le%d" % p)
                vo = 0 if p == 0 else 64
                step = 16
                for s0 in range(0, NKS, step):
                    s1 = min(NKS, s0 + step)
                    S.op("sp", lambda e, s0=s0, s1=s1: e.dma_start(out=d["VAv"][:, s0:s1, vo:vo + DH],
                                                                  in_=VT[s0 * 128:s1 * 128, r0:r0 + DH].rearrange("(s p) d -> p s d", p=128)),
                         reads=allVT, writes=[d["VA"].b], dma="lf%d" % p)
                S.op("sp", lambda e: e.dma_start(out=d["G"].ap[vo:vo + 64, :], in_=GT[r0:r0 + DH, :]), reads=allGT, writes=[d["G"].b], dma="lg%d" % p)

            def finish(hd, po, ncols, qc0):
                d = sets[hd % 2]
                p = hd % 2
                orow = slice(0, 64) if p == 0 else slice(64, 128)
                drow = slice(64, 128) if p == 0 else slice(0, 64)
                rc_ = rc.next()
                tmp_ = tmp.next()
                og_ = og.next()
                S.op("dve", lambda e: e.reciprocal(out=rc_.ap[drow, 0:ncols], in_=po.ap[drow, 0:ncols]), reads=[po.b], writes=[rc_.b])
                S.op("dve", lambda e: e.tensor_tensor(out=tmp_.ap[orow, 0:ncols], in0=po.ap[orow, 0:ncols], in1=rc_.ap[drow, 0:ncols], op=ALU.mult),
                     reads=[po.b, rc_.b], writes=[tmp_.b])
                S.op("pool", lambda e: e.tensor_tensor(out=og_.ap[orow, 0:ncols], in0=tmp_.ap[orow, 0:ncols], in1=d["G"].ap[orow, qc0:qc0 + ncols], op=ALU.mult),
                     reads=[tmp_.b, d["G"].b], writes=[og_.b])
                S.op("sp", lambda e: e.dma_start(out=OGT[hd * DH:(hd + 1) * DH, qc0:qc0 + ncols], in_=og_.ap[orow, 0:ncols]),
                     reads=[og_.b], writes=[dbuf("OGT", qc0 // 512)], dma="st_og%d" % (og.i % 2))

            def do_head(hd):
                d = sets[hd % 2]
                QA, KA, VA, VAv = d["QA"], d["KA"], d["VA"], d["VAv"]

                def s_step(kt, qb):
                    off = max(0, kt - 4 * qb) * 128
                    ps_ = psS.next()
                    q0 = qb * 512

                    def f(e):
                        if kt >= 4 * qb:
                            e.matmul(ps_.ap[:, off:off + 128], lhsT=idb, rhs=masknegb, start=True, stop=False)
                            ins = e.matmul(ps_.ap[:, off:off + 128], lhsT=KA.ap[0:70, kt * 128:(kt + 1) * 128],
                                           rhs=QA.ap[0:70, q0 + off:q0 + off + 128], start=False, stop=True)
                            if off + 128 < 512:
                                ins = e.matmul(ps_.ap[:, off + 128:512], lhsT=KA.ap[0:70, kt * 128:(kt + 1) * 128],
                                               rhs=QA.ap[0:70, q0 + off + 128:q0 + 512], start=True, stop=True)
                            return ins
                        return e.matmul(ps_.ap[:, 0:512], lhsT=KA.ap[0:70, kt * 128:(kt + 1) * 128], rhs=QA.ap[0:70, q0:q0 + 512],
                                        start=True, stop=True)
                    S.op("pe", f, reads=[KA.b, QA.b] + CONSTB, writes=[ps_.b])
                    pt_ = PT.next()
                    S.op("act", lambda e: e.activation(out=pt_.ap[:, off:512], in_=ps_.ap[:, off:512], func=AF.Exp), reads=[ps_.b], writes=[pt_.b])
                    return (kt, off, pt_)

                def pv_step(item, po, nkt):
                    kt, off, pt_ = item
                    S.op("pe", lambda e: e.matmul(po.ap[:, off:512], lhsT=VAv[:, kt, :], rhs=pt_.ap[:, off:512], start=(kt == 0), stop=(kt == nkt - 1),
                                                  skip_group_check=True),
                         reads=[VA.b, pt_.b], writes=[po.b])

                for qb in range(NQB):
                    po = psO.next()
                    nkt = 4 * qb + 4
                    pend = []
                    for kt in range(nkt):
                        pend.append(s_step(kt, qb))
                        if len(pend) > 3:
                            pv_step(pend.pop(0), po, nkt)
                    while pend:
                        pv_step(pend.pop(0), po, nkt)
                    finish(hd, po, 512, qb * 512)

                po = psO.next()

                def samp_step(kt):
                    ps_ = psS.next()
                    kc0 = T + kt * 128
                    pt_ = PT.next()
                    if kt < NP:
                        S.op("pe", lambda e: e.matmul(ps_.ap[:, 0:128], lhsT=KA.ap[0:70, kc0:kc0 + 128], rhs=QA.ap[0:70, T:TS], start=True, stop=True),
                             reads=[KA.b, QA.b], writes=[ps_.b])
                        S.op("act", lambda e: e.activation(out=pt_.ap[:, 0:128], in_=ps_.ap[:, 0:128], func=AF.Exp), reads=[ps_.b], writes=[pt_.b])
                        S.op("pe", lambda e: e.matmul(po.ap[:, 0:128], lhsT=VAv[:, NT + kt, :], rhs=pt_.ap[:, 0:128], start=(kt == 0), stop=False,
                                                      skip_group_check=True),
                             reads=[VA.b, pt_.b], writes=[po.b])
                    else:
                        def f(e):
                            e.matmul(ps_.ap[0:DS, 0:128], lhsT=idb[0:DS, 0:DS], rhs=masknegb[0:DS, :], start=True, stop=False)
                            return e.matmul(ps_.ap[0:DS, 0:128], lhsT=KA.ap[0:70, kc0:kc0 + DS], rhs=QA.ap[0:70, T:TS], start=False, stop=True)
                        S.op("pe", f, reads=[KA.b, QA.b] + CONSTB, writes=[ps_.b])
                        S.op("act", lambda e: e.activation(out=pt_.ap[0:DS, 0:128], in_=ps_.ap[0:DS, 0:128], func=AF.Exp), reads=[ps_.b], writes=[pt_.b])
                        S.op("pe", lambda e: e.matmul(po.ap[:, 0:128], lhsT=VAv[0:DS, NT + kt, :], rhs=pt_.ap[0:DS, 0:128], start=(NP == 0), stop=True,
                                                      skip_group_check=True),
                             reads=[VA.b, pt_.b], writes=[po.b])
                for kt in range(NP + 1):
                    samp_step(kt)
                finish(hd, po, 128, T)

            load_head(0)
            for hd in range(NH):
                if hd + 1 < NH:
                    load_head(hd + 1)
                do_head(hd)
            S.barrier()
            A.off = PERS

        def phase_fox_c1(layer):
            j = layer // 2
            wo = A.alloc(8 * D, BF16, "wo")
            wo.ap = wo.ap.rearrange("p (c n) -> p c n", c=8)
            S.op("sp", lambda e: e.dma_start(out=wo.ap, in_=wb_fout[j].rearrange("(c p) n -> p c n", p=128)), reads=[dbuf("wb_fout", j)], writes=[wo.b], dma="wld0")
            gpost = A.alloc(D, F32, "gpost")
            load_bcast(gpost, post_mix[layer:layer + 1, :], "g0")
            gpre2 = A.alloc(D, F32, "gpre2")
            load_bcast(gpre2, pre_ffn[layer:layer + 1, :], "g1")
            tt = make_tail_tiles()
            xr = Ring([A.alloc(D, F32, "x%d" % i) for i in range(2)])
            ogr = Ring([A.alloc(D, BF16, "ogb%d" % i) for i in range(2)])
            slots = Ring([slot2(1), slot2(3), slot2(5)])
            allOG = [dbuf("OGT", q) for q in range(T // 512 + 1)]

            def load(t):
                xt = xr.next()
                src, rd = xsrc(layer, t)
                S.op("sp", lambda e: e.dma_start(out=xt.ap, in_=src), reads=rd, writes=[xt.b], dma="ldx%d" % (t % 2))
                ogt = ogr.next()
                S.op("sp", lambda e: e.dma_start(out=ogt.ap.rearrange("p (c n) -> p c n", c=8),
                                                 in_=OGT[:, qcol(t):qcol(t) + 128].rearrange("(c p) n -> p c n", p=128)),
                     reads=[dbuf("OGT", qcol(t) // 512)], writes=[ogt.b], dma="ldo%d" % (t % 2))
                return xt, ogt

            def do_tile(t, x, ogt):
                sl = slots.next()
                proj_tok(ogt, wo, 0, sl)
                tail(layer, t, sl, x, tt, gpost, gpre2, bank[0])

            nxt = load(0)
            for t in range(NTS):
                x, ogt = nxt
                if t + 1 < NTS:
                    nxt = load(t + 1)
                do_tile(t, x, ogt)
            S.barrier()
            A.off = PERS

        def phase_ffn(layer):
            last = (layer == NL - 1)
            wd = A.alloc(32 * D, BF16, "wd")
            wd.ap = wd.ap.rearrange("p (c n) -> p c n", c=32)
            for q in range(4):
                S.op("sp", lambda e, q=q: e.dma_start(out=wd.ap[:, q * 8:(q + 1) * 8, :],
                                                      in_=wb_down[layer, q * 1024:(q + 1) * 1024, :].rearrange("(c p) n -> p c n", p=128)),
                     reads=[dbuf("wb_down", layer)], writes=[wd.b], dma="wld%d" % (q % 2))
            gpost = A.alloc(D, F32, "gpostf")
            load_bcast(gpost, post_ffn[layer:layer + 1, :], "g0")
            NWR = 6
            wur = Ring([A.alloc(D, BF16, "wu%d" % i) for i in range(NWR)])
            hb = Ring([A.alloc(8 * 512, BF16, "hb%d" % i) for i in range(2)])
            xb = Ring([A.alloc(4 * D, F32, "xb%d" % i) for i in range(2)])
            u2 = A.alloc(32 * 512, BF16, "u2")
            u2v = u2.ap.rearrange("p (j n) -> p j n", j=32)
            rr = Ring([A.alloc(512, F32, "rr%d" % i) for i in range(3)])
            junk = A.alloc(D, BF16, "fjunk")
            ssr = Ring([A.alloc(8, F32, "fss%d" % i) for i in range(4)])
            tmp = A.alloc(D, F32, "ftmp")
            xo = Ring([A.alloc(D, F32, "xo%d" % i) for i in range(2)])
            psU = Ring([bank[0], bank[1], bank[2]])
            psD = Ring([slot2(3), slot2(5)])
            blocks = []
            t0 = 0
            while t0 < NTS:
                nt_ = min(4, NTS - t0)
                if t0 < NT and t0 + nt_ > NT:
                    nt_ = NT - t0
                blocks.append((t0, nt_))
                t0 += nt_
            wcount = [0]

            def load_w(jj):
                wt = wur.next()
                S.op("sp", lambda e: e.dma_start(out=wt.ap, in_=wb_up[layer, jj].rearrange("p c n -> p (c n)")),
                     reads=[dbuf("wb_up", layer)], writes=[wt.b], dma="lwu%d" % (wur.i % NWR))
                return wt

            def load_blk(bi):
                t0, nt_ = blocks[bi]
                ntok = nt_ * 128
                h_ = hb.next()
                S.op("sp", lambda e: e.dma_start(out=h_.ap.rearrange("p (c n) -> p c n", c=8)[:, :, 0:ntok],
                                                 in_=H2T[:, qcol(t0):qcol(t0) + ntok].rearrange("(c p) n -> p c n", p=128)),
                     reads=[dbuf("H2T", t) for t in range(t0, t0 + nt_)], writes=[h_.b], dma="lhb%d" % (bi % 2))
                x_ = xb.next()
                S.op("sp", lambda e: e.dma_start(out=x_.ap.rearrange("p (t d) -> p t d", t=4)[:, 0:nt_, :],
                                                 in_=xres[t0:t0 + nt_].rearrange("t p d -> p t d")),
                     reads=[dbuf("xres", t) for t in range(t0, t0 + nt_)], writes=[x_.b], dma="lxb%d" % (bi % 2))
                return h_, x_

            PRE = 4
            wq = []
            total_w = len(blocks) * 32
            for i in range(min(PRE, total_w)):
                wq.append(load_w(i % 32))
            wissued = [len(wq)]

            def do_up(jj, h_, ntok):
                hv = h_.ap.rearrange("p (c n) -> p c n", c=8)
                wt = wq.pop(0)
                if wissued[0] < total_w:
                    wq.append(load_w(wissued[0] % 32))
                    wissued[0] += 1
                wv = wt.ap.rearrange("p (c n) -> p c n", c=8)
                pu = psU.next()

                def f(e):
                    ins = None
                    for kc in range(8):
                        ins = e.matmul(pu.ap[:, 0:ntok], lhsT=wv[:, kc, :], rhs=hv[:, kc, 0:ntok], start=(kc == 0), stop=(kc == 7))
                    return ins
                S.op("pe", f, reads=[wt.b, h_.b], writes=[pu.b])
                r_ = rr.next()
                S.op("act", lambda e: e.activation(out=r_.ap[:, 0:ntok], in_=pu.ap[:, 0:ntok], func=AF.Relu), reads=[pu.b], writes=[r_.b])
                S.op("pool", lambda e: e.tensor_tensor(out=u2v[:, jj, 0:ntok], in0=r_.ap[:, 0:ntok], in1=r_.ap[:, 0:ntok], op=ALU.mult),
                     reads=[r_.b], writes=[u2.b])

            def do_down(t, ti, x_):
                pd = psD.next()

                def f(e):
                    ins = None
                    for half in range(2):
                        for jj in range(32):
                            ins = e.matmul(pd.ap[:, half * 512:(half + 1) * 512], lhsT=u2v[:, jj, ti * 128:(ti + 1) * 128],
                                           rhs=wd.ap[:, jj, half * 512:(half + 1) * 512], start=(jj == 0), stop=(jj == 31))
                    return ins
                S.op("pe", f, reads=[u2.b, wd.b], writes=[pd.b])
                ss_ = ssr.next()
                stats(pd.ap, pd.b, D, junk, ss_)
                S.op("dve", lambda e: e.scalar_tensor_tensor(out=tmp.ap, in0=pd.ap, scalar=ss_.ap[:, 2:3], in1=gpost.ap, op0=ALU.mult, op1=ALU.mult),
                     reads=[pd.b, ss_.b, gpost.b], writes=[tmp.b])
                xo_ = xo.next()
                xin = x_.ap[:, ti * D:(ti + 1) * D]
                S.op("pool", lambda e: e.tensor_tensor(out=xo_.ap, in0=xin, in1=tmp.ap, op=ALU.add), reads=[x_.b, tmp.b], writes=[xo_.b])
                if last:
                    if t < NT:
                        S.op("sp", lambda e: e.dma_start(out=y_p[t * 128:(t + 1) * 128, :], in_=xo_.ap), reads=[xo_.b], dma="st_y%d" % (t % 2))
                    else:
                        S.op("sp", lambda e: e.dma_start(out=y_s[:, :], in_=xo_.ap[0:DS, :]), reads=[xo_.b], dma="st_y%d" % (t % 2))
                else:
                    nrow = 128 if t < NT else DS
                    S.op("sp", lambda e: e.dma_start(out=xres[t, 0:nrow, :], in_=xo_.ap[0:nrow, :]), reads=[xo_.b], writes=[dbuf("xres", t)], dma="st_y%d" % (t % 2))

            nxt = load_blk(0)
            for bi, (t0, nt_) in enumerate(blocks):
                h_, x_ = nxt
                if bi + 1 < len(blocks):
                    nxt = load_blk(bi + 1)
                for jj in range(32):
                    do_up(jj, h_, nt_ * 128)
                for ti in range(nt_):
                    do_down(t0 + ti, ti, x_)
            S.barrier()
            A.off = PERS

        def phase_hgrn(layer):
            j = layer // 2
            w = A.alloc(8 * 4 * D, BF16, "hw")
            w.ap = w.ap.rearrange("p (c n) -> p c n", c=8)

            def ld_w(c):
                S.op("sp", lambda e: e.dma_start(out=w.ap[:, c, :], in_=wb_hin[j, c * 128:(c + 1) * 128, :]),
                     reads=[dbuf("wb_hin", j)], writes=[w.b], dma="wld%d" % (c % 2))
            for c in range(8):
                ld_w(c)
            wo = A.alloc(8 * D, BF16, "hwo")
            wo.ap = wo.ap.rearrange("p (c n) -> p c n", c=8)
            S.op("sp", lambda e: e.dma_start(out=wo.ap, in_=wb_hout[j].rearrange("(c p) n -> p c n", p=128)), reads=[dbuf("wb_hout", j)], writes=[wo.b], dma="wld0")
            gpre = A.alloc(D, F32, "gpre")
            load_bcast(gpre, pre_mix[layer:layer + 1, :], "g0")
            gout = A.alloc(D, F32, "gout")
            load_bcast(gout, hgrn_out_norm[j:j + 1, :], "g1")
            gpost = A.alloc(D, F32, "gpost")
            load_bcast(gpost, post_mix[layer:layer + 1, :], "g2")
            gpre2 = A.alloc(D, F32, "gpre2")
            load_bcast(gpre2, pre_ffn[layer:layer + 1, :], "g3")
            om = oml[layer]
            tt = make_tail_tiles()
            xr = Ring([A.alloc(D, F32, "x%d" % i) for i in range(3)])
            junk = tt["junk"]
            ss = [A.alloc(8, F32, "ss%d" % i) for i in range(4)]
            h = A.alloc(D, BF16, "h")
            hT = A.alloc(D, BF16, "hT")
            qf = A.alloc(D, F32, "qf")
            kk = A.alloc(D, F32, "kk")
            lg = A.alloc(D, F32, "lg")
            dcl = A.alloc(D, F32, "dcl")
            E2 = A.alloc(D, F32, "E2")
            qt = A.alloc(D, BF16, "qt")
            vbr = [A.alloc(D, BF16, "vb%d" % i) for i in range(2)]
            gtr = [A.alloc(D, F32, "gt%d" % i) for i in range(2)]
            ktr = [A.alloc(D, BF16, "kt%d" % i) for i in range(2)]
            qTr = [A.alloc(D, BF16, "qT%d" % i) for i in range(2)]
            kTr = [A.alloc(D, BF16, "kT%d" % i) for i in range(2)]
            decr = [A.alloc(32, F32, "dec%d" % i) for i in range(2)]
            ATs = Ring([A.alloc(128, BF16, "ATs%d" % i) for i in range(2)])
            AT32 = Ring([A.alloc(128, F32, "AT32_%d" % i) for i in range(2)])
            Sst = A.alloc(HH * 128, F32, "Sst")
            Sb = Ring([A.alloc(128, BF16, "Sb%d" % i) for i in range(2)])
            T1 = Ring([A.alloc(128, F32, "T1_%d" % i) for i in range(2)])
            on2 = A.alloc(D, BF16, "on2")
            onT = A.alloc(D, BF16, "onT")
            Sh = [Tl(Sst.ap[:, hd * 128:(hd + 1) * 128], "S%d" % hd) for hd in range(HH)]
            slotsA = Ring([slot2(1), slot2(3)])
            slotB = slot2(5)
            tb = bank[0]
            b7 = psum[:, 7, :]
            decp = Tl(b7[:, 0:32], "decp")
            psA = Ring([Tl(b7[:, 64:192], "psA0"), Tl(b7[:, 192:320], "psA1")])
            psK = Tl(b7[:, 320:448], "psK")

            def load_x(t):
                xt = xr.next()
                src, rd = xsrc(layer, t)
                S.op("sp", lambda e: e.dma_start(out=xt.ap, in_=src), reads=rd, writes=[xt.b], dma="ldx%d" % (t % 3))
                return xt

            def zero_state(hd):
                S.op("pool", lambda e: e.memset(Sh[hd].ap, 0.0), writes=[Sh[hd].b])
            for hd in range(HH):
                zero_state(hd)

            def do_head(hd, nv, dec_, so, qT, kT, kt_, vb):
                Sp = Sb.next()
                S_ = Sh[hd]
                S.op("dve", lambda e: e.tensor_scalar(out=Sp.ap, in0=S_.ap, scalar1=dec_.ap[:, hd * 4:hd * 4 + 1], scalar2=None, op0=ALU.mult),
                     reads=[S_.b, dec_.b], writes=[Sp.b])
                pa = psA.next()
                S.op("pe", lambda e: e.matmul(pa.ap[0:nv, 0:nv], lhsT=kT.ap[:, hd * 128:hd * 128 + nv], rhs=qT.ap[:, hd * 128:hd * 128 + nv], start=True, stop=True),
                     reads=[kT.b, qT.b], writes=[pa.b])
                at = ATs.next()
                at32 = AT32.next()
                S.op("dve", lambda e: e.tensor_tensor(out=at32.ap[0:nv, 0:nv], in0=pa.ap[0:nv, 0:nv], in1=himb[0:nv, 0:nv], op=ALU.min),
                     reads=[pa.b] + CONSTB, writes=[at32.b])
                S.op("dve", lambda e: e.tensor_tensor(out=at.ap[0:nv, 0:nv], in0=at32.ap[0:nv, 0:nv], in1=lomb[0:nv, 0:nv], op=ALU.max),
                     reads=[at32.b] + CONSTB, writes=[at.b])

                def f_o(e):
                    e.matmul(so.ap[0:nv, hd * 128:(hd + 1) * 128], lhsT=qT.ap[:, hd * 128:hd * 128 + nv], rhs=Sp.ap, start=True, stop=False)
                    return e.matmul(so.ap[0:nv, hd * 128:(hd + 1) * 128], lhsT=at.ap[0:nv, 0:nv], rhs=vb.ap[0:nv, hd * 128:(hd + 1) * 128], start=False, stop=True)
                S.op("pe", f_o, reads=[qT.b, Sp.b, at.b, vb.b], writes=[so.b])
                S.op("pe", lambda e: e.matmul(psK.ap, lhsT=kt_.ap[0:nv, hd * 128:(hd + 1) * 128], rhs=vb.ap[0:nv, hd * 128:(hd + 1) * 128], start=True, stop=True),
                     reads=[kt_.b, vb.b], writes=[psK.b])
                t1 = T1.next()
                S.op("pool", lambda e: e.tensor_scalar(out=t1.ap, in0=S_.ap, scalar1=dec_.ap[:, hd * 4 + 1:hd * 4 + 2], scalar2=None, op0=ALU.mult),
                     reads=[S_.b, dec_.b], writes=[t1.b])
                S.op("dve", lambda e: e.scalar_tensor_tensor(out=S_.ap, in0=psK.ap, scalar=dec_.ap[:, hd * 4 + 2:hd * 4 + 3], in1=t1.ap,
                                                             op0=ALU.mult, op1=ALU.add),
                     reads=[psK.b, dec_.b, t1.b], writes=[S_.b])

            def stage_a(t, x):
                samp = (t == NT)
                nv = DS if samp else 128
                d1 = d1s if samp else d1p
                sel = sels if samp else selp
                vb, gt, kt_, qT, kT, dec_ = vbr[t % 2], gtr[t % 2], ktr[t % 2], qTr[t % 2], kTr[t % 2], decr[t % 2]
                ss_ = ss[t % 4]
                stats(x.ap, x.b, D, junk, ss_)
                S.op("dve", lambda e: e.scalar_tensor_tensor(out=h.ap, in0=x.ap, scalar=ss_.ap[:, 2:3], in1=gpre.ap, op0=ALU.mult, op1=ALU.mult),
                     reads=[x.b, ss_.b, gpre.b], writes=[h.b])
                transposes(h, hT, tb, "act")
                sl1 = slotsA.next()
                proj_tok(hT, w, 0, sl1)
                S.op("act", lambda e: e.activation(out=qf.ap, in_=sl1.ap, func=AF.Silu), reads=[sl1.b], writes=[qf.b])
                yield
                sl2 = slotsA.next()
                proj_tok(hT, w, D, sl2)
                S.op("act", lambda e: e.activation(out=kk.ap, in_=sl2.ap, func=AF.Sigmoid, scale=-1.0), reads=[sl2.b], writes=[kk.b])
                S.op("dve", lambda e: e.tensor_tensor(out=kk.ap, in0=kk.ap, in1=om.ap, op=ALU.mult), reads=[kk.b, om.b], writes=[kk.b])
                S.op("act", lambda e: e.activation(out=lg.ap, in_=kk.ap, func=AF.Ln, scale=-1.0, bias=1.0), reads=[kk.b], writes=[lg.b])
                yield
                sl3 = slotsA.next()
                proj_tok(hT, w, 2 * D, sl3)
                S.op("act", lambda e: e.activation(out=vb.ap, in_=sl3.ap, func=AF.Copy), reads=[sl3.b], writes=[vb.b])
                yield
                sl4 = slotsA.next()
                proj_tok(hT, w, 3 * D, sl4)
                S.op("act", lambda e: e.activation(out=gt.ap, in_=sl4.ap, func=AF.Silu), reads=[sl4.b], writes=[gt.b])
                yield
                sl5 = slotsA.next()

                def f_d(e):
                    e.matmul(sl5.ap[:, 0:512], lhsT=d1[0:nv, :], rhs=lg.ap[0:nv, 0:512], start=True, stop=True)
                    return e.matmul(sl5.ap[:, 512:1024], lhsT=d1[0:nv, :], rhs=lg.ap[0:nv, 512:1024], start=True, stop=True)
                S.op("pe", f_d, reads=[lg.b] + CONSTB, writes=[sl5.b])

                def f_dec(e):
                    ins = None
                    for hd in range(HH):
                        ins = e.matmul(decp.ap[:, hd * 4:hd * 4 + 3], lhsT=lg.ap[0:nv, hd * 128:(hd + 1) * 128], rhs=sel[0:nv, :], start=True, stop=True)
                    return ins
                S.op("pe", f_dec, reads=[lg.b] + CONSTB, writes=[decp.b])
                S.op("dve", lambda e: e.tensor_scalar(out=dcl.ap, in0=sl5.ap, scalar1=-80.0, scalar2=80.0, op0=ALU.max, op1=ALU.min), reads=[sl5.b], writes=[dcl.b])
                S.op("act", lambda e: e.activation(out=E2.ap, in_=dcl.ap, func=AF.Exp, scale=-1.0), reads=[dcl.b], writes=[E2.b])
                S.op("act", lambda e: e.activation(out=dcl.ap, in_=dcl.ap, func=AF.Exp), reads=[dcl.b, E2.b], writes=[dcl.b])
                S.op("act", lambda e: e.activation(out=dec_.ap, in_=decp.ap, func=AF.Exp), reads=[decp.b], writes=[dec_.b])
                S.op("dve", lambda e: e.tensor_tensor(out=qt.ap, in0=qf.ap, in1=dcl.ap, op=ALU.mult), reads=[qf.b, dcl.b], writes=[qt.b])
                S.op("pool", lambda e: e.tensor_tensor(out=kt_.ap, in0=kk.ap, in1=E2.ap, op=ALU.mult), reads=[kk.b, E2.b], writes=[kt_.b])
                yield
                transposes(qt, qT, tb, "dve")
                transposes(kt_, kT, tb, "act")
                yield

            def stage_b(t, x):
                samp = (t == NT)
                nv = DS if samp else 128
                vb, gt, kt_, qT, kT, dec_ = vbr[t % 2], gtr[t % 2], ktr[t % 2], qTr[t % 2], kTr[t % 2], decr[t % 2]
                if samp:
                    S.op("sp", lambda e: e.dma_start(out=s_p[j].rearrange("h k v -> k h v"), in_=Sst.ap.rearrange("p (h v) -> p h v", h=HH)),
                         reads=[s_.b for s_ in Sh], dma="st_s")
                    S.op("sp", lambda e: e.dma_start(out=Sst.ap.rearrange("p (h v) -> p h v", h=HH), in_=st[j].rearrange("h k v -> k h v")),
                         reads=[s_.b for s_ in Sh], writes=[s_.b for s_ in Sh], dma="ld_s")
                so = slotB
                for hd in range(HH):
                    do_head(hd, nv, dec_, so, qT, kT, kt_, vb)
                    if hd % 2 == 1:
                        yield
                ss2 = ss[(t + 2) % 4]
                stats(so.ap, so.b, D, junk, ss2)
                tmp = tt["tmp"]
                S.op("dve", lambda e: e.scalar_tensor_tensor(out=tmp.ap, in0=so.ap, scalar=ss2.ap[:, 2:3], in1=gout.ap, op0=ALU.mult, op1=ALU.mult),
                     reads=[so.b, ss2.b, gout.b], writes=[tmp.b])
                S.op("pool", lambda e: e.tensor_tensor(out=on2.ap, in0=tmp.ap, in1=gt.ap, op=ALU.mult), reads=[tmp.b, gt.b], writes=[on2.b])
                transposes(on2, onT, tb, "act")
                proj_tok(onT, wo, 0, so)
                yield
                tail(layer, t, so, x, tt, gpost, gpre2, tb)
                yield

            def drain(g):
                if g is None:
                    return None
                try:
                    next(g)
                    return g
                except StopIteration:
                    return None

            xs = {0: load_x(0)}
            if NTS > 1:
                xs[1] = load_x(1)
            ga = stage_a(0, xs[0])
            while ga is not None:
                ga = drain(ga)
            for t in range(NTS):
                if t + 2 < NTS:
                    xs[t + 2] = load_x(t + 2)
                ga = stage_a(t + 1, xs[t + 1]) if t + 1 < NTS else None
                gb = stage_b(t, xs[t])
                while ga is not None or gb is not None:
                    ga = drain(ga)
                    gb = drain(gb)
            S.op("sp", lambda e: e.dma_start(out=s_s[j].rearrange("h k v -> k h v"), in_=Sst.ap.rearrange("p (h v) -> p h v", h=HH)),
                 reads=[s_.b for s_ in Sh], dma="st_s")
            S.barrier()
            A.off = PERS

        phase_wcast()
        for layer in range(NL):
            if layer % 2 == 0:
                LF = phase_fox_a(layer)
                phase_fox_a2(layer, LF)
                phase_fox_b(layer)
                phase_fox_c1(layer)
            else:
                phase_hgrn(layer)
            phase_ffn(layer)
        S.emit()
    return nc


_CACHE = {}


def get_nc(T, PAST, NL=4):
    key = (T, PAST, NL)
    if key not in _CACHE:
        _CACHE[key] = build(T, PAST, NL)
    return _CACHE[key]


def make_in_map(c, inp, T, PAST):
    f = lambda a: np.ascontiguousarray(np.asarray(a, dtype=np.float32))
    m = {
        "x_p": f(inp["x_prompt"][c]),
        "x_s": f(inp["x_sample"][c]),
        "ck": f(np.asarray(inp["cache_k"])[:, c].reshape(-1, PAST, D)),
        "cv": f(np.asarray(inp["cache_v"])[:, c].reshape(-1, PAST, D)),
        "clf": f(np.asarray(inp["cache_logf"])[:, c]),
        "st": f(np.asarray(inp["state_s"])[:, c]),
        "fox_w_in": f(inp["fox_w_in"]), "fox_b_f": f(inp["fox_b_f"]), "fox_q_norm": f(inp["fox_q_norm"]),
        "fox_k_norm": f(inp["fox_k_norm"]), "fox_w_out": f(inp["fox_w_out"]), "hgrn_w_in": f(inp["hgrn_w_in"]),
        "hgrn_lb": f(inp["hgrn_lb_logits"]), "hgrn_out_norm": f(inp["hgrn_out_norm"]), "hgrn_w_out": f(inp["hgrn_w_out"]),
        "pre_mix": f(inp["pre_mix_norm"]), "post_mix": f(inp["post_mix_norm"]), "pre_ffn": f(inp["pre_ffn_norm"]),
        "post_ffn": f(inp["post_ffn_norm"]), "ffn_up": f(inp["ffn_w_up"]), "ffn_down": f(inp["ffn_w_down"]),
        "consts": make_consts(),
    }
    return m


def assemble(results, B, T):
    def st(name, shape_tail=None):
        return np.stack([np.asarray(r[name], dtype=np.float32) for r in results], axis=0)
    y_p = st("y_p")
    y_s = st("y_s")
    k_p = np.moveaxis(st("k_p"), 0, 1).reshape(-1, B, T, NH, DH)
    v_p = np.moveaxis(st("v_p"), 0, 1).reshape(-1, B, T, NH, DH)
    lf_p = np.moveaxis(st("lf_p"), 0, 1)
    s_p = np.moveaxis(st("s_p"), 0, 1)
    k_s = np.moveaxis(st("k_s"), 0, 1).reshape(-1, B, DS, NH, DH)
    v_s = np.moveaxis(st("v_s"), 0, 1).reshape(-1, B, DS, NH, DH)
    lf_s = np.moveaxis(st("lf_s"), 0, 1)
    s_s = np.moveaxis(st("s_s"), 0, 1)
    return (y_p, y_s, k_p, v_p, lf_p, s_p, k_s, v_s, lf_s, s_s)


def kernel(**inputs):
    xp = np.asarray(inputs["x_prompt"])
    B, T = xp.shape[0], xp.shape[1]
    PAST = np.asarray(inputs["cache_k"]).shape[2]
    nc = get_nc(T, PAST)
    in_maps = [make_in_map(c, inputs, T, PAST) for c in range(B)]
    res = run_bass_kernel_spmd(nc, in_maps, core_ids=list(range(B)))
    return assemble(res.results, B, T)
```

```python
import contextlib
import numpy as np
import concourse.bass as bass
import concourse.mybir as mybir
from concourse.bass_utils import run_bass_kernel_spmd

F32 = mybir.dt.float32
BF16 = mybir.dt.bfloat16
AF = mybir.ActivationFunctionType
ALU = mybir.AluOpType
AX = mybir.AxisListType

D = 1024
NH = 16
DH = 64
HH = 8
DFF = 4096
EPS = 1e-6
DS = 32
FW = 4 * D + NH


class Buf:
    __slots__ = ("name", "w", "r")

    def __init__(self, name=""):
        self.name = name
        self.w = None
        self.r = []


class Op:
    __slots__ = ("eng", "fn", "deps", "odeps", "sig", "sigval", "dma", "cost", "seq", "bar")

    def __init__(self, eng, fn, dma):
        self.eng = eng
        self.fn = fn
        self.deps = []
        self.odeps = []
        self.sig = False
        self.sigval = None
        self.dma = dma
        self.cost = 1000
        self.seq = 0
        self.bar = 0


DEFCOST = {"pe": 2000, "act": 1100, "dve": 800, "pool": 2300, "sp": 150}
DMA_LAT = 3000
RESCHEDULE = True
RESCHED_SEGS = {2, 5, 6, 8, 9, 12, 13, 15}
RESCHED_ENGS = None


class Sched:
    CE = ("pe", "act", "dve", "pool")
    ENGS = ("pe", "act", "dve", "pool", "sp")

    def __init__(self, nc):
        self.nc = nc
        self.ops = {e: [] for e in self.ENGS}
        self.streams = {}
        self.last_dma = {}
        self.nseq = 0

    def op(self, eng, fn, reads=(), writes=(), dma=None, extra=(), cost=None):
        o = Op(eng, fn, dma)
        o.cost = cost if cost is not None else DEFCOST[eng]
        self.nseq += 1
        o.seq = self.nseq
        deps = []
        seen = set()

        def add(d):
            if d is None or d is o or id(d) in seen:
                return
            seen.add(id(d))
            deps.append(d)

        for b in reads:
            add(b.w)
        for b in writes:
            add(b.w)
            for r in b.r:
                add(r)
        for d in extra:
            add(d)
        if dma is not None:
            add(self.last_dma.get(dma))
            self.last_dma[dma] = o
            n = self.streams.setdefault(dma, [0])
            n[0] += 1
            o.sigval = 16 * n[0]
        for d in deps:
            if d.dma is None and d.eng == eng and eng == "pe":
                o.odeps.append(d)
                continue
            o.deps.append(d)
            if d.dma is None:
                d.sig = True
        for b in writes:
            b.w = o
            b.r = []
        for b in reads:
            if b.w is not o:
                b.r.append(o)
        self.ops[eng].append(o)
        return o

    def barrier(self):
        firsts = []
        alld = list(self.last_dma.values())
        for e in self.CE:
            o = Op(e, lambda eh: eh.nop(), None)
            o.bar = 1
            for d in alld:
                o.deps.append(d)
            self.ops[e].append(o)
            firsts.append(o)
        for e in self.ENGS:
            o = Op(e, lambda eh: eh.nop(), None)
            o.bar = 2
            for d in firsts:
                if d.eng == e:
                    continue
                o.deps.append(d)
                d.sig = True
            self.ops[e].append(o)

    def _reschedule(self):
        import heapq
        segs = {e: [] for e in self.ENGS}
        nseg = 0
        for e in self.ENGS:
            cur = []
            bars = []
            for o in self.ops[e]:
                if o.bar:
                    bars.append(o)
                    if o.bar == 2:
                        segs[e].append((cur, bars))
                        cur, bars = [], []
                else:
                    assert not bars
                    cur.append(o)
            segs[e].append((cur, bars))
            nseg = max(nseg, len(segs[e]))
        for e in self.ENGS:
            while len(segs[e]) < nseg:
                segs[e].append(([], []))
        for k in range(nseg):
            allops = []
            for e in self.ENGS:
                allops.extend(segs[e][k][0])
            if RESCHEDULE and allops and (RESCHED_SEGS is None or k in RESCHED_SEGS):
                inseg = {id(o) for o in allops}
                nd = {}
                users = {}
                for o in allops:
                    c = 0
                    for d in o.deps + o.odeps:
                        if id(d) in inseg:
                            c += 1
                            users.setdefault(id(d), []).append(o)
                    nd[id(o)] = c
                ready = {e: [] for e in self.ENGS}
                for o in allops:
                    if nd[id(o)] == 0:
                        heapq.heappush(ready[o.eng], (o.seq, id(o), o))
                free_at = {e: 0 for e in self.ENGS}
                start = {}
                events = [(0, 0)]
                evn = 1
                pending = []
                ndone = 0
                now = 0
                while ndone < len(allops):
                    while pending and pending[0][0] <= now:
                        _, _, o = heapq.heappop(pending)
                        ndone += 1
                        for u in users.get(id(o), ()):
                            nd[id(u)] -= 1
                            if nd[id(u)] == 0:
                                heapq.heappush(ready[u.eng], (u.seq, id(u), u))
                    if ndone >= len(allops):
                        break
                    progressed = False
                    for e in self.ENGS:
                        if free_at[e] <= now and ready[e]:
                            _, _, o = heapq.heappop(ready[e])
                            start[id(o)] = now
                            free_at[e] = now + o.cost
                            done = now + (o.cost + DMA_LAT if o.dma is not None else o.cost)
                            evn += 1
                            heapq.heappush(pending, (done, evn, o))
                            progressed = True
                    if progressed:
                        continue
                    nxt = []
                    if pending:
                        nxt.append(pending[0][0])
                    for e in self.ENGS:
                        if ready[e] and free_at[e] > now:
                            nxt.append(free_at[e])
                    if not nxt:
                        left = [o for o in allops if id(o) not in start]
                        print("STUCK0 seg", k, "nops", len(allops), "ndone", ndone, "started", len(start), "uniq", len({id(o) for o in allops}))
                        o = left[0]
                        print("STUCK seg", k, "nops", len(allops), "left", len(left), "first eng", o.eng, "seq", o.seq, "nd", nd[id(o)],
                              [(d.eng, d.seq, id(d) in inseg, id(d) in start, d.bar) for d in o.deps + o.odeps])
                    assert nxt, "scheduler stuck"
                    now = max(now + 1, min(nxt))
                for e in self.ENGS:
                    if RESCHED_ENGS is None or e in RESCHED_ENGS:
                        segs[e][k][0].sort(key=lambda o: (start[id(o)], o.seq))
            for e in self.CE:
                lst, bars = segs[e][k]
                for bo in bars:
                    if bo.bar == 1 and lst:
                        bo.deps.append(lst[-1])
                        lst[-1].sig = True
        for e in self.ENGS:
            out = []
            for lst, bars in segs[e]:
                out.extend(lst)
                out.extend(bars)
            self.ops[e] = out

    def _check(self):
        pos = {e: 0 for e in self.ENGS}
        done = set()
        n = sum(len(v) for v in self.ops.values())
        while len(done) < n:
            prog = False
            for e in self.ENGS:
                while pos[e] < len(self.ops[e]):
                    o = self.ops[e][pos[e]]
                    if all(id(d) in done for d in o.deps) and all(id(d) in done for d in o.odeps):
                        done.add(id(o))
                        pos[e] += 1
                        prog = True
                    else:
                        break
            if not prog:
                for e in self.ENGS:
                    if pos[e] < len(self.ops[e]):
                        o = self.ops[e][pos[e]]
                        print("DEADLOCK", e, "pos", pos[e], "seq", o.seq, "bar", o.bar, "waiting on",
                              [(d.eng, d.seq, d.bar, d.dma) for d in o.deps + o.odeps if id(d) not in done])
                raise RuntimeError("deadlock in emitted order")

    def emit(self):
        nc = self.nc
        self._reschedule()
        self._check()
        with contextlib.ExitStack() as es:
            esem = {e: es.enter_context(nc.semaphore("s_" + e)) for e in self.CE}
            ssem = {k: es.enter_context(nc.semaphore("d%d" % i)) for i, k in enumerate(self.streams)}
            for e in self.CE:
                c = 0
                for o in self.ops[e]:
                    if o.dma is None and o.sig:
                        c += 1
                        o.sigval = c
            block = es.enter_context(nc.Block())

            def run(ename, eh):
                waited = {}
                for o in self.ops[ename]:
                    for d in o.deps:
                        sem = ssem[d.dma] if d.dma is not None else esem[d.eng]
                        key = id(sem)
                        if waited.get(key, 0) >= d.sigval:
                            continue
                        waited[key] = d.sigval
                        eh.wait_ge(sem, d.sigval)
                    ins = o.fn(eh)
                    if o.dma is not None:
                        ins.then_inc(ssem[o.dma], 16)
                    elif o.sig:
                        ins.then_inc(esem[ename], 1)
                if ename == "sp":
                    for k, n in self.streams.items():
                        eh.wait_ge(ssem[k], 16 * n[0])

            @block.tensor
            def _(e):
                run("pe", e)

            @block.scalar
            def _(e):
                run("act", e)

            @block.vector
            def _(e):
                run("dve", e)

            @block.gpsimd
            def _(e):
                run("pool", e)

            @block.sync
            def _(e):
                run("sp", e)


class Tl:
    __slots__ = ("ap", "b")

    def __init__(self, ap, name=""):
        self.ap = ap
        self.b = Buf(name)


class Arena:
    def __init__(self, ap, size):
        self.ap = ap
        self.size = size
        self.off = 0

    def alloc(self, n, dt=BF16, name=""):
        w = n * 2 if dt == F32 else n
        off = self.off
        self.off += (w + 31) // 32 * 32
        assert self.off <= self.size, ("SBUF arena overflow", name, self.off, self.size)
        v = self.ap[:, off:off + w]
        if dt == F32:
            v = v.bitcast(F32)
        return Tl(v, name)


class Ring:
    def __init__(self, tiles):
        self.t = tiles
        self.i = -1

    def next(self):
        self.i += 1
        return self.t[self.i % len(self.t)]


def make_consts():
    s = np.arange(128)[:, None]
    t = np.arange(128)[None, :]
    c = {}
    c["ident"] = (s == t).astype(np.float32)
    c["tri"] = (s <= t).astype(np.float32)
    c["maskneg"] = np.where(s > t, -30000.0, 0.0).astype(np.float32)
    c["d1p"] = ((s <= t).astype(np.float32) - (s <= 63).astype(np.float32) * np.ones_like(t, dtype=np.float32))
    vs = (s < DS) & (t < DS)
    c["d1s"] = np.where(vs, (s <= t).astype(np.float32) - (s <= 15).astype(np.float32), 0.0).astype(np.float32)
    sel = np.zeros((128, 8), np.float32)
    sv = np.arange(128)
    sel[:, 0] = sv <= 63
    sel[:, 1] = 1.0
    sel[:, 2] = sv > 63
    sel[:, 4] = sv <= 15
    sel[:, 5] = sv < DS
    sel[:, 6] = (sv > 15) & (sv < DS)
    c["sel"] = sel
    c["him"] = np.where(s <= t, 3.0e38, 0.0).astype(np.float32)
    c["lom"] = np.where(s <= t, -3.0e38, 0.0).astype(np.float32)
    return np.concatenate([c["ident"], c["tri"], c["maskneg"], c["him"], c["lom"], c["d1p"], c["d1s"], c["sel"]], axis=1).astype(np.float32)


NCONST = 7 * 128 + 8


def build(T, PAST, NL=4):
    NT = T // 128
    NP = PAST // 128
    NTS = NT + 1
    TS = T + 128
    KC = T + PAST + 128
    NKS = KC // 128
    NA = (NL + 1) // 2
    NR = NL // 2

    nc = bass.Bass("TRN2", target_bir_lowering=False)

    def din(name, shape):
        return nc.dram_tensor(name, list(shape), F32, kind="ExternalInput").ap()

    def dout(name, shape):
        return nc.dram_tensor(name, list(shape), F32, kind="ExternalOutput").ap()

    def dscr(name, shape, dt):
        return nc.dram_tensor(name, list(shape), dt, kind="Internal").ap()

    x_p = din("x_p", [T, D])
    x_s = din("x_s", [DS, D])
    ck = din("ck", [NA, PAST, D])
    cv = din("cv", [NA, PAST, D])
    clf = din("clf", [NA, PAST, NH])
    st = din("st", [NR, HH, 128, 128])
    fox_w_in = din("fox_w_in", [NA, D, FW])
    fox_b_f = din("fox_b_f", [NA, NH])
    fox_q_norm = din("fox_q_norm", [NA, DH])
    fox_k_norm = din("fox_k_norm", [NA, DH])
    fox_w_out = din("fox_w_out", [NA, D, D])
    hgrn_w_in = din("hgrn_w_in", [NR, D, 4 * D])
    hgrn_lb = din("hgrn_lb", [4, D])
    hgrn_out_norm = din("hgrn_out_norm", [NR, D])
    hgrn_w_out = din("hgrn_w_out", [NR, D, D])
    pre_mix = din("pre_mix", [NL, D])
    post_mix = din("post_mix", [NL, D])
    pre_ffn = din("pre_ffn", [NL, D])
    post_ffn = din("post_ffn", [NL, D])
    ffn_up = din("ffn_up", [NL, D, DFF])
    ffn_down = din("ffn_down", [NL, DFF, D])
    consts = din("consts", [128, NCONST])

    y_p = dout("y_p", [T, D])
    y_s = dout("y_s", [DS, D])
    k_p = dout("k_p", [NA, T, D])
    v_p = dout("v_p", [NA, T, D])
    lf_p = dout("lf_p", [NA, T, NH])
    s_p = dout("s_p", [NR, HH, 128, 128])
    k_s = dout("k_s", [NA, DS, D])
    v_s = dout("v_s", [NA, DS, D])
    lf_s = dout("lf_s", [NA, DS, NH])
    s_s = dout("s_s", [NR, HH, 128, 128])

    xres = dscr("xres", [NTS, 128, D], F32)
    wb_fin = dscr("wb_fin", [NA, D, FW], BF16)
    wb_fout = dscr("wb_fout", [NA, D, D], BF16)
    wb_hin = dscr("wb_hin", [NR, D, 4 * D], BF16)
    wb_hout = dscr("wb_hout", [NR, D, D], BF16)
    wb_up = dscr("wb_up", [NL, 32, 128, 8, 128], BF16)
    wb_down = dscr("wb_down", [NL, DFF, D], BF16)
    QT = dscr("QT", [D, TS], BF16)
    KT = dscr("KT", [D, KC], BF16)
    VT = dscr("VT", [KC, D], BF16)
    GT = dscr("GT", [D, TS], BF16)
    CTs = dscr("CTs", [NH, 6, KC], BF16)
    OGT = dscr("OGT", [D, TS], BF16)
    H2T = dscr("H2T", [D, TS], BF16)

    db = {}

    def dbuf(*key):
        if key not in db:
            db[key] = Buf(str(key))
        return db[key]

    ARENA = 106000
    with contextlib.ExitStack() as es:
        arena_t = es.enter_context(nc.sbuf_tensor("arena", [128, ARENA], BF16))
        psum = es.enter_context(nc.psum_tensor("psum", [128, 8, 512], F32))
        A = Arena(arena_t, ARENA)
        S = Sched(nc)

        bank = [Tl(psum[:, k, :], "bank%d" % k) for k in range(8)]

        def slot2(k):
            return Tl(psum[:, k:k + 2, :].rearrange("p a b -> p (a b)"), "slot%d" % k)

        cst = A.alloc(NCONST, F32, "consts")
        S.op("sp", lambda e: e.dma_start(out=cst.ap, in_=consts[:, :]), writes=[cst.b], dma="cst")
        identf = cst.ap[:, 0:128]
        trif = cst.ap[:, 128:256]
        masknegf = cst.ap[:, 256:384]
        d1p = cst.ap[:, 640:768]
        d1s = cst.ap[:, 768:896]
        selp = cst.ap[:, 896:899]
        sels = cst.ap[:, 900:903]
        cb = A.alloc(5 * 128, BF16, "constb")
        S.op("dve", lambda e: e.tensor_copy(out=cb.ap, in_=cst.ap[:, 0:640]), reads=[cst.b], writes=[cb.b])
        idb = cb.ap[:, 0:128]
        mask01b = cb.ap[:, 128:256]
        masknegb = cb.ap[:, 256:384]
        himb = cb.ap[:, 384:512]
        lomb = cb.ap[:, 512:640]
        CONSTB = [cst.b, cb.b]
        nhalf = A.alloc(16, F32, "nhalf")
        S.op("pool", lambda e: e.memset(nhalf.ap, -0.5), writes=[nhalf.b])

        oml = {}
        if NR > 0:
            for layer in range(1, NL, 2):
                oml[layer] = A.alloc(D, F32, "oml%d" % layer)
            keep = A.off
            L = [A.alloc(D, F32, "lbl%d" % i) for i in range(4)]
            mx = A.alloc(D, F32, "lbmx")
            sm = A.alloc(D, F32, "lbsum")

            def ld_l(i):
                S.op("sp", lambda e: e.dma_start(out=L[i].ap, in_=hgrn_lb[i:i + 1, :].partition_broadcast(128)), writes=[L[i].b], dma="lb%d" % i)
            for i in range(4):
                ld_l(i)

            def tt_(out, a_, b_, op):
                S.op("dve", lambda e: e.tensor_tensor(out=out.ap, in0=a_.ap, in1=b_.ap, op=op), reads=[a_.b, b_.b], writes=[out.b])
            tt_(mx, L[0], L[1], ALU.max)
            tt_(mx, mx, L[2], ALU.max)
            tt_(mx, mx, L[3], ALU.max)

            def ex_(i):
                tt_(L[i], L[i], mx, ALU.subtract)
                S.op("act", lambda e: e.activation(out=L[i].ap, in_=L[i].ap, func=AF.Exp), reads=[L[i].b], writes=[L[i].b])
            for i in range(4):
                ex_(i)
            tt_(sm, L[0], L[1], ALU.add)
            tt_(sm, sm, L[2], ALU.add)
            tt_(sm, sm, L[3], ALU.add)
            S.op("dve", lambda e: e.reciprocal(out=sm.ap, in_=sm.ap), reads=[sm.b], writes=[sm.b])

            def mk_oml(layer):
                o_ = oml[layer]
                S.op("dve", lambda e: e.tensor_copy(out=o_.ap, in_=L[1].ap), reads=[L[1].b], writes=[o_.b])
                for i in range(2, layer + 1):
                    tt_(o_, o_, L[i], ALU.add)
                tt_(o_, o_, sm, ALU.mult)
                S.op("dve", lambda e: e.tensor_scalar(out=o_.ap, in0=o_.ap, scalar1=-1.0, scalar2=1.0, op0=ALU.mult, op1=ALU.add),
                     reads=[o_.b], writes=[o_.b])
            for layer in range(1, NL, 2):
                mk_oml(layer)
            S.barrier()
            A.off = keep

        PERS = A.off

        def stats(src_ap, src_b, n, junk, ss):
            S.op("pool", lambda e: e.memset(ss.ap[:, 0:1], 0.0), writes=[ss.b], cost=80)
            S.op("act", lambda e: e.activation(out=junk.ap[:, 0:n], in_=src_ap, func=AF.Square, accum_out=ss.ap[:, 0:1]),
                 reads=[src_b, ss.b], writes=[ss.b])
            S.op("act", lambda e: e.activation(out=ss.ap[:, 1:2], in_=ss.ap[:, 0:1], func=AF.Sqrt, scale=1.0 / n, bias=EPS),
                 reads=[ss.b], writes=[ss.b], cost=1500)
            S.op("dve", lambda e: e.reciprocal(out=ss.ap[:, 2:3], in_=ss.ap[:, 1:2]), reads=[ss.b], writes=[ss.b], cost=200)

        def transposes(src, dstT, tb, copy_eng):
            tv = tb.ap.bitcast(BF16)

            def f(e):
                ins = None
                for c in range(8):
                    ins = e.transpose(out=tv[:, c * 128:(c + 1) * 128], in_=src.ap[:, c * 128:(c + 1) * 128], identity=idb)
                return ins
            S.op("pe", f, reads=[src.b] + CONSTB, writes=[tb.b], cost=900)
            if copy_eng == "act":
                S.op("act", lambda e: e.activation(out=dstT.ap, in_=tv, func=AF.Copy), reads=[tb.b], writes=[dstT.b])
            else:
                S.op("dve", lambda e: e.tensor_copy(out=dstT.ap, in_=tv), reads=[tb.b], writes=[dstT.b])

        def proj_tok(hT, w, wcol0, sl):
            def f(e):
                ins = None
                for half in range(2):
                    for kc in range(8):
                        ins = e.matmul(sl.ap[:, half * 512:(half + 1) * 512], lhsT=hT.ap[:, kc * 128:(kc + 1) * 128],
                                       rhs=w.ap[:, kc, wcol0 + half * 512: wcol0 + (half + 1) * 512],
                                       start=(kc == 0), stop=(kc == 7))
                return ins
            S.op("pe", f, reads=[hT.b, w.b], writes=[sl.b], cost=4300)

        def load_bcast(dst, src_row, key):
            S.op("sp", lambda e: e.dma_start(out=dst.ap, in_=src_row.partition_broadcast(128)), writes=[dst.b], dma=key)

        def xsrc(layer, t):
            if layer == 0 and t < NT:
                return x_p[t * 128:(t + 1) * 128, :], []
            return xres[t], [dbuf("xres", t)]

        def qcol(t):
            return t * 128

        def kcol(t):
            return t * 128 if t < NT else T + PAST

        def phase_wcast():
            mark = A.off
            CH = 4096
            NWB = 4
            fb = [A.alloc(CH, F32, "wf%d" % i) for i in range(NWB)]
            bb = [A.alloc(CH, BF16, "wb%d" % i) for i in range(NWB)]
            jobs = []

            def flat(ap2d):
                return ap2d.rearrange("r c -> (r c)").rearrange("(p n) -> p n", p=128)

            for j in range(NA):
                jobs.append((flat(fox_w_in[j]), flat(wb_fin[j]), dbuf("wb_fin", j)))
                jobs.append((flat(fox_w_out[j]), flat(wb_fout[j]), dbuf("wb_fout", j)))
            for j in range(NR):
                jobs.append((flat(hgrn_w_in[j]), flat(wb_hin[j]), dbuf("wb_hin", j)))
                jobs.append((flat(hgrn_w_out[j]), flat(wb_hout[j]), dbuf("wb_hout", j)))
            for l in range(NL):
                jobs.append((flat(ffn_down[l]), flat(wb_down[l]), dbuf("wb_down", l)))
            step = 0
            engs = ("dve", "pool", "act")
            for src, dst, tok in jobs:
                n = src.shape[1]
                for c0 in range(0, n, CH):
                    w_ = min(CH, n - c0)
                    f_, b_ = fb[step % NWB], bb[step % NWB]
                    S.op("sp", lambda e, f_=f_, src=src, c0=c0, w_=w_: e.dma_start(out=f_.ap[:, 0:w_], in_=src[:, c0:c0 + w_]),
                         writes=[f_.b], dma="wl%d" % (step % NWB))
                    eng = engs[step % 3]
                    if eng == "act":
                        S.op("act", lambda e, f_=f_, b_=b_, w_=w_: e.activation(out=b_.ap[:, 0:w_], in_=f_.ap[:, 0:w_], func=AF.Copy),
                             reads=[f_.b], writes=[b_.b])
                    else:
                        S.op(eng, lambda e, f_=f_, b_=b_, w_=w_: e.tensor_copy(out=b_.ap[:, 0:w_], in_=f_.ap[:, 0:w_]),
                             reads=[f_.b], writes=[b_.b])
                    S.op("sp", lambda e, b_=b_, dst=dst, c0=c0, w_=w_: e.dma_start(out=dst[:, c0:c0 + w_], in_=b_.ap[:, 0:w_]),
                         reads=[b_.b], writes=[tok], dma="ws%d" % (step % NWB))
                    step += 1
            for l in range(NL):
                for c in range(8):
                    f_, b_ = fb[step % NWB], bb[step % NWB]
                    S.op("sp", lambda e, f_=f_, l=l, c=c: e.dma_start(out=f_.ap[:, 0:DFF], in_=ffn_up[l, c * 128:(c + 1) * 128, :]),
                         writes=[f_.b], dma="wl%d" % (step % NWB))
                    eng = engs[step % 3]
                    if eng == "act":
                        S.op("act", lambda e, f_=f_, b_=b_: e.activation(out=b_.ap[:, 0:DFF], in_=f_.ap[:, 0:DFF], func=AF.Copy),
                             reads=[f_.b], writes=[b_.b])
                    else:
                        S.op(eng, lambda e, f_=f_, b_=b_: e.tensor_copy(out=b_.ap[:, 0:DFF], in_=f_.ap[:, 0:DFF]),
                             reads=[f_.b], writes=[b_.b])
                    S.op("sp", lambda e, b_=b_, l=l, c=c: e.dma_start(
                        out=wb_up[l, :, :, c, :].rearrange("j p n -> p j n"),
                        in_=b_.ap[:, 0:DFF].rearrange("p (j n) -> p j n", n=128)),
                        reads=[b_.b], writes=[dbuf("wb_up", l)], dma="ws%d" % (step % NWB))
                    step += 1
            xt = fb[step % NWB]
            S.op("pool", lambda e: e.memset(xt.ap[:, 0:D], 0.0), writes=[xt.b])
            S.op("sp", lambda e: e.dma_start(out=xt.ap[0:DS, 0:D], in_=x_s[:, :]), reads=[xt.b], writes=[xt.b], dma="wl%d" % (step % NWB))
            S.op("sp", lambda e: e.dma_start(out=xres[NT], in_=xt.ap[:, 0:D]), reads=[xt.b], writes=[dbuf("xres", NT)], dma="ws%d" % (step % NWB))
            S.barrier()
            A.off = mark

        def make_tail_tiles():
            d = {}
            d["junk"] = A.alloc(D, BF16, "tjunk")
            d["ss"] = [A.alloc(8, F32, "tss%d" % i) for i in range(4)]
            d["tmp"] = A.alloc(D, F32, "ttmp")
            d["xn"] = Ring([A.alloc(D, F32, "txn%d" % i) for i in range(1)])
            d["h2"] = A.alloc(D, BF16, "th2")
            d["h2T"] = Ring([A.alloc(D, BF16, "th2T%d" % i) for i in range(1)])
            return d

        def tail(layer, t, sl, x, tt, gpost, gpre2, tb):
            ssA = tt["ss"][(2 * t) % 4]
            ssB = tt["ss"][(2 * t + 1) % 4]
            stats(sl.ap, sl.b, D, tt["junk"], ssA)
            tmp = tt["tmp"]
            S.op("dve", lambda e: e.scalar_tensor_tensor(out=tmp.ap, in0=sl.ap, scalar=ssA.ap[:, 2:3], in1=gpost.ap, op0=ALU.mult, op1=ALU.mult),
                 reads=[sl.b, ssA.b, gpost.b], writes=[tmp.b])
            xn = tt["xn"].next()
            S.op("pool", lambda e: e.tensor_tensor(out=xn.ap, in0=x.ap, in1=tmp.ap, op=ALU.add), reads=[x.b, tmp.b], writes=[xn.b])
            nrow = 128 if t < NT else DS
            S.op("sp", lambda e: e.dma_start(out=xres[t, 0:nrow, :], in_=xn.ap[0:nrow, :]), reads=[xn.b], writes=[dbuf("xres", t)], dma="st_x%d" % (t % 2))
            stats(xn.ap, xn.b, D, tt["junk"], ssB)
            h2 = tt["h2"]
            S.op("dve", lambda e: e.scalar_tensor_tensor(out=h2.ap, in0=xn.ap, scalar=ssB.ap[:, 2:3], in1=gpre2.ap, op0=ALU.mult, op1=ALU.mult),
                 reads=[xn.b, ssB.b, gpre2.b], writes=[h2.b])
            h2T = tt["h2T"].next()
            transposes(h2, h2T, tb, "act")
            S.op("sp", lambda e: e.dma_start(out=H2T[:, qcol(t):qcol(t) + 128].rearrange("(c p) n -> p c n", p=128),
                                             in_=h2T.ap.rearrange("p (c n) -> p c n", c=8)),
                 reads=[h2T.b], writes=[dbuf("H2T", t)], dma="st_h%d" % (t % 2))

        def phase_fox_a(layer):
            j = layer // 2
            LF = A.alloc(NKS * NH, F32, "LF")
            after_lf = A.off
            w = A.alloc(8 * FW, BF16, "fw")
            w.ap = w.ap.rearrange("p (c n) -> p c n", c=8)

            def ld_w(c):
                S.op("sp", lambda e: e.dma_start(out=w.ap[:, c, :], in_=wb_fin[j, c * 128:(c + 1) * 128, :]),
                     reads=[dbuf("wb_fin", j)], writes=[w.b], dma="wld%d" % (c % 2))
            for c in range(8):
                ld_w(c)
            gpre = A.alloc(D, F32, "gpre")
            load_bcast(gpre, pre_mix[layer:layer + 1, :], "g0")
            gq = A.alloc(DH, F32, "gq")
            load_bcast(gq, fox_q_norm[j:j + 1, :], "g1")
            gk = A.alloc(DH, F32, "gk")
            load_bcast(gk, fox_k_norm[j:j + 1, :], "g2")
            bfb = A.alloc(NH, F32, "bfb")
            load_bcast(bfb, fox_b_f[j:j + 1, :], "g3")
            xr = Ring([A.alloc(D, F32, "x%d" % i) for i in range(2)])
            junk = A.alloc(D, BF16, "junk")
            ss = [A.alloc(8, F32, "ss%d" % i) for i in range(2)]
            h = A.alloc(D, BF16, "h")
            hT = A.alloc(D, BF16, "hT")
            sq = A.alloc(D, F32, "sq")
            sq2 = A.alloc(D, F32, "sq2")
            ssq = [A.alloc(64, F32, "ssq%d" % i) for i in range(2)]
            qn = A.alloc(D, BF16, "qn")
            kf = Ring([A.alloc(D, F32, "kf%d" % i) for i in range(2)])
            kb = A.alloc(D, BF16, "kb")
            vf = Ring([A.alloc(D, F32, "vf%d" % i) for i in range(2)])
            vb = Ring([A.alloc(D, BF16, "vb%d" % i) for i in range(2)])
            QTs = Ring([A.alloc(D, BF16, "QTs%d" % i) for i in range(2)])
            KTs = Ring([A.alloc(D, BF16, "KTs%d" % i) for i in range(2)])
            GTs = Ring([A.alloc(D, BF16, "GTs%d" % i) for i in range(2)])
            f1 = [A.alloc(64, F32, "f1_%d" % i) for i in range(2)]
            slots = Ring([slot2(1), slot2(3), slot2(5)])
            tb = bank[0]
            fzb = bank[7]

            def k_to_scratch(kbt, col):
                KTt = KTs.next()
                transposes(kbt, KTt, tb, "dve")
                S.op("sp", lambda e: e.dma_start(out=KT[:, col:col + 128].rearrange("(c p) n -> p c n", p=128),
                                                 in_=KTt.ap.rearrange("p (c n) -> p c n", c=8)),
                     reads=[KTt.b], writes=[dbuf("KT", col)], dma="st_kt%d" % (KTs.i % 2))

            def v_to_scratch(vbt, col, par):
                S.op("sp", lambda e: e.dma_start(out=VT[col:col + 128, :], in_=vbt.ap), reads=[vbt.b], writes=[dbuf("VT", col)],
                     dma="st_vt%d" % par)

            def do_past(jt):
                col = T + jt * 128
                kft = kf.next()
                S.op("sp", lambda e: e.dma_start(out=kft.ap, in_=ck[j, jt * 128:(jt + 1) * 128, :]), writes=[kft.b], dma="ldk%d" % (kf.i % 2))
                S.op("pool", lambda e: e.tensor_copy(out=kb.ap, in_=kft.ap), reads=[kft.b], writes=[kb.b])
                k_to_scratch(kb, col)
                vft = vf.next()
                S.op("sp", lambda e: e.dma_start(out=vft.ap, in_=cv[j, jt * 128:(jt + 1) * 128, :]), writes=[vft.b], dma="ldv%d" % (vf.i % 2))
                vbt = vb.next()
                S.op("pool", lambda e: e.tensor_copy(out=vbt.ap, in_=vft.ap), reads=[vft.b], writes=[vbt.b])
                v_to_scratch(vbt, col, vb.i % 2)
                slot_ = NT + jt
                S.op("sp", lambda e: e.dma_start(out=LF.ap[:, slot_ * NH:(slot_ + 1) * NH], in_=clf[j, jt * 128:(jt + 1) * 128, :]),
                     writes=[LF.b], dma="ldlf")
            for jt in range(NP):
                do_past(jt)

            def load_x(t):
                xt = xr.next()
                src, rd = xsrc(layer, t)
                S.op("sp", lambda e: e.dma_start(out=xt.ap, in_=src), reads=rd, writes=[xt.b], dma="ldx%d" % (t % 2))
                return xt

            def headnorm(which, sl_, t):
                ssq_ = ssq[t % 2]
                nv = 128 if t < NT else DS
                sqt = sq if which == "q" else sq2
                o0 = 0 if which == "q" else 32
                sq3 = sqt.ap.rearrange("p (h d) -> p h d", h=NH)
                S.op("act", lambda e: e.activation(out=sqt.ap, in_=sl_.ap, func=AF.Square), reads=[sl_.b], writes=[sqt.b])
                S.op("dve", lambda e: e.tensor_reduce(out=ssq_.ap[:, o0:o0 + NH], in_=sq3, axis=AX.X, op=ALU.add), reads=[sqt.b], writes=[ssq_.b])
                S.op("act", lambda e: e.activation(out=ssq_.ap[:, o0 + NH:o0 + 2 * NH], in_=ssq_.ap[:, o0:o0 + NH], func=AF.Sqrt,
                                                   scale=1.0 / DH, bias=EPS), reads=[ssq_.b], writes=[ssq_.b])
                S.op("dve", lambda e: e.reciprocal(out=ssq_.ap[:, o0:o0 + NH], in_=ssq_.ap[:, o0 + NH:o0 + 2 * NH]),
                     reads=[ssq_.b], writes=[ssq_.b])
                if which == "q":
                    S.op("dve", lambda e: e.tensor_scalar(out=ssq_.ap[:, o0:o0 + NH], in0=ssq_.ap[:, o0:o0 + NH], scalar1=DH ** -0.5, scalar2=None, op0=ALU.mult),
                         reads=[ssq_.b], writes=[ssq_.b])
                S.op("dve", lambda e: e.tensor_tensor(
                    out=sq3, in0=sl_.ap.rearrange("p (h d) -> p h d", h=NH),
                    in1=ssq_.ap[:, o0:o0 + NH].unsqueeze(2).to_broadcast([128, NH, DH]), op=ALU.mult),
                    reads=[sl_.b, ssq_.b], writes=[sqt.b])
                if which == "q":
                    S.op("pool", lambda e: e.tensor_tensor(
                        out=qn.ap.rearrange("p (h d) -> p h d", h=NH), in0=sq3,
                        in1=gq.ap.unsqueeze(1).to_broadcast([128, NH, DH]), op=ALU.mult), reads=[sqt.b, gq.b], writes=[qn.b])
                else:
                    kft = kf.next()
                    S.op("pool", lambda e: e.tensor_tensor(
                        out=kft.ap.rearrange("p (h d) -> p h d", h=NH), in0=sq3,
                        in1=gk.ap.unsqueeze(1).to_broadcast([128, NH, DH]), op=ALU.mult), reads=[sqt.b, gk.b], writes=[kft.b])
                    S.op("pool", lambda e: e.tensor_copy(out=kb.ap, in_=kft.ap), reads=[kft.b], writes=[kb.b])
                    kdst = k_p[j, t * 128:(t + 1) * 128, :] if t < NT else k_s[j, :, :]
                    S.op("sp", lambda e: e.dma_start(out=kdst, in_=kft.ap[0:nv, :]), reads=[kft.b], dma="ldk%d" % (kf.i % 2))

            def do_tile(t, x):
                nv = 128 if t < NT else DS
                ss_ = ss[t % 2]
                stats(x.ap, x.b, D, junk, ss_)
                S.op("dve", lambda e: e.scalar_tensor_tensor(out=h.ap, in0=x.ap, scalar=ss_.ap[:, 2:3], in1=gpre.ap, op0=ALU.mult, op1=ALU.mult),
                     reads=[x.b, ss_.b, gpre.b], writes=[h.b])
                transposes(h, hT, tb, "act")
                slq = slots.next()
                proj_tok(hT, w, 0, slq)
                slk = slots.next()
                proj_tok(hT, w, D, slk)
                slv = slots.next()
                proj_tok(hT, w, 2 * D, slv)

                def f_fz(e):
                    ins = None
                    for kc in range(8):
                        ins = e.matmul(fzb.ap[:, 0:NH], lhsT=hT.ap[:, kc * 128:(kc + 1) * 128], rhs=w.ap[:, kc, 4 * D:4 * D + NH],
                                       start=(kc == 0), stop=(kc == 7))
                    return ins
                S.op("pe", f_fz, reads=[hT.b, w.b], writes=[fzb.b], cost=500)
                headnorm("q", slq, t)
                headnorm("k", slk, t)
                vft = vf.next()
                S.op("act", lambda e: e.activation(out=vft.ap, in_=slv.ap, func=AF.Copy), reads=[slv.b], writes=[vft.b])
                vbt = vb.next()
                S.op("pool", lambda e: e.tensor_copy(out=vbt.ap, in_=vft.ap), reads=[vft.b], writes=[vbt.b])
                vdst = v_p[j, t * 128:(t + 1) * 128, :] if t < NT else v_s[j, :, :]
                S.op("sp", lambda e: e.dma_start(out=vdst, in_=vft.ap[0:nv, :]), reads=[vft.b], dma="ldv%d" % (vf.i % 2))
                v_to_scratch(vbt, kcol(t), vb.i % 2)
                slg = slots.next()

                def f_g(e):
                    ins = None
                    for c in range(8):
                        for kc in range(8):
                            ins = e.matmul(slg.ap[:, c * 128:(c + 1) * 128], lhsT=w.ap[:, kc, 3 * D + c * 128:3 * D + (c + 1) * 128],
                                           rhs=hT.ap[:, kc * 128:(kc + 1) * 128], start=(kc == 0), stop=(kc == 7))
                    return ins
                S.op("pe", f_g, reads=[hT.b, w.b], writes=[slg.b], cost=4500)
                GTt = GTs.next()
                S.op("act", lambda e: e.activation(out=GTt.ap, in_=slg.ap, func=AF.Sigmoid), reads=[slg.b], writes=[GTt.b])
                S.op("sp", lambda e: e.dma_start(out=GT[:, qcol(t):qcol(t) + 128].rearrange("(c p) n -> p c n", p=128),
                                                 in_=GTt.ap.rearrange("p (c n) -> p c n", c=8)),
                     reads=[GTt.b], writes=[dbuf("GT", t)], dma="st_gt%d" % (t % 2))
                f1_ = f1[t % 2]
                slot_ = t if t < NT else NT + NP
                lfv = LF.ap[:, slot_ * NH:(slot_ + 1) * NH]
                S.op("dve", lambda e: e.tensor_tensor(out=f1_.ap[:, 0:NH], in0=fzb.ap[:, 0:NH], in1=bfb.ap, op=ALU.add),
                     reads=[fzb.b, bfb.b], writes=[f1_.b])
                S.op("act", lambda e: e.activation(out=f1_.ap[:, 16:32], in_=f1_.ap[:, 0:NH], func=AF.Exp, scale=-1.0), reads=[f1_.b], writes=[f1_.b])
                S.op("act", lambda e: e.activation(out=f1_.ap[:, 32:48], in_=f1_.ap[:, 16:32], func=AF.Ln, bias=1.0), reads=[f1_.b], writes=[f1_.b])
                S.op("dve", lambda e: e.tensor_scalar(out=lfv, in0=f1_.ap[:, 32:48], scalar1=-1.0, scalar2=None, op0=ALU.mult),
                     reads=[f1_.b], writes=[LF.b])
                ldst = lf_p[j, t * 128:(t + 1) * 128, :] if t < NT else lf_s[j, :, :]
                S.op("sp", lambda e: e.dma_start(out=ldst, in_=lfv[0:nv, :]), reads=[LF.b], dma="st_lf")
                QTt = QTs.next()
                transposes(qn, QTt, tb, "act")
                S.op("sp", lambda e: e.dma_start(out=QT[:, qcol(t):qcol(t) + 128].rearrange("(c p) n -> p c n", p=128),
                                                 in_=QTt.ap.rearrange("p (c n) -> p c n", c=8)),
                     reads=[QTt.b], writes=[dbuf("QT", t)], dma="st_qt%d" % (t % 2))
                k_to_scratch(kb, kcol(t))

            xt_next = load_x(0)
            for t in range(NTS):
                x = xt_next
                if t + 1 < NTS:
                    xt_next = load_x(t + 1)
                do_tile(t, x)
            S.barrier()
            A.off = after_lf
            return LF

        def phase_fox_a2(layer, LF):
            CT = A.alloc(KC, F32, "CT")
            psb = Ring([bank[1], bank[2], bank[3], bank[4]])
            ngrp = (NKS + 3) // 4

            def do_grp(g):
                s0 = g * 4
                ns = min(4, NKS - s0)
                pb = psb.next()

                def f(e):
                    ins = None
                    for i in range(ns):
                        s_ = s0 + i
                        ins = e.matmul(pb.ap[0:NH, i * 128:(i + 1) * 128], lhsT=LF.ap[:, s_ * NH:(s_ + 1) * NH], rhs=trif, start=True, stop=True)
                    return ins
                S.op("pe", f, reads=[LF.b] + CONSTB, writes=[pb.b])
                S.op("act", lambda e: e.activation(out=CT.ap[0:NH, s0 * 128:(s0 + ns) * 128], in_=pb.ap[0:NH, 0:ns * 128], func=AF.Copy),
                     reads=[pb.b], writes=[CT.b])
            for g in range(ngrp):
                do_grp(g)

            def fix(s_):
                S.op("dve", lambda e: e.tensor_scalar(out=CT.ap[0:NH, s_ * 128:(s_ + 1) * 128], in0=CT.ap[0:NH, s_ * 128:(s_ + 1) * 128],
                                                      scalar1=CT.ap[0:NH, s_ * 128 - 1:s_ * 128], scalar2=None, op0=ALU.add),
                     reads=[CT.b], writes=[CT.b])
            for s_ in range(1, NKS):
                if s_ == NT:
                    continue
                fix(s_)
            CW = 1024
            r1 = A.alloc(CW, F32, "r1")
            r2 = A.alloc(CW, F32, "r2")
            o6 = Ring([A.alloc(6 * CW, BF16, "o6_%d" % i) for i in range(2)])

            def do_chunk(c0):
                w_ = min(CW, KC - c0)
                o_ = o6.next()
                ov = o_.ap.rearrange("p (a n) -> p a n", a=6)
                cs = CT.ap[0:NH, c0:c0 + w_]
                S.op("act", lambda e: e.activation(out=ov[0:NH, 0, 0:w_], in_=cs, func=AF.Copy), reads=[CT.b], writes=[o_.b])
                S.op("dve", lambda e: e.tensor_tensor(out=r1.ap[0:NH, 0:w_], in0=cs, in1=ov[0:NH, 0, 0:w_], op=ALU.subtract),
                     reads=[CT.b, o_.b], writes=[r1.b])
                S.op("act", lambda e: e.activation(out=ov[0:NH, 1, 0:w_], in_=r1.ap[0:NH, 0:w_], func=AF.Copy), reads=[r1.b], writes=[o_.b])
                S.op("dve", lambda e: e.tensor_tensor(out=r2.ap[0:NH, 0:w_], in0=r1.ap[0:NH, 0:w_], in1=ov[0:NH, 1, 0:w_], op=ALU.subtract),
                     reads=[r1.b, o_.b], writes=[r2.b])
                S.op("act", lambda e: e.activation(out=ov[0:NH, 2, 0:w_], in_=r2.ap[0:NH, 0:w_], func=AF.Copy), reads=[r2.b], writes=[o_.b])
                S.op("pool", lambda e: e.tensor_scalar(out=ov[0:NH, 3:6, 0:w_], in0=ov[0:NH, 0:3, 0:w_], scalar1=-1.0, scalar2=None, op0=ALU.mult),
                     reads=[o_.b], writes=[o_.b])
                S.op("sp", lambda e: e.dma_start(out=CTs[:, :, c0:c0 + w_], in_=ov[0:NH, :, 0:w_]),
                     reads=[o_.b], writes=[dbuf("CTs", c0)], dma="st_ct%d" % (o6.i % 2))
            for c0 in range(0, KC, CW):
                do_chunk(c0)
            S.barrier()
            A.off = PERS

        def phase_fox_b(layer):
            NQB = T // 512
            sets = []
            for i in range(2):
                d = {}
                d["QA"] = A.alloc(TS, BF16, "QA%d" % i)
                d["KA"] = A.alloc(KC, BF16, "KA%d" % i)
                d["VA"] = A.alloc(NKS * 128, BF16, "VA%d" % i)
                d["G"] = A.alloc(TS, BF16, "G%d" % i)
                d["VAv"] = d["VA"].ap.rearrange("p (s n) -> p s n", n=128)
                S.op("pool", lambda e, d=d: e.memset(d["QA"].ap[64:70, :], 1.0), writes=[d["QA"].b])
                S.op("pool", lambda e, d=d: e.memset(d["KA"].ap[64:70, :], 1.0), writes=[d["KA"].b])
                S.op("pool", lambda e, d=d: e.memset(d["VA"].ap, 1.0), writes=[d["VA"].b])
                sets.append(d)
            PT = Ring([A.alloc(512, BF16, "PT%d" % i) for i in range(6)])
            rc = Ring([A.alloc(512, F32, "rc%d" % i) for i in range(2)])
            tmp = Ring([A.alloc(512, F32, "otmp%d" % i) for i in range(2)])
            og = Ring([A.alloc(512, BF16, "og%d" % i) for i in range(2)])
            psS = Ring([bank[0], bank[1], bank[2], bank[3], bank[6], bank[7]])
            psO = Ring([bank[4], bank[5]])
            allQT = [dbuf("QT", t) for t in range(NTS)]
            allGT = [dbuf("GT", t) for t in range(NTS)]
            allKT = [dbuf("KT", c) for c in [t * 128 for t in range(NT)] + [T + i * 128 for i in range(NP + 1)]]
            allVT = [dbuf("VT", c) for c in [t * 128 for t in range(NT)] + [T + i * 128 for i in range(NP + 1)]]
            allCT = [dbuf("CTs", c0) for c0 in range(0, KC, 1024)]

            def load_head(hd):
                d = sets[hd % 2]
                p = hd % 2
                r0 = hd * DH
                S.op("sp", lambda e: e.dma_start(out=d["QA"].ap[0:64, :], in_=QT[r0:r0 + DH, :]), reads=allQT, writes=[d["QA"].b], dma="la%d" % p)
                S.op("sp", lambda e: e.dma_start(out=d["QA"].ap[64:67, 0:T], in_=CTs[hd, 0:3, 0:T]), reads=allCT, writes=[d["QA"].b], dma="lb%d" % p)
                S.op("sp", lambda e: e.dma_start(out=d["QA"].ap[64:67, T:TS], in_=CTs[hd, 0:3, T + PAST:KC]), reads=allCT, writes=[d["QA"].b], dma="lc%d" % p)
                S.op("sp", lambda e: e.dma_start(out=d["KA"].ap[0:64, :], in_=KT[r0:r0 + DH, :]), reads=allKT, writes=[d["KA"].b], dma="ld%d" % p)
                S.op("sp", lambda e: e.dma_start(out=d["KA"].ap[67:70, :], in_=CTs[hd, 3:6, :]), reads=allCT, writes=[d["KA"].b], dma="le%d" % p)
                vo = 0 if p == 0 else 64
                step = 16
                for s0 in range(0, NKS, step):
                    s1 = min(NKS, s0 + step)
                    S.op("sp", lambda e, s0=s0, s1=s1: e.dma_start(out=d["VAv"][:, s0:s1, vo:vo + DH],
                                                                  in_=VT[s0 * 128:s1 * 128, r0:r0 + DH].rearrange("(s p) d -> p s d", p=128)),
                         reads=allVT, writes=[d["VA"].b], dma="lf%d" % p)
                S.op("sp", lambda e: e.dma_start(out=d["G"].ap[vo:vo + 64, :], in_=GT[r0:r0 + DH, :]), reads=allGT, writes=[d["G"].b], dma="lg%d" % p)

            def finish(hd, po, ncols, qc0):
                d = sets[hd % 2]
                p = hd % 2
                orow = slice(0, 64) if p == 0 else slice(64, 128)
                drow = slice(64, 128) if p == 0 else slice(0, 64)
                rc_ = rc.next()
                tmp_ = tmp.next()
                og_ = og.next()
                S.op("dve", lambda e: e.reciprocal(out=rc_.ap[drow, 0:ncols], in_=po.ap[drow, 0:ncols]), reads=[po.b], writes=[rc_.b])
                S.op("dve", lambda e: e.tensor_tensor(out=tmp_.ap[orow, 0:ncols], in0=po.ap[orow, 0:ncols], in1=rc_.ap[drow, 0:ncols], op=ALU.mult),
                     reads=[po.b, rc_.b], writes=[tmp_.b])
                S.op("pool", lambda e: e.tensor_tensor(out=og_.ap[orow, 0:ncols], in0=tmp_.ap[orow, 0:ncols], in1=d["G"].ap[orow, qc0:qc0 + ncols], op=ALU.mult),
                     reads=[tmp_.b, d["G"].b], writes=[og_.b])
                S.op("sp", lambda e: e.dma_start(out=OGT[hd * DH:(hd + 1) * DH, qc0:qc0 + ncols], in_=og_.ap[orow, 0:ncols]),
                     reads=[og_.b], writes=[dbuf("OGT", qc0 // 512)], dma="st_og%d" % (og.i % 2))

            def do_head(hd):
                d = sets[hd % 2]
                QA, KA, VA, VAv = d["QA"], d["KA"], d["VA"], d["VAv"]

                def s_step(kt, qb):
                    off = max(0, kt - 4 * qb) * 128
                    ps_ = psS.next()
                    q0 = qb * 512

                    def f(e):
                        if kt >= 4 * qb:
                            e.matmul(ps_.ap[:, off:off + 128], lhsT=idb, rhs=masknegb, start=True, stop=False)
                            ins = e.matmul(ps_.ap[:, off:off + 128], lhsT=KA.ap[0:70, kt * 128:(kt + 1) * 128],
                                           rhs=QA.ap[0:70, q0 + off:q0 + off + 128], start=False, stop=True)
                            if off + 128 < 512:
                                ins = e.matmul(ps_.ap[:, off + 128:512], lhsT=KA.ap[0:70, kt * 128:(kt + 1) * 128],
                                               rhs=QA.ap[0:70, q0 + off + 128:q0 + 512], start=True, stop=True)
                            return ins
                        return e.matmul(ps_.ap[:, 0:512], lhsT=KA.ap[0:70, kt * 128:(kt + 1) * 128], rhs=QA.ap[0:70, q0:q0 + 512],
                                        start=True, stop=True)
                    S.op("pe", f, reads=[KA.b, QA.b] + CONSTB, writes=[ps_.b], cost=280)
                    pt_ = PT.next()
                    S.op("act", lambda e: e.activation(out=pt_.ap[:, off:512], in_=ps_.ap[:, off:512], func=AF.Exp), reads=[ps_.b], writes=[pt_.b], cost=500)
                    return (kt, off, pt_)

                def pv_step(item, po, nkt):
                    kt, off, pt_ = item
                    S.op("pe", lambda e: e.matmul(po.ap[:, off:512], lhsT=VAv[:, kt, :], rhs=pt_.ap[:, off:512], start=(kt == 0), stop=(kt == nkt - 1),
                                                  skip_group_check=True),
                         reads=[VA.b, pt_.b], writes=[po.b], cost=280)

                for qb in range(NQB):
                    po = psO.next()
                    nkt = 4 * qb + 4
                    pend = []
                    for kt in range(nkt):
                        pend.append(s_step(kt, qb))
                        if len(pend) > 3:
                            pv_step(pend.pop(0), po, nkt)
                    while pend:
                        pv_step(pend.pop(0), po, nkt)
                    finish(hd, po, 512, qb * 512)

                po = psO.next()

                def samp_step(kt):
                    ps_ = psS.next()
                    kc0 = T + kt * 128
                    pt_ = PT.next()
                    if kt < NP:
                        S.op("pe", lambda e: e.matmul(ps_.ap[:, 0:128], lhsT=KA.ap[0:70, kc0:kc0 + 128], rhs=QA.ap[0:70, T:TS], start=True, stop=True),
                             reads=[KA.b, QA.b], writes=[ps_.b])
                        S.op("act", lambda e: e.activation(out=pt_.ap[:, 0:128], in_=ps_.ap[:, 0:128], func=AF.Exp), reads=[ps_.b], writes=[pt_.b])
                        S.op("pe", lambda e: e.matmul(po.ap[:, 0:128], lhsT=VAv[:, NT + kt, :], rhs=pt_.ap[:, 0:128], start=(kt == 0), stop=False,
                                                      skip_group_check=True),
                             reads=[VA.b, pt_.b], writes=[po.b])
                    else:
                        def f(e):
                            e.matmul(ps_.ap[0:DS, 0:128], lhsT=idb[0:DS, 0:DS], rhs=masknegb[0:DS, :], start=True, stop=False)
                            return e.matmul(ps_.ap[0:DS, 0:128], lhsT=KA.ap[0:70, kc0:kc0 + DS], rhs=QA.ap[0:70, T:TS], start=False, stop=True)
                        S.op("pe", f, reads=[KA.b, QA.b] + CONSTB, writes=[ps_.b])
                        S.op("act", lambda e: e.activation(out=pt_.ap[0:DS, 0:128], in_=ps_.ap[0:DS, 0:128], func=AF.Exp), reads=[ps_.b], writes=[pt_.b])
                        S.op("pe", lambda e: e.matmul(po.ap[:, 0:128], lhsT=VAv[0:DS, NT + kt, :], rhs=pt_.ap[0:DS, 0:128], start=(NP == 0), stop=True,
                                                      skip_group_check=True),
                             reads=[VA.b, pt_.b], writes=[po.b])
                for kt in range(NP + 1):
                    samp_step(kt)
                finish(hd, po, 128, T)

            load_head(0)
            for hd in range(NH):
                if hd + 1 < NH:
                    load_head(hd + 1)
                do_head(hd)
            S.barrier()
            A.off = PERS

        def phase_fox_c1(layer):
            j = layer // 2
            wo = A.alloc(8 * D, BF16, "wo")
            wo.ap = wo.ap.rearrange("p (c n) -> p c n", c=8)
            S.op("sp", lambda e: e.dma_start(out=wo.ap, in_=wb_fout[j].rearrange("(c p) n -> p c n", p=128)), reads=[dbuf("wb_fout", j)], writes=[wo.b], dma="wld0")
            gpost = A.alloc(D, F32, "gpost")
            load_bcast(gpost, post_mix[layer:layer + 1, :], "g0")
            gpre2 = A.alloc(D, F32, "gpre2")
            load_bcast(gpre2, pre_ffn[layer:layer + 1, :], "g1")
            tt = make_tail_tiles()
            xr = Ring([A.alloc(D, F32, "x%d" % i) for i in range(2)])
            ogr = Ring([A.alloc(D, BF16, "ogb%d" % i) for i in range(2)])
            slots = Ring([slot2(1), slot2(3), slot2(5)])
            allOG = [dbuf("OGT", q) for q in range(T // 512 + 1)]

            def load(t):
                xt = xr.next()
                src, rd = xsrc(layer, t)
                S.op("sp", lambda e: e.dma_start(out=xt.ap, in_=src), reads=rd, writes=[xt.b], dma="ldx%d" % (t % 2))
                ogt = ogr.next()
                S.op("sp", lambda e: e.dma_start(out=ogt.ap.rearrange("p (c n) -> p c n", c=8),
                                                 in_=OGT[:, qcol(t):qcol(t) + 128].rearrange("(c p) n -> p c n", p=128)),
                     reads=[dbuf("OGT", qcol(t) // 512)], writes=[ogt.b], dma="ldo%d" % (t % 2))
                return xt, ogt

            def do_tile(t, x, ogt):
                sl = slots.next()
                proj_tok(ogt, wo, 0, sl)
                tail(layer, t, sl, x, tt, gpost, gpre2, bank[0])

            nxt = load(0)
            for t in range(NTS):
                x, ogt = nxt
                if t + 1 < NTS:
                    nxt = load(t + 1)
                do_tile(t, x, ogt)
            S.barrier()
            A.off = PERS

        def phase_ffn(layer):
            last = (layer == NL - 1)
            wd = A.alloc(32 * D, BF16, "wd")
            wd.ap = wd.ap.rearrange("p (c n) -> p c n", c=32)
            for q in range(4):
                S.op("sp", lambda e, q=q: e.dma_start(out=wd.ap[:, q * 8:(q + 1) * 8, :],
                                                      in_=wb_down[layer, q * 1024:(q + 1) * 1024, :].rearrange("(c p) n -> p c n", p=128)),
                     reads=[dbuf("wb_down", layer)], writes=[wd.b], dma="wld%d" % (q % 2))
            gpost = A.alloc(D, F32, "gpostf")
            load_bcast(gpost, post_ffn[layer:layer + 1, :], "g0")
            NWR = 6
            wur = Ring([A.alloc(D, BF16, "wu%d" % i) for i in range(NWR)])
            hb = Ring([A.alloc(8 * 512, BF16, "hb%d" % i) for i in range(2)])
            xb = Ring([A.alloc(4 * D, F32, "xb%d" % i) for i in range(2)])
            u2 = A.alloc(32 * 512, BF16, "u2")
            u2v = u2.ap.rearrange("p (j n) -> p j n", j=32)
            rr = Ring([A.alloc(512, F32, "rr%d" % i) for i in range(3)])
            junk = A.alloc(D, BF16, "fjunk")
            ssr = Ring([A.alloc(8, F32, "fss%d" % i) for i in range(4)])
            tmp = A.alloc(D, F32, "ftmp")
            xo = Ring([A.alloc(D, F32, "xo%d" % i) for i in range(2)])
            psU = Ring([bank[0], bank[1], bank[2]])
            psD = Ring([slot2(3), slot2(5)])
            blocks = []
            t0 = 0
            while t0 < NTS:
                nt_ = min(4, NTS - t0)
                if t0 < NT and t0 + nt_ > NT:
                    nt_ = NT - t0
                blocks.append((t0, nt_))
                t0 += nt_
            wcount = [0]

            def load_w(jj):
                wt = wur.next()
                S.op("sp", lambda e: e.dma_start(out=wt.ap, in_=wb_up[layer, jj].rearrange("p c n -> p (c n)")),
                     reads=[dbuf("wb_up", layer)], writes=[wt.b], dma="lwu%d" % (wur.i % NWR))
                return wt

            def load_blk(bi):
                t0, nt_ = blocks[bi]
                ntok = nt_ * 128
                h_ = hb.next()
                S.op("sp", lambda e: e.dma_start(out=h_.ap.rearrange("p (c n) -> p c n", c=8)[:, :, 0:ntok],
                                                 in_=H2T[:, qcol(t0):qcol(t0) + ntok].rearrange("(c p) n -> p c n", p=128)),
                     reads=[dbuf("H2T", t) for t in range(t0, t0 + nt_)], writes=[h_.b], dma="lhb%d" % (bi % 2))
                x_ = xb.next()
                S.op("sp", lambda e: e.dma_start(out=x_.ap.rearrange("p (t d) -> p t d", t=4)[:, 0:nt_, :],
                                                 in_=xres[t0:t0 + nt_].rearrange("t p d -> p t d")),
                     reads=[dbuf("xres", t) for t in range(t0, t0 + nt_)], writes=[x_.b], dma="lxb%d" % (bi % 2))
                return h_, x_

            PRE = 4
            wq = []
            total_w = len(blocks) * 32
            for i in range(min(PRE, total_w)):
                wq.append(load_w(i % 32))
            wissued = [len(wq)]

            def do_up(jj, h_, ntok):
                hv = h_.ap.rearrange("p (c n) -> p c n", c=8)
                wt = wq.pop(0)
                if wissued[0] < total_w:
                    wq.append(load_w(wissued[0] % 32))
                    wissued[0] += 1
                wv = wt.ap.rearrange("p (c n) -> p c n", c=8)
                pu = psU.next()

                def f(e):
                    ins = None
                    for kc in range(8):
                        ins = e.matmul(pu.ap[:, 0:ntok], lhsT=wv[:, kc, :], rhs=hv[:, kc, 0:ntok], start=(kc == 0), stop=(kc == 7))
                    return ins
                S.op("pe", f, reads=[wt.b, h_.b], writes=[pu.b], cost=2150)
                r_ = rr.next()
                S.op("act", lambda e: e.activation(out=r_.ap[:, 0:ntok], in_=pu.ap[:, 0:ntok], func=AF.Relu), reads=[pu.b], writes=[r_.b], cost=600)
                S.op("pool", lambda e: e.tensor_tensor(out=u2v[:, jj, 0:ntok], in0=r_.ap[:, 0:ntok], in1=r_.ap[:, 0:ntok], op=ALU.mult),
                     reads=[r_.b], writes=[u2.b], cost=1200)

            def do_down(t, ti, x_):
                pd = psD.next()

                def f(e):
                    ins = None
                    for half in range(2):
                        for jj in range(32):
                            ins = e.matmul(pd.ap[:, half * 512:(half + 1) * 512], lhsT=u2v[:, jj, ti * 128:(ti + 1) * 128],
                                           rhs=wd.ap[:, jj, half * 512:(half + 1) * 512], start=(jj == 0), stop=(jj == 31))
                    return ins
                S.op("pe", f, reads=[u2.b, wd.b], writes=[pd.b], cost=17000)
                ss_ = ssr.next()
                stats(pd.ap, pd.b, D, junk, ss_)
                S.op("dve", lambda e: e.scalar_tensor_tensor(out=tmp.ap, in0=pd.ap, scalar=ss_.ap[:, 2:3], in1=gpost.ap, op0=ALU.mult, op1=ALU.mult),
                     reads=[pd.b, ss_.b, gpost.b], writes=[tmp.b])
                xo_ = xo.next()
                xin = x_.ap[:, ti * D:(ti + 1) * D]
                S.op("pool", lambda e: e.tensor_tensor(out=xo_.ap, in0=xin, in1=tmp.ap, op=ALU.add), reads=[x_.b, tmp.b], writes=[xo_.b])
                if last:
                    if t < NT:
                        S.op("sp", lambda e: e.dma_start(out=y_p[t * 128:(t + 1) * 128, :], in_=xo_.ap), reads=[xo_.b], dma="st_y%d" % (t % 2))
                    else:
                        S.op("sp", lambda e: e.dma_start(out=y_s[:, :], in_=xo_.ap[0:DS, :]), reads=[xo_.b], dma="st_y%d" % (t % 2))
                else:
                    nrow = 128 if t < NT else DS
                    S.op("sp", lambda e: e.dma_start(out=xres[t, 0:nrow, :], in_=xo_.ap[0:nrow, :]), reads=[xo_.b], writes=[dbuf("xres", t)], dma="st_y%d" % (t % 2))

            nxt = load_blk(0)
            for bi, (t0, nt_) in enumerate(blocks):
                h_, x_ = nxt
                if bi + 1 < len(blocks):
                    nxt = load_blk(bi + 1)
                for jj in range(32):
                    do_up(jj, h_, nt_ * 128)
                for ti in range(nt_):
                    do_down(t0 + ti, ti, x_)
            S.barrier()
            A.off = PERS

        def phase_hgrn(layer):
            j = layer // 2
            w = A.alloc(8 * 4 * D, BF16, "hw")
            w.ap = w.ap.rearrange("p (c n) -> p c n", c=8)

            def ld_w(c):
                S.op("sp", lambda e: e.dma_start(out=w.ap[:, c, :], in_=wb_hin[j, c * 128:(c + 1) * 128, :]),
                     reads=[dbuf("wb_hin", j)], writes=[w.b], dma="wld%d" % (c % 2))
            for c in range(8):
                ld_w(c)
            wo = A.alloc(8 * D, BF16, "hwo")
            wo.ap = wo.ap.rearrange("p (c n) -> p c n", c=8)
            S.op("sp", lambda e: e.dma_start(out=wo.ap, in_=wb_hout[j].rearrange("(c p) n -> p c n", p=128)), reads=[dbuf("wb_hout", j)], writes=[wo.b], dma="wld0")
            gpre = A.alloc(D, F32, "gpre")
            load_bcast(gpre, pre_mix[layer:layer + 1, :], "g0")
            gout = A.alloc(D, F32, "gout")
            load_bcast(gout, hgrn_out_norm[j:j + 1, :], "g1")
            gpost = A.alloc(D, F32, "gpost")
            load_bcast(gpost, post_mix[layer:layer + 1, :], "g2")
            gpre2 = A.alloc(D, F32, "gpre2")
            load_bcast(gpre2, pre_ffn[layer:layer + 1, :], "g3")
            om = oml[layer]
            tt = make_tail_tiles()
            xr = Ring([A.alloc(D, F32, "x%d" % i) for i in range(4)])
            otmp = A.alloc(D, F32, "otmp")
            junk = tt["junk"]
            ss = [A.alloc(8, F32, "ss%d" % i) for i in range(4)]
            h = A.alloc(D, BF16, "h")
            hT = A.alloc(D, BF16, "hT")
            qf = A.alloc(D, F32, "qf")
            kk = A.alloc(D, F32, "kk")
            lg = A.alloc(D, F32, "lg")
            dcl = A.alloc(D, F32, "dcl")
            E2 = A.alloc(D, F32, "E2")
            qt = A.alloc(D, BF16, "qt")
            vbr = [A.alloc(D, BF16, "vb%d" % i) for i in range(2)]
            gtr = [A.alloc(D, F32, "gt%d" % i) for i in range(2)]
            ktr = [A.alloc(D, BF16, "kt%d" % i) for i in range(2)]
            qTr = [A.alloc(D, BF16, "qT%d" % i) for i in range(2)]
            kTr = [A.alloc(D, BF16, "kT%d" % i) for i in range(2)]
            decr = [A.alloc(32, F32, "dec%d" % i) for i in range(2)]
            ATs = Ring([A.alloc(128, BF16, "ATs%d" % i) for i in range(2)])
            AT32 = Ring([A.alloc(128, F32, "AT32_%d" % i) for i in range(2)])
            Sst = A.alloc(HH * 128, F32, "Sst")
            Sb = Ring([A.alloc(128, BF16, "Sb%d" % i) for i in range(2)])
            T1 = Ring([A.alloc(128, F32, "T1_%d" % i) for i in range(2)])
            on2 = A.alloc(D, BF16, "on2")
            onT = A.alloc(D, BF16, "onT")
            Sh = [Tl(Sst.ap[:, hd * 128:(hd + 1) * 128], "S%d" % hd) for hd in range(HH)]
            slotsA = Ring([slot2(1), slot2(3)])
            slotB = slot2(5)
            tb = bank[0]
            b7 = psum[:, 7, :]
            decp = Tl(b7[:, 0:32], "decp")
            psA = Ring([Tl(b7[:, 64:192], "psA0"), Tl(b7[:, 192:320], "psA1")])
            psK = Tl(b7[:, 320:448], "psK")

            def load_x(t):
                xt = xr.next()
                src, rd = xsrc(layer, t)
                S.op("sp", lambda e: e.dma_start(out=xt.ap, in_=src), reads=rd, writes=[xt.b], dma="ldx%d" % (t % 4))
                return xt

            def zero_state(hd):
                S.op("pool", lambda e: e.memset(Sh[hd].ap, 0.0), writes=[Sh[hd].b])
            for hd in range(HH):
                zero_state(hd)

            def do_head(hd, nv, dec_, so, qT, kT, kt_, vb):
                Sp = Sb.next()
                S_ = Sh[hd]
                S.op("dve", lambda e: e.tensor_scalar(out=Sp.ap, in0=S_.ap, scalar1=dec_.ap[:, hd * 4:hd * 4 + 1], scalar2=None, op0=ALU.mult),
                     reads=[S_.b, dec_.b], writes=[Sp.b], cost=220)
                pa = psA.next()
                S.op("pe", lambda e: e.matmul(pa.ap[0:nv, 0:nv], lhsT=kT.ap[:, hd * 128:hd * 128 + nv], rhs=qT.ap[:, hd * 128:hd * 128 + nv], start=True, stop=True),
                     reads=[kT.b, qT.b], writes=[pa.b], cost=120)
                at = ATs.next()
                at32 = AT32.next()
                S.op("dve", lambda e: e.tensor_tensor(out=at32.ap[0:nv, 0:nv], in0=pa.ap[0:nv, 0:nv], in1=himb[0:nv, 0:nv], op=ALU.min),
                     reads=[pa.b] + CONSTB, writes=[at32.b], cost=320)
                S.op("dve", lambda e: e.tensor_tensor(out=at.ap[0:nv, 0:nv], in0=at32.ap[0:nv, 0:nv], in1=lomb[0:nv, 0:nv], op=ALU.max),
                     reads=[at32.b] + CONSTB, writes=[at.b], cost=320)

                def f_o(e):
                    e.matmul(so.ap[0:nv, hd * 128:(hd + 1) * 128], lhsT=qT.ap[:, hd * 128:hd * 128 + nv], rhs=Sp.ap, start=True, stop=False)
                    return e.matmul(so.ap[0:nv, hd * 128:(hd + 1) * 128], lhsT=at.ap[0:nv, 0:nv], rhs=vb.ap[0:nv, hd * 128:(hd + 1) * 128], start=False, stop=True)
                S.op("pe", f_o, reads=[qT.b, Sp.b, at.b, vb.b], writes=[so.b], cost=250)
                S.op("pe", lambda e: e.matmul(psK.ap, lhsT=kt_.ap[0:nv, hd * 128:(hd + 1) * 128], rhs=vb.ap[0:nv, hd * 128:(hd + 1) * 128], start=True, stop=True),
                     reads=[kt_.b, vb.b], writes=[psK.b], cost=120)
                t1 = T1.next()
                S.op("dve", lambda e: e.tensor_scalar(out=t1.ap, in0=S_.ap, scalar1=dec_.ap[:, hd * 4 + 1:hd * 4 + 2], scalar2=None, op0=ALU.mult),
                     reads=[S_.b, dec_.b], writes=[t1.b], cost=220)
                S.op("dve", lambda e: e.scalar_tensor_tensor(out=S_.ap, in0=psK.ap, scalar=dec_.ap[:, hd * 4 + 2:hd * 4 + 3], in1=t1.ap,
                                                             op0=ALU.mult, op1=ALU.add),
                     reads=[psK.b, dec_.b, t1.b], writes=[S_.b], cost=380)

            def stage_a(t, x):
                samp = (t == NT)
                nv = DS if samp else 128
                d1 = d1s if samp else d1p
                sel = sels if samp else selp
                vb, gt, kt_, qT, kT, dec_ = vbr[t % 2], gtr[t % 2], ktr[t % 2], qTr[t % 2], kTr[t % 2], decr[t % 2]
                ss_ = ss[t % 4]
                stats(x.ap, x.b, D, junk, ss_)
                S.op("dve", lambda e: e.scalar_tensor_tensor(out=h.ap, in0=x.ap, scalar=ss_.ap[:, 2:3], in1=gpre.ap, op0=ALU.mult, op1=ALU.mult),
                     reads=[x.b, ss_.b, gpre.b], writes=[h.b])
                transposes(h, hT, tb, "act")
                sl1 = slotsA.next()
                proj_tok(hT, w, 0, sl1)
                S.op("act", lambda e: e.activation(out=qf.ap, in_=sl1.ap, func=AF.Silu), reads=[sl1.b], writes=[qf.b])
                yield
                sl2 = slotsA.next()
                proj_tok(hT, w, D, sl2)
                S.op("act", lambda e: e.activation(out=kk.ap, in_=sl2.ap, func=AF.Sigmoid, scale=-1.0), reads=[sl2.b], writes=[kk.b])
                S.op("dve", lambda e: e.tensor_tensor(out=kk.ap, in0=kk.ap, in1=om.ap, op=ALU.mult), reads=[kk.b, om.b], writes=[kk.b])
                S.op("act", lambda e: e.activation(out=lg.ap, in_=kk.ap, func=AF.Ln, scale=-1.0, bias=1.0), reads=[kk.b], writes=[lg.b])
                yield
                sl3 = slotsA.next()
                proj_tok(hT, w, 2 * D, sl3)
                S.op("act", lambda e: e.activation(out=vb.ap, in_=sl3.ap, func=AF.Copy), reads=[sl3.b], writes=[vb.b])
                yield
                sl4 = slotsA.next()
                proj_tok(hT, w, 3 * D, sl4)
                S.op("act", lambda e: e.activation(out=gt.ap, in_=sl4.ap, func=AF.Silu), reads=[sl4.b], writes=[gt.b])
                yield
                sl5 = slotsA.next()

                def f_d(e):
                    e.matmul(sl5.ap[:, 0:512], lhsT=d1[0:nv, :], rhs=lg.ap[0:nv, 0:512], start=True, stop=True)
                    return e.matmul(sl5.ap[:, 512:1024], lhsT=d1[0:nv, :], rhs=lg.ap[0:nv, 512:1024], start=True, stop=True)
                S.op("pe", f_d, reads=[lg.b] + CONSTB, writes=[sl5.b], cost=2200)

                def f_dec(e):
                    ins = None
                    for hd in range(HH):
                        ins = e.matmul(decp.ap[:, hd * 4:hd * 4 + 3], lhsT=lg.ap[0:nv, hd * 128:(hd + 1) * 128], rhs=sel[0:nv, :], start=True, stop=True)
                    return ins
                S.op("pe", f_dec, reads=[lg.b] + CONSTB, writes=[decp.b], cost=900)
                S.op("dve", lambda e: e.tensor_scalar(out=dcl.ap, in0=sl5.ap, scalar1=-80.0, scalar2=80.0, op0=ALU.max, op1=ALU.min), reads=[sl5.b], writes=[dcl.b])
                S.op("act", lambda e: e.activation(out=E2.ap, in_=dcl.ap, func=AF.Exp, scale=-1.0), reads=[dcl.b], writes=[E2.b])
                S.op("act", lambda e: e.activation(out=dcl.ap, in_=dcl.ap, func=AF.Exp), reads=[dcl.b, E2.b], writes=[dcl.b])
                S.op("act", lambda e: e.activation(out=dec_.ap, in_=decp.ap, func=AF.Exp), reads=[decp.b], writes=[dec_.b], cost=1400)
                S.op("dve", lambda e: e.tensor_tensor(out=qt.ap, in0=qf.ap, in1=dcl.ap, op=ALU.mult), reads=[qf.b, dcl.b], writes=[qt.b])
                S.op("pool", lambda e: e.tensor_tensor(out=kt_.ap, in0=kk.ap, in1=E2.ap, op=ALU.mult), reads=[kk.b, E2.b], writes=[kt_.b])
                yield
                transposes(qt, qT, tb, "dve")
                transposes(kt_, kT, tb, "act")
                yield

            def stage_b1(t):
                samp = (t == NT)
                nv = DS if samp else 128
                vb, gt, kt_, qT, kT, dec_ = vbr[t % 2], gtr[t % 2], ktr[t % 2], qTr[t % 2], kTr[t % 2], decr[t % 2]
                if samp:
                    S.op("sp", lambda e: e.dma_start(out=s_p[j].rearrange("h k v -> k h v"), in_=Sst.ap.rearrange("p (h v) -> p h v", h=HH)),
                         reads=[s_.b for s_ in Sh], dma="st_s")
                    S.op("sp", lambda e: e.dma_start(out=Sst.ap.rearrange("p (h v) -> p h v", h=HH), in_=st[j].rearrange("h k v -> k h v")),
                         reads=[s_.b for s_ in Sh], writes=[s_.b for s_ in Sh], dma="ld_s")
                so = slotB
                for hd in range(HH):
                    do_head(hd, nv, dec_, so, qT, kT, kt_, vb)
                    if hd % 2 == 1:
                        yield
                ss2 = ss[(t + 2) % 4]
                stats(so.ap, so.b, D, junk, ss2)
                S.op("dve", lambda e: e.scalar_tensor_tensor(out=otmp.ap, in0=so.ap, scalar=ss2.ap[:, 2:3], in1=gout.ap, op0=ALU.mult, op1=ALU.mult),
                     reads=[so.b, ss2.b, gout.b], writes=[otmp.b])
                yield

            def stage_b2(t, x):
                gt = gtr[t % 2]
                S.op("pool", lambda e: e.tensor_tensor(out=on2.ap, in0=otmp.ap, in1=gt.ap, op=ALU.mult), reads=[otmp.b, gt.b], writes=[on2.b])
                transposes(on2, onT, tb, "act")
                yield
                sl6 = slotsA.next()
                proj_tok(onT, wo, 0, sl6)
                yield
                tail(layer, t, sl6, x, tt, gpost, gpre2, tb)
                yield

            def drain(g):
                if g is None:
                    return None
                try:
                    next(g)
                    return g
                except StopIteration:
                    return None

            def pipeline(tiles):
                n = len(tiles)
                xs = {0: load_x(tiles[0])}
                for step in range(n + 2):
                    if step + 1 < n:
                        xs[step + 1] = load_x(tiles[step + 1])
                    gens = []
                    if step < n:
                        gens.append(stage_a(tiles[step], xs[step]))
                    if 0 <= step - 1 < n:
                        gens.append(stage_b1(tiles[step - 1]))
                    if 0 <= step - 2 < n:
                        gens.append(stage_b2(tiles[step - 2], xs[step - 2]))
                    while gens:
                        gens = [g for g in (drain(g) for g in gens) if g is not None]

            pipeline(list(range(NTS)))
            S.op("sp", lambda e: e.dma_start(out=s_s[j].rearrange("h k v -> k h v"), in_=Sst.ap.rearrange("p (h v) -> p h v", h=HH)),
                 reads=[s_.b for s_ in Sh], dma="st_s")
            S.barrier()
            A.off = PERS

        phase_wcast()
        for layer in range(NL):
            if layer % 2 == 0:
                LF = phase_fox_a(layer)
                phase_fox_a2(layer, LF)
                phase_fox_b(layer)
                phase_fox_c1(layer)
            else:
                phase_hgrn(layer)
            phase_ffn(layer)
        S.emit()
    return nc


_CACHE = {}


def get_nc(T, PAST, NL=4):
    key = (T, PAST, NL)
    if key not in _CACHE:
        _CACHE[key] = build(T, PAST, NL)
    return _CACHE[key]


def make_in_map(c, inp, T, PAST):
    f = lambda a: np.ascontiguousarray(np.asarray(a, dtype=np.float32))
    m = {
        "x_p": f(inp["x_prompt"][c]),
        "x_s": f(inp["x_sample"][c]),
        "ck": f(np.asarray(inp["cache_k"])[:, c].reshape(-1, PAST, D)),
        "cv": f(np.asarray(inp["cache_v"])[:, c].reshape(-1, PAST, D)),
        "clf": f(np.asarray(inp["cache_logf"])[:, c]),
        "st": f(np.asarray(inp["state_s"])[:, c]),
        "fox_w_in": f(inp["fox_w_in"]), "fox_b_f": f(inp["fox_b_f"]), "fox_q_norm": f(inp["fox_q_norm"]),
        "fox_k_norm": f(inp["fox_k_norm"]), "fox_w_out": f(inp["fox_w_out"]), "hgrn_w_in": f(inp["hgrn_w_in"]),
        "hgrn_lb": f(inp["hgrn_lb_logits"]), "hgrn_out_norm": f(inp["hgrn_out_norm"]), "hgrn_w_out": f(inp["hgrn_w_out"]),
        "pre_mix": f(inp["pre_mix_norm"]), "post_mix": f(inp["post_mix_norm"]), "pre_ffn": f(inp["pre_ffn_norm"]),
        "post_ffn": f(inp["post_ffn_norm"]), "ffn_up": f(inp["ffn_w_up"]), "ffn_down": f(inp["ffn_w_down"]),
        "consts": make_consts(),
    }
    return m


def assemble(results, B, T):
    def st(name, shape_tail=None):
        return np.stack([np.asarray(r[name], dtype=np.float32) for r in results], axis=0)
    y_p = st("y_p")
    y_s = st("y_s")
    k_p = np.moveaxis(st("k_p"), 0, 1).reshape(-1, B, T, NH, DH)
    v_p = np.moveaxis(st("v_p"), 0, 1).reshape(-1, B, T, NH, DH)
    lf_p = np.moveaxis(st("lf_p"), 0, 1)
    s_p = np.moveaxis(st("s_p"), 0, 1)
    k_s = np.moveaxis(st("k_s"), 0, 1).reshape(-1, B, DS, NH, DH)
    v_s = np.moveaxis(st("v_s"), 0, 1).reshape(-1, B, DS, NH, DH)
    lf_s = np.moveaxis(st("lf_s"), 0, 1)
    s_s = np.moveaxis(st("s_s"), 0, 1)
    return (y_p, y_s, k_p, v_p, lf_p, s_p, k_s, v_s, lf_s, s_s)


def kernel(**inputs):
    xp = np.asarray(inputs["x_prompt"])
    B, T = xp.shape[0], xp.shape[1]
    PAST = np.asarray(inputs["cache_k"]).shape[2]
    nc = get_nc(T, PAST)
    in_maps = [make_in_map(c, inputs, T, PAST) for c in range(B)]
    res = run_bass_kernel_spmd(nc, in_maps, core_ids=list(range(B)))
    return assemble(res.results, B, T)
```

```python
import contextlib
import numpy as np
import concourse.bass as bass
import concourse.mybir as mybir
from concourse.bass_utils import run_bass_kernel_spmd

F32 = mybir.dt.float32
BF16 = mybir.dt.bfloat16
AF = mybir.ActivationFunctionType
ALU = mybir.AluOpType
AX = mybir.AxisListType

D = 1024
NH = 16
DH = 64
HH = 8
DFF = 4096
EPS = 1e-6
DS = 32
FW = 4 * D + NH


class Buf:
    __slots__ = ("name", "w", "r")

    def __init__(self, name=""):
        self.name = name
        self.w = None
        self.r = []


class Op:
    __slots__ = ("eng", "fn", "deps", "odeps", "sig", "sigval", "dma", "cost", "seq", "bar")

    def __init__(self, eng, fn, dma):
        self.eng = eng
        self.fn = fn
        self.deps = []
        self.odeps = []
        self.sig = False
        self.sigval = None
        self.dma = dma
        self.cost = 1000
        self.seq = 0
        self.bar = 0


DEFCOST = {"pe": 2000, "act": 1100, "dve": 800, "pool": 2300, "sp": 150}
DMA_LAT = 3000
RESCHEDULE = True
RESCHED_SEGS = {1, 2, 3, 4, 5, 6, 8, 9, 10, 11, 12, 13, 15}
RESCHED_ENGS = None


class Sched:
    CE = ("pe", "act", "dve", "pool")
    ENGS = ("pe", "act", "dve", "pool", "sp")

    def __init__(self, nc):
        self.nc = nc
        self.ops = {e: [] for e in self.ENGS}
        self.streams = {}
        self.last_dma = {}
        self.nseq = 0

    def op(self, eng, fn, reads=(), writes=(), dma=None, extra=(), cost=None):
        o = Op(eng, fn, dma)
        o.cost = cost if cost is not None else DEFCOST[eng]
        self.nseq += 1
        o.seq = self.nseq
        deps = []
        seen = set()

        def add(d):
            if d is None or d is o or id(d) in seen:
                return
            seen.add(id(d))
            deps.append(d)

        for b in reads:
            add(b.w)
        for b in writes:
            add(b.w)
            for r in b.r:
                add(r)
        for d in extra:
            add(d)
        if dma is not None:
            add(self.last_dma.get(dma))
            self.last_dma[dma] = o
            n = self.streams.setdefault(dma, [0])
            n[0] += 1
            o.sigval = 16 * n[0]
        for d in deps:
            if d.dma is None and d.eng == eng and eng == "pe":
                o.odeps.append(d)
                continue
            o.deps.append(d)
            if d.dma is None:
                d.sig = True
        for b in writes:
            b.w = o
            b.r = []
        for b in reads:
            if b.w is not o:
                b.r.append(o)
        self.ops[eng].append(o)
        return o

    def barrier(self):
        firsts = []
        alld = list(self.last_dma.values())
        for e in self.CE:
            o = Op(e, lambda eh: eh.nop(), None)
            o.bar = 1
            for d in alld:
                o.deps.append(d)
            self.ops[e].append(o)
            firsts.append(o)
        for e in self.ENGS:
            o = Op(e, lambda eh: eh.nop(), None)
            o.bar = 2
            for d in firsts:
                if d.eng == e:
                    continue
                o.deps.append(d)
                d.sig = True
            self.ops[e].append(o)

    def _reschedule(self):
        import heapq
        segs = {e: [] for e in self.ENGS}
        nseg = 0
        for e in self.ENGS:
            cur = []
            bars = []
            for o in self.ops[e]:
                if o.bar:
                    bars.append(o)
                    if o.bar == 2:
                        segs[e].append((cur, bars))
                        cur, bars = [], []
                else:
                    assert not bars
                    cur.append(o)
            segs[e].append((cur, bars))
            nseg = max(nseg, len(segs[e]))
        for e in self.ENGS:
            while len(segs[e]) < nseg:
                segs[e].append(([], []))
        for k in range(nseg):
            allops = []
            for e in self.ENGS:
                allops.extend(segs[e][k][0])
            if RESCHEDULE and allops and (RESCHED_SEGS is None or k in RESCHED_SEGS):
                inseg = {id(o) for o in allops}
                nd = {}
                users = {}
                for o in allops:
                    c = 0
                    for d in o.deps + o.odeps:
                        if id(d) in inseg:
                            c += 1
                            users.setdefault(id(d), []).append(o)
                    nd[id(o)] = c
                ready = {e: [] for e in self.ENGS}
                for o in allops:
                    if nd[id(o)] == 0:
                        heapq.heappush(ready[o.eng], (o.seq, id(o), o))
                free_at = {e: 0 for e in self.ENGS}
                start = {}
                events = [(0, 0)]
                evn = 1
                pending = []
                ndone = 0
                now = 0
                while ndone < len(allops):
                    while pending and pending[0][0] <= now:
                        _, _, o = heapq.heappop(pending)
                        ndone += 1
                        for u in users.get(id(o), ()):
                            nd[id(u)] -= 1
                            if nd[id(u)] == 0:
                                heapq.heappush(ready[u.eng], (u.seq, id(u), u))
                    if ndone >= len(allops):
                        break
                    progressed = False
                    for e in self.ENGS:
                        if free_at[e] <= now and ready[e]:
                            _, _, o = heapq.heappop(ready[e])
                            start[id(o)] = now
                            free_at[e] = now + o.cost
                            done = now + (o.cost + DMA_LAT if o.dma is not None else o.cost)
                            evn += 1
                            heapq.heappush(pending, (done, evn, o))
                            progressed = True
                    if progressed:
                        continue
                    nxt = []
                    if pending:
                        nxt.append(pending[0][0])
                    for e in self.ENGS:
                        if ready[e] and free_at[e] > now:
                            nxt.append(free_at[e])
                    if not nxt:
                        left = [o for o in allops if id(o) not in start]
                        print("STUCK0 seg", k, "nops", len(allops), "ndone", ndone, "started", len(start), "uniq", len({id(o) for o in allops}))
                        o = left[0]
                        print("STUCK seg", k, "nops", len(allops), "left", len(left), "first eng", o.eng, "seq", o.seq, "nd", nd[id(o)],
                              [(d.eng, d.seq, id(d) in inseg, id(d) in start, d.bar) for d in o.deps + o.odeps])
                    assert nxt, "scheduler stuck"
                    now = max(now + 1, min(nxt))
                for e in self.ENGS:
                    if RESCHED_ENGS is None or e in RESCHED_ENGS:
                        segs[e][k][0].sort(key=lambda o: (start[id(o)], o.seq))
            for e in self.CE:
                lst, bars = segs[e][k]
                for bo in bars:
                    if bo.bar == 1 and lst:
                        bo.deps.append(lst[-1])
                        lst[-1].sig = True
        for e in self.ENGS:
            out = []
            for lst, bars in segs[e]:
                out.extend(lst)
                out.extend(bars)
            self.ops[e] = out

    def _check(self):
        pos = {e: 0 for e in self.ENGS}
        done = set()
        n = sum(len(v) for v in self.ops.values())
        while len(done) < n:
            prog = False
            for e in self.ENGS:
                while pos[e] < len(self.ops[e]):
                    o = self.ops[e][pos[e]]
                    if all(id(d) in done for d in o.deps) and all(id(d) in done for d in o.odeps):
                        done.add(id(o))
                        pos[e] += 1
                        prog = True
                    else:
                        break
            if not prog:
                for e in self.ENGS:
                    if pos[e] < len(self.ops[e]):
                        o = self.ops[e][pos[e]]
                        print("DEADLOCK", e, "pos", pos[e], "seq", o.seq, "bar", o.bar, "waiting on",
                              [(d.eng, d.seq, d.bar, d.dma) for d in o.deps + o.odeps if id(d) not in done])
                raise RuntimeError("deadlock in emitted order")

    def emit(self):
        nc = self.nc
        self._reschedule()
        self._check()
        with contextlib.ExitStack() as es:
            esem = {e: es.enter_context(nc.semaphore("s_" + e)) for e in self.CE}
            ssem = {k: es.enter_context(nc.semaphore("d%d" % i)) for i, k in enumerate(self.streams)}
            for e in self.CE:
                c = 0
                for o in self.ops[e]:
                    if o.dma is None and o.sig:
                        c += 1
                        o.sigval = c
            block = es.enter_context(nc.Block())

            def run(ename, eh):
                waited = {}
                for o in self.ops[ename]:
                    for d in o.deps:
                        sem = ssem[d.dma] if d.dma is not None else esem[d.eng]
                        key = id(sem)
                        if waited.get(key, 0) >= d.sigval:
                            continue
                        waited[key] = d.sigval
                        eh.wait_ge(sem, d.sigval)
                    ins = o.fn(eh)
                    if o.dma is not None:
                        ins.then_inc(ssem[o.dma], 16)
                    elif o.sig:
                        ins.then_inc(esem[ename], 1)
                if ename == "sp":
                    for k, n in self.streams.items():
                        eh.wait_ge(ssem[k], 16 * n[0])

            @block.tensor
            def _(e):
                run("pe", e)

            @block.scalar
            def _(e):
                run("act", e)

            @block.vector
            def _(e):
                run("dve", e)

            @block.gpsimd
            def _(e):
                run("pool", e)

            @block.sync
            def _(e):
                run("sp", e)


class Tl:
    __slots__ = ("ap", "b")

    def __init__(self, ap, name=""):
        self.ap = ap
        self.b = Buf(name)


class Arena:
    def __init__(self, ap, size):
        self.ap = ap
        self.size = size
        self.off = 0

    def alloc(self, n, dt=BF16, name=""):
        w = n * 2 if dt == F32 else n
        off = self.off
        self.off += (w + 31) // 32 * 32
        assert self.off <= self.size, ("SBUF arena overflow", name, self.off, self.size)
        v = self.ap[:, off:off + w]
        if dt == F32:
            v = v.bitcast(F32)
        return Tl(v, name)


class Ring:
    def __init__(self, tiles):
        self.t = tiles
        self.i = -1

    def next(self):
        self.i += 1
        return self.t[self.i % len(self.t)]


def make_consts():
    s = np.arange(128)[:, None]
    t = np.arange(128)[None, :]
    c = {}
    c["ident"] = (s == t).astype(np.float32)
    c["tri"] = (s <= t).astype(np.float32)
    c["maskneg"] = np.where(s > t, -30000.0, 0.0).astype(np.float32)
    c["d1p"] = ((s <= t).astype(np.float32) - (s <= 63).astype(np.float32) * np.ones_like(t, dtype=np.float32))
    vs = (s < DS) & (t < DS)
    c["d1s"] = np.where(vs, (s <= t).astype(np.float32) - (s <= 15).astype(np.float32), 0.0).astype(np.float32)
    sel = np.zeros((128, 8), np.float32)
    sv = np.arange(128)
    sel[:, 0] = sv <= 63
    sel[:, 1] = 1.0
    sel[:, 2] = sv > 63
    sel[:, 4] = sv <= 15
    sel[:, 5] = sv < DS
    sel[:, 6] = (sv > 15) & (sv < DS)
    c["sel"] = sel
    c["him"] = np.where(s <= t, 3.0e38, 0.0).astype(np.float32)
    c["lom"] = np.where(s <= t, -3.0e38, 0.0).astype(np.float32)
    return np.concatenate([c["ident"], c["tri"], c["maskneg"], c["him"], c["lom"], c["d1p"], c["d1s"], c["sel"]], axis=1).astype(np.float32)


NCONST = 7 * 128 + 8


def build(T, PAST, NL=4):
    NT = T // 128
    NP = PAST // 128
    NTS = NT + 1
    TS = T + 128
    KC = T + PAST + 128
    NKS = KC // 128
    NA = (NL + 1) // 2
    NR = NL // 2

    nc = bass.Bass("TRN2", target_bir_lowering=False)

    def din(name, shape):
        return nc.dram_tensor(name, list(shape), F32, kind="ExternalInput").ap()

    def dout(name, shape):
        return nc.dram_tensor(name, list(shape), F32, kind="ExternalOutput").ap()

    def dscr(name, shape, dt):
        return nc.dram_tensor(name, list(shape), dt, kind="Internal").ap()

    x_p = din("x_p", [T, D])
    x_s = din("x_s", [DS, D])
    ck = din("ck", [NA, PAST, D])
    cv = din("cv", [NA, PAST, D])
    clf = din("clf", [NA, PAST, NH])
    st = din("st", [NR, HH, 128, 128])
    fox_w_in = din("fox_w_in", [NA, D, FW])
    fox_b_f = din("fox_b_f", [NA, NH])
    fox_q_norm = din("fox_q_norm", [NA, DH])
    fox_k_norm = din("fox_k_norm", [NA, DH])
    fox_w_out = din("fox_w_out", [NA, D, D])
    hgrn_w_in = din("hgrn_w_in", [NR, D, 4 * D])
    hgrn_lb = din("hgrn_lb", [4, D])
    hgrn_out_norm = din("hgrn_out_norm", [NR, D])
    hgrn_w_out = din("hgrn_w_out", [NR, D, D])
    pre_mix = din("pre_mix", [NL, D])
    post_mix = din("post_mix", [NL, D])
    pre_ffn = din("pre_ffn", [NL, D])
    post_ffn = din("post_ffn", [NL, D])
    ffn_up = din("ffn_up", [NL, D, DFF])
    ffn_down = din("ffn_down", [NL, DFF, D])
    consts = din("consts", [128, NCONST])

    y_p = dout("y_p", [T, D])
    y_s = dout("y_s", [DS, D])
    k_p = dout("k_p", [NA, T, D])
    v_p = dout("v_p", [NA, T, D])
    lf_p = dout("lf_p", [NA, T, NH])
    s_p = dout("s_p", [NR, HH, 128, 128])
    k_s = dout("k_s", [NA, DS, D])
    v_s = dout("v_s", [NA, DS, D])
    lf_s = dout("lf_s", [NA, DS, NH])
    s_s = dout("s_s", [NR, HH, 128, 128])

    xres = dscr("xres", [NTS, 128, D], F32)
    wb_fin = dscr("wb_fin", [NA, D, FW], BF16)
    wb_fout = dscr("wb_fout", [NA, D, D], BF16)
    wb_hin = dscr("wb_hin", [NR, D, 4 * D], BF16)
    wb_hout = dscr("wb_hout", [NR, D, D], BF16)
    wb_up = dscr("wb_up", [NL, 32, 128, 8, 128], BF16)
    wb_down = dscr("wb_down", [NL, DFF, D], BF16)
    QT = dscr("QT", [D, TS], BF16)
    KT = dscr("KT", [D, KC], BF16)
    VT = dscr("VT", [KC, D], BF16)
    GT = dscr("GT", [D, TS], BF16)
    CTs = dscr("CTs", [NH, 6, KC], BF16)
    OGT = dscr("OGT", [D, TS], BF16)
    H2T = dscr("H2T", [D, TS], BF16)

    db = {}

    def dbuf(*key):
        if key not in db:
            db[key] = Buf(str(key))
        return db[key]

    ARENA = 106000
    with contextlib.ExitStack() as es:
        arena_t = es.enter_context(nc.sbuf_tensor("arena", [128, ARENA], BF16))
        psum = es.enter_context(nc.psum_tensor("psum", [128, 8, 512], F32))
        A = Arena(arena_t, ARENA)
        S = Sched(nc)

        bank = [Tl(psum[:, k, :], "bank%d" % k) for k in range(8)]

        def slot2(k):
            return Tl(psum[:, k:k + 2, :].rearrange("p a b -> p (a b)"), "slot%d" % k)

        cst = A.alloc(NCONST, F32, "consts")
        S.op("sp", lambda e: e.dma_start(out=cst.ap, in_=consts[:, :]), writes=[cst.b], dma="cst")
        identf = cst.ap[:, 0:128]
        trif = cst.ap[:, 128:256]
        masknegf = cst.ap[:, 256:384]
        d1p = cst.ap[:, 640:768]
        d1s = cst.ap[:, 768:896]
        selp = cst.ap[:, 896:899]
        sels = cst.ap[:, 900:903]
        cb = A.alloc(5 * 128, BF16, "constb")
        S.op("dve", lambda e: e.tensor_copy(out=cb.ap, in_=cst.ap[:, 0:640]), reads=[cst.b], writes=[cb.b])
        idb = cb.ap[:, 0:128]
        mask01b = cb.ap[:, 128:256]
        masknegb = cb.ap[:, 256:384]
        himb = cb.ap[:, 384:512]
        lomb = cb.ap[:, 512:640]
        CONSTB = [cst.b, cb.b]
        nhalf = A.alloc(16, F32, "nhalf")
        S.op("pool", lambda e: e.memset(nhalf.ap, -0.5), writes=[nhalf.b])

        oml = {}
        if NR > 0:
            for layer in range(1, NL, 2):
                oml[layer] = A.alloc(D, F32, "oml%d" % layer)
            keep = A.off
            L = [A.alloc(D, F32, "lbl%d" % i) for i in range(4)]
            mx = A.alloc(D, F32, "lbmx")
            sm = A.alloc(D, F32, "lbsum")

            def ld_l(i):
                S.op("sp", lambda e: e.dma_start(out=L[i].ap, in_=hgrn_lb[i:i + 1, :].partition_broadcast(128)), writes=[L[i].b], dma="lb%d" % i)
            for i in range(4):
                ld_l(i)

            def tt_(out, a_, b_, op):
                S.op("dve", lambda e: e.tensor_tensor(out=out.ap, in0=a_.ap, in1=b_.ap, op=op), reads=[a_.b, b_.b], writes=[out.b])
            tt_(mx, L[0], L[1], ALU.max)
            tt_(mx, mx, L[2], ALU.max)
            tt_(mx, mx, L[3], ALU.max)

            def ex_(i):
                tt_(L[i], L[i], mx, ALU.subtract)
                S.op("act", lambda e: e.activation(out=L[i].ap, in_=L[i].ap, func=AF.Exp), reads=[L[i].b], writes=[L[i].b])
            for i in range(4):
                ex_(i)
            tt_(sm, L[0], L[1], ALU.add)
            tt_(sm, sm, L[2], ALU.add)
            tt_(sm, sm, L[3], ALU.add)
            S.op("dve", lambda e: e.reciprocal(out=sm.ap, in_=sm.ap), reads=[sm.b], writes=[sm.b])

            def mk_oml(layer):
                o_ = oml[layer]
                S.op("dve", lambda e: e.tensor_copy(out=o_.ap, in_=L[1].ap), reads=[L[1].b], writes=[o_.b])
                for i in range(2, layer + 1):
                    tt_(o_, o_, L[i], ALU.add)
                tt_(o_, o_, sm, ALU.mult)
                S.op("dve", lambda e: e.tensor_scalar(out=o_.ap, in0=o_.ap, scalar1=-1.0, scalar2=1.0, op0=ALU.mult, op1=ALU.add),
                     reads=[o_.b], writes=[o_.b])
            for layer in range(1, NL, 2):
                mk_oml(layer)
            S.barrier()
            A.off = keep

        PERS = A.off

        def stats(src_ap, src_b, n, junk, ss):
            S.op("pool", lambda e: e.memset(ss.ap[:, 0:1], 0.0), writes=[ss.b], cost=80)
            S.op("act", lambda e: e.activation(out=junk.ap[:, 0:n], in_=src_ap, func=AF.Square, accum_out=ss.ap[:, 0:1]),
                 reads=[src_b, ss.b], writes=[ss.b])
            S.op("act", lambda e: e.activation(out=ss.ap[:, 1:2], in_=ss.ap[:, 0:1], func=AF.Sqrt, scale=1.0 / n, bias=EPS),
                 reads=[ss.b], writes=[ss.b], cost=1500)
            S.op("dve", lambda e: e.reciprocal(out=ss.ap[:, 2:3], in_=ss.ap[:, 1:2]), reads=[ss.b], writes=[ss.b], cost=200)

        def transposes(src, dstT, tb, copy_eng):
            tv = tb.ap.bitcast(BF16)

            def f(e):
                ins = None
                for c in range(8):
                    ins = e.transpose(out=tv[:, c * 128:(c + 1) * 128], in_=src.ap[:, c * 128:(c + 1) * 128], identity=idb)
                return ins
            S.op("pe", f, reads=[src.b] + CONSTB, writes=[tb.b], cost=900)
            if copy_eng == "act":
                S.op("act", lambda e: e.activation(out=dstT.ap, in_=tv, func=AF.Copy), reads=[tb.b], writes=[dstT.b])
            else:
                S.op("dve", lambda e: e.tensor_copy(out=dstT.ap, in_=tv), reads=[tb.b], writes=[dstT.b])

        def proj_tok(hT, w, wcol0, sl):
            def f(e):
                ins = None
                for half in range(2):
                    for kc in range(8):
                        ins = e.matmul(sl.ap[:, half * 512:(half + 1) * 512], lhsT=hT.ap[:, kc * 128:(kc + 1) * 128],
                                       rhs=w.ap[:, kc, wcol0 + half * 512: wcol0 + (half + 1) * 512],
                                       start=(kc == 0), stop=(kc == 7))
                return ins
            S.op("pe", f, reads=[hT.b, w.b], writes=[sl.b], cost=4300)

        def load_bcast(dst, src_row, key):
            S.op("sp", lambda e: e.dma_start(out=dst.ap, in_=src_row.partition_broadcast(128)), writes=[dst.b], dma=key)

        def xsrc(layer, t):
            if layer == 0 and t < NT:
                return x_p[t * 128:(t + 1) * 128, :], []
            return xres[t], [dbuf("xres", t)]

        def qcol(t):
            return t * 128

        def kcol(t):
            return t * 128 if t < NT else T + PAST

        def phase_wcast():
            mark = A.off
            CH = 4096
            NWB = 4
            fb = [A.alloc(CH, F32, "wf%d" % i) for i in range(NWB)]
            bb = [A.alloc(CH, BF16, "wb%d" % i) for i in range(NWB)]
            jobs = []

            def flat(ap2d):
                return ap2d.rearrange("r c -> (r c)").rearrange("(p n) -> p n", p=128)

            for j in range(NA):
                jobs.append((flat(fox_w_in[j]), flat(wb_fin[j]), dbuf("wb_fin", j)))
                jobs.append((flat(fox_w_out[j]), flat(wb_fout[j]), dbuf("wb_fout", j)))
            for j in range(NR):
                jobs.append((flat(hgrn_w_in[j]), flat(wb_hin[j]), dbuf("wb_hin", j)))
                jobs.append((flat(hgrn_w_out[j]), flat(wb_hout[j]), dbuf("wb_hout", j)))
            for l in range(NL):
                jobs.append((flat(ffn_down[l]), flat(wb_down[l]), dbuf("wb_down", l)))
            step = 0
            engs = ("dve", "pool", "act")
            for src, dst, tok in jobs:
                n = src.shape[1]
                for c0 in range(0, n, CH):
                    w_ = min(CH, n - c0)
                    f_, b_ = fb[step % NWB], bb[step % NWB]
                    S.op("sp", lambda e, f_=f_, src=src, c0=c0, w_=w_: e.dma_start(out=f_.ap[:, 0:w_], in_=src[:, c0:c0 + w_]),
                         writes=[f_.b], dma="wl%d" % (step % NWB))
                    eng = engs[step % 3]
                    if eng == "act":
                        S.op("act", lambda e, f_=f_, b_=b_, w_=w_: e.activation(out=b_.ap[:, 0:w_], in_=f_.ap[:, 0:w_], func=AF.Copy),
                             reads=[f_.b], writes=[b_.b])
                    else:
                        S.op(eng, lambda e, f_=f_, b_=b_, w_=w_: e.tensor_copy(out=b_.ap[:, 0:w_], in_=f_.ap[:, 0:w_]),
                             reads=[f_.b], writes=[b_.b])
                    S.op("sp", lambda e, b_=b_, dst=dst, c0=c0, w_=w_: e.dma_start(out=dst[:, c0:c0 + w_], in_=b_.ap[:, 0:w_]),
                         reads=[b_.b], writes=[tok], dma="ws%d" % (step % NWB))
                    step += 1
            for l in range(NL):
                for c in range(8):
                    f_, b_ = fb[step % NWB], bb[step % NWB]
                    S.op("sp", lambda e, f_=f_, l=l, c=c: e.dma_start(out=f_.ap[:, 0:DFF], in_=ffn_up[l, c * 128:(c + 1) * 128, :]),
                         writes=[f_.b], dma="wl%d" % (step % NWB))
                    eng = engs[step % 3]
                    if eng == "act":
                        S.op("act", lambda e, f_=f_, b_=b_: e.activation(out=b_.ap[:, 0:DFF], in_=f_.ap[:, 0:DFF], func=AF.Copy),
                             reads=[f_.b], writes=[b_.b])
                    else:
                        S.op(eng, lambda e, f_=f_, b_=b_: e.tensor_copy(out=b_.ap[:, 0:DFF], in_=f_.ap[:, 0:DFF]),
                             reads=[f_.b], writes=[b_.b])
                    S.op("sp", lambda e, b_=b_, l=l, c=c: e.dma_start(
                        out=wb_up[l, :, :, c, :].rearrange("j p n -> p j n"),
                        in_=b_.ap[:, 0:DFF].rearrange("p (j n) -> p j n", n=128)),
                        reads=[b_.b], writes=[dbuf("wb_up", l)], dma="ws%d" % (step % NWB))
                    step += 1
            xt = fb[step % NWB]
            S.op("pool", lambda e: e.memset(xt.ap[:, 0:D], 0.0), writes=[xt.b])
            S.op("sp", lambda e: e.dma_start(out=xt.ap[0:DS, 0:D], in_=x_s[:, :]), reads=[xt.b], writes=[xt.b], dma="wl%d" % (step % NWB))
            S.op("sp", lambda e: e.dma_start(out=xres[NT], in_=xt.ap[:, 0:D]), reads=[xt.b], writes=[dbuf("xres", NT)], dma="ws%d" % (step % NWB))
            S.barrier()
            A.off = mark

        def make_tail_tiles():
            d = {}
            d["junk"] = A.alloc(D, BF16, "tjunk")
            d["ss"] = [A.alloc(8, F32, "tss%d" % i) for i in range(4)]
            d["tmp"] = A.alloc(D, F32, "ttmp")
            d["xn"] = Ring([A.alloc(D, F32, "txn%d" % i) for i in range(1)])
            d["h2"] = A.alloc(D, BF16, "th2")
            d["h2T"] = Ring([A.alloc(D, BF16, "th2T%d" % i) for i in range(1)])
            return d

        def tail(layer, t, sl, x, tt, gpost, gpre2, tb):
            ssA = tt["ss"][(2 * t) % 4]
            ssB = tt["ss"][(2 * t + 1) % 4]
            stats(sl.ap, sl.b, D, tt["junk"], ssA)
            tmp = tt["tmp"]
            S.op("dve", lambda e: e.scalar_tensor_tensor(out=tmp.ap, in0=sl.ap, scalar=ssA.ap[:, 2:3], in1=gpost.ap, op0=ALU.mult, op1=ALU.mult),
                 reads=[sl.b, ssA.b, gpost.b], writes=[tmp.b])
            xn = tt["xn"].next()
            S.op("pool", lambda e: e.tensor_tensor(out=xn.ap, in0=x.ap, in1=tmp.ap, op=ALU.add), reads=[x.b, tmp.b], writes=[xn.b])
            nrow = 128 if t < NT else DS
            S.op("sp", lambda e: e.dma_start(out=xres[t, 0:nrow, :], in_=xn.ap[0:nrow, :]), reads=[xn.b], writes=[dbuf("xres", t)], dma="st_x%d" % (t % 2))
            stats(xn.ap, xn.b, D, tt["junk"], ssB)
            h2 = tt["h2"]
            S.op("dve", lambda e: e.scalar_tensor_tensor(out=h2.ap, in0=xn.ap, scalar=ssB.ap[:, 2:3], in1=gpre2.ap, op0=ALU.mult, op1=ALU.mult),
                 reads=[xn.b, ssB.b, gpre2.b], writes=[h2.b])
            h2T = tt["h2T"].next()
            transposes(h2, h2T, tb, "act")
            S.op("sp", lambda e: e.dma_start(out=H2T[:, qcol(t):qcol(t) + 128].rearrange("(c p) n -> p c n", p=128),
                                             in_=h2T.ap.rearrange("p (c n) -> p c n", c=8)),
                 reads=[h2T.b], writes=[dbuf("H2T", t)], dma="st_h%d" % (t % 2))

        def phase_fox_a(layer):
            j = layer // 2
            LF = A.alloc(NKS * NH, F32, "LF")
            after_lf = A.off
            w = A.alloc(8 * FW, BF16, "fw")
            w.ap = w.ap.rearrange("p (c n) -> p c n", c=8)

            def ld_w(c):
                S.op("sp", lambda e: e.dma_start(out=w.ap[:, c, :], in_=wb_fin[j, c * 128:(c + 1) * 128, :]),
                     reads=[dbuf("wb_fin", j)], writes=[w.b], dma="wld%d" % (c % 2))
            for c in range(8):
                ld_w(c)
            gpre = A.alloc(D, F32, "gpre")
            load_bcast(gpre, pre_mix[layer:layer + 1, :], "g0")
            gq = A.alloc(DH, F32, "gq")
            load_bcast(gq, fox_q_norm[j:j + 1, :], "g1")
            gk = A.alloc(DH, F32, "gk")
            load_bcast(gk, fox_k_norm[j:j + 1, :], "g2")
            bfb = A.alloc(NH, F32, "bfb")
            load_bcast(bfb, fox_b_f[j:j + 1, :], "g3")
            xr = Ring([A.alloc(D, F32, "x%d" % i) for i in range(2)])
            junk = A.alloc(D, BF16, "junk")
            ss = [A.alloc(8, F32, "ss%d" % i) for i in range(2)]
            h = A.alloc(D, BF16, "h")
            hT = A.alloc(D, BF16, "hT")
            sq = A.alloc(D, F32, "sq")
            sq2 = A.alloc(D, F32, "sq2")
            ssq = [A.alloc(64, F32, "ssq%d" % i) for i in range(2)]
            qn = A.alloc(D, BF16, "qn")
            kf = Ring([A.alloc(D, F32, "kf%d" % i) for i in range(2)])
            kb = A.alloc(D, BF16, "kb")
            vf = Ring([A.alloc(D, F32, "vf%d" % i) for i in range(2)])
            vb = Ring([A.alloc(D, BF16, "vb%d" % i) for i in range(2)])
            QTs = Ring([A.alloc(D, BF16, "QTs%d" % i) for i in range(2)])
            KTs = Ring([A.alloc(D, BF16, "KTs%d" % i) for i in range(2)])
            GTs = Ring([A.alloc(D, BF16, "GTs%d" % i) for i in range(2)])
            f1 = [A.alloc(64, F32, "f1_%d" % i) for i in range(2)]
            slots = Ring([slot2(1), slot2(3), slot2(5)])
            tb = bank[0]
            fzb = bank[7]

            def k_to_scratch(kbt, col):
                KTt = KTs.next()
                transposes(kbt, KTt, tb, "dve")
                S.op("sp", lambda e: e.dma_start(out=KT[:, col:col + 128].rearrange("(c p) n -> p c n", p=128),
                                                 in_=KTt.ap.rearrange("p (c n) -> p c n", c=8)),
                     reads=[KTt.b], writes=[dbuf("KT", col)], dma="st_kt%d" % (KTs.i % 2))

            def v_to_scratch(vbt, col, par):
                S.op("sp", lambda e: e.dma_start(out=VT[col:col + 128, :], in_=vbt.ap), reads=[vbt.b], writes=[dbuf("VT", col)],
                     dma="st_vt%d" % par)

            def do_past(jt):
                col = T + jt * 128
                kft = kf.next()
                S.op("sp", lambda e: e.dma_start(out=kft.ap, in_=ck[j, jt * 128:(jt + 1) * 128, :]), writes=[kft.b], dma="ldk%d" % (kf.i % 2))
                S.op("pool", lambda e: e.tensor_copy(out=kb.ap, in_=kft.ap), reads=[kft.b], writes=[kb.b])
                k_to_scratch(kb, col)
                vft = vf.next()
                S.op("sp", lambda e: e.dma_start(out=vft.ap, in_=cv[j, jt * 128:(jt + 1) * 128, :]), writes=[vft.b], dma="ldv%d" % (vf.i % 2))
                vbt = vb.next()
                S.op("pool", lambda e: e.tensor_copy(out=vbt.ap, in_=vft.ap), reads=[vft.b], writes=[vbt.b])
                v_to_scratch(vbt, col, vb.i % 2)
                slot_ = NT + jt
                S.op("sp", lambda e: e.dma_start(out=LF.ap[:, slot_ * NH:(slot_ + 1) * NH], in_=clf[j, jt * 128:(jt + 1) * 128, :]),
                     writes=[LF.b], dma="ldlf")
            for jt in range(NP):
                do_past(jt)

            def load_x(t):
                xt = xr.next()
                src, rd = xsrc(layer, t)
                S.op("sp", lambda e: e.dma_start(out=xt.ap, in_=src), reads=rd, writes=[xt.b], dma="ldx%d" % (t % 2))
                return xt

            def headnorm(which, sl_, t):
                ssq_ = ssq[t % 2]
                nv = 128 if t < NT else DS
                sqt = sq if which == "q" else sq2
                o0 = 0 if which == "q" else 32
                sq3 = sqt.ap.rearrange("p (h d) -> p h d", h=NH)
                S.op("act", lambda e: e.activation(out=sqt.ap, in_=sl_.ap, func=AF.Square), reads=[sl_.b], writes=[sqt.b])
                S.op("dve", lambda e: e.tensor_reduce(out=ssq_.ap[:, o0:o0 + NH], in_=sq3, axis=AX.X, op=ALU.add), reads=[sqt.b], writes=[ssq_.b])
                S.op("act", lambda e: e.activation(out=ssq_.ap[:, o0 + NH:o0 + 2 * NH], in_=ssq_.ap[:, o0:o0 + NH], func=AF.Sqrt,
                                                   scale=1.0 / DH, bias=EPS), reads=[ssq_.b], writes=[ssq_.b])
                S.op("dve", lambda e: e.reciprocal(out=ssq_.ap[:, o0:o0 + NH], in_=ssq_.ap[:, o0 + NH:o0 + 2 * NH]),
                     reads=[ssq_.b], writes=[ssq_.b])
                if which == "q":
                    S.op("dve", lambda e: e.tensor_scalar(out=ssq_.ap[:, o0:o0 + NH], in0=ssq_.ap[:, o0:o0 + NH], scalar1=DH ** -0.5, scalar2=None, op0=ALU.mult),
                         reads=[ssq_.b], writes=[ssq_.b])
                S.op("dve", lambda e: e.tensor_tensor(
                    out=sq3, in0=sl_.ap.rearrange("p (h d) -> p h d", h=NH),
                    in1=ssq_.ap[:, o0:o0 + NH].unsqueeze(2).to_broadcast([128, NH, DH]), op=ALU.mult),
                    reads=[sl_.b, ssq_.b], writes=[sqt.b])
                if which == "q":
                    S.op("pool", lambda e: e.tensor_tensor(
                        out=qn.ap.rearrange("p (h d) -> p h d", h=NH), in0=sq3,
                        in1=gq.ap.unsqueeze(1).to_broadcast([128, NH, DH]), op=ALU.mult), reads=[sqt.b, gq.b], writes=[qn.b])
                else:
                    kft = kf.next()
                    S.op("pool", lambda e: e.tensor_tensor(
                        out=kft.ap.rearrange("p (h d) -> p h d", h=NH), in0=sq3,
                        in1=gk.ap.unsqueeze(1).to_broadcast([128, NH, DH]), op=ALU.mult), reads=[sqt.b, gk.b], writes=[kft.b])
                    S.op("pool", lambda e: e.tensor_copy(out=kb.ap, in_=kft.ap), reads=[kft.b], writes=[kb.b])
                    kdst = k_p[j, t * 128:(t + 1) * 128, :] if t < NT else k_s[j, :, :]
                    S.op("sp", lambda e: e.dma_start(out=kdst, in_=kft.ap[0:nv, :]), reads=[kft.b], dma="ldk%d" % (kf.i % 2))

            def do_tile(t, x):
                nv = 128 if t < NT else DS
                ss_ = ss[t % 2]
                stats(x.ap, x.b, D, junk, ss_)
                S.op("dve", lambda e: e.scalar_tensor_tensor(out=h.ap, in0=x.ap, scalar=ss_.ap[:, 2:3], in1=gpre.ap, op0=ALU.mult, op1=ALU.mult),
                     reads=[x.b, ss_.b, gpre.b], writes=[h.b])
                transposes(h, hT, tb, "act")
                slq = slots.next()
                proj_tok(hT, w, 0, slq)
                slk = slots.next()
                proj_tok(hT, w, D, slk)
                slv = slots.next()
                proj_tok(hT, w, 2 * D, slv)

                def f_fz(e):
                    ins = None
                    for kc in range(8):
                        ins = e.matmul(fzb.ap[:, 0:NH], lhsT=hT.ap[:, kc * 128:(kc + 1) * 128], rhs=w.ap[:, kc, 4 * D:4 * D + NH],
                                       start=(kc == 0), stop=(kc == 7))
                    return ins
                S.op("pe", f_fz, reads=[hT.b, w.b], writes=[fzb.b], cost=500)
                headnorm("q", slq, t)
                headnorm("k", slk, t)
                vft = vf.next()
                S.op("act", lambda e: e.activation(out=vft.ap, in_=slv.ap, func=AF.Copy), reads=[slv.b], writes=[vft.b])
                vbt = vb.next()
                S.op("pool", lambda e: e.tensor_copy(out=vbt.ap, in_=vft.ap), reads=[vft.b], writes=[vbt.b])
                vdst = v_p[j, t * 128:(t + 1) * 128, :] if t < NT else v_s[j, :, :]
                S.op("sp", lambda e: e.dma_start(out=vdst, in_=vft.ap[0:nv, :]), reads=[vft.b], dma="ldv%d" % (vf.i % 2))
                v_to_scratch(vbt, kcol(t), vb.i % 2)
                slg = slots.next()

                def f_g(e):
                    ins = None
                    for c in range(8):
                        for kc in range(8):
                            ins = e.matmul(slg.ap[:, c * 128:(c + 1) * 128], lhsT=w.ap[:, kc, 3 * D + c * 128:3 * D + (c + 1) * 128],
                                           rhs=hT.ap[:, kc * 128:(kc + 1) * 128], start=(kc == 0), stop=(kc == 7))
                    return ins
                S.op("pe", f_g, reads=[hT.b, w.b], writes=[slg.b], cost=4500)
                GTt = GTs.next()
                S.op("act", lambda e: e.activation(out=GTt.ap, in_=slg.ap, func=AF.Sigmoid), reads=[slg.b], writes=[GTt.b])
                S.op("sp", lambda e: e.dma_start(out=GT[:, qcol(t):qcol(t) + 128].rearrange("(c p) n -> p c n", p=128),
                                                 in_=GTt.ap.rearrange("p (c n) -> p c n", c=8)),
                     reads=[GTt.b], writes=[dbuf("GT", t)], dma="st_gt%d" % (t % 2))
                f1_ = f1[t % 2]
                slot_ = t if t < NT else NT + NP
                lfv = LF.ap[:, slot_ * NH:(slot_ + 1) * NH]
                S.op("dve", lambda e: e.tensor_tensor(out=f1_.ap[:, 0:NH], in0=fzb.ap[:, 0:NH], in1=bfb.ap, op=ALU.add),
                     reads=[fzb.b, bfb.b], writes=[f1_.b])
                S.op("act", lambda e: e.activation(out=f1_.ap[:, 16:32], in_=f1_.ap[:, 0:NH], func=AF.Exp, scale=-1.0), reads=[f1_.b], writes=[f1_.b])
                S.op("act", lambda e: e.activation(out=f1_.ap[:, 32:48], in_=f1_.ap[:, 16:32], func=AF.Ln, bias=1.0), reads=[f1_.b], writes=[f1_.b])
                S.op("dve", lambda e: e.tensor_scalar(out=lfv, in0=f1_.ap[:, 32:48], scalar1=-1.0, scalar2=None, op0=ALU.mult),
                     reads=[f1_.b], writes=[LF.b])
                ldst = lf_p[j, t * 128:(t + 1) * 128, :] if t < NT else lf_s[j, :, :]
                S.op("sp", lambda e: e.dma_start(out=ldst, in_=lfv[0:nv, :]), reads=[LF.b], dma="st_lf")
                QTt = QTs.next()
                transposes(qn, QTt, tb, "act")
                S.op("sp", lambda e: e.dma_start(out=QT[:, qcol(t):qcol(t) + 128].rearrange("(c p) n -> p c n", p=128),
                                                 in_=QTt.ap.rearrange("p (c n) -> p c n", c=8)),
                     reads=[QTt.b], writes=[dbuf("QT", t)], dma="st_qt%d" % (t % 2))
                k_to_scratch(kb, kcol(t))

            xt_next = load_x(0)
            for t in range(NTS):
                x = xt_next
                if t + 1 < NTS:
                    xt_next = load_x(t + 1)
                do_tile(t, x)
            S.barrier()
            A.off = after_lf
            return LF

        def phase_fox_a2(layer, LF):
            CT = A.alloc(KC, F32, "CT")
            psb = Ring([bank[1], bank[2], bank[3], bank[4]])
            ngrp = (NKS + 3) // 4

            def do_grp(g):
                s0 = g * 4
                ns = min(4, NKS - s0)
                pb = psb.next()

                def f(e):
                    ins = None
                    for i in range(ns):
                        s_ = s0 + i
                        ins = e.matmul(pb.ap[0:NH, i * 128:(i + 1) * 128], lhsT=LF.ap[:, s_ * NH:(s_ + 1) * NH], rhs=trif, start=True, stop=True)
                    return ins
                S.op("pe", f, reads=[LF.b] + CONSTB, writes=[pb.b])
                S.op("act", lambda e: e.activation(out=CT.ap[0:NH, s0 * 128:(s0 + ns) * 128], in_=pb.ap[0:NH, 0:ns * 128], func=AF.Copy),
                     reads=[pb.b], writes=[CT.b])
            for g in range(ngrp):
                do_grp(g)

            def fix(s_):
                S.op("dve", lambda e: e.tensor_scalar(out=CT.ap[0:NH, s_ * 128:(s_ + 1) * 128], in0=CT.ap[0:NH, s_ * 128:(s_ + 1) * 128],
                                                      scalar1=CT.ap[0:NH, s_ * 128 - 1:s_ * 128], scalar2=None, op0=ALU.add),
                     reads=[CT.b], writes=[CT.b])
            for s_ in range(1, NKS):
                if s_ == NT:
                    continue
                fix(s_)
            CW = 1024
            r1 = A.alloc(CW, F32, "r1")
            r2 = A.alloc(CW, F32, "r2")
            o6 = Ring([A.alloc(6 * CW, BF16, "o6_%d" % i) for i in range(2)])

            def do_chunk(c0):
                w_ = min(CW, KC - c0)
                o_ = o6.next()
                ov = o_.ap.rearrange("p (a n) -> p a n", a=6)
                cs = CT.ap[0:NH, c0:c0 + w_]
                S.op("act", lambda e: e.activation(out=ov[0:NH, 0, 0:w_], in_=cs, func=AF.Copy), reads=[CT.b], writes=[o_.b])
                S.op("dve", lambda e: e.tensor_tensor(out=r1.ap[0:NH, 0:w_], in0=cs, in1=ov[0:NH, 0, 0:w_], op=ALU.subtract),
                     reads=[CT.b, o_.b], writes=[r1.b])
                S.op("act", lambda e: e.activation(out=ov[0:NH, 1, 0:w_], in_=r1.ap[0:NH, 0:w_], func=AF.Copy), reads=[r1.b], writes=[o_.b])
                S.op("dve", lambda e: e.tensor_tensor(out=r2.ap[0:NH, 0:w_], in0=r1.ap[0:NH, 0:w_], in1=ov[0:NH, 1, 0:w_], op=ALU.subtract),
                     reads=[r1.b, o_.b], writes=[r2.b])
                S.op("act", lambda e: e.activation(out=ov[0:NH, 2, 0:w_], in_=r2.ap[0:NH, 0:w_], func=AF.Copy), reads=[r2.b], writes=[o_.b])
                S.op("pool", lambda e: e.tensor_scalar(out=ov[0:NH, 3:6, 0:w_], in0=ov[0:NH, 0:3, 0:w_], scalar1=-1.0, scalar2=None, op0=ALU.mult),
                     reads=[o_.b], writes=[o_.b])
                S.op("sp", lambda e: e.dma_start(out=CTs[:, :, c0:c0 + w_], in_=ov[0:NH, :, 0:w_]),
                     reads=[o_.b], writes=[dbuf("CTs", c0)], dma="st_ct%d" % (o6.i % 2))
            for c0 in range(0, KC, CW):
                do_chunk(c0)
            S.barrier()
            A.off = PERS

        def phase_fox_b(layer):
            NQB = T // 512
            sets = []
            for i in range(2):
                d = {}
                d["QA"] = A.alloc(TS, BF16, "QA%d" % i)
                d["KA"] = A.alloc(KC, BF16, "KA%d" % i)
                d["VA"] = A.alloc(NKS * 128, BF16, "VA%d" % i)
                d["G"] = A.alloc(TS, BF16, "G%d" % i)
                d["VAv"] = d["VA"].ap.rearrange("p (s n) -> p s n", n=128)
                S.op("pool", lambda e, d=d: e.memset(d["QA"].ap[64:70, :], 1.0), writes=[d["QA"].b])
                S.op("pool", lambda e, d=d: e.memset(d["KA"].ap[64:70, :], 1.0), writes=[d["KA"].b])
                S.op("pool", lambda e, d=d: e.memset(d["VA"].ap, 1.0), writes=[d["VA"].b])
                sets.append(d)
            PT = Ring([A.alloc(512, BF16, "PT%d" % i) for i in range(6)])
            rc = Ring([A.alloc(512, F32, "rc%d" % i) for i in range(2)])
            tmp = Ring([A.alloc(512, F32, "otmp%d" % i) for i in range(2)])
            og = Ring([A.alloc(512, BF16, "og%d" % i) for i in range(2)])
            psS = Ring([bank[0], bank[1], bank[2], bank[3], bank[6], bank[7]])
            psO = Ring([bank[4], bank[5]])
            allQT = [dbuf("QT", t) for t in range(NTS)]
            allGT = [dbuf("GT", t) for t in range(NTS)]
            allKT = [dbuf("KT", c) for c in [t * 128 for t in range(NT)] + [T + i * 128 for i in range(NP + 1)]]
            allVT = [dbuf("VT", c) for c in [t * 128 for t in range(NT)] + [T + i * 128 for i in range(NP + 1)]]
            allCT = [dbuf("CTs", c0) for c0 in range(0, KC, 1024)]

            def load_head(hd):
                d = sets[hd % 2]
                p = hd % 2
                r0 = hd * DH
                S.op("sp", lambda e: e.dma_start(out=d["QA"].ap[0:64, :], in_=QT[r0:r0 + DH, :]), reads=allQT, writes=[d["QA"].b], dma="la%d" % p)
                S.op("sp", lambda e: e.dma_start(out=d["QA"].ap[64:67, 0:T], in_=CTs[hd, 0:3, 0:T]), reads=allCT, writes=[d["QA"].b], dma="lb%d" % p)
                S.op("sp", lambda e: e.dma_start(out=d["QA"].ap[64:67, T:TS], in_=CTs[hd, 0:3, T + PAST:KC]), reads=allCT, writes=[d["QA"].b], dma="lc%d" % p)
                S.op("sp", lambda e: e.dma_start(out=d["KA"].ap[0:64, :], in_=KT[r0:r0 + DH, :]), reads=allKT, writes=[d["KA"].b], dma="ld%d" % p)
                S.op("sp", lambda e: e.dma_start(out=d["KA"].ap[67:70, :], in_=CTs[hd, 3:6, :]), reads=allCT, writes=[d["KA"].b], dma="le%d" % p)
                vo = 0 if p == 0 else 64
                step = 16
                for s0 in range(0, NKS, step):
                    s1 = min(NKS, s0 + step)
                    S.op("sp", lambda e, s0=s0, s1=s1: e.dma_start(out=d["VAv"][:, s0:s1, vo:vo + DH],
                                                                  in_=VT[s0 * 128:s1 * 128, r0:r0 + DH].rearrange("(s p) d -> p s d", p=128)),
                         reads=allVT, writes=[d["VA"].b], dma="lf%d" % p)
                S.op("sp", lambda e: e.dma_start(out=d["G"].ap[vo:vo + 64, :], in_=GT[r0:r0 + DH, :]), reads=allGT, writes=[d["G"].b], dma="lg%d" % p)

            def finish(hd, po, ncols, qc0):
                d = sets[hd % 2]
                p = hd % 2
                orow = slice(0, 64) if p == 0 else slice(64, 128)
                drow = slice(64, 128) if p == 0 else slice(0, 64)
                rc_ = rc.next()
                tmp_ = tmp.next()
                og_ = og.next()
                S.op("dve", lambda e: e.reciprocal(out=rc_.ap[drow, 0:ncols], in_=po.ap[drow, 0:ncols]), reads=[po.b], writes=[rc_.b])
                S.op("dve", lambda e: e.tensor_tensor(out=tmp_.ap[orow, 0:ncols], in0=po.ap[orow, 0:ncols], in1=rc_.ap[drow, 0:ncols], op=ALU.mult),
                     reads=[po.b, rc_.b], writes=[tmp_.b])
                S.op("pool", lambda e: e.tensor_tensor(out=og_.ap[orow, 0:ncols], in0=tmp_.ap[orow, 0:ncols], in1=d["G"].ap[orow, qc0:qc0 + ncols], op=ALU.mult),
                     reads=[tmp_.b, d["G"].b], writes=[og_.b])
                S.op("sp", lambda e: e.dma_start(out=OGT[hd * DH:(hd + 1) * DH, qc0:qc0 + ncols], in_=og_.ap[orow, 0:ncols]),
                     reads=[og_.b], writes=[dbuf("OGT", qc0 // 512)], dma="st_og%d" % (og.i % 2))

            def do_head(hd):
                d = sets[hd % 2]
                QA, KA, VA, VAv = d["QA"], d["KA"], d["VA"], d["VAv"]

                def s_step(kt, qb):
                    off = max(0, kt - 4 * qb) * 128
                    ps_ = psS.next()
                    q0 = qb * 512

                    def f(e):
                        if kt >= 4 * qb:
                            e.matmul(ps_.ap[:, off:off + 128], lhsT=idb, rhs=masknegb, start=True, stop=False)
                            ins = e.matmul(ps_.ap[:, off:off + 128], lhsT=KA.ap[0:70, kt * 128:(kt + 1) * 128],
                                           rhs=QA.ap[0:70, q0 + off:q0 + off + 128], start=False, stop=True)
                            if off + 128 < 512:
                                ins = e.matmul(ps_.ap[:, off + 128:512], lhsT=KA.ap[0:70, kt * 128:(kt + 1) * 128],
                                               rhs=QA.ap[0:70, q0 + off + 128:q0 + 512], start=True, stop=True)
                            return ins
                        return e.matmul(ps_.ap[:, 0:512], lhsT=KA.ap[0:70, kt * 128:(kt + 1) * 128], rhs=QA.ap[0:70, q0:q0 + 512],
                                        start=True, stop=True)
                    S.op("pe", f, reads=[KA.b, QA.b] + CONSTB, writes=[ps_.b], cost=280)
                    pt_ = PT.next()
                    S.op("act", lambda e: e.activation(out=pt_.ap[:, off:512], in_=ps_.ap[:, off:512], func=AF.Exp), reads=[ps_.b], writes=[pt_.b], cost=500)
                    return (kt, off, pt_)

                def pv_step(item, po, nkt):
                    kt, off, pt_ = item
                    S.op("pe", lambda e: e.matmul(po.ap[:, off:512], lhsT=VAv[:, kt, :], rhs=pt_.ap[:, off:512], start=(kt == 0), stop=(kt == nkt - 1),
                                                  skip_group_check=True),
                         reads=[VA.b, pt_.b], writes=[po.b], cost=280)

                for qb in range(NQB):
                    po = psO.next()
                    nkt = 4 * qb + 4
                    pend = []
                    for kt in range(nkt):
                        pend.append(s_step(kt, qb))
                        if len(pend) > 3:
                            pv_step(pend.pop(0), po, nkt)
                    while pend:
                        pv_step(pend.pop(0), po, nkt)
                    finish(hd, po, 512, qb * 512)

                po = psO.next()

                def samp_step(kt):
                    ps_ = psS.next()
                    kc0 = T + kt * 128
                    pt_ = PT.next()
                    if kt < NP:
                        S.op("pe", lambda e: e.matmul(ps_.ap[:, 0:128], lhsT=KA.ap[0:70, kc0:kc0 + 128], rhs=QA.ap[0:70, T:TS], start=True, stop=True),
                             reads=[KA.b, QA.b], writes=[ps_.b])
                        S.op("act", lambda e: e.activation(out=pt_.ap[:, 0:128], in_=ps_.ap[:, 0:128], func=AF.Exp), reads=[ps_.b], writes=[pt_.b])
                        S.op("pe", lambda e: e.matmul(po.ap[:, 0:128], lhsT=VAv[:, NT + kt, :], rhs=pt_.ap[:, 0:128], start=(kt == 0), stop=False,
                                                      skip_group_check=True),
                             reads=[VA.b, pt_.b], writes=[po.b])
                    else:
                        def f(e):
                            e.matmul(ps_.ap[0:DS, 0:128], lhsT=idb[0:DS, 0:DS], rhs=masknegb[0:DS, :], start=True, stop=False)
                            return e.matmul(ps_.ap[0:DS, 0:128], lhsT=KA.ap[0:70, kc0:kc0 + DS], rhs=QA.ap[0:70, T:TS], start=False, stop=True)
                        S.op("pe", f, reads=[KA.b, QA.b] + CONSTB, writes=[ps_.b])
                        S.op("act", lambda e: e.activation(out=pt_.ap[0:DS, 0:128], in_=ps_.ap[0:DS, 0:128], func=AF.Exp), reads=[ps_.b], writes=[pt_.b])
                        S.op("pe", lambda e: e.matmul(po.ap[:, 0:128], lhsT=VAv[0:DS, NT + kt, :], rhs=pt_.ap[0:DS, 0:128], start=(NP == 0), stop=True,
                                                      skip_group_check=True),
                             reads=[VA.b, pt_.b], writes=[po.b])
                for kt in range(NP + 1):
                    samp_step(kt)
                finish(hd, po, 128, T)

            load_head(0)
            for hd in range(NH):
                if hd + 1 < NH:
                    load_head(hd + 1)
                do_head(hd)
            S.barrier()
            A.off = PERS

        def phase_fox_c1(layer):
            j = layer // 2
            wo = A.alloc(8 * D, BF16, "wo")
            wo.ap = wo.ap.rearrange("p (c n) -> p c n", c=8)
            S.op("sp", lambda e: e.dma_start(out=wo.ap, in_=wb_fout[j].rearrange("(c p) n -> p c n", p=128)), reads=[dbuf("wb_fout", j)], writes=[wo.b], dma="wld0")
            gpost = A.alloc(D, F32, "gpost")
            load_bcast(gpost, post_mix[layer:layer + 1, :], "g0")
            gpre2 = A.alloc(D, F32, "gpre2")
            load_bcast(gpre2, pre_ffn[layer:layer + 1, :], "g1")
            tt = make_tail_tiles()
            xr = Ring([A.alloc(D, F32, "x%d" % i) for i in range(2)])
            ogr = Ring([A.alloc(D, BF16, "ogb%d" % i) for i in range(2)])
            slots = Ring([slot2(1), slot2(3), slot2(5)])
            allOG = [dbuf("OGT", q) for q in range(T // 512 + 1)]

            def load(t):
                xt = xr.next()
                src, rd = xsrc(layer, t)
                S.op("sp", lambda e: e.dma_start(out=xt.ap, in_=src), reads=rd, writes=[xt.b], dma="ldx%d" % (t % 2))
                ogt = ogr.next()
                S.op("sp", lambda e: e.dma_start(out=ogt.ap.rearrange("p (c n) -> p c n", c=8),
                                                 in_=OGT[:, qcol(t):qcol(t) + 128].rearrange("(c p) n -> p c n", p=128)),
                     reads=[dbuf("OGT", qcol(t) // 512)], writes=[ogt.b], dma="ldo%d" % (t % 2))
                return xt, ogt

            def do_tile(t, x, ogt):
                sl = slots.next()
                proj_tok(ogt, wo, 0, sl)
                tail(layer, t, sl, x, tt, gpost, gpre2, bank[0])

            nxt = load(0)
            for t in range(NTS):
                x, ogt = nxt
                if t + 1 < NTS:
                    nxt = load(t + 1)
                do_tile(t, x, ogt)
            S.barrier()
            A.off = PERS

        def phase_ffn(layer):
            last = (layer == NL - 1)
            wd = A.alloc(32 * D, BF16, "wd")
            wd.ap = wd.ap.rearrange("p (c n) -> p c n", c=32)
            for q in range(4):
                S.op("sp", lambda e, q=q: e.dma_start(out=wd.ap[:, q * 8:(q + 1) * 8, :],
                                                      in_=wb_down[layer, q * 1024:(q + 1) * 1024, :].rearrange("(c p) n -> p c n", p=128)),
                     reads=[dbuf("wb_down", layer)], writes=[wd.b], dma="wld%d" % (q % 2))
            gpost = A.alloc(D, F32, "gpostf")
            load_bcast(gpost, post_ffn[layer:layer + 1, :], "g0")
            NWR = 6
            wur = Ring([A.alloc(D, BF16, "wu%d" % i) for i in range(NWR)])
            hb = Ring([A.alloc(8 * 512, BF16, "hb%d" % i) for i in range(2)])
            xb = Ring([A.alloc(4 * D, F32, "xb%d" % i) for i in range(2)])
            u2 = A.alloc(32 * 512, BF16, "u2")
            u2v = u2.ap.rearrange("p (j n) -> p j n", j=32)
            rr = Ring([A.alloc(512, F32, "rr%d" % i) for i in range(3)])
            junk = A.alloc(D, BF16, "fjunk")
            ssr = Ring([A.alloc(8, F32, "fss%d" % i) for i in range(4)])
            tmp = A.alloc(D, F32, "ftmp")
            xo = Ring([A.alloc(D, F32, "xo%d" % i) for i in range(2)])
            psU = Ring([bank[0], bank[1], bank[2]])
            psD = Ring([slot2(3), slot2(5)])
            blocks = []
            t0 = 0
            while t0 < NTS:
                nt_ = min(4, NTS - t0)
                if t0 < NT and t0 + nt_ > NT:
                    nt_ = NT - t0
                blocks.append((t0, nt_))
                t0 += nt_
            wcount = [0]

            def load_w(jj):
                wt = wur.next()
                S.op("sp", lambda e: e.dma_start(out=wt.ap, in_=wb_up[layer, jj].rearrange("p c n -> p (c n)")),
                     reads=[dbuf("wb_up", layer)], writes=[wt.b], dma="lwu%d" % (wur.i % NWR))
                return wt

            def load_blk(bi):
                t0, nt_ = blocks[bi]
                ntok = nt_ * 128
                h_ = hb.next()
                S.op("sp", lambda e: e.dma_start(out=h_.ap.rearrange("p (c n) -> p c n", c=8)[:, :, 0:ntok],
                                                 in_=H2T[:, qcol(t0):qcol(t0) + ntok].rearrange("(c p) n -> p c n", p=128)),
                     reads=[dbuf("H2T", t) for t in range(t0, t0 + nt_)], writes=[h_.b], dma="lhb%d" % (bi % 2))
                x_ = xb.next()
                S.op("sp", lambda e: e.dma_start(out=x_.ap.rearrange("p (t d) -> p t d", t=4)[:, 0:nt_, :],
                                                 in_=xres[t0:t0 + nt_].rearrange("t p d -> p t d")),
                     reads=[dbuf("xres", t) for t in range(t0, t0 + nt_)], writes=[x_.b], dma="lxb%d" % (bi % 2))
                return h_, x_

            PRE = 4
            wq = []
            total_w = len(blocks) * 32
            for i in range(min(PRE, total_w)):
                wq.append(load_w(i % 32))
            wissued = [len(wq)]

            def do_up(jj, h_, ntok):
                hv = h_.ap.rearrange("p (c n) -> p c n", c=8)
                wt = wq.pop(0)
                if wissued[0] < total_w:
                    wq.append(load_w(wissued[0] % 32))
                    wissued[0] += 1
                wv = wt.ap.rearrange("p (c n) -> p c n", c=8)
                pu = psU.next()

                def f(e):
                    ins = None
                    for kc in range(8):
                        ins = e.matmul(pu.ap[:, 0:ntok], lhsT=wv[:, kc, :], rhs=hv[:, kc, 0:ntok], start=(kc == 0), stop=(kc == 7))
                    return ins
                S.op("pe", f, reads=[wt.b, h_.b], writes=[pu.b], cost=2150)
                r_ = rr.next()
                S.op("act", lambda e: e.activation(out=r_.ap[:, 0:ntok], in_=pu.ap[:, 0:ntok], func=AF.Relu), reads=[pu.b], writes=[r_.b], cost=600)
                S.op("pool", lambda e: e.tensor_tensor(out=u2v[:, jj, 0:ntok], in0=r_.ap[:, 0:ntok], in1=r_.ap[:, 0:ntok], op=ALU.mult),
                     reads=[r_.b], writes=[u2.b], cost=1200)

            def do_down(t, ti, x_):
                pd = psD.next()

                def f(e):
                    ins = None
                    for half in range(2):
                        for jj in range(32):
                            ins = e.matmul(pd.ap[:, half * 512:(half + 1) * 512], lhsT=u2v[:, jj, ti * 128:(ti + 1) * 128],
                                           rhs=wd.ap[:, jj, half * 512:(half + 1) * 512], start=(jj == 0), stop=(jj == 31))
                    return ins
                S.op("pe", f, reads=[u2.b, wd.b], writes=[pd.b], cost=17000)
                ss_ = ssr.next()
                stats(pd.ap, pd.b, D, junk, ss_)
                S.op("dve", lambda e: e.scalar_tensor_tensor(out=tmp.ap, in0=pd.ap, scalar=ss_.ap[:, 2:3], in1=gpost.ap, op0=ALU.mult, op1=ALU.mult),
                     reads=[pd.b, ss_.b, gpost.b], writes=[tmp.b])
                xo_ = xo.next()
                xin = x_.ap[:, ti * D:(ti + 1) * D]
                S.op("pool", lambda e: e.tensor_tensor(out=xo_.ap, in0=xin, in1=tmp.ap, op=ALU.add), reads=[x_.b, tmp.b], writes=[xo_.b])
                if last:
                    if t < NT:
                        S.op("sp", lambda e: e.dma_start(out=y_p[t * 128:(t + 1) * 128, :], in_=xo_.ap), reads=[xo_.b], dma="st_y%d" % (t % 2))
                    else:
                        S.op("sp", lambda e: e.dma_start(out=y_s[:, :], in_=xo_.ap[0:DS, :]), reads=[xo_.b], dma="st_y%d" % (t % 2))
                else:
                    nrow = 128 if t < NT else DS
                    S.op("sp", lambda e: e.dma_start(out=xres[t, 0:nrow, :], in_=xo_.ap[0:nrow, :]), reads=[xo_.b], writes=[dbuf("xres", t)], dma="st_y%d" % (t % 2))

            nxt = load_blk(0)
            for bi, (t0, nt_) in enumerate(blocks):
                h_, x_ = nxt
                if bi + 1 < len(blocks):
                    nxt = load_blk(bi + 1)
                for jj in range(32):
                    do_up(jj, h_, nt_ * 128)
                for ti in range(nt_):
                    do_down(t0 + ti, ti, x_)
            S.barrier()
            A.off = PERS

        def phase_hgrn(layer):
            j = layer // 2
            w = A.alloc(8 * 4 * D, BF16, "hw")
            w.ap = w.ap.rearrange("p (c n) -> p c n", c=8)

            def ld_w(c):
                S.op("sp", lambda e: e.dma_start(out=w.ap[:, c, :], in_=wb_hin[j, c * 128:(c + 1) * 128, :]),
                     reads=[dbuf("wb_hin", j)], writes=[w.b], dma="wld%d" % (c % 2))
            for c in range(8):
                ld_w(c)
            wo = A.alloc(8 * D, BF16, "hwo")
            wo.ap = wo.ap.rearrange("p (c n) -> p c n", c=8)
            S.op("sp", lambda e: e.dma_start(out=wo.ap, in_=wb_hout[j].rearrange("(c p) n -> p c n", p=128)), reads=[dbuf("wb_hout", j)], writes=[wo.b], dma="wld0")
            gpre = A.alloc(D, F32, "gpre")
            load_bcast(gpre, pre_mix[layer:layer + 1, :], "g0")
            gout = A.alloc(D, F32, "gout")
            load_bcast(gout, hgrn_out_norm[j:j + 1, :], "g1")
            gpost = A.alloc(D, F32, "gpost")
            load_bcast(gpost, post_mix[layer:layer + 1, :], "g2")
            gpre2 = A.alloc(D, F32, "gpre2")
            load_bcast(gpre2, pre_ffn[layer:layer + 1, :], "g3")
            om = oml[layer]
            tt = make_tail_tiles()
            xr = Ring([A.alloc(D, F32, "x%d" % i) for i in range(4)])
            otmp = A.alloc(D, F32, "otmp")
            junk = tt["junk"]
            ss = [A.alloc(8, F32, "ss%d" % i) for i in range(4)]
            h = A.alloc(D, BF16, "h")
            hT = A.alloc(D, BF16, "hT")
            qf = A.alloc(D, F32, "qf")
            kk = A.alloc(D, F32, "kk")
            lg = A.alloc(D, F32, "lg")
            dcl = A.alloc(D, F32, "dcl")
            E2 = A.alloc(D, F32, "E2")
            qt = A.alloc(D, BF16, "qt")
            vbr = [A.alloc(D, BF16, "vb%d" % i) for i in range(2)]
            gtr = [A.alloc(D, F32, "gt%d" % i) for i in range(2)]
            ktr = [A.alloc(D, BF16, "kt%d" % i) for i in range(2)]
            qTr = [A.alloc(D, BF16, "qT%d" % i) for i in range(2)]
            kTr = [A.alloc(D, BF16, "kT%d" % i) for i in range(2)]
            decr = [A.alloc(32, F32, "dec%d" % i) for i in range(2)]
            ATs = Ring([A.alloc(128, BF16, "ATs%d" % i) for i in range(2)])
            AT32 = Ring([A.alloc(128, F32, "AT32_%d" % i) for i in range(2)])
            Sst = A.alloc(HH * 128, F32, "Sst")
            Sb = Ring([A.alloc(128, BF16, "Sb%d" % i) for i in range(2)])
            T1 = Ring([A.alloc(128, F32, "T1_%d" % i) for i in range(2)])
            on2 = A.alloc(D, BF16, "on2")
            onT = A.alloc(D, BF16, "onT")
            Sh = [Tl(Sst.ap[:, hd * 128:(hd + 1) * 128], "S%d" % hd) for hd in range(HH)]
            slotsA = Ring([slot2(1), slot2(3)])
            slotB = slot2(5)
            tb = bank[0]
            b7 = psum[:, 7, :]
            decp = Tl(b7[:, 0:32], "decp")
            psA = Ring([Tl(b7[:, 64:192], "psA0"), Tl(b7[:, 192:320], "psA1")])
            psK = Tl(b7[:, 320:448], "psK")

            def load_x(t):
                xt = xr.next()
                src, rd = xsrc(layer, t)
                S.op("sp", lambda e: e.dma_start(out=xt.ap, in_=src), reads=rd, writes=[xt.b], dma="ldx%d" % (t % 4))
                return xt

            def zero_state(hd):
                S.op("pool", lambda e: e.memset(Sh[hd].ap, 0.0), writes=[Sh[hd].b])
            for hd in range(HH):
                zero_state(hd)

            def do_head(hd, nv, dec_, so, qT, kT, kt_, vb):
                Sp = Sb.next()
                S_ = Sh[hd]
                S.op("dve", lambda e: e.tensor_scalar(out=Sp.ap, in0=S_.ap, scalar1=dec_.ap[:, hd * 4:hd * 4 + 1], scalar2=None, op0=ALU.mult),
                     reads=[S_.b, dec_.b], writes=[Sp.b], cost=220)
                pa = psA.next()
                S.op("pe", lambda e: e.matmul(pa.ap[0:nv, 0:nv], lhsT=kT.ap[:, hd * 128:hd * 128 + nv], rhs=qT.ap[:, hd * 128:hd * 128 + nv], start=True, stop=True),
                     reads=[kT.b, qT.b], writes=[pa.b], cost=120)
                at = ATs.next()
                at32 = AT32.next()
                S.op("dve", lambda e: e.tensor_tensor(out=at32.ap[0:nv, 0:nv], in0=pa.ap[0:nv, 0:nv], in1=himb[0:nv, 0:nv], op=ALU.min),
                     reads=[pa.b] + CONSTB, writes=[at32.b], cost=320)
                S.op("dve", lambda e: e.tensor_tensor(out=at.ap[0:nv, 0:nv], in0=at32.ap[0:nv, 0:nv], in1=lomb[0:nv, 0:nv], op=ALU.max),
                     reads=[at32.b] + CONSTB, writes=[at.b], cost=320)

                def f_o(e):
                    e.matmul(so.ap[0:nv, hd * 128:(hd + 1) * 128], lhsT=qT.ap[:, hd * 128:hd * 128 + nv], rhs=Sp.ap, start=True, stop=False)
                    return e.matmul(so.ap[0:nv, hd * 128:(hd + 1) * 128], lhsT=at.ap[0:nv, 0:nv], rhs=vb.ap[0:nv, hd * 128:(hd + 1) * 128], start=False, stop=True)
                S.op("pe", f_o, reads=[qT.b, Sp.b, at.b, vb.b], writes=[so.b], cost=250)
                S.op("pe", lambda e: e.matmul(psK.ap, lhsT=kt_.ap[0:nv, hd * 128:(hd + 1) * 128], rhs=vb.ap[0:nv, hd * 128:(hd + 1) * 128], start=True, stop=True),
                     reads=[kt_.b, vb.b], writes=[psK.b], cost=120)
                t1 = T1.next()
                S.op("dve", lambda e: e.tensor_scalar(out=t1.ap, in0=S_.ap, scalar1=dec_.ap[:, hd * 4 + 1:hd * 4 + 2], scalar2=None, op0=ALU.mult),
                     reads=[S_.b, dec_.b], writes=[t1.b], cost=220)
                S.op("dve", lambda e: e.scalar_tensor_tensor(out=S_.ap, in0=psK.ap, scalar=dec_.ap[:, hd * 4 + 2:hd * 4 + 3], in1=t1.ap,
                                                             op0=ALU.mult, op1=ALU.add),
                     reads=[psK.b, dec_.b, t1.b], writes=[S_.b], cost=380)

            def stage_a(t, x):
                samp = (t == NT)
                nv = DS if samp else 128
                d1 = d1s if samp else d1p
                sel = sels if samp else selp
                vb, gt, kt_, qT, kT, dec_ = vbr[t % 2], gtr[t % 2], ktr[t % 2], qTr[t % 2], kTr[t % 2], decr[t % 2]
                ss_ = ss[t % 4]
                stats(x.ap, x.b, D, junk, ss_)
                S.op("dve", lambda e: e.scalar_tensor_tensor(out=h.ap, in0=x.ap, scalar=ss_.ap[:, 2:3], in1=gpre.ap, op0=ALU.mult, op1=ALU.mult),
                     reads=[x.b, ss_.b, gpre.b], writes=[h.b])
                transposes(h, hT, tb, "act")
                sl1 = slotsA.next()
                proj_tok(hT, w, 0, sl1)
                S.op("act", lambda e: e.activation(out=qf.ap, in_=sl1.ap, func=AF.Silu), reads=[sl1.b], writes=[qf.b])
                yield
                sl2 = slotsA.next()
                proj_tok(hT, w, D, sl2)
                S.op("act", lambda e: e.activation(out=kk.ap, in_=sl2.ap, func=AF.Sigmoid, scale=-1.0), reads=[sl2.b], writes=[kk.b])
                S.op("dve", lambda e: e.tensor_tensor(out=kk.ap, in0=kk.ap, in1=om.ap, op=ALU.mult), reads=[kk.b, om.b], writes=[kk.b])
                S.op("act", lambda e: e.activation(out=lg.ap, in_=kk.ap, func=AF.Ln, scale=-1.0, bias=1.0), reads=[kk.b], writes=[lg.b])
                yield
                sl3 = slotsA.next()
                proj_tok(hT, w, 2 * D, sl3)
                S.op("act", lambda e: e.activation(out=vb.ap, in_=sl3.ap, func=AF.Copy), reads=[sl3.b], writes=[vb.b])
                yield
                sl4 = slotsA.next()
                proj_tok(hT, w, 3 * D, sl4)
                S.op("act", lambda e: e.activation(out=gt.ap, in_=sl4.ap, func=AF.Silu), reads=[sl4.b], writes=[gt.b])
                yield
                sl5 = slotsA.next()

                def f_d(e):
                    e.matmul(sl5.ap[:, 0:512], lhsT=d1[0:nv, :], rhs=lg.ap[0:nv, 0:512], start=True, stop=True)
                    return e.matmul(sl5.ap[:, 512:1024], lhsT=d1[0:nv, :], rhs=lg.ap[0:nv, 512:1024], start=True, stop=True)
                S.op("pe", f_d, reads=[lg.b] + CONSTB, writes=[sl5.b], cost=2200)

                def f_dec(e):
                    ins = None
                    for hd in range(HH):
                        ins = e.matmul(decp.ap[:, hd * 4:hd * 4 + 3], lhsT=lg.ap[0:nv, hd * 128:(hd + 1) * 128], rhs=sel[0:nv, :], start=True, stop=True)
                    return ins
                S.op("pe", f_dec, reads=[lg.b] + CONSTB, writes=[decp.b], cost=900)
                S.op("dve", lambda e: e.tensor_scalar(out=dcl.ap, in0=sl5.ap, scalar1=-80.0, scalar2=80.0, op0=ALU.max, op1=ALU.min), reads=[sl5.b], writes=[dcl.b])
                S.op("act", lambda e: e.activation(out=E2.ap, in_=dcl.ap, func=AF.Exp, scale=-1.0), reads=[dcl.b], writes=[E2.b])
                S.op("act", lambda e: e.activation(out=dcl.ap, in_=dcl.ap, func=AF.Exp), reads=[dcl.b, E2.b], writes=[dcl.b])
                S.op("act", lambda e: e.activation(out=dec_.ap, in_=decp.ap, func=AF.Exp), reads=[decp.b], writes=[dec_.b], cost=1400)
                S.op("dve", lambda e: e.tensor_tensor(out=qt.ap, in0=qf.ap, in1=dcl.ap, op=ALU.mult), reads=[qf.b, dcl.b], writes=[qt.b])
                S.op("pool", lambda e: e.tensor_tensor(out=kt_.ap, in0=kk.ap, in1=E2.ap, op=ALU.mult), reads=[kk.b, E2.b], writes=[kt_.b])
                yield
                transposes(qt, qT, tb, "dve")
                transposes(kt_, kT, tb, "act")
                yield

            def stage_b1(t):
                samp = (t == NT)
                nv = DS if samp else 128
                vb, gt, kt_, qT, kT, dec_ = vbr[t % 2], gtr[t % 2], ktr[t % 2], qTr[t % 2], kTr[t % 2], decr[t % 2]
                if samp:
                    S.op("sp", lambda e: e.dma_start(out=s_p[j].rearrange("h k v -> k h v"), in_=Sst.ap.rearrange("p (h v) -> p h v", h=HH)),
                         reads=[s_.b for s_ in Sh], dma="st_s")
                    S.op("sp", lambda e: e.dma_start(out=Sst.ap.rearrange("p (h v) -> p h v", h=HH), in_=st[j].rearrange("h k v -> k h v")),
                         reads=[s_.b for s_ in Sh], writes=[s_.b for s_ in Sh], dma="ld_s")
                so = slotB
                for hd in range(HH):
                    do_head(hd, nv, dec_, so, qT, kT, kt_, vb)
                    if hd % 2 == 1:
                        yield
                ss2 = ss[(t + 2) % 4]
                stats(so.ap, so.b, D, junk, ss2)
                S.op("dve", lambda e: e.scalar_tensor_tensor(out=otmp.ap, in0=so.ap, scalar=ss2.ap[:, 2:3], in1=gout.ap, op0=ALU.mult, op1=ALU.mult),
                     reads=[so.b, ss2.b, gout.b], writes=[otmp.b])
                yield

            def stage_b2(t, x):
                gt = gtr[t % 2]
                S.op("pool", lambda e: e.tensor_tensor(out=on2.ap, in0=otmp.ap, in1=gt.ap, op=ALU.mult), reads=[otmp.b, gt.b], writes=[on2.b])
                transposes(on2, onT, tb, "act")
                yield
                sl6 = slotsA.next()
                proj_tok(onT, wo, 0, sl6)
                yield
                tail(layer, t, sl6, x, tt, gpost, gpre2, tb)
                yield

            def drain(g):
                if g is None:
                    return None
                try:
                    next(g)
                    return g
                except StopIteration:
                    return None

            def pipeline(tiles):
                n = len(tiles)
                xs = {0: load_x(tiles[0])}
                for step in range(n + 2):
                    if step + 1 < n:
                        xs[step + 1] = load_x(tiles[step + 1])
                    gens = []
                    if step < n:
                        gens.append(stage_a(tiles[step], xs[step]))
                    if 0 <= step - 1 < n:
                        gens.append(stage_b1(tiles[step - 1]))
                    if 0 <= step - 2 < n:
                        gens.append(stage_b2(tiles[step - 2], xs[step - 2]))
                    while gens:
                        gens = [g for g in (drain(g) for g in gens) if g is not None]

            pipeline(list(range(NTS)))
            S.op("sp", lambda e: e.dma_start(out=s_s[j].rearrange("h k v -> k h v"), in_=Sst.ap.rearrange("p (h v) -> p h v", h=HH)),
                 reads=[s_.b for s_ in Sh], dma="st_s")
            S.barrier()
            A.off = PERS

        phase_wcast()
        for layer in range(NL):
            if layer % 2 == 0:
                LF = phase_fox_a(layer)
                phase_fox_a2(layer, LF)
                phase_fox_b(layer)
                phase_fox_c1(layer)
            else:
                phase_hgrn(layer)
            phase_ffn(layer)
        S.emit()
    return nc


_CACHE = {}


def get_nc(T, PAST, NL=4):
    key = (T, PAST, NL)
    if key not in _CACHE:
        _CACHE[key] = build(T, PAST, NL)
    return _CACHE[key]


def make_in_map(c, inp, T, PAST):
    f = lambda a: np.ascontiguousarray(np.asarray(a, dtype=np.float32))
    m = {
        "x_p": f(inp["x_prompt"][c]),
        "x_s": f(inp["x_sample"][c]),
        "ck": f(np.asarray(inp["cache_k"])[:, c].reshape(-1, PAST, D)),
        "cv": f(np.asarray(inp["cache_v"])[:, c].reshape(-1, PAST, D)),
        "clf": f(np.asarray(inp["cache_logf"])[:, c]),
        "st": f(np.asarray(inp["state_s"])[:, c]),
        "fox_w_in": f(inp["fox_w_in"]), "fox_b_f": f(inp["fox_b_f"]), "fox_q_norm": f(inp["fox_q_norm"]),
        "fox_k_norm": f(inp["fox_k_norm"]), "fox_w_out": f(inp["fox_w_out"]), "hgrn_w_in": f(inp["hgrn_w_in"]),
        "hgrn_lb": f(inp["hgrn_lb_logits"]), "hgrn_out_norm": f(inp["hgrn_out_norm"]), "hgrn_w_out": f(inp["hgrn_w_out"]),
        "pre_mix": f(inp["pre_mix_norm"]), "post_mix": f(inp["post_mix_norm"]), "pre_ffn": f(inp["pre_ffn_norm"]),
        "post_ffn": f(inp["post_ffn_norm"]), "ffn_up": f(inp["ffn_w_up"]), "ffn_down": f(inp["ffn_w_down"]),
        "consts": make_consts(),
    }
    return m


def assemble(results, B, T):
    def st(name, shape_tail=None):
        return np.stack([np.asarray(r[name], dtype=np.float32) for r in results], axis=0)
    y_p = st("y_p")
    y_s = st("y_s")
    k_p = np.moveaxis(st("k_p"), 0, 1).reshape(-1, B, T, NH, DH)
    v_p = np.moveaxis(st("v_p"), 0, 1).reshape(-1, B, T, NH, DH)
    lf_p = np.moveaxis(st("lf_p"), 0, 1)
    s_p = np.moveaxis(st("s_p"), 0, 1)
    k_s = np.moveaxis(st("k_s"), 0, 1).reshape(-1, B, DS, NH, DH)
    v_s = np.moveaxis(st("v_s"), 0, 1).reshape(-1, B, DS, NH, DH)
    lf_s = np.moveaxis(st("lf_s"), 0, 1)
    s_s = np.moveaxis(st("s_s"), 0, 1)
    return (y_p, y_s, k_p, v_p, lf_p, s_p, k_s, v_s, lf_s, s_s)


def kernel(**inputs):
    xp = np.asarray(inputs["x_prompt"])
    B, T = xp.shape[0], xp.shape[1]
    PAST = np.asarray(inputs["cache_k"]).shape[2]
    nc = get_nc(T, PAST)
    in_maps = [make_in_map(c, inputs, T, PAST) for c in range(B)]
    res = run_bass_kernel_spmd(nc, in_maps, core_ids=list(range(B)))
    return assemble(res.results, B, T)
```

```python
import contextlib
import numpy as np
import concourse.bass as bass
import concourse.mybir as mybir
from concourse.bass_utils import run_bass_kernel_spmd

F32 = mybir.dt.float32
BF16 = mybir.dt.bfloat16
AF = mybir.ActivationFunctionType
ALU = mybir.AluOpType
AX = mybir.AxisListType

D = 1024
NH = 16
DH = 64
HH = 8
DFF = 4096
EPS = 1e-6
DS = 32
FW = 4 * D + NH


class Buf:
    __slots__ = ("name", "w", "r")

    def __init__(self, name=""):
        self.name = name
        self.w = None
        self.r = []


class Op:
    __slots__ = ("eng", "fn", "deps", "odeps", "sig", "sigval", "dma", "cost", "seq", "bar")

    def __init__(self, eng, fn, dma):
        self.eng = eng
        self.fn = fn
        self.deps = []
        self.odeps = []
        self.sig = False
        self.sigval = None
        self.dma = dma
        self.cost = 1000
        self.seq = 0
        self.bar = 0


DEFCOST = {"pe": 2000, "act": 1100, "dve": 800, "pool": 2300, "sp": 150}
DMA_LAT = 3000
RESCHEDULE = True
RESCHED_SEGS = None
RESCHED_ENGS = None


class Sched:
    CE = ("pe", "act", "dve", "pool")
    ENGS = ("pe", "act", "dve", "pool", "sp")

    def __init__(self, nc):
        self.nc = nc
        self.ops = {e: [] for e in self.ENGS}
        self.streams = {}
        self.last_dma = {}
        self.nseq = 0

    def op(self, eng, fn, reads=(), writes=(), dma=None, extra=(), cost=None):
        o = Op(eng, fn, dma)
        o.cost = cost if cost is not None else DEFCOST[eng]
        self.nseq += 1
        o.seq = self.nseq
        deps = []
        seen = set()

        def add(d):
            if d is None or d is o or id(d) in seen:
                return
            seen.add(id(d))
            deps.append(d)

        for b in reads:
            add(b.w)
        for b in writes:
            add(b.w)
            for r in b.r:
                add(r)
        for d in extra:
            add(d)
        if dma is not None:
            add(self.last_dma.get(dma))
            self.last_dma[dma] = o
            n = self.streams.setdefault(dma, [0])
            n[0] += 1
            o.sigval = 16 * n[0]
        for d in deps:
            if d.dma is None and d.eng == eng and eng == "pe":
                o.odeps.append(d)
                continue
            o.deps.append(d)
            if d.dma is None:
                d.sig = True
        for b in writes:
            b.w = o
            b.r = []
        for b in reads:
            if b.w is not o:
                b.r.append(o)
        self.ops[eng].append(o)
        return o

    def barrier(self):
        firsts = []
        alld = list(self.last_dma.values())
        for e in self.CE:
            o = Op(e, lambda eh: eh.nop(), None)
            o.bar = 1
            for d in alld:
                o.deps.append(d)
            self.ops[e].append(o)
            firsts.append(o)
        for e in self.ENGS:
            o = Op(e, lambda eh: eh.nop(), None)
            o.bar = 2
            for d in firsts:
                if d.eng == e:
                    continue
                o.deps.append(d)
                d.sig = True
            self.ops[e].append(o)

    def _reschedule(self):
        import heapq
        segs = {e: [] for e in self.ENGS}
        nseg = 0
        for e in self.ENGS:
            cur = []
            bars = []
            for o in self.ops[e]:
                if o.bar:
                    bars.append(o)
                    if o.bar == 2:
                        segs[e].append((cur, bars))
                        cur, bars = [], []
                else:
                    assert not bars
                    cur.append(o)
            segs[e].append((cur, bars))
            nseg = max(nseg, len(segs[e]))
        for e in self.ENGS:
            while len(segs[e]) < nseg:
                segs[e].append(([], []))
        for k in range(nseg):
            allops = []
            for e in self.ENGS:
                allops.extend(segs[e][k][0])
            if RESCHEDULE and allops and (RESCHED_SEGS is None or k in RESCHED_SEGS):
                inseg = {id(o) for o in allops}
                nd = {}
                users = {}
                for o in allops:
                    c = 0
                    for d in o.deps + o.odeps:
                        if id(d) in inseg:
                            c += 1
                            users.setdefault(id(d), []).append(o)
                    nd[id(o)] = c
                ready = {e: [] for e in self.ENGS}
                for o in allops:
                    if nd[id(o)] == 0:
                        heapq.heappush(ready[o.eng], (o.seq, id(o), o))
                free_at = {e: 0 for e in self.ENGS}
                start = {}
                events = [(0, 0)]
                evn = 1
                pending = []
                ndone = 0
                now = 0
                while ndone < len(allops):
                    while pending and pending[0][0] <= now:
                        _, _, o = heapq.heappop(pending)
                        ndone += 1
                        for u in users.get(id(o), ()):
                            nd[id(u)] -= 1
                            if nd[id(u)] == 0:
                                heapq.heappush(ready[u.eng], (u.seq, id(u), u))
                    if ndone >= len(allops):
                        break
                    progressed = False
                    for e in self.ENGS:
                        if free_at[e] <= now and ready[e]:
                            _, _, o = heapq.heappop(ready[e])
                            start[id(o)] = now
                            free_at[e] = now + o.cost
                            done = now + (o.cost + DMA_LAT if o.dma is not None else o.cost)
                            evn += 1
                            heapq.heappush(pending, (done, evn, o))
                            progressed = True
                    if progressed:
                        continue
                    nxt = []
                    if pending:
                        nxt.append(pending[0][0])
                    for e in self.ENGS:
                        if ready[e] and free_at[e] > now:
                            nxt.append(free_at[e])
                    if not nxt:
                        left = [o for o in allops if id(o) not in start]
                        print("STUCK0 seg", k, "nops", len(allops), "ndone", ndone, "started", len(start), "uniq", len({id(o) for o in allops}))
                        o = left[0]
                        print("STUCK seg", k, "nops", len(allops), "left", len(left), "first eng", o.eng, "seq", o.seq, "nd", nd[id(o)],
                              [(d.eng, d.seq, id(d) in inseg, id(d) in start, d.bar) for d in o.deps + o.odeps])
                    assert nxt, "scheduler stuck"
                    now = max(now + 1, min(nxt))
                for e in self.ENGS:
                    if RESCHED_ENGS is None or e in RESCHED_ENGS:
                        segs[e][k][0].sort(key=lambda o: (start[id(o)], o.seq))
            for e in self.CE:
                lst, bars = segs[e][k]
                for bo in bars:
                    if bo.bar == 1 and lst:
                        bo.deps.append(lst[-1])
                        lst[-1].sig = True
        for e in self.ENGS:
            out = []
            for lst, bars in segs[e]:
                out.extend(lst)
                out.extend(bars)
            self.ops[e] = out

    def _check(self):
        pos = {e: 0 for e in self.ENGS}
        done = set()
        n = sum(len(v) for v in self.ops.values())
        while len(done) < n:
            prog = False
            for e in self.ENGS:
                while pos[e] < len(self.ops[e]):
                    o = self.ops[e][pos[e]]
                    if all(id(d) in done for d in o.deps) and all(id(d) in done for d in o.odeps):
                        done.add(id(o))
                        pos[e] += 1
                        prog = True
                    else:
                        break
            if not prog:
                for e in self.ENGS:
                    if pos[e] < len(self.ops[e]):
                        o = self.ops[e][pos[e]]
                        print("DEADLOCK", e, "pos", pos[e], "seq", o.seq, "bar", o.bar, "waiting on",
                              [(d.eng, d.seq, d.bar, d.dma) for d in o.deps + o.odeps if id(d) not in done])
                raise RuntimeError("deadlock in emitted order")

    def emit(self):
        nc = self.nc
        self._reschedule()
        self._check()
        with contextlib.ExitStack() as es:
            esem = {e: es.enter_context(nc.semaphore("s_" + e)) for e in self.CE}
            ssem = {k: es.enter_context(nc.semaphore("d%d" % i)) for i, k in enumerate(self.streams)}
            for e in self.CE:
                c = 0
                for o in self.ops[e]:
                    if o.dma is None and o.sig:
                        c += 1
                        o.sigval = c
            block = es.enter_context(nc.Block())

            def run(ename, eh):
                waited = {}
                for o in self.ops[ename]:
                    for d in o.deps:
                        sem = ssem[d.dma] if d.dma is not None else esem[d.eng]
                        key = id(sem)
                        if waited.get(key, 0) >= d.sigval:
                            continue
                        waited[key] = d.sigval
                        eh.wait_ge(sem, d.sigval)
                    ins = o.fn(eh)
                    if o.dma is not None:
                        ins.then_inc(ssem[o.dma], 16)
                    elif o.sig:
                        ins.then_inc(esem[ename], 1)
                if ename == "sp":
                    for k, n in self.streams.items():
                        eh.wait_ge(ssem[k], 16 * n[0])

            @block.tensor
            def _(e):
                run("pe", e)

            @block.scalar
            def _(e):
                run("act", e)

            @block.vector
            def _(e):
                run("dve", e)

            @block.gpsimd
            def _(e):
                run("pool", e)

            @block.sync
            def _(e):
                run("sp", e)


class Tl:
    __slots__ = ("ap", "b")

    def __init__(self, ap, name=""):
        self.ap = ap
        self.b = Buf(name)


class Arena:
    def __init__(self, ap, size):
        self.ap = ap
        self.size = size
        self.off = 0

    def alloc(self, n, dt=BF16, name=""):
        w = n * 2 if dt == F32 else n
        off = self.off
        self.off += (w + 31) // 32 * 32
        assert self.off <= self.size, ("SBUF arena overflow", name, self.off, self.size)
        v = self.ap[:, off:off + w]
        if dt == F32:
            v = v.bitcast(F32)
        return Tl(v, name)


class Ring:
    def __init__(self, tiles):
        self.t = tiles
        self.i = -1

    def next(self):
        self.i += 1
        return self.t[self.i % len(self.t)]


def make_consts():
    s = np.arange(128)[:, None]
    t = np.arange(128)[None, :]
    c = {}
    c["ident"] = (s == t).astype(np.float32)
    c["tri"] = (s <= t).astype(np.float32)
    c["maskneg"] = np.where(s > t, -30000.0, 0.0).astype(np.float32)
    c["d1p"] = ((s <= t).astype(np.float32) - (s <= 63).astype(np.float32) * np.ones_like(t, dtype=np.float32))
    vs = (s < DS) & (t < DS)
    c["d1s"] = np.where(vs, (s <= t).astype(np.float32) - (s <= 15).astype(np.float32), 0.0).astype(np.float32)
    sel = np.zeros((128, 8), np.float32)
    sv = np.arange(128)
    sel[:, 0] = sv <= 63
    sel[:, 1] = 1.0
    sel[:, 2] = sv > 63
    sel[:, 4] = sv <= 15
    sel[:, 5] = sv < DS
    sel[:, 6] = (sv > 15) & (sv < DS)
    c["sel"] = sel
    c["him"] = np.where(s <= t, 3.0e38, 0.0).astype(np.float32)
    c["lom"] = np.where(s <= t, -3.0e38, 0.0).astype(np.float32)
    return np.concatenate([c["ident"], c["tri"], c["maskneg"], c["him"], c["lom"], c["d1p"], c["d1s"], c["sel"]], axis=1).astype(np.float32)


NCONST = 7 * 128 + 8


def build(T, PAST, NL=4):
    NT = T // 128
    NP = PAST // 128
    NTS = NT + 1
    TS = T + 128
    KC = T + PAST + 128
    NKS = KC // 128
    NA = (NL + 1) // 2
    NR = NL // 2

    nc = bass.Bass("TRN2", target_bir_lowering=False)

    def din(name, shape):
        return nc.dram_tensor(name, list(shape), F32, kind="ExternalInput").ap()

    def dout(name, shape):
        return nc.dram_tensor(name, list(shape), F32, kind="ExternalOutput").ap()

    def dscr(name, shape, dt):
        return nc.dram_tensor(name, list(shape), dt, kind="Internal").ap()

    x_p = din("x_p", [T, D])
    x_s = din("x_s", [DS, D])
    ck = din("ck", [NA, PAST, D])
    cv = din("cv", [NA, PAST, D])
    clf = din("clf", [NA, PAST, NH])
    st = din("st", [NR, HH, 128, 128])
    fox_w_in = din("fox_w_in", [NA, D, FW])
    fox_b_f = din("fox_b_f", [NA, NH])
    fox_q_norm = din("fox_q_norm", [NA, DH])
    fox_k_norm = din("fox_k_norm", [NA, DH])
    fox_w_out = din("fox_w_out", [NA, D, D])
    hgrn_w_in = din("hgrn_w_in", [NR, D, 4 * D])
    hgrn_lb = din("hgrn_lb", [4, D])
    hgrn_out_norm = din("hgrn_out_norm", [NR, D])
    hgrn_w_out = din("hgrn_w_out", [NR, D, D])
    pre_mix = din("pre_mix", [NL, D])
    post_mix = din("post_mix", [NL, D])
    pre_ffn = din("pre_ffn", [NL, D])
    post_ffn = din("post_ffn", [NL, D])
    ffn_up = din("ffn_up", [NL, D, DFF])
    ffn_down = din("ffn_down", [NL, DFF, D])
    consts = din("consts", [128, NCONST])

    y_p = dout("y_p", [T, D])
    y_s = dout("y_s", [DS, D])
    k_p = dout("k_p", [NA, T, D])
    v_p = dout("v_p", [NA, T, D])
    lf_p = dout("lf_p", [NA, T, NH])
    s_p = dout("s_p", [NR, HH, 128, 128])
    k_s = dout("k_s", [NA, DS, D])
    v_s = dout("v_s", [NA, DS, D])
    lf_s = dout("lf_s", [NA, DS, NH])
    s_s = dout("s_s", [NR, HH, 128, 128])

    xres = dscr("xres", [NTS, 128, D], F32)
    wb_fin = dscr("wb_fin", [NA, D, FW], BF16)
    wb_fout = dscr("wb_fout", [NA, D, D], BF16)
    wb_hin = dscr("wb_hin", [NR, D, 4 * D], BF16)
    wb_hout = dscr("wb_hout", [NR, D, D], BF16)
    wb_up = dscr("wb_up", [NL, 32, 128, 8, 128], BF16)
    wb_down = dscr("wb_down", [NL, DFF, D], BF16)
    QT = dscr("QT", [D, TS], BF16)
    KT = dscr("KT", [D, KC], BF16)
    VT = dscr("VT", [KC, D], BF16)
    GT = dscr("GT", [D, TS], BF16)
    CTs = dscr("CTs", [NH, 6, KC], BF16)
    OGT = dscr("OGT", [D, TS], BF16)
    H2T = dscr("H2T", [D, TS], BF16)

    db = {}

    def dbuf(*key):
        if key not in db:
            db[key] = Buf(str(key))
        return db[key]

    ARENA = 106000
    with contextlib.ExitStack() as es:
        arena_t = es.enter_context(nc.sbuf_tensor("arena", [128, ARENA], BF16))
        psum = es.enter_context(nc.psum_tensor("psum", [128, 8, 512], F32))
        A = Arena(arena_t, ARENA)
        S = Sched(nc)

        bank = [Tl(psum[:, k, :], "bank%d" % k) for k in range(8)]

        def slot2(k):
            return Tl(psum[:, k:k + 2, :].rearrange("p a b -> p (a b)"), "slot%d" % k)

        cst = A.alloc(NCONST, F32, "consts")
        S.op("sp", lambda e: e.dma_start(out=cst.ap, in_=consts[:, :]), writes=[cst.b], dma="cst")
        identf = cst.ap[:, 0:128]
        trif = cst.ap[:, 128:256]
        masknegf = cst.ap[:, 256:384]
        d1p = cst.ap[:, 640:768]
        d1s = cst.ap[:, 768:896]
        selp = cst.ap[:, 896:899]
        sels = cst.ap[:, 900:903]
        cb = A.alloc(5 * 128, BF16, "constb")
        S.op("dve", lambda e: e.tensor_copy(out=cb.ap, in_=cst.ap[:, 0:640]), reads=[cst.b], writes=[cb.b])
        idb = cb.ap[:, 0:128]
        mask01b = cb.ap[:, 128:256]
        masknegb = cb.ap[:, 256:384]
        himb = cb.ap[:, 384:512]
        lomb = cb.ap[:, 512:640]
        CONSTB = [cst.b, cb.b]
        nhalf = A.alloc(16, F32, "nhalf")
        S.op("pool", lambda e: e.memset(nhalf.ap, -0.5), writes=[nhalf.b])

        oml = {}
        if NR > 0:
            for layer in range(1, NL, 2):
                oml[layer] = A.alloc(D, F32, "oml%d" % layer)
            keep = A.off
            L = [A.alloc(D, F32, "lbl%d" % i) for i in range(4)]
            mx = A.alloc(D, F32, "lbmx")
            sm = A.alloc(D, F32, "lbsum")

            def ld_l(i):
                S.op("sp", lambda e: e.dma_start(out=L[i].ap, in_=hgrn_lb[i:i + 1, :].partition_broadcast(128)), writes=[L[i].b], dma="lb%d" % i)
            for i in range(4):
                ld_l(i)

            def tt_(out, a_, b_, op):
                S.op("dve", lambda e: e.tensor_tensor(out=out.ap, in0=a_.ap, in1=b_.ap, op=op), reads=[a_.b, b_.b], writes=[out.b])
            tt_(mx, L[0], L[1], ALU.max)
            tt_(mx, mx, L[2], ALU.max)
            tt_(mx, mx, L[3], ALU.max)

            def ex_(i):
                tt_(L[i], L[i], mx, ALU.subtract)
                S.op("act", lambda e: e.activation(out=L[i].ap, in_=L[i].ap, func=AF.Exp), reads=[L[i].b], writes=[L[i].b])
            for i in range(4):
                ex_(i)
            tt_(sm, L[0], L[1], ALU.add)
            tt_(sm, sm, L[2], ALU.add)
            tt_(sm, sm, L[3], ALU.add)
            S.op("dve", lambda e: e.reciprocal(out=sm.ap, in_=sm.ap), reads=[sm.b], writes=[sm.b])

            def mk_oml(layer):
                o_ = oml[layer]
                S.op("dve", lambda e: e.tensor_copy(out=o_.ap, in_=L[1].ap), reads=[L[1].b], writes=[o_.b])
                for i in range(2, layer + 1):
                    tt_(o_, o_, L[i], ALU.add)
                tt_(o_, o_, sm, ALU.mult)
                S.op("dve", lambda e: e.tensor_scalar(out=o_.ap, in0=o_.ap, scalar1=-1.0, scalar2=1.0, op0=ALU.mult, op1=ALU.add),
                     reads=[o_.b], writes=[o_.b])
            for layer in range(1, NL, 2):
                mk_oml(layer)
            S.barrier()
            A.off = keep

        PERS = A.off

        def stats(src_ap, src_b, n, junk, ss):
            S.op("pool", lambda e: e.memset(ss.ap[:, 0:1], 0.0), writes=[ss.b], cost=80)
            S.op("act", lambda e: e.activation(out=junk.ap[:, 0:n], in_=src_ap, func=AF.Square, accum_out=ss.ap[:, 0:1]),
                 reads=[src_b, ss.b], writes=[ss.b])
            S.op("act", lambda e: e.activation(out=ss.ap[:, 1:2], in_=ss.ap[:, 0:1], func=AF.Sqrt, scale=1.0 / n, bias=EPS),
                 reads=[ss.b], writes=[ss.b], cost=1500)
            S.op("dve", lambda e: e.reciprocal(out=ss.ap[:, 2:3], in_=ss.ap[:, 1:2]), reads=[ss.b], writes=[ss.b], cost=200)

        def transposes(src, dstT, tb, copy_eng):
            tv = tb.ap.bitcast(BF16)

            def f(e):
                ins = None
                for c in range(8):
                    ins = e.transpose(out=tv[:, c * 128:(c + 1) * 128], in_=src.ap[:, c * 128:(c + 1) * 128], identity=idb)
                return ins
            S.op("pe", f, reads=[src.b] + CONSTB, writes=[tb.b], cost=900)
            if copy_eng == "act":
                S.op("act", lambda e: e.activation(out=dstT.ap, in_=tv, func=AF.Copy), reads=[tb.b], writes=[dstT.b])
            else:
                S.op("dve", lambda e: e.tensor_copy(out=dstT.ap, in_=tv), reads=[tb.b], writes=[dstT.b])

        def proj_tok(hT, w, wcol0, sl):
            def f(e):
                ins = None
                for half in range(2):
                    for kc in range(8):
                        ins = e.matmul(sl.ap[:, half * 512:(half + 1) * 512], lhsT=hT.ap[:, kc * 128:(kc + 1) * 128],
                                       rhs=w.ap[:, kc, wcol0 + half * 512: wcol0 + (half + 1) * 512],
                                       start=(kc == 0), stop=(kc == 7))
                return ins
            S.op("pe", f, reads=[hT.b, w.b], writes=[sl.b], cost=4300)

        def load_bcast(dst, src_row, key):
            S.op("sp", lambda e: e.dma_start(out=dst.ap, in_=src_row.partition_broadcast(128)), writes=[dst.b], dma=key)

        def xsrc(layer, t):
            if layer == 0 and t < NT:
                return x_p[t * 128:(t + 1) * 128, :], []
            return xres[t], [dbuf("xres", t)]

        def qcol(t):
            return t * 128

        def kcol(t):
            return t * 128 if t < NT else T + PAST

        def phase_wcast():
            mark = A.off
            CH = 4096
            NWB = 4
            fb = [A.alloc(CH, F32, "wf%d" % i) for i in range(NWB)]
            bb = [A.alloc(CH, BF16, "wb%d" % i) for i in range(NWB)]
            jobs = []

            def flat(ap2d):
                return ap2d.rearrange("r c -> (r c)").rearrange("(p n) -> p n", p=128)

            for j in range(NA):
                jobs.append((flat(fox_w_in[j]), flat(wb_fin[j]), dbuf("wb_fin", j)))
                jobs.append((flat(fox_w_out[j]), flat(wb_fout[j]), dbuf("wb_fout", j)))
            for j in range(NR):
                jobs.append((flat(hgrn_w_in[j]), flat(wb_hin[j]), dbuf("wb_hin", j)))
                jobs.append((flat(hgrn_w_out[j]), flat(wb_hout[j]), dbuf("wb_hout", j)))
            for l in range(NL):
                jobs.append((flat(ffn_down[l]), flat(wb_down[l]), dbuf("wb_down", l)))
            step = 0
            engs = ("dve", "pool", "act")
            for src, dst, tok in jobs:
                n = src.shape[1]
                for c0 in range(0, n, CH):
                    w_ = min(CH, n - c0)
                    f_, b_ = fb[step % NWB], bb[step % NWB]
                    S.op("sp", lambda e, f_=f_, src=src, c0=c0, w_=w_: e.dma_start(out=f_.ap[:, 0:w_], in_=src[:, c0:c0 + w_]),
                         writes=[f_.b], dma="wl%d" % (step % NWB))
                    eng = engs[step % 3]
                    if eng == "act":
                        S.op("act", lambda e, f_=f_, b_=b_, w_=w_: e.activation(out=b_.ap[:, 0:w_], in_=f_.ap[:, 0:w_], func=AF.Copy),
                             reads=[f_.b], writes=[b_.b])
                    else:
                        S.op(eng, lambda e, f_=f_, b_=b_, w_=w_: e.tensor_copy(out=b_.ap[:, 0:w_], in_=f_.ap[:, 0:w_]),
                             reads=[f_.b], writes=[b_.b])
                    S.op("sp", lambda e, b_=b_, dst=dst, c0=c0, w_=w_: e.dma_start(out=dst[:, c0:c0 + w_], in_=b_.ap[:, 0:w_]),
                         reads=[b_.b], writes=[tok], dma="ws%d" % (step % NWB))
                    step += 1
            for l in range(NL):
                for c in range(8):
                    f_, b_ = fb[step % NWB], bb[step % NWB]
                    S.op("sp", lambda e, f_=f_, l=l, c=c: e.dma_start(out=f_.ap[:, 0:DFF], in_=ffn_up[l, c * 128:(c + 1) * 128, :]),
                         writes=[f_.b], dma="wl%d" % (step % NWB))
                    eng = engs[step % 3]
                    if eng == "act":
                        S.op("act", lambda e, f_=f_, b_=b_: e.activation(out=b_.ap[:, 0:DFF], in_=f_.ap[:, 0:DFF], func=AF.Copy),
                             reads=[f_.b], writes=[b_.b])
                    else:
                        S.op(eng, lambda e, f_=f_, b_=b_: e.tensor_copy(out=b_.ap[:, 0:DFF], in_=f_.ap[:, 0:DFF]),
                             reads=[f_.b], writes=[b_.b])
                    S.op("sp", lambda e, b_=b_, l=l, c=c: e.dma_start(
                        out=wb_up[l, :, :, c, :].rearrange("j p n -> p j n"),
                        in_=b_.ap[:, 0:DFF].rearrange("p (j n) -> p j n", n=128)),
                        reads=[b_.b], writes=[dbuf("wb_up", l)], dma="ws%d" % (step % NWB))
                    step += 1
            xt = fb[step % NWB]
            S.op("pool", lambda e: e.memset(xt.ap[:, 0:D], 0.0), writes=[xt.b])
            S.op("sp", lambda e: e.dma_start(out=xt.ap[0:DS, 0:D], in_=x_s[:, :]), reads=[xt.b], writes=[xt.b], dma="wl%d" % (step % NWB))
            S.op("sp", lambda e: e.dma_start(out=xres[NT], in_=xt.ap[:, 0:D]), reads=[xt.b], writes=[dbuf("xres", NT)], dma="ws%d" % (step % NWB))
            S.barrier()
            A.off = mark

        def make_tail_tiles():
            d = {}
            d["junk"] = A.alloc(D, BF16, "tjunk")
            d["ss"] = [A.alloc(8, F32, "tss%d" % i) for i in range(4)]
            d["tmp"] = A.alloc(D, F32, "ttmp")
            d["xn"] = Ring([A.alloc(D, F32, "txn%d" % i) for i in range(1)])
            d["h2"] = A.alloc(D, BF16, "th2")
            d["h2T"] = Ring([A.alloc(D, BF16, "th2T%d" % i) for i in range(1)])
            return d

        def tail(layer, t, sl, x, tt, gpost, gpre2, tb):
            ssA = tt["ss"][(2 * t) % 4]
            ssB = tt["ss"][(2 * t + 1) % 4]
            stats(sl.ap, sl.b, D, tt["junk"], ssA)
            tmp = tt["tmp"]
            S.op("dve", lambda e: e.scalar_tensor_tensor(out=tmp.ap, in0=sl.ap, scalar=ssA.ap[:, 2:3], in1=gpost.ap, op0=ALU.mult, op1=ALU.mult),
                 reads=[sl.b, ssA.b, gpost.b], writes=[tmp.b])
            xn = tt["xn"].next()
            S.op("pool", lambda e: e.tensor_tensor(out=xn.ap, in0=x.ap, in1=tmp.ap, op=ALU.add), reads=[x.b, tmp.b], writes=[xn.b])
            nrow = 128 if t < NT else DS
            S.op("sp", lambda e: e.dma_start(out=xres[t, 0:nrow, :], in_=xn.ap[0:nrow, :]), reads=[xn.b], writes=[dbuf("xres", t)], dma="st_x%d" % (t % 2))
            stats(xn.ap, xn.b, D, tt["junk"], ssB)
            h2 = tt["h2"]
            S.op("dve", lambda e: e.scalar_tensor_tensor(out=h2.ap, in0=xn.ap, scalar=ssB.ap[:, 2:3], in1=gpre2.ap, op0=ALU.mult, op1=ALU.mult),
                 reads=[xn.b, ssB.b, gpre2.b], writes=[h2.b])
            h2T = tt["h2T"].next()
            transposes(h2, h2T, tb, "act")
            S.op("sp", lambda e: e.dma_start(out=H2T[:, qcol(t):qcol(t) + 128].rearrange("(c p) n -> p c n", p=128),
                                             in_=h2T.ap.rearrange("p (c n) -> p c n", c=8)),
                 reads=[h2T.b], writes=[dbuf("H2T", t)], dma="st_h%d" % (t % 2))

        def phase_fox_a(layer):
            j = layer // 2
            LF = A.alloc(NKS * NH, F32, "LF")
            after_lf = A.off
            w = A.alloc(8 * FW, BF16, "fw")
            w.ap = w.ap.rearrange("p (c n) -> p c n", c=8)

            def ld_w(c):
                S.op("sp", lambda e: e.dma_start(out=w.ap[:, c, :], in_=wb_fin[j, c * 128:(c + 1) * 128, :]),
                     reads=[dbuf("wb_fin", j)], writes=[w.b], dma="wld%d" % (c % 2))
            for c in range(8):
                ld_w(c)
            gpre = A.alloc(D, F32, "gpre")
            load_bcast(gpre, pre_mix[layer:layer + 1, :], "g0")
            gq = A.alloc(DH, F32, "gq")
            load_bcast(gq, fox_q_norm[j:j + 1, :], "g1")
            gk = A.alloc(DH, F32, "gk")
            load_bcast(gk, fox_k_norm[j:j + 1, :], "g2")
            bfb = A.alloc(NH, F32, "bfb")
            load_bcast(bfb, fox_b_f[j:j + 1, :], "g3")
            xr = Ring([A.alloc(D, F32, "x%d" % i) for i in range(2)])
            junk = A.alloc(D, BF16, "junk")
            ss = [A.alloc(8, F32, "ss%d" % i) for i in range(2)]
            h = A.alloc(D, BF16, "h")
            hT = A.alloc(D, BF16, "hT")
            sq = A.alloc(D, F32, "sq")
            sq2 = A.alloc(D, F32, "sq2")
            ssq = [A.alloc(64, F32, "ssq%d" % i) for i in range(2)]
            qn = A.alloc(D, BF16, "qn")
            kf = Ring([A.alloc(D, F32, "kf%d" % i) for i in range(2)])
            kb = A.alloc(D, BF16, "kb")
            vf = Ring([A.alloc(D, F32, "vf%d" % i) for i in range(2)])
            vb = Ring([A.alloc(D, BF16, "vb%d" % i) for i in range(2)])
            QTs = Ring([A.alloc(D, BF16, "QTs%d" % i) for i in range(2)])
            KTs = Ring([A.alloc(D, BF16, "KTs%d" % i) for i in range(2)])
            GTs = Ring([A.alloc(D, BF16, "GTs%d" % i) for i in range(2)])
            f1 = [A.alloc(64, F32, "f1_%d" % i) for i in range(2)]
            slots = Ring([slot2(1), slot2(3), slot2(5)])
            tb = bank[0]
            fzb = bank[7]

            def k_to_scratch(kbt, col):
                KTt = KTs.next()
                transposes(kbt, KTt, tb, "dve")
                S.op("sp", lambda e: e.dma_start(out=KT[:, col:col + 128].rearrange("(c p) n -> p c n", p=128),
                                                 in_=KTt.ap.rearrange("p (c n) -> p c n", c=8)),
                     reads=[KTt.b], writes=[dbuf("KT", col)], dma="st_kt%d" % (KTs.i % 2))

            def v_to_scratch(vbt, col, par):
                S.op("sp", lambda e: e.dma_start(out=VT[col:col + 128, :], in_=vbt.ap), reads=[vbt.b], writes=[dbuf("VT", col)],
                     dma="st_vt%d" % par)

            def do_past(jt):
                col = T + jt * 128
                kft = kf.next()
                S.op("sp", lambda e: e.dma_start(out=kft.ap, in_=ck[j, jt * 128:(jt + 1) * 128, :]), writes=[kft.b], dma="ldk%d" % (kf.i % 2))
                S.op("pool", lambda e: e.tensor_copy(out=kb.ap, in_=kft.ap), reads=[kft.b], writes=[kb.b])
                k_to_scratch(kb, col)
                vft = vf.next()
                S.op("sp", lambda e: e.dma_start(out=vft.ap, in_=cv[j, jt * 128:(jt + 1) * 128, :]), writes=[vft.b], dma="ldv%d" % (vf.i % 2))
                vbt = vb.next()
                S.op("pool", lambda e: e.tensor_copy(out=vbt.ap, in_=vft.ap), reads=[vft.b], writes=[vbt.b])
                v_to_scratch(vbt, col, vb.i % 2)
                slot_ = NT + jt
                S.op("sp", lambda e: e.dma_start(out=LF.ap[:, slot_ * NH:(slot_ + 1) * NH], in_=clf[j, jt * 128:(jt + 1) * 128, :]),
                     writes=[LF.b], dma="ldlf")
            for jt in range(NP):
                do_past(jt)

            def load_x(t):
                xt = xr.next()
                src, rd = xsrc(layer, t)
                S.op("sp", lambda e: e.dma_start(out=xt.ap, in_=src), reads=rd, writes=[xt.b], dma="ldx%d" % (t % 2))
                return xt

            def headnorm(which, sl_, t):
                ssq_ = ssq[t % 2]
                nv = 128 if t < NT else DS
                sqt = sq if which == "q" else sq2
                o0 = 0 if which == "q" else 32
                sq3 = sqt.ap.rearrange("p (h d) -> p h d", h=NH)
                S.op("act", lambda e: e.activation(out=sqt.ap, in_=sl_.ap, func=AF.Square), reads=[sl_.b], writes=[sqt.b])
                S.op("dve", lambda e: e.tensor_reduce(out=ssq_.ap[:, o0:o0 + NH], in_=sq3, axis=AX.X, op=ALU.add), reads=[sqt.b], writes=[ssq_.b])
                S.op("act", lambda e: e.activation(out=ssq_.ap[:, o0 + NH:o0 + 2 * NH], in_=ssq_.ap[:, o0:o0 + NH], func=AF.Sqrt,
                                                   scale=1.0 / DH, bias=EPS), reads=[ssq_.b], writes=[ssq_.b])
                S.op("dve", lambda e: e.reciprocal(out=ssq_.ap[:, o0:o0 + NH], in_=ssq_.ap[:, o0 + NH:o0 + 2 * NH]),
                     reads=[ssq_.b], writes=[ssq_.b])
                if which == "q":
                    S.op("dve", lambda e: e.tensor_scalar(out=ssq_.ap[:, o0:o0 + NH], in0=ssq_.ap[:, o0:o0 + NH], scalar1=DH ** -0.5, scalar2=None, op0=ALU.mult),
                         reads=[ssq_.b], writes=[ssq_.b])
                S.op("dve", lambda e: e.tensor_tensor(
                    out=sq3, in0=sl_.ap.rearrange("p (h d) -> p h d", h=NH),
                    in1=ssq_.ap[:, o0:o0 + NH].unsqueeze(2).to_broadcast([128, NH, DH]), op=ALU.mult),
                    reads=[sl_.b, ssq_.b], writes=[sqt.b])
                if which == "q":
                    S.op("pool", lambda e: e.tensor_tensor(
                        out=qn.ap.rearrange("p (h d) -> p h d", h=NH), in0=sq3,
                        in1=gq.ap.unsqueeze(1).to_broadcast([128, NH, DH]), op=ALU.mult), reads=[sqt.b, gq.b], writes=[qn.b])
                else:
                    kft = kf.next()
                    S.op("pool", lambda e: e.tensor_tensor(
                        out=kft.ap.rearrange("p (h d) -> p h d", h=NH), in0=sq3,
                        in1=gk.ap.unsqueeze(1).to_broadcast([128, NH, DH]), op=ALU.mult), reads=[sqt.b, gk.b], writes=[kft.b])
                    S.op("pool", lambda e: e.tensor_copy(out=kb.ap, in_=kft.ap), reads=[kft.b], writes=[kb.b])
                    kdst = k_p[j, t * 128:(t + 1) * 128, :] if t < NT else k_s[j, :, :]
                    S.op("sp", lambda e: e.dma_start(out=kdst, in_=kft.ap[0:nv, :]), reads=[kft.b], dma="ldk%d" % (kf.i % 2))

            def do_tile(t, x):
                nv = 128 if t < NT else DS
                ss_ = ss[t % 2]
                stats(x.ap, x.b, D, junk, ss_)
                S.op("dve", lambda e: e.scalar_tensor_tensor(out=h.ap, in0=x.ap, scalar=ss_.ap[:, 2:3], in1=gpre.ap, op0=ALU.mult, op1=ALU.mult),
                     reads=[x.b, ss_.b, gpre.b], writes=[h.b])
                transposes(h, hT, tb, "act")
                slq = slots.next()
                proj_tok(hT, w, 0, slq)
                slk = slots.next()
                proj_tok(hT, w, D, slk)
                slv = slots.next()
                proj_tok(hT, w, 2 * D, slv)

                def f_fz(e):
                    ins = None
                    for kc in range(8):
                        ins = e.matmul(fzb.ap[:, 0:NH], lhsT=hT.ap[:, kc * 128:(kc + 1) * 128], rhs=w.ap[:, kc, 4 * D:4 * D + NH],
                                       start=(kc == 0), stop=(kc == 7))
                    return ins
                S.op("pe", f_fz, reads=[hT.b, w.b], writes=[fzb.b], cost=500)
                headnorm("q", slq, t)
                headnorm("k", slk, t)
                vft = vf.next()
                S.op("act", lambda e: e.activation(out=vft.ap, in_=slv.ap, func=AF.Copy), reads=[slv.b], writes=[vft.b])
                vbt = vb.next()
                S.op("pool", lambda e: e.tensor_copy(out=vbt.ap, in_=vft.ap), reads=[vft.b], writes=[vbt.b])
                vdst = v_p[j, t * 128:(t + 1) * 128, :] if t < NT else v_s[j, :, :]
                S.op("sp", lambda e: e.dma_start(out=vdst, in_=vft.ap[0:nv, :]), reads=[vft.b], dma="ldv%d" % (vf.i % 2))
                v_to_scratch(vbt, kcol(t), vb.i % 2)
                slg = slots.next()

                def f_g(e):
                    ins = None
                    for c in range(8):
                        for kc in range(8):
                            ins = e.matmul(slg.ap[:, c * 128:(c + 1) * 128], lhsT=w.ap[:, kc, 3 * D + c * 128:3 * D + (c + 1) * 128],
                                           rhs=hT.ap[:, kc * 128:(kc + 1) * 128], start=(kc == 0), stop=(kc == 7))
                    return ins
                S.op("pe", f_g, reads=[hT.b, w.b], writes=[slg.b], cost=4500)
                GTt = GTs.next()
                S.op("act", lambda e: e.activation(out=GTt.ap, in_=slg.ap, func=AF.Sigmoid), reads=[slg.b], writes=[GTt.b])
                S.op("sp", lambda e: e.dma_start(out=GT[:, qcol(t):qcol(t) + 128].rearrange("(c p) n -> p c n", p=128),
                                                 in_=GTt.ap.rearrange("p (c n) -> p c n", c=8)),
                     reads=[GTt.b], writes=[dbuf("GT", t)], dma="st_gt%d" % (t % 2))
                f1_ = f1[t % 2]
                slot_ = t if t < NT else NT + NP
                lfv = LF.ap[:, slot_ * NH:(slot_ + 1) * NH]
                S.op("dve", lambda e: e.tensor_tensor(out=f1_.ap[:, 0:NH], in0=fzb.ap[:, 0:NH], in1=bfb.ap, op=ALU.add),
                     reads=[fzb.b, bfb.b], writes=[f1_.b])
                S.op("act", lambda e: e.activation(out=f1_.ap[:, 16:32], in_=f1_.ap[:, 0:NH], func=AF.Exp, scale=-1.0), reads=[f1_.b], writes=[f1_.b])
                S.op("act", lambda e: e.activation(out=f1_.ap[:, 32:48], in_=f1_.ap[:, 16:32], func=AF.Ln, bias=1.0), reads=[f1_.b], writes=[f1_.b])
                S.op("dve", lambda e: e.tensor_scalar(out=lfv, in0=f1_.ap[:, 32:48], scalar1=-1.0, scalar2=None, op0=ALU.mult),
                     reads=[f1_.b], writes=[LF.b])
                ldst = lf_p[j, t * 128:(t + 1) * 128, :] if t < NT else lf_s[j, :, :]
                S.op("sp", lambda e: e.dma_start(out=ldst, in_=lfv[0:nv, :]), reads=[LF.b], dma="st_lf")
                QTt = QTs.next()
                transposes(qn, QTt, tb, "act")
                S.op("sp", lambda e: e.dma_start(out=QT[:, qcol(t):qcol(t) + 128].rearrange("(c p) n -> p c n", p=128),
                                                 in_=QTt.ap.rearrange("p (c n) -> p c n", c=8)),
                     reads=[QTt.b], writes=[dbuf("QT", t)], dma="st_qt%d" % (t % 2))
                k_to_scratch(kb, kcol(t))

            xt_next = load_x(0)
            for t in range(NTS):
                x = xt_next
                if t + 1 < NTS:
                    xt_next = load_x(t + 1)
                do_tile(t, x)
            S.barrier()
            A.off = after_lf
            return LF

        def phase_fox_a2(layer, LF):
            CT = A.alloc(KC, F32, "CT")
            psb = Ring([bank[1], bank[2], bank[3], bank[4]])
            ngrp = (NKS + 3) // 4

            def do_grp(g):
                s0 = g * 4
                ns = min(4, NKS - s0)
                pb = psb.next()

                def f(e):
                    ins = None
                    for i in range(ns):
                        s_ = s0 + i
                        ins = e.matmul(pb.ap[0:NH, i * 128:(i + 1) * 128], lhsT=LF.ap[:, s_ * NH:(s_ + 1) * NH], rhs=trif, start=True, stop=True)
                    return ins
                S.op("pe", f, reads=[LF.b] + CONSTB, writes=[pb.b])
                S.op("act", lambda e: e.activation(out=CT.ap[0:NH, s0 * 128:(s0 + ns) * 128], in_=pb.ap[0:NH, 0:ns * 128], func=AF.Copy),
                     reads=[pb.b], writes=[CT.b])
            for g in range(ngrp):
                do_grp(g)

            def fix(s_):
                S.op("dve", lambda e: e.tensor_scalar(out=CT.ap[0:NH, s_ * 128:(s_ + 1) * 128], in0=CT.ap[0:NH, s_ * 128:(s_ + 1) * 128],
                                                      scalar1=CT.ap[0:NH, s_ * 128 - 1:s_ * 128], scalar2=None, op0=ALU.add),
                     reads=[CT.b], writes=[CT.b])
            for s_ in range(1, NKS):
                if s_ == NT:
                    continue
                fix(s_)
            CW = 1024
            r1 = A.alloc(CW, F32, "r1")
            r2 = A.alloc(CW, F32, "r2")
            o6 = Ring([A.alloc(6 * CW, BF16, "o6_%d" % i) for i in range(2)])

            def do_chunk(c0):
                w_ = min(CW, KC - c0)
                o_ = o6.next()
                ov = o_.ap.rearrange("p (a n) -> p a n", a=6)
                cs = CT.ap[0:NH, c0:c0 + w_]
                S.op("act", lambda e: e.activation(out=ov[0:NH, 0, 0:w_], in_=cs, func=AF.Copy), reads=[CT.b], writes=[o_.b])
                S.op("dve", lambda e: e.tensor_tensor(out=r1.ap[0:NH, 0:w_], in0=cs, in1=ov[0:NH, 0, 0:w_], op=ALU.subtract),
                     reads=[CT.b, o_.b], writes=[r1.b])
                S.op("act", lambda e: e.activation(out=ov[0:NH, 1, 0:w_], in_=r1.ap[0:NH, 0:w_], func=AF.Copy), reads=[r1.b], writes=[o_.b])
                S.op("dve", lambda e: e.tensor_tensor(out=r2.ap[0:NH, 0:w_], in0=r1.ap[0:NH, 0:w_], in1=ov[0:NH, 1, 0:w_], op=ALU.subtract),
                     reads=[r1.b, o_.b], writes=[r2.b])
                S.op("act", lambda e: e.activation(out=ov[0:NH, 2, 0:w_], in_=r2.ap[0:NH, 0:w_], func=AF.Copy), reads=[r2.b], writes=[o_.b])
                S.op("pool", lambda e: e.tensor_scalar(out=ov[0:NH, 3:6, 0:w_], in0=ov[0:NH, 0:3, 0:w_], scalar1=-1.0, scalar2=None, op0=ALU.mult),
                     reads=[o_.b], writes=[o_.b])
                S.op("sp", lambda e: e.dma_start(out=CTs[:, :, c0:c0 + w_], in_=ov[0:NH, :, 0:w_]),
                     reads=[o_.b], writes=[dbuf("CTs", c0)], dma="st_ct%d" % (o6.i % 2))
            for c0 in range(0, KC, CW):
                do_chunk(c0)
            S.barrier()
            A.off = PERS

        def phase_fox_b(layer):
            NQB = T // 512
            sets = []
            for i in range(2):
                d = {}
                d["QA"] = A.alloc(TS, BF16, "QA%d" % i)
                d["KA"] = A.alloc(KC, BF16, "KA%d" % i)
                d["VA"] = A.alloc(NKS * 128, BF16, "VA%d" % i)
                d["G"] = A.alloc(TS, BF16, "G%d" % i)
                d["VAv"] = d["VA"].ap.rearrange("p (s n) -> p s n", n=128)
                S.op("pool", lambda e, d=d: e.memset(d["QA"].ap[64:70, :], 1.0), writes=[d["QA"].b])
                S.op("pool", lambda e, d=d: e.memset(d["KA"].ap[64:70, :], 1.0), writes=[d["KA"].b])
                S.op("pool", lambda e, d=d: e.memset(d["VA"].ap, 1.0), writes=[d["VA"].b])
                sets.append(d)
            PT = Ring([A.alloc(512, BF16, "PT%d" % i) for i in range(6)])
            rc = Ring([A.alloc(512, F32, "rc%d" % i) for i in range(2)])
            tmp = Ring([A.alloc(512, F32, "otmp%d" % i) for i in range(2)])
            og = Ring([A.alloc(512, BF16, "og%d" % i) for i in range(2)])
            psS = Ring([bank[0], bank[1], bank[2], bank[3], bank[6], bank[7]])
            psO = Ring([bank[4], bank[5]])
            allQT = [dbuf("QT", t) for t in range(NTS)]
            allGT = [dbuf("GT", t) for t in range(NTS)]
            allKT = [dbuf("KT", c) for c in [t * 128 for t in range(NT)] + [T + i * 128 for i in range(NP + 1)]]
            allVT = [dbuf("VT", c) for c in [t * 128 for t in range(NT)] + [T + i * 128 for i in range(NP + 1)]]
            allCT = [dbuf("CTs", c0) for c0 in range(0, KC, 1024)]

            def load_head(hd):
                d = sets[hd % 2]
                p = hd % 2
                r0 = hd * DH
                S.op("sp", lambda e: e.dma_start(out=d["QA"].ap[0:64, :], in_=QT[r0:r0 + DH, :]), reads=allQT, writes=[d["QA"].b], dma="la%d" % p)
                S.op("sp", lambda e: e.dma_start(out=d["QA"].ap[64:67, 0:T], in_=CTs[hd, 0:3, 0:T]), reads=allCT, writes=[d["QA"].b], dma="lb%d" % p)
                S.op("sp", lambda e: e.dma_start(out=d["QA"].ap[64:67, T:TS], in_=CTs[hd, 0:3, T + PAST:KC]), reads=allCT, writes=[d["QA"].b], dma="lc%d" % p)
                S.op("sp", lambda e: e.dma_start(out=d["KA"].ap[0:64, :], in_=KT[r0:r0 + DH, :]), reads=allKT, writes=[d["KA"].b], dma="ld%d" % p)
                S.op("sp", lambda e: e.dma_start(out=d["KA"].ap[67:70, :], in_=CTs[hd, 3:6, :]), reads=allCT, writes=[d["KA"].b], dma="le%d" % p)
                vo = 0 if p == 0 else 64
                step = 16
                for s0 in range(0, NKS, step):
                    s1 = min(NKS, s0 + step)
                    S.op("sp", lambda e, s0=s0, s1=s1: e.dma_start(out=d["VAv"][:, s0:s1, vo:vo + DH],
                                                                  in_=VT[s0 * 128:s1 * 128, r0:r0 + DH].rearrange("(s p) d -> p s d", p=128)),
                         reads=allVT, writes=[d["VA"].b], dma="lf%d" % p)
                S.op("sp", lambda e: e.dma_start(out=d["G"].ap[vo:vo + 64, :], in_=GT[r0:r0 + DH, :]), reads=allGT, writes=[d["G"].b], dma="lg%d" % p)

            def finish(hd, po, ncols, qc0):
                d = sets[hd % 2]
                p = hd % 2
                orow = slice(0, 64) if p == 0 else slice(64, 128)
                drow = slice(64, 128) if p == 0 else slice(0, 64)
                rc_ = rc.next()
                tmp_ = tmp.next()
                og_ = og.next()
                S.op("dve", lambda e: e.reciprocal(out=rc_.ap[drow, 0:ncols], in_=po.ap[drow, 0:ncols]), reads=[po.b], writes=[rc_.b])
                S.op("dve", lambda e: e.tensor_tensor(out=tmp_.ap[orow, 0:ncols], in0=po.ap[orow, 0:ncols], in1=rc_.ap[drow, 0:ncols], op=ALU.mult),
                     reads=[po.b, rc_.b], writes=[tmp_.b])
                S.op("pool", lambda e: e.tensor_tensor(out=og_.ap[orow, 0:ncols], in0=tmp_.ap[orow, 0:ncols], in1=d["G"].ap[orow, qc0:qc0 + ncols], op=ALU.mult),
                     reads=[tmp_.b, d["G"].b], writes=[og_.b])
                S.op("sp", lambda e: e.dma_start(out=OGT[hd * DH:(hd + 1) * DH, qc0:qc0 + ncols], in_=og_.ap[orow, 0:ncols]),
                     reads=[og_.b], writes=[dbuf("OGT", qc0 // 512)], dma="st_og%d" % (og.i % 2))

            def do_head(hd):
                d = sets[hd % 2]
                QA, KA, VA, VAv = d["QA"], d["KA"], d["VA"], d["VAv"]

                def s_step(kt, qb):
                    off = max(0, kt - 4 * qb) * 128
                    ps_ = psS.next()
                    q0 = qb * 512

                    def f(e):
                        if kt >= 4 * qb:
                            e.matmul(ps_.ap[:, off:off + 128], lhsT=idb, rhs=masknegb, start=True, stop=False)
                            ins = e.matmul(ps_.ap[:, off:off + 128], lhsT=KA.ap[0:70, kt * 128:(kt + 1) * 128],
                                           rhs=QA.ap[0:70, q0 + off:q0 + off + 128], start=False, stop=True)
                            if off + 128 < 512:
                                ins = e.matmul(ps_.ap[:, off + 128:512], lhsT=KA.ap[0:70, kt * 128:(kt + 1) * 128],
                                               rhs=QA.ap[0:70, q0 + off + 128:q0 + 512], start=True, stop=True)
                            return ins
                        return e.matmul(ps_.ap[:, 0:512], lhsT=KA.ap[0:70, kt * 128:(kt + 1) * 128], rhs=QA.ap[0:70, q0:q0 + 512],
                                        start=True, stop=True)
                    S.op("pe", f, reads=[KA.b, QA.b] + CONSTB, writes=[ps_.b], cost=280)
                    pt_ = PT.next()
                    S.op("act", lambda e: e.activation(out=pt_.ap[:, off:512], in_=ps_.ap[:, off:512], func=AF.Exp), reads=[ps_.b], writes=[pt_.b], cost=500)
                    return (kt, off, pt_)

                def pv_step(item, po, nkt):
                    kt, off, pt_ = item
                    S.op("pe", lambda e: e.matmul(po.ap[:, off:512], lhsT=VAv[:, kt, :], rhs=pt_.ap[:, off:512], start=(kt == 0), stop=(kt == nkt - 1),
                                                  skip_group_check=True),
                         reads=[VA.b, pt_.b], writes=[po.b], cost=280)

                for qb in range(NQB):
                    po = psO.next()
                    nkt = 4 * qb + 4
                    pend = []
                    for kt in range(nkt):
                        pend.append(s_step(kt, qb))
                        if len(pend) > 3:
                            pv_step(pend.pop(0), po, nkt)
                    while pend:
                        pv_step(pend.pop(0), po, nkt)
                    finish(hd, po, 512, qb * 512)

                po = psO.next()

                def samp_step(kt):
                    ps_ = psS.next()
                    kc0 = T + kt * 128
                    pt_ = PT.next()
                    if kt < NP:
                        S.op("pe", lambda e: e.matmul(ps_.ap[:, 0:128], lhsT=KA.ap[0:70, kc0:kc0 + 128], rhs=QA.ap[0:70, T:TS], start=True, stop=True),
                             reads=[KA.b, QA.b], writes=[ps_.b])
                        S.op("act", lambda e: e.activation(out=pt_.ap[:, 0:128], in_=ps_.ap[:, 0:128], func=AF.Exp), reads=[ps_.b], writes=[pt_.b])
                        S.op("pe", lambda e: e.matmul(po.ap[:, 0:128], lhsT=VAv[:, NT + kt, :], rhs=pt_.ap[:, 0:128], start=(kt == 0), stop=False,
                                                      skip_group_check=True),
                             reads=[VA.b, pt_.b], writes=[po.b])
                    else:
                        def f(e):
                            e.matmul(ps_.ap[0:DS, 0:128], lhsT=idb[0:DS, 0:DS], rhs=masknegb[0:DS, :], start=True, stop=False)
                            return e.matmul(ps_.ap[0:DS, 0:128], lhsT=KA.ap[0:70, kc0:kc0 + DS], rhs=QA.ap[0:70, T:TS], start=False, stop=True)
                        S.op("pe", f, reads=[KA.b, QA.b] + CONSTB, writes=[ps_.b])
                        S.op("act", lambda e: e.activation(out=pt_.ap[0:DS, 0:128], in_=ps_.ap[0:DS, 0:128], func=AF.Exp), reads=[ps_.b], writes=[pt_.b])
                        S.op("pe", lambda e: e.matmul(po.ap[:, 0:128], lhsT=VAv[0:DS, NT + kt, :], rhs=pt_.ap[0:DS, 0:128], start=(NP == 0), stop=True,
                                                      skip_group_check=True),
                             reads=[VA.b, pt_.b], writes=[po.b])
                for kt in range(NP + 1):
                    samp_step(kt)
                finish(hd, po, 128, T)

            load_head(0)
            for hd in range(NH):
                if hd + 1 < NH:
                    load_head(hd + 1)
                do_head(hd)
            S.barrier()
            A.off = PERS

        def phase_fox_c1(layer):
            j = layer // 2
            wo = A.alloc(8 * D, BF16, "wo")
            wo.ap = wo.ap.rearrange("p (c n) -> p c n", c=8)
            S.op("sp", lambda e: e.dma_start(out=wo.ap, in_=wb_fout[j].rearrange("(c p) n -> p c n", p=128)), reads=[dbuf("wb_fout", j)], writes=[wo.b], dma="wld0")
            gpost = A.alloc(D, F32, "gpost")
            load_bcast(gpost, post_mix[layer:layer + 1, :], "g0")
            gpre2 = A.alloc(D, F32, "gpre2")
            load_bcast(gpre2, pre_ffn[layer:layer + 1, :], "g1")
            tt = make_tail_tiles()
            xr = Ring([A.alloc(D, F32, "x%d" % i) for i in range(2)])
            ogr = Ring([A.alloc(D, BF16, "ogb%d" % i) for i in range(2)])
            slots = Ring([slot2(1), slot2(3), slot2(5)])
            allOG = [dbuf("OGT", q) for q in range(T // 512 + 1)]

            def load(t):
                xt = xr.next()
                src, rd = xsrc(layer, t)
                S.op("sp", lambda e: e.dma_start(out=xt.ap, in_=src), reads=rd, writes=[xt.b], dma="ldx%d" % (t % 2))
                ogt = ogr.next()
                S.op("sp", lambda e: e.dma_start(out=ogt.ap.rearrange("p (c n) -> p c n", c=8),
                                                 in_=OGT[:, qcol(t):qcol(t) + 128].rearrange("(c p) n -> p c n", p=128)),
                     reads=[dbuf("OGT", qcol(t) // 512)], writes=[ogt.b], dma="ldo%d" % (t % 2))
                return xt, ogt

            def do_tile(t, x, ogt):
                sl = slots.next()
                proj_tok(ogt, wo, 0, sl)
                tail(layer, t, sl, x, tt, gpost, gpre2, bank[0])

            nxt = load(0)
            for t in range(NTS):
                x, ogt = nxt
                if t + 1 < NTS:
                    nxt = load(t + 1)
                do_tile(t, x, ogt)
            S.barrier()
            A.off = PERS

        def phase_ffn(layer):
            last = (layer == NL - 1)
            wd = A.alloc(32 * D, BF16, "wd")
            wd.ap = wd.ap.rearrange("p (c n) -> p c n", c=32)
            for q in range(4):
                S.op("sp", lambda e, q=q: e.dma_start(out=wd.ap[:, q * 8:(q + 1) * 8, :],
                                                      in_=wb_down[layer, q * 1024:(q + 1) * 1024, :].rearrange("(c p) n -> p c n", p=128)),
                     reads=[dbuf("wb_down", layer)], writes=[wd.b], dma="wld%d" % (q % 2))
            gpost = A.alloc(D, F32, "gpostf")
            load_bcast(gpost, post_ffn[layer:layer + 1, :], "g0")
            NWR = 6
            wur = Ring([A.alloc(D, BF16, "wu%d" % i) for i in range(NWR)])
            hb = Ring([A.alloc(8 * 512, BF16, "hb%d" % i) for i in range(2)])
            xb = Ring([A.alloc(4 * D, F32, "xb%d" % i) for i in range(2)])
            u2 = A.alloc(32 * 512, BF16, "u2")
            u2v = u2.ap.rearrange("p (j n) -> p j n", j=32)
            rr = Ring([A.alloc(512, F32, "rr%d" % i) for i in range(3)])
            junk = A.alloc(D, BF16, "fjunk")
            ssr = Ring([A.alloc(8, F32, "fss%d" % i) for i in range(4)])
            tmp = A.alloc(D, F32, "ftmp")
            xo = Ring([A.alloc(D, F32, "xo%d" % i) for i in range(2)])
            psU = Ring([bank[0], bank[1], bank[2]])
            psD = Ring([slot2(3), slot2(5)])
            blocks = []
            t0 = 0
            while t0 < NTS:
                nt_ = min(4, NTS - t0)
                if t0 < NT and t0 + nt_ > NT:
                    nt_ = NT - t0
                blocks.append((t0, nt_))
                t0 += nt_
            wcount = [0]

            def load_w(jj):
                wt = wur.next()
                S.op("sp", lambda e: e.dma_start(out=wt.ap, in_=wb_up[layer, jj].rearrange("p c n -> p (c n)")),
                     reads=[dbuf("wb_up", layer)], writes=[wt.b], dma="lwu%d" % (wur.i % NWR))
                return wt

            def load_blk(bi):
                t0, nt_ = blocks[bi]
                ntok = nt_ * 128
                h_ = hb.next()
                S.op("sp", lambda e: e.dma_start(out=h_.ap.rearrange("p (c n) -> p c n", c=8)[:, :, 0:ntok],
                                                 in_=H2T[:, qcol(t0):qcol(t0) + ntok].rearrange("(c p) n -> p c n", p=128)),
                     reads=[dbuf("H2T", t) for t in range(t0, t0 + nt_)], writes=[h_.b], dma="lhb%d" % (bi % 2))
                x_ = xb.next()
                S.op("sp", lambda e: e.dma_start(out=x_.ap.rearrange("p (t d) -> p t d", t=4)[:, 0:nt_, :],
                                                 in_=xres[t0:t0 + nt_].rearrange("t p d -> p t d")),
                     reads=[dbuf("xres", t) for t in range(t0, t0 + nt_)], writes=[x_.b], dma="lxb%d" % (bi % 2))
                return h_, x_

            PRE = 4
            wq = []
            total_w = len(blocks) * 32
            for i in range(min(PRE, total_w)):
                wq.append(load_w(i % 32))
            wissued = [len(wq)]

            def do_up(jj, h_, ntok):
                hv = h_.ap.rearrange("p (c n) -> p c n", c=8)
                wt = wq.pop(0)
                if wissued[0] < total_w:
                    wq.append(load_w(wissued[0] % 32))
                    wissued[0] += 1
                wv = wt.ap.rearrange("p (c n) -> p c n", c=8)
                pu = psU.next()

                def f(e):
                    ins = None
                    for kc in range(8):
                        ins = e.matmul(pu.ap[:, 0:ntok], lhsT=wv[:, kc, :], rhs=hv[:, kc, 0:ntok], start=(kc == 0), stop=(kc == 7))
                    return ins
                S.op("pe", f, reads=[wt.b, h_.b], writes=[pu.b], cost=2150)
                r_ = rr.next()
                S.op("act", lambda e: e.activation(out=r_.ap[:, 0:ntok], in_=pu.ap[:, 0:ntok], func=AF.Relu), reads=[pu.b], writes=[r_.b], cost=600)
                S.op("pool", lambda e: e.tensor_tensor(out=u2v[:, jj, 0:ntok], in0=r_.ap[:, 0:ntok], in1=r_.ap[:, 0:ntok], op=ALU.mult),
                     reads=[r_.b], writes=[u2.b], cost=1200)

            def do_down(t, ti, x_):
                pd = psD.next()

                def f(e):
                    ins = None
                    for half in range(2):
                        for jj in range(32):
                            ins = e.matmul(pd.ap[:, half * 512:(half + 1) * 512], lhsT=u2v[:, jj, ti * 128:(ti + 1) * 128],
                                           rhs=wd.ap[:, jj, half * 512:(half + 1) * 512], start=(jj == 0), stop=(jj == 31))
                    return ins
                S.op("pe", f, reads=[u2.b, wd.b], writes=[pd.b], cost=17000)
                ss_ = ssr.next()
                stats(pd.ap, pd.b, D, junk, ss_)
                S.op("dve", lambda e: e.scalar_tensor_tensor(out=tmp.ap, in0=pd.ap, scalar=ss_.ap[:, 2:3], in1=gpost.ap, op0=ALU.mult, op1=ALU.mult),
                     reads=[pd.b, ss_.b, gpost.b], writes=[tmp.b])
                xo_ = xo.next()
                xin = x_.ap[:, ti * D:(ti + 1) * D]
                S.op("pool", lambda e: e.tensor_tensor(out=xo_.ap, in0=xin, in1=tmp.ap, op=ALU.add), reads=[x_.b, tmp.b], writes=[xo_.b])
                if last:
                    if t < NT:
                        S.op("sp", lambda e: e.dma_start(out=y_p[t * 128:(t + 1) * 128, :], in_=xo_.ap), reads=[xo_.b], dma="st_y%d" % (t % 2))
                    else:
                        S.op("sp", lambda e: e.dma_start(out=y_s[:, :], in_=xo_.ap[0:DS, :]), reads=[xo_.b], dma="st_y%d" % (t % 2))
                else:
                    nrow = 128 if t < NT else DS
                    S.op("sp", lambda e: e.dma_start(out=xres[t, 0:nrow, :], in_=xo_.ap[0:nrow, :]), reads=[xo_.b], writes=[dbuf("xres", t)], dma="st_y%d" % (t % 2))

            nxt = load_blk(0)
            for bi, (t0, nt_) in enumerate(blocks):
                h_, x_ = nxt
                if bi + 1 < len(blocks):
                    nxt = load_blk(bi + 1)
                for jj in range(32):
                    do_up(jj, h_, nt_ * 128)
                for ti in range(nt_):
                    do_down(t0 + ti, ti, x_)
            S.barrier()
            A.off = PERS

        def phase_hgrn(layer):
            j = layer // 2
            w = A.alloc(8 * 4 * D, BF16, "hw")
            w.ap = w.ap.rearrange("p (c n) -> p c n", c=8)

            def ld_w(c):
                S.op("sp", lambda e: e.dma_start(out=w.ap[:, c, :], in_=wb_hin[j, c * 128:(c + 1) * 128, :]),
                     reads=[dbuf("wb_hin", j)], writes=[w.b], dma="wld%d" % (c % 2))
            for c in range(8):
                ld_w(c)
            wo = A.alloc(8 * D, BF16, "hwo")
            wo.ap = wo.ap.rearrange("p (c n) -> p c n", c=8)
            S.op("sp", lambda e: e.dma_start(out=wo.ap, in_=wb_hout[j].rearrange("(c p) n -> p c n", p=128)), reads=[dbuf("wb_hout", j)], writes=[wo.b], dma="wld0")
            gpre = A.alloc(D, F32, "gpre")
            load_bcast(gpre, pre_mix[layer:layer + 1, :], "g0")
            gout = A.alloc(D, F32, "gout")
            load_bcast(gout, hgrn_out_norm[j:j + 1, :], "g1")
            gpost = A.alloc(D, F32, "gpost")
            load_bcast(gpost, post_mix[layer:layer + 1, :], "g2")
            gpre2 = A.alloc(D, F32, "gpre2")
            load_bcast(gpre2, pre_ffn[layer:layer + 1, :], "g3")
            om = oml[layer]
            tt = make_tail_tiles()
            xr = Ring([A.alloc(D, F32, "x%d" % i) for i in range(4)])
            otmp = A.alloc(D, F32, "otmp")
            junk = tt["junk"]
            ss = [A.alloc(8, F32, "ss%d" % i) for i in range(4)]
            h = A.alloc(D, BF16, "h")
            hT = A.alloc(D, BF16, "hT")
            qf = A.alloc(D, F32, "qf")
            kk = A.alloc(D, F32, "kk")
            lg = A.alloc(D, F32, "lg")
            dcl = A.alloc(D, F32, "dcl")
            E2 = A.alloc(D, F32, "E2")
            qt = A.alloc(D, BF16, "qt")
            vbr = [A.alloc(D, BF16, "vb%d" % i) for i in range(2)]
            gtr = [A.alloc(D, F32, "gt%d" % i) for i in range(2)]
            ktr = [A.alloc(D, BF16, "kt%d" % i) for i in range(2)]
            qTr = [A.alloc(D, BF16, "qT%d" % i) for i in range(2)]
            kTr = [A.alloc(D, BF16, "kT%d" % i) for i in range(2)]
            decr = [A.alloc(32, F32, "dec%d" % i) for i in range(2)]
            ATs = Ring([A.alloc(128, BF16, "ATs%d" % i) for i in range(2)])
            AT32 = Ring([A.alloc(128, F32, "AT32_%d" % i) for i in range(2)])
            Sst = A.alloc(HH * 128, F32, "Sst")
            Sb = Ring([A.alloc(128, BF16, "Sb%d" % i) for i in range(2)])
            T1 = Ring([A.alloc(128, F32, "T1_%d" % i) for i in range(2)])
            on2 = A.alloc(D, BF16, "on2")
            onT = A.alloc(D, BF16, "onT")
            Sh = [Tl(Sst.ap[:, hd * 128:(hd + 1) * 128], "S%d" % hd) for hd in range(HH)]
            slotsA = Ring([slot2(1)])
            slotB = slot2(3)
            tb = bank[0]
            decp = Tl(psum[:, 5, 0:32], "decp")
            psA = Ring([Tl(psum[:, 6, 0:128], "psA0")])
            psK = Tl(psum[:, 7, 0:128], "psK")

            def load_x(t):
                xt = xr.next()
                src, rd = xsrc(layer, t)
                S.op("sp", lambda e: e.dma_start(out=xt.ap, in_=src), reads=rd, writes=[xt.b], dma="ldx%d" % (t % 4))
                return xt

            def zero_state(hd):
                S.op("pool", lambda e: e.memset(Sh[hd].ap, 0.0), writes=[Sh[hd].b])
            for hd in range(HH):
                zero_state(hd)

            def do_head(hd, nv, dec_, so, qT, kT, kt_, vb):
                Sp = Sb.next()
                S_ = Sh[hd]
                S.op("dve", lambda e: e.tensor_scalar(out=Sp.ap, in0=S_.ap, scalar1=dec_.ap[:, hd * 4:hd * 4 + 1], scalar2=None, op0=ALU.mult),
                     reads=[S_.b, dec_.b], writes=[Sp.b], cost=220)
                pa = psA.next()
                S.op("pe", lambda e: e.matmul(pa.ap[0:nv, 0:nv], lhsT=kT.ap[:, hd * 128:hd * 128 + nv], rhs=qT.ap[:, hd * 128:hd * 128 + nv], start=True, stop=True),
                     reads=[kT.b, qT.b], writes=[pa.b], cost=120)
                at = ATs.next()
                at32 = AT32.next()
                S.op("dve", lambda e: e.tensor_tensor(out=at32.ap[0:nv, 0:nv], in0=pa.ap[0:nv, 0:nv], in1=himb[0:nv, 0:nv], op=ALU.min),
                     reads=[pa.b] + CONSTB, writes=[at32.b], cost=320)
                S.op("dve", lambda e: e.tensor_tensor(out=at.ap[0:nv, 0:nv], in0=at32.ap[0:nv, 0:nv], in1=lomb[0:nv, 0:nv], op=ALU.max),
                     reads=[at32.b] + CONSTB, writes=[at.b], cost=320)

                def f_o(e):
                    e.matmul(so.ap[0:nv, hd * 128:(hd + 1) * 128], lhsT=qT.ap[:, hd * 128:hd * 128 + nv], rhs=Sp.ap, start=True, stop=False)
                    return e.matmul(so.ap[0:nv, hd * 128:(hd + 1) * 128], lhsT=at.ap[0:nv, 0:nv], rhs=vb.ap[0:nv, hd * 128:(hd + 1) * 128], start=False, stop=True)
                S.op("pe", f_o, reads=[qT.b, Sp.b, at.b, vb.b], writes=[so.b], cost=250)
                S.op("pe", lambda e: e.matmul(psK.ap, lhsT=kt_.ap[0:nv, hd * 128:(hd + 1) * 128], rhs=vb.ap[0:nv, hd * 128:(hd + 1) * 128], start=True, stop=True),
                     reads=[kt_.b, vb.b], writes=[psK.b], cost=120)
                t1 = T1.next()
                S.op("dve", lambda e: e.tensor_scalar(out=t1.ap, in0=S_.ap, scalar1=dec_.ap[:, hd * 4 + 1:hd * 4 + 2], scalar2=None, op0=ALU.mult),
                     reads=[S_.b, dec_.b], writes=[t1.b], cost=220)
                S.op("dve", lambda e: e.scalar_tensor_tensor(out=S_.ap, in0=psK.ap, scalar=dec_.ap[:, hd * 4 + 2:hd * 4 + 3], in1=t1.ap,
                                                             op0=ALU.mult, op1=ALU.add),
                     reads=[psK.b, dec_.b, t1.b], writes=[S_.b], cost=380)

            def stage_a(t, x):
                samp = (t == NT)
                nv = DS if samp else 128
                d1 = d1s if samp else d1p
                sel = sels if samp else selp
                vb, gt, kt_, qT, kT, dec_ = vbr[t % 2], gtr[t % 2], ktr[t % 2], qTr[t % 2], kTr[t % 2], decr[t % 2]
                ss_ = ss[t % 4]
                stats(x.ap, x.b, D, junk, ss_)
                S.op("dve", lambda e: e.scalar_tensor_tensor(out=h.ap, in0=x.ap, scalar=ss_.ap[:, 2:3], in1=gpre.ap, op0=ALU.mult, op1=ALU.mult),
                     reads=[x.b, ss_.b, gpre.b], writes=[h.b])
                transposes(h, hT, tb, "act")
                sl1 = slotsA.next()
                proj_tok(hT, w, 0, sl1)
                S.op("act", lambda e: e.activation(out=qf.ap, in_=sl1.ap, func=AF.Silu), reads=[sl1.b], writes=[qf.b])
                yield
                sl2 = slotsA.next()
                proj_tok(hT, w, D, sl2)
                S.op("act", lambda e: e.activation(out=kk.ap, in_=sl2.ap, func=AF.Sigmoid, scale=-1.0), reads=[sl2.b], writes=[kk.b])
                S.op("dve", lambda e: e.tensor_tensor(out=kk.ap, in0=kk.ap, in1=om.ap, op=ALU.mult), reads=[kk.b, om.b], writes=[kk.b])
                S.op("act", lambda e: e.activation(out=lg.ap, in_=kk.ap, func=AF.Ln, scale=-1.0, bias=1.0), reads=[kk.b], writes=[lg.b])
                yield
                sl3 = slotsA.next()
                proj_tok(hT, w, 2 * D, sl3)
                S.op("act", lambda e: e.activation(out=vb.ap, in_=sl3.ap, func=AF.Copy), reads=[sl3.b], writes=[vb.b])
                yield
                sl4 = slotsA.next()
                proj_tok(hT, w, 3 * D, sl4)
                S.op("act", lambda e: e.activation(out=gt.ap, in_=sl4.ap, func=AF.Silu), reads=[sl4.b], writes=[gt.b])
                yield
                sl5 = slotsA.next()

                def f_d(e):
                    e.matmul(sl5.ap[:, 0:512], lhsT=d1[0:nv, :], rhs=lg.ap[0:nv, 0:512], start=True, stop=True)
                    return e.matmul(sl5.ap[:, 512:1024], lhsT=d1[0:nv, :], rhs=lg.ap[0:nv, 512:1024], start=True, stop=True)
                S.op("pe", f_d, reads=[lg.b] + CONSTB, writes=[sl5.b], cost=2200)

                def f_dec(e):
                    ins = None
                    for hd in range(HH):
                        ins = e.matmul(decp.ap[:, hd * 4:hd * 4 + 3], lhsT=lg.ap[0:nv, hd * 128:(hd + 1) * 128], rhs=sel[0:nv, :], start=True, stop=True)
                    return ins
                S.op("pe", f_dec, reads=[lg.b] + CONSTB, writes=[decp.b], cost=900)
                S.op("dve", lambda e: e.tensor_scalar(out=dcl.ap, in0=sl5.ap, scalar1=-80.0, scalar2=80.0, op0=ALU.max, op1=ALU.min), reads=[sl5.b], writes=[dcl.b])
                S.op("act", lambda e: e.activation(out=E2.ap, in_=dcl.ap, func=AF.Exp, scale=-1.0), reads=[dcl.b], writes=[E2.b])
                S.op("act", lambda e: e.activation(out=dcl.ap, in_=dcl.ap, func=AF.Exp), reads=[dcl.b, E2.b], writes=[dcl.b])
                S.op("act", lambda e: e.activation(out=dec_.ap, in_=decp.ap, func=AF.Exp), reads=[decp.b], writes=[dec_.b], cost=1400)
                S.op("dve", lambda e: e.tensor_tensor(out=qt.ap, in0=qf.ap, in1=dcl.ap, op=ALU.mult), reads=[qf.b, dcl.b], writes=[qt.b])
                S.op("pool", lambda e: e.tensor_tensor(out=kt_.ap, in0=kk.ap, in1=E2.ap, op=ALU.mult), reads=[kk.b, E2.b], writes=[kt_.b])
                yield
                transposes(qt, qT, tb, "dve")
                transposes(kt_, kT, tb, "act")
                yield

            def stage_b1(t):
                samp = (t == NT)
                nv = DS if samp else 128
                vb, gt, kt_, qT, kT, dec_ = vbr[t % 2], gtr[t % 2], ktr[t % 2], qTr[t % 2], kTr[t % 2], decr[t % 2]
                if samp:
                    S.op("sp", lambda e: e.dma_start(out=s_p[j].rearrange("h k v -> k h v"), in_=Sst.ap.rearrange("p (h v) -> p h v", h=HH)),
                         reads=[s_.b for s_ in Sh], dma="st_s")
                    S.op("sp", lambda e: e.dma_start(out=Sst.ap.rearrange("p (h v) -> p h v", h=HH), in_=st[j].rearrange("h k v -> k h v")),
                         reads=[s_.b for s_ in Sh], writes=[s_.b for s_ in Sh], dma="ld_s")
                so = slotB
                for hd in range(HH):
                    do_head(hd, nv, dec_, so, qT, kT, kt_, vb)
                    if hd % 2 == 1:
                        yield
                ss2 = ss[(t + 2) % 4]
                stats(so.ap, so.b, D, junk, ss2)
                S.op("dve", lambda e: e.scalar_tensor_tensor(out=otmp.ap, in0=so.ap, scalar=ss2.ap[:, 2:3], in1=gout.ap, op0=ALU.mult, op1=ALU.mult),
                     reads=[so.b, ss2.b, gout.b], writes=[otmp.b])
                yield

            def stage_b2(t, x):
                gt = gtr[t % 2]
                S.op("pool", lambda e: e.tensor_tensor(out=on2.ap, in0=otmp.ap, in1=gt.ap, op=ALU.mult), reads=[otmp.b, gt.b], writes=[on2.b])
                transposes(on2, onT, tb, "act")
                yield
                sl6 = slotsA.next()
                proj_tok(onT, wo, 0, sl6)
                tail(layer, t, sl6, x, tt, gpost, gpre2, tb)
                yield

            def drain(g):
                if g is None:
                    return None
                try:
                    next(g)
                    return g
                except StopIteration:
                    return None

            def pipeline(tiles):
                n = len(tiles)
                xs = {0: load_x(tiles[0])}
                for step in range(n + 2):
                    if step + 1 < n:
                        xs[step + 1] = load_x(tiles[step + 1])
                    gens = []
                    if step < n:
                        gens.append(stage_a(tiles[step], xs[step]))
                    if 0 <= step - 1 < n:
                        gens.append(stage_b1(tiles[step - 1]))
                    if 0 <= step - 2 < n:
                        gens.append(stage_b2(tiles[step - 2], xs[step - 2]))
                    while gens:
                        gens = [g for g in (drain(g) for g in gens) if g is not None]

            pipeline(list(range(NTS)))
            S.op("sp", lambda e: e.dma_start(out=s_s[j].rearrange("h k v -> k h v"), in_=Sst.ap.rearrange("p (h v) -> p h v", h=HH)),
                 reads=[s_.b for s_ in Sh], dma="st_s")
            S.barrier()
            A.off = PERS

        phase_wcast()
        for layer in range(NL):
            if layer % 2 == 0:
                LF = phase_fox_a(layer)
                phase_fox_a2(layer, LF)
                phase_fox_b(layer)
                phase_fox_c1(layer)
            else:
                phase_hgrn(layer)
            phase_ffn(layer)
        S.emit()
    return nc


_CACHE = {}


def get_nc(T, PAST, NL=4):
    key = (T, PAST, NL)
    if key not in _CACHE:
        _CACHE[key] = build(T, PAST, NL)
    return _CACHE[key]


def make_in_map(c, inp, T, PAST):
    f = lambda a: np.ascontiguousarray(np.asarray(a, dtype=np.float32))
    m = {
        "x_p": f(inp["x_prompt"][c]),
        "x_s": f(inp["x_sample"][c]),
        "ck": f(np.asarray(inp["cache_k"])[:, c].reshape(-1, PAST, D)),
        "cv": f(np.asarray(inp["cache_v"])[:, c].reshape(-1, PAST, D)),
        "clf": f(np.asarray(inp["cache_logf"])[:, c]),
        "st": f(np.asarray(inp["state_s"])[:, c]),
        "fox_w_in": f(inp["fox_w_in"]), "fox_b_f": f(inp["fox_b_f"]), "fox_q_norm": f(inp["fox_q_norm"]),
        "fox_k_norm": f(inp["fox_k_norm"]), "fox_w_out": f(inp["fox_w_out"]), "hgrn_w_in": f(inp["hgrn_w_in"]),
        "hgrn_lb": f(inp["hgrn_lb_logits"]), "hgrn_out_norm": f(inp["hgrn_out_norm"]), "hgrn_w_out": f(inp["hgrn_w_out"]),
        "pre_mix": f(inp["pre_mix_norm"]), "post_mix": f(inp["post_mix_norm"]), "pre_ffn": f(inp["pre_ffn_norm"]),
        "post_ffn": f(inp["post_ffn_norm"]), "ffn_up": f(inp["ffn_w_up"]), "ffn_down": f(inp["ffn_w_down"]),
        "consts": make_consts(),
    }
    return m


def assemble(results, B, T):
    def st(name, shape_tail=None):
        return np.stack([np.asarray(r[name], dtype=np.float32) for r in results], axis=0)
    y_p = st("y_p")
    y_s = st("y_s")
    k_p = np.moveaxis(st("k_p"), 0, 1).reshape(-1, B, T, NH, DH)
    v_p = np.moveaxis(st("v_p"), 0, 1).reshape(-1, B, T, NH, DH)
    lf_p = np.moveaxis(st("lf_p"), 0, 1)
    s_p = np.moveaxis(st("s_p"), 0, 1)
    k_s = np.moveaxis(st("k_s"), 0, 1).reshape(-1, B, DS, NH, DH)
    v_s = np.moveaxis(st("v_s"), 0, 1).reshape(-1, B, DS, NH, DH)
    lf_s = np.moveaxis(st("lf_s"), 0, 1)
    s_s = np.moveaxis(st("s_s"), 0, 1)
    return (y_p, y_s, k_p, v_p, lf_p, s_p, k_s, v_s, lf_s, s_s)


def kernel(**inputs):
    xp = np.asarray(inputs["x_prompt"])
    B, T = xp.shape[0], xp.shape[1]
    PAST = np.asarray(inputs["cache_k"]).shape[2]
    nc = get_nc(T, PAST)
    in_maps = [make_in_map(c, inputs, T, PAST) for c in range(B)]
    res = run_bass_kernel_spmd(nc, in_maps, core_ids=list(range(B)))
    return assemble(res.results, B, T)
```

```python
import contextlib
import numpy as np
import concourse.bass as bass
import concourse.mybir as mybir
from concourse.bass_utils import run_bass_kernel_spmd

F32 = mybir.dt.float32
BF16 = mybir.dt.bfloat16
AF = mybir.ActivationFunctionType
ALU = mybir.AluOpType
AX = mybir.AxisListType

D = 1024
NH = 16
DH = 64
HH = 8
DFF = 4096
EPS = 1e-6
DS = 32
FW = 4 * D + NH


class Buf:
    __slots__ = ("name", "w", "r")

    def __init__(self, name=""):
        self.name = name
        self.w = None
        self.r = []


class Op:
    __slots__ = ("eng", "fn", "deps", "odeps", "sig", "sigval", "dma", "cost", "seq", "bar")

    def __init__(self, eng, fn, dma):
        self.eng = eng
        self.fn = fn
        self.deps = []
        self.odeps = []
        self.sig = False
        self.sigval = None
        self.dma = dma
        self.cost = 1000
        self.seq = 0
        self.bar = 0


DEFCOST = {"pe": 2000, "act": 1100, "dve": 800, "pool": 2300, "sp": 150}
DMA_LAT = 3000
RESCHEDULE = True
RESCHED_SEGS = None
RESCHED_ENGS = None


class Sched:
    CE = ("pe", "act", "dve", "pool")
    ENGS = ("pe", "act", "dve", "pool", "sp")

    def __init__(self, nc):
        self.nc = nc
        self.ops = {e: [] for e in self.ENGS}
        self.streams = {}
        self.last_dma = {}
        self.nseq = 0

    def op(self, eng, fn, reads=(), writes=(), dma=None, extra=(), cost=None):
        o = Op(eng, fn, dma)
        o.cost = cost if cost is not None else DEFCOST[eng]
        self.nseq += 1
        o.seq = self.nseq
        deps = []
        seen = set()

        def add(d):
            if d is None or d is o or id(d) in seen:
                return
            seen.add(id(d))
            deps.append(d)

        for b in reads:
            add(b.w)
        for b in writes:
            add(b.w)
            for r in b.r:
                add(r)
        for d in extra:
            add(d)
        if dma is not None:
            add(self.last_dma.get(dma))
            self.last_dma[dma] = o
            n = self.streams.setdefault(dma, [0])
            n[0] += 1
            o.sigval = 16 * n[0]
        for d in deps:
            if d.dma is None and d.eng == eng and eng == "pe":
                o.odeps.append(d)
                continue
            o.deps.append(d)
            if d.dma is None:
                d.sig = True
        for b in writes:
            b.w = o
            b.r = []
        for b in reads:
            if b.w is not o:
                b.r.append(o)
        self.ops[eng].append(o)
        return o

    def barrier(self):
        firsts = []
        alld = list(self.last_dma.values())
        for e in self.CE:
            o = Op(e, lambda eh: eh.nop(), None)
            o.bar = 1
            for d in alld:
                o.deps.append(d)
            self.ops[e].append(o)
            firsts.append(o)
        for e in self.ENGS:
            o = Op(e, lambda eh: eh.nop(), None)
            o.bar = 2
            for d in firsts:
                if d.eng == e:
                    continue
                o.deps.append(d)
                d.sig = True
            self.ops[e].append(o)

    def _reschedule(self):
        import heapq
        segs = {e: [] for e in self.ENGS}
        nseg = 0
        for e in self.ENGS:
            cur = []
            bars = []
            for o in self.ops[e]:
                if o.bar:
                    bars.append(o)
                    if o.bar == 2:
                        segs[e].append((cur, bars))
                        cur, bars = [], []
                else:
                    assert not bars
                    cur.append(o)
            segs[e].append((cur, bars))
            nseg = max(nseg, len(segs[e]))
        for e in self.ENGS:
            while len(segs[e]) < nseg:
                segs[e].append(([], []))
        for k in range(nseg):
            allops = []
            for e in self.ENGS:
                allops.extend(segs[e][k][0])
            if RESCHEDULE and allops and (RESCHED_SEGS is None or k in RESCHED_SEGS):
                inseg = {id(o) for o in allops}
                nd = {}
                users = {}
                for o in allops:
                    c = 0
                    for d in o.deps + o.odeps:
                        if id(d) in inseg:
                            c += 1
                            users.setdefault(id(d), []).append(o)
                    nd[id(o)] = c
                ready = {e: [] for e in self.ENGS}
                for o in allops:
                    if nd[id(o)] == 0:
                        heapq.heappush(ready[o.eng], (o.seq, id(o), o))
                free_at = {e: 0 for e in self.ENGS}
                start = {}
                events = [(0, 0)]
                evn = 1
                pending = []
                ndone = 0
                now = 0
                while ndone < len(allops):
                    while pending and pending[0][0] <= now:
                        _, _, o = heapq.heappop(pending)
                        ndone += 1
                        for u in users.get(id(o), ()):
                            nd[id(u)] -= 1
                            if nd[id(u)] == 0:
                                heapq.heappush(ready[u.eng], (u.seq, id(u), u))
                    if ndone >= len(allops):
                        break
                    progressed = False
                    for e in self.ENGS:
                        if free_at[e] <= now and ready[e]:
                            _, _, o = heapq.heappop(ready[e])
                            start[id(o)] = now
                            free_at[e] = now + o.cost
                            done = now + (o.cost + DMA_LAT if o.dma is not None else o.cost)
                            evn += 1
                            heapq.heappush(pending, (done, evn, o))
                            progressed = True
                    if progressed:
                        continue
                    nxt = []
                    if pending:
                        nxt.append(pending[0][0])
                    for e in self.ENGS:
                        if ready[e] and free_at[e] > now:
                            nxt.append(free_at[e])
                    if not nxt:
                        left = [o for o in allops if id(o) not in start]
                        print("STUCK0 seg", k, "nops", len(allops), "ndone", ndone, "started", len(start), "uniq", len({id(o) for o in allops}))
                        o = left[0]
                        print("STUCK seg", k, "nops", len(allops), "left", len(left), "first eng", o.eng, "seq", o.seq, "nd", nd[id(o)],
                              [(d.eng, d.seq, id(d) in inseg, id(d) in start, d.bar) for d in o.deps + o.odeps])
                    assert nxt, "scheduler stuck"
                    now = max(now + 1, min(nxt))
                for e in self.ENGS:
                    if RESCHED_ENGS is None or e in RESCHED_ENGS:
                        segs[e][k][0].sort(key=lambda o: (start[id(o)], o.seq))
            for e in self.CE:
                lst, bars = segs[e][k]
                for bo in bars:
                    if bo.bar == 1 and lst:
                        bo.deps.append(lst[-1])
                        lst[-1].sig = True
        for e in self.ENGS:
            out = []
            for lst, bars in segs[e]:
                out.extend(lst)
                out.extend(bars)
            self.ops[e] = out

    def _check(self):
        pos = {e: 0 for e in self.ENGS}
        done = set()
        n = sum(len(v) for v in self.ops.values())
        while len(done) < n:
            prog = False
            for e in self.ENGS:
                while pos[e] < len(self.ops[e]):
                    o = self.ops[e][pos[e]]
                    if all(id(d) in done for d in o.deps) and all(id(d) in done for d in o.odeps):
                        done.add(id(o))
                        pos[e] += 1
                        prog = True
                    else:
                        break
            if not prog:
                for e in self.ENGS:
                    if pos[e] < len(self.ops[e]):
                        o = self.ops[e][pos[e]]
                        print("DEADLOCK", e, "pos", pos[e], "seq", o.seq, "bar", o.bar, "waiting on",
                              [(d.eng, d.seq, d.bar, d.dma) for d in o.deps + o.odeps if id(d) not in done])
                raise RuntimeError("deadlock in emitted order")

    def emit(self):
        nc = self.nc
        self._reschedule()
        self._check()
        with contextlib.ExitStack() as es:
            esem = {e: es.enter_context(nc.semaphore("s_" + e)) for e in self.CE}
            ssem = {k: es.enter_context(nc.semaphore("d%d" % i)) for i, k in enumerate(self.streams)}
            for e in self.CE:
                c = 0
                for o in self.ops[e]:
                    if o.dma is None and o.sig:
                        c += 1
                        o.sigval = c
            block = es.enter_context(nc.Block())

            def run(ename, eh):
                waited = {}
                for o in self.ops[ename]:
                    for d in o.deps:
                        sem = ssem[d.dma] if d.dma is not None else esem[d.eng]
                        key = id(sem)
                        if waited.get(key, 0) >= d.sigval:
                            continue
                        waited[key] = d.sigval
                        eh.wait_ge(sem, d.sigval)
                    ins = o.fn(eh)
                    if o.dma is not None:
                        ins.then_inc(ssem[o.dma], 16)
                    elif o.sig:
                        ins.then_inc(esem[ename], 1)
                if ename == "sp":
                    for k, n in self.streams.items():
                        eh.wait_ge(ssem[k], 16 * n[0])

            @block.tensor
            def _(e):
                run("pe", e)

            @block.scalar
            def _(e):
                run("act", e)

            @block.vector
            def _(e):
                run("dve", e)

            @block.gpsimd
            def _(e):
                run("pool", e)

            @block.sync
            def _(e):
                run("sp", e)


class Tl:
    __slots__ = ("ap", "b")

    def __init__(self, ap, name=""):
        self.ap = ap
        self.b = Buf(name)


class Arena:
    def __init__(self, ap, size):
        self.ap = ap
        self.size = size
        self.off = 0

    def alloc(self, n, dt=BF16, name=""):
        w = n * 2 if dt == F32 else n
        off = self.off
        self.off += (w + 31) // 32 * 32
        assert self.off <= self.size, ("SBUF arena overflow", name, self.off, self.size)
        v = self.ap[:, off:off + w]
        if dt == F32:
            v = v.bitcast(F32)
        return Tl(v, name)


class Ring:
    def __init__(self, tiles):
        self.t = tiles
        self.i = -1

    def next(self):
        self.i += 1
        return self.t[self.i % len(self.t)]


def make_consts():
    s = np.arange(128)[:, None]
    t = np.arange(128)[None, :]
    c = {}
    c["ident"] = (s == t).astype(np.float32)
    c["tri"] = (s <= t).astype(np.float32)
    c["maskneg"] = np.where(s > t, -30000.0, 0.0).astype(np.float32)
    c["d1p"] = ((s <= t).astype(np.float32) - (s <= 63).astype(np.float32) * np.ones_like(t, dtype=np.float32))
    vs = (s < DS) & (t < DS)
    c["d1s"] = np.where(vs, (s <= t).astype(np.float32) - (s <= 15).astype(np.float32), 0.0).astype(np.float32)
    sel = np.zeros((128, 8), np.float32)
    sv = np.arange(128)
    sel[:, 0] = sv <= 63
    sel[:, 1] = 1.0
    sel[:, 2] = sv > 63
    sel[:, 4] = sv <= 15
    sel[:, 5] = sv < DS
    sel[:, 6] = (sv > 15) & (sv < DS)
    c["sel"] = sel
    c["him"] = np.where(s <= t, 3.0e38, 0.0).astype(np.float32)
    c["lom"] = np.where(s <= t, -3.0e38, 0.0).astype(np.float32)
    return np.concatenate([c["ident"], c["tri"], c["maskneg"], c["him"], c["lom"], c["d1p"], c["d1s"], c["sel"]], axis=1).astype(np.float32)


NCONST = 7 * 128 + 8


def build(T, PAST, NL=4):
    NT = T // 128
    NP = PAST // 128
    NTS = NT + 1
    TS = T + 128
    KC = T + PAST + 128
    NKS = KC // 128
    NA = (NL + 1) // 2
    NR = NL // 2

    nc = bass.Bass("TRN2", target_bir_lowering=False)

    def din(name, shape):
        return nc.dram_tensor(name, list(shape), F32, kind="ExternalInput").ap()

    def dout(name, shape):
        return nc.dram_tensor(name, list(shape), F32, kind="ExternalOutput").ap()

    def dscr(name, shape, dt):
        return nc.dram_tensor(name, list(shape), dt, kind="Internal").ap()

    x_p = din("x_p", [T, D])
    x_s = din("x_s", [DS, D])
    ck = din("ck", [NA, PAST, D])
    cv = din("cv", [NA, PAST, D])
    clf = din("clf", [NA, PAST, NH])
    st = din("st", [NR, HH, 128, 128])
    fox_w_in = din("fox_w_in", [NA, D, FW])
    fox_b_f = din("fox_b_f", [NA, NH])
    fox_q_norm = din("fox_q_norm", [NA, DH])
    fox_k_norm = din("fox_k_norm", [NA, DH])
    fox_w_out = din("fox_w_out", [NA, D, D])
    hgrn_w_in = din("hgrn_w_in", [NR, D, 4 * D])
    hgrn_lb = din("hgrn_lb", [4, D])
    hgrn_out_norm = din("hgrn_out_norm", [NR, D])
    hgrn_w_out = din("hgrn_w_out", [NR, D, D])
    pre_mix = din("pre_mix", [NL, D])
    post_mix = din("post_mix", [NL, D])
    pre_ffn = din("pre_ffn", [NL, D])
    post_ffn = din("post_ffn", [NL, D])
    ffn_up = din("ffn_up", [NL, D, DFF])
    ffn_down = din("ffn_down", [NL, DFF, D])
    consts = din("consts", [128, NCONST])

    y_p = dout("y_p", [T, D])
    y_s = dout("y_s", [DS, D])
    k_p = dout("k_p", [NA, T, D])
    v_p = dout("v_p", [NA, T, D])
    lf_p = dout("lf_p", [NA, T, NH])
    s_p = dout("s_p", [NR, HH, 128, 128])
    k_s = dout("k_s", [NA, DS, D])
    v_s = dout("v_s", [NA, DS, D])
    lf_s = dout("lf_s", [NA, DS, NH])
    s_s = dout("s_s", [NR, HH, 128, 128])

    xres = dscr("xres", [NTS, 128, D], F32)
    wb_fin = dscr("wb_fin", [NA, D, FW], BF16)
    wb_fout = dscr("wb_fout", [NA, D, D], BF16)
    wb_hin = dscr("wb_hin", [NR, D, 4 * D], BF16)
    wb_hout = dscr("wb_hout", [NR, D, D], BF16)
    wb_up = dscr("wb_up", [NL, 32, 128, 8, 128], BF16)
    wb_down = dscr("wb_down", [NL, DFF, D], BF16)
    QT = dscr("QT", [D, TS], BF16)
    KT = dscr("KT", [D, KC], BF16)
    VT = dscr("VT", [KC, D], BF16)
    GT = dscr("GT", [D, TS], BF16)
    CTs = dscr("CTs", [NH, 6, KC], BF16)
    OGT = dscr("OGT", [D, TS], BF16)
    H2T = dscr("H2T", [D, TS], BF16)

    db = {}

    def dbuf(*key):
        if key not in db:
            db[key] = Buf(str(key))
        return db[key]

    ARENA = 106000
    with contextlib.ExitStack() as es:
        arena_t = es.enter_context(nc.sbuf_tensor("arena", [128, ARENA], BF16))
        psum = es.enter_context(nc.psum_tensor("psum", [128, 8, 512], F32))
        A = Arena(arena_t, ARENA)
        S = Sched(nc)

        bank = [Tl(psum[:, k, :], "bank%d" % k) for k in range(8)]

        def slot2(k):
            return Tl(psum[:, k:k + 2, :].rearrange("p a b -> p (a b)"), "slot%d" % k)

        cst = A.alloc(NCONST, F32, "consts")
        S.op("sp", lambda e: e.dma_start(out=cst.ap, in_=consts[:, :]), writes=[cst.b], dma="cst")
        identf = cst.ap[:, 0:128]
        trif = cst.ap[:, 128:256]
        masknegf = cst.ap[:, 256:384]
        d1p = cst.ap[:, 640:768]
        d1s = cst.ap[:, 768:896]
        selp = cst.ap[:, 896:899]
        sels = cst.ap[:, 900:903]
        cb = A.alloc(5 * 128, BF16, "constb")
        S.op("dve", lambda e: e.tensor_copy(out=cb.ap, in_=cst.ap[:, 0:640]), reads=[cst.b], writes=[cb.b])
        idb = cb.ap[:, 0:128]
        mask01b = cb.ap[:, 128:256]
        masknegb = cb.ap[:, 256:384]
        himb = cb.ap[:, 384:512]
        lomb = cb.ap[:, 512:640]
        CONSTB = [cst.b, cb.b]
        nhalf = A.alloc(16, F32, "nhalf")
        S.op("pool", lambda e: e.memset(nhalf.ap, -0.5), writes=[nhalf.b])

        oml = {}
        if NR > 0:
            for layer in range(1, NL, 2):
                oml[layer] = A.alloc(D, F32, "oml%d" % layer)
            keep = A.off
            L = [A.alloc(D, F32, "lbl%d" % i) for i in range(4)]
            mx = A.alloc(D, F32, "lbmx")
            sm = A.alloc(D, F32, "lbsum")

            def ld_l(i):
                S.op("sp", lambda e: e.dma_start(out=L[i].ap, in_=hgrn_lb[i:i + 1, :].partition_broadcast(128)), writes=[L[i].b], dma="lb%d" % i)
            for i in range(4):
                ld_l(i)

            def tt_(out, a_, b_, op):
                S.op("dve", lambda e: e.tensor_tensor(out=out.ap, in0=a_.ap, in1=b_.ap, op=op), reads=[a_.b, b_.b], writes=[out.b])
            tt_(mx, L[0], L[1], ALU.max)
            tt_(mx, mx, L[2], ALU.max)
            tt_(mx, mx, L[3], ALU.max)

            def ex_(i):
                tt_(L[i], L[i], mx, ALU.subtract)
                S.op("act", lambda e: e.activation(out=L[i].ap, in_=L[i].ap, func=AF.Exp), reads=[L[i].b], writes=[L[i].b])
            for i in range(4):
                ex_(i)
            tt_(sm, L[0], L[1], ALU.add)
            tt_(sm, sm, L[2], ALU.add)
            tt_(sm, sm, L[3], ALU.add)
            S.op("dve", lambda e: e.reciprocal(out=sm.ap, in_=sm.ap), reads=[sm.b], writes=[sm.b])

            def mk_oml(layer):
                o_ = oml[layer]
                S.op("dve", lambda e: e.tensor_copy(out=o_.ap, in_=L[1].ap), reads=[L[1].b], writes=[o_.b])
                for i in range(2, layer + 1):
                    tt_(o_, o_, L[i], ALU.add)
                tt_(o_, o_, sm, ALU.mult)
                S.op("dve", lambda e: e.tensor_scalar(out=o_.ap, in0=o_.ap, scalar1=-1.0, scalar2=1.0, op0=ALU.mult, op1=ALU.add),
                     reads=[o_.b], writes=[o_.b])
            for layer in range(1, NL, 2):
                mk_oml(layer)
            S.barrier()
            A.off = keep

        PERS = A.off

        def stats(src_ap, src_b, n, junk, ss):
            S.op("pool", lambda e: e.memset(ss.ap[:, 0:1], 0.0), writes=[ss.b], cost=80)
            S.op("act", lambda e: e.activation(out=junk.ap[:, 0:n], in_=src_ap, func=AF.Square, accum_out=ss.ap[:, 0:1]),
                 reads=[src_b, ss.b], writes=[ss.b])
            S.op("act", lambda e: e.activation(out=ss.ap[:, 1:2], in_=ss.ap[:, 0:1], func=AF.Sqrt, scale=1.0 / n, bias=EPS),
                 reads=[ss.b], writes=[ss.b], cost=1500)
            S.op("dve", lambda e: e.reciprocal(out=ss.ap[:, 2:3], in_=ss.ap[:, 1:2]), reads=[ss.b], writes=[ss.b], cost=200)

        def transposes(src, dstT, tb, copy_eng):
            tv = tb.ap.bitcast(BF16)

            def f(e):
                ins = None
                for c in range(8):
                    ins = e.transpose(out=tv[:, c * 128:(c + 1) * 128], in_=src.ap[:, c * 128:(c + 1) * 128], identity=idb)
                return ins
            S.op("pe", f, reads=[src.b] + CONSTB, writes=[tb.b], cost=900)
            if copy_eng == "act":
                S.op("act", lambda e: e.activation(out=dstT.ap, in_=tv, func=AF.Copy), reads=[tb.b], writes=[dstT.b])
            else:
                S.op("dve", lambda e: e.tensor_copy(out=dstT.ap, in_=tv), reads=[tb.b], writes=[dstT.b])

        def proj_tok(hT, w, wcol0, sl):
            def f(e):
                ins = None
                for half in range(2):
                    for kc in range(8):
                        ins = e.matmul(sl.ap[:, half * 512:(half + 1) * 512], lhsT=hT.ap[:, kc * 128:(kc + 1) * 128],
                                       rhs=w.ap[:, kc, wcol0 + half * 512: wcol0 + (half + 1) * 512],
                                       start=(kc == 0), stop=(kc == 7))
                return ins
            S.op("pe", f, reads=[hT.b, w.b], writes=[sl.b], cost=4300)

        def load_bcast(dst, src_row, key):
            S.op("sp", lambda e: e.dma_start(out=dst.ap, in_=src_row.partition_broadcast(128)), writes=[dst.b], dma=key)

        def xsrc(layer, t):
            if layer == 0 and t < NT:
                return x_p[t * 128:(t + 1) * 128, :], []
            return xres[t], [dbuf("xres", t)]

        def qcol(t):
            return t * 128

        def kcol(t):
            return t * 128 if t < NT else T + PAST

        def phase_wcast():
            mark = A.off
            CH = 4096
            NWB = 4
            fb = [A.alloc(CH, F32, "wf%d" % i) for i in range(NWB)]
            bb = [A.alloc(CH, BF16, "wb%d" % i) for i in range(NWB)]
            jobs = []

            def flat(ap2d):
                return ap2d.rearrange("r c -> (r c)").rearrange("(p n) -> p n", p=128)

            for j in range(NA):
                jobs.append((flat(fox_w_in[j]), flat(wb_fin[j]), dbuf("wb_fin", j)))
                jobs.append((flat(fox_w_out[j]), flat(wb_fout[j]), dbuf("wb_fout", j)))
            for j in range(NR):
                jobs.append((flat(hgrn_w_in[j]), flat(wb_hin[j]), dbuf("wb_hin", j)))
                jobs.append((flat(hgrn_w_out[j]), flat(wb_hout[j]), dbuf("wb_hout", j)))
            for l in range(NL):
                jobs.append((flat(ffn_down[l]), flat(wb_down[l]), dbuf("wb_down", l)))
            step = 0
            engs = ("dve", "pool", "act")
            for src, dst, tok in jobs:
                n = src.shape[1]
                for c0 in range(0, n, CH):
                    w_ = min(CH, n - c0)
                    f_, b_ = fb[step % NWB], bb[step % NWB]
                    S.op("sp", lambda e, f_=f_, src=src, c0=c0, w_=w_: e.dma_start(out=f_.ap[:, 0:w_], in_=src[:, c0:c0 + w_]),
                         writes=[f_.b], dma="wl%d" % (step % NWB))
                    eng = engs[step % 3]
                    if eng == "act":
                        S.op("act", lambda e, f_=f_, b_=b_, w_=w_: e.activation(out=b_.ap[:, 0:w_], in_=f_.ap[:, 0:w_], func=AF.Copy),
                             reads=[f_.b], writes=[b_.b])
                    else:
                        S.op(eng, lambda e, f_=f_, b_=b_, w_=w_: e.tensor_copy(out=b_.ap[:, 0:w_], in_=f_.ap[:, 0:w_]),
                             reads=[f_.b], writes=[b_.b])
                    S.op("sp", lambda e, b_=b_, dst=dst, c0=c0, w_=w_: e.dma_start(out=dst[:, c0:c0 + w_], in_=b_.ap[:, 0:w_]),
                         reads=[b_.b], writes=[tok], dma="ws%d" % (step % NWB))
                    step += 1
            for l in range(NL):
                for c in range(8):
                    f_, b_ = fb[step % NWB], bb[step % NWB]
                    S.op("sp", lambda e, f_=f_, l=l, c=c: e.dma_start(out=f_.ap[:, 0:DFF], in_=ffn_up[l, c * 128:(c + 1) * 128, :]),
                         writes=[f_.b], dma="wl%d" % (step % NWB))
                    eng = engs[step % 3]
                    if eng == "act":
                        S.op("act", lambda e, f_=f_, b_=b_: e.activation(out=b_.ap[:, 0:DFF], in_=f_.ap[:, 0:DFF], func=AF.Copy),
                             reads=[f_.b], writes=[b_.b])
                    else:
                        S.op(eng, lambda e, f_=f_, b_=b_: e.tensor_copy(out=b_.ap[:, 0:DFF], in_=f_.ap[:, 0:DFF]),
                             reads=[f_.b], writes=[b_.b])
                    S.op("sp", lambda e, b_=b_, l=l, c=c: e.dma_start(
                        out=wb_up[l, :, :, c, :].rearrange("j p n -> p j n"),
                        in_=b_.ap[:, 0:DFF].rearrange("p (j n) -> p j n", n=128)),
                        reads=[b_.b], writes=[dbuf("wb_up", l)], dma="ws%d" % (step % NWB))
                    step += 1
            xt = fb[step % NWB]
            S.op("pool", lambda e: e.memset(xt.ap[:, 0:D], 0.0), writes=[xt.b])
            S.op("sp", lambda e: e.dma_start(out=xt.ap[0:DS, 0:D], in_=x_s[:, :]), reads=[xt.b], writes=[xt.b], dma="wl%d" % (step % NWB))
            S.op("sp", lambda e: e.dma_start(out=xres[NT], in_=xt.ap[:, 0:D]), reads=[xt.b], writes=[dbuf("xres", NT)], dma="ws%d" % (step % NWB))
            S.barrier()
            A.off = mark

        def make_tail_tiles():
            d = {}
            d["junk"] = A.alloc(D, BF16, "tjunk")
            d["ss"] = [A.alloc(8, F32, "tss%d" % i) for i in range(4)]
            d["tmp"] = A.alloc(D, F32, "ttmp")
            d["xn"] = Ring([A.alloc(D, F32, "txn%d" % i) for i in range(1)])
            d["h2"] = A.alloc(D, BF16, "th2")
            d["h2T"] = Ring([A.alloc(D, BF16, "th2T%d" % i) for i in range(1)])
            return d

        def tail(layer, t, sl, x, tt, gpost, gpre2, tb):
            ssA = tt["ss"][(2 * t) % 4]
            ssB = tt["ss"][(2 * t + 1) % 4]
            stats(sl.ap, sl.b, D, tt["junk"], ssA)
            tmp = tt["tmp"]
            S.op("dve", lambda e: e.scalar_tensor_tensor(out=tmp.ap, in0=sl.ap, scalar=ssA.ap[:, 2:3], in1=gpost.ap, op0=ALU.mult, op1=ALU.mult),
                 reads=[sl.b, ssA.b, gpost.b], writes=[tmp.b])
            xn = tt["xn"].next()
            S.op("pool", lambda e: e.tensor_tensor(out=xn.ap, in0=x.ap, in1=tmp.ap, op=ALU.add), reads=[x.b, tmp.b], writes=[xn.b])
            nrow = 128 if t < NT else DS
            S.op("sp", lambda e: e.dma_start(out=xres[t, 0:nrow, :], in_=xn.ap[0:nrow, :]), reads=[xn.b], writes=[dbuf("xres", t)], dma="st_x%d" % (t % 2))
            stats(xn.ap, xn.b, D, tt["junk"], ssB)
            h2 = tt["h2"]
            S.op("dve", lambda e: e.scalar_tensor_tensor(out=h2.ap, in0=xn.ap, scalar=ssB.ap[:, 2:3], in1=gpre2.ap, op0=ALU.mult, op1=ALU.mult),
                 reads=[xn.b, ssB.b, gpre2.b], writes=[h2.b])
            h2T = tt["h2T"].next()
            transposes(h2, h2T, tb, "act")
            S.op("sp", lambda e: e.dma_start(out=H2T[:, qcol(t):qcol(t) + 128].rearrange("(c p) n -> p c n", p=128),
                                             in_=h2T.ap.rearrange("p (c n) -> p c n", c=8)),
                 reads=[h2T.b], writes=[dbuf("H2T", t)], dma="st_h%d" % (t % 2))

        def phase_fox_a(layer):
            j = layer // 2
            LF = A.alloc(NKS * NH, F32, "LF")
            after_lf = A.off
            w = A.alloc(8 * FW, BF16, "fw")
            w.ap = w.ap.rearrange("p (c n) -> p c n", c=8)

            def ld_w(c):
                S.op("sp", lambda e: e.dma_start(out=w.ap[:, c, :], in_=wb_fin[j, c * 128:(c + 1) * 128, :]),
                     reads=[dbuf("wb_fin", j)], writes=[w.b], dma="wld%d" % (c % 2))
            for c in range(8):
                ld_w(c)
            gpre = A.alloc(D, F32, "gpre")
            load_bcast(gpre, pre_mix[layer:layer + 1, :], "g0")
            gq = A.alloc(DH, F32, "gq")
            load_bcast(gq, fox_q_norm[j:j + 1, :], "g1")
            gk = A.alloc(DH, F32, "gk")
            load_bcast(gk, fox_k_norm[j:j + 1, :], "g2")
            bfb = A.alloc(NH, F32, "bfb")
            load_bcast(bfb, fox_b_f[j:j + 1, :], "g3")
            xr = Ring([A.alloc(D, F32, "x%d" % i) for i in range(2)])
            junk = A.alloc(D, BF16, "junk")
            ss = [A.alloc(8, F32, "ss%d" % i) for i in range(2)]
            h = A.alloc(D, BF16, "h")
            hT = A.alloc(D, BF16, "hT")
            sq = A.alloc(D, F32, "sq")
            sq2 = A.alloc(D, F32, "sq2")
            ssq = [A.alloc(64, F32, "ssq%d" % i) for i in range(2)]
            qn = A.alloc(D, BF16, "qn")
            kf = Ring([A.alloc(D, F32, "kf%d" % i) for i in range(2)])
            kb = A.alloc(D, BF16, "kb")
            vf = Ring([A.alloc(D, F32, "vf%d" % i) for i in range(2)])
            vb = Ring([A.alloc(D, BF16, "vb%d" % i) for i in range(2)])
            QTs = Ring([A.alloc(D, BF16, "QTs%d" % i) for i in range(2)])
            KTs = Ring([A.alloc(D, BF16, "KTs%d" % i) for i in range(2)])
            GTs = Ring([A.alloc(D, BF16, "GTs%d" % i) for i in range(2)])
            f1 = [A.alloc(64, F32, "f1_%d" % i) for i in range(2)]
            slots = Ring([slot2(1), slot2(3), slot2(5)])
            tb = bank[0]
            fzb = bank[7]

            def k_to_scratch(kbt, col):
                KTt = KTs.next()
                transposes(kbt, KTt, tb, "dve")
                S.op("sp", lambda e: e.dma_start(out=KT[:, col:col + 128].rearrange("(c p) n -> p c n", p=128),
                                                 in_=KTt.ap.rearrange("p (c n) -> p c n", c=8)),
                     reads=[KTt.b], writes=[dbuf("KT", col)], dma="st_kt%d" % (KTs.i % 2))

            def v_to_scratch(vbt, col, par):
                S.op("sp", lambda e: e.dma_start(out=VT[col:col + 128, :], in_=vbt.ap), reads=[vbt.b], writes=[dbuf("VT", col)],
                     dma="st_vt%d" % par)

            def do_past(jt):
                col = T + jt * 128
                kft = kf.next()
                S.op("sp", lambda e: e.dma_start(out=kft.ap, in_=ck[j, jt * 128:(jt + 1) * 128, :]), writes=[kft.b], dma="ldk%d" % (kf.i % 2))
                S.op("pool", lambda e: e.tensor_copy(out=kb.ap, in_=kft.ap), reads=[kft.b], writes=[kb.b])
                k_to_scratch(kb, col)
                vft = vf.next()
                S.op("sp", lambda e: e.dma_start(out=vft.ap, in_=cv[j, jt * 128:(jt + 1) * 128, :]), writes=[vft.b], dma="ldv%d" % (vf.i % 2))
                vbt = vb.next()
                S.op("pool", lambda e: e.tensor_copy(out=vbt.ap, in_=vft.ap), reads=[vft.b], writes=[vbt.b])
                v_to_scratch(vbt, col, vb.i % 2)
                slot_ = NT + jt
                S.op("sp", lambda e: e.dma_start(out=LF.ap[:, slot_ * NH:(slot_ + 1) * NH], in_=clf[j, jt * 128:(jt + 1) * 128, :]),
                     writes=[LF.b], dma="ldlf")
            for jt in range(NP):
                do_past(jt)

            def load_x(t):
                xt = xr.next()
                src, rd = xsrc(layer, t)
                S.op("sp", lambda e: e.dma_start(out=xt.ap, in_=src), reads=rd, writes=[xt.b], dma="ldx%d" % (t % 2))
                return xt

            def headnorm(which, sl_, t):
                ssq_ = ssq[t % 2]
                nv = 128 if t < NT else DS
                sqt = sq if which == "q" else sq2
                o0 = 0 if which == "q" else 32
                sq3 = sqt.ap.rearrange("p (h d) -> p h d", h=NH)
                S.op("act", lambda e: e.activation(out=sqt.ap, in_=sl_.ap, func=AF.Square), reads=[sl_.b], writes=[sqt.b])
                S.op("dve", lambda e: e.tensor_reduce(out=ssq_.ap[:, o0:o0 + NH], in_=sq3, axis=AX.X, op=ALU.add), reads=[sqt.b], writes=[ssq_.b])
                S.op("act", lambda e: e.activation(out=ssq_.ap[:, o0 + NH:o0 + 2 * NH], in_=ssq_.ap[:, o0:o0 + NH], func=AF.Sqrt,
                                                   scale=1.0 / DH, bias=EPS), reads=[ssq_.b], writes=[ssq_.b])
                S.op("dve", lambda e: e.reciprocal(out=ssq_.ap[:, o0:o0 + NH], in_=ssq_.ap[:, o0 + NH:o0 + 2 * NH]),
                     reads=[ssq_.b], writes=[ssq_.b])
                if which == "q":
                    S.op("dve", lambda e: e.tensor_scalar(out=ssq_.ap[:, o0:o0 + NH], in0=ssq_.ap[:, o0:o0 + NH], scalar1=DH ** -0.5, scalar2=None, op0=ALU.mult),
                         reads=[ssq_.b], writes=[ssq_.b])
                S.op("dve", lambda e: e.tensor_tensor(
                    out=sq3, in0=sl_.ap.rearrange("p (h d) -> p h d", h=NH),
                    in1=ssq_.ap[:, o0:o0 + NH].unsqueeze(2).to_broadcast([128, NH, DH]), op=ALU.mult),
                    reads=[sl_.b, ssq_.b], writes=[sqt.b])
                if which == "q":
                    S.op("pool", lambda e: e.tensor_tensor(
                        out=qn.ap.rearrange("p (h d) -> p h d", h=NH), in0=sq3,
                        in1=gq.ap.unsqueeze(1).to_broadcast([128, NH, DH]), op=ALU.mult), reads=[sqt.b, gq.b], writes=[qn.b])
                else:
                    kft = kf.next()
                    S.op("pool", lambda e: e.tensor_tensor(
                        out=kft.ap.rearrange("p (h d) -> p h d", h=NH), in0=sq3,
                        in1=gk.ap.unsqueeze(1).to_broadcast([128, NH, DH]), op=ALU.mult), reads=[sqt.b, gk.b], writes=[kft.b])
                    S.op("pool", lambda e: e.tensor_copy(out=kb.ap, in_=kft.ap), reads=[kft.b], writes=[kb.b])
                    kdst = k_p[j, t * 128:(t + 1) * 128, :] if t < NT else k_s[j, :, :]
                    S.op("sp", lambda e: e.dma_start(out=kdst, in_=kft.ap[0:nv, :]), reads=[kft.b], dma="ldk%d" % (kf.i % 2))

            def do_tile(t, x):
                nv = 128 if t < NT else DS
                ss_ = ss[t % 2]
                stats(x.ap, x.b, D, junk, ss_)
                S.op("dve", lambda e: e.scalar_tensor_tensor(out=h.ap, in0=x.ap, scalar=ss_.ap[:, 2:3], in1=gpre.ap, op0=ALU.mult, op1=ALU.mult),
                     reads=[x.b, ss_.b, gpre.b], writes=[h.b])
                transposes(h, hT, tb, "act")
                slq = slots.next()
                proj_tok(hT, w, 0, slq)
                slk = slots.next()
                proj_tok(hT, w, D, slk)
                slv = slots.next()
                proj_tok(hT, w, 2 * D, slv)

                def f_fz(e):
                    ins = None
                    for kc in range(8):
                        ins = e.matmul(fzb.ap[:, 0:NH], lhsT=hT.ap[:, kc * 128:(kc + 1) * 128], rhs=w.ap[:, kc, 4 * D:4 * D + NH],
                                       start=(kc == 0), stop=(kc == 7))
                    return ins
                S.op("pe", f_fz, reads=[hT.b, w.b], writes=[fzb.b], cost=500)
                headnorm("q", slq, t)
                headnorm("k", slk, t)
                vft = vf.next()
                S.op("act", lambda e: e.activation(out=vft.ap, in_=slv.ap, func=AF.Copy), reads=[slv.b], writes=[vft.b])
                vbt = vb.next()
                S.op("pool", lambda e: e.tensor_copy(out=vbt.ap, in_=vft.ap), reads=[vft.b], writes=[vbt.b])
                vdst = v_p[j, t * 128:(t + 1) * 128, :] if t < NT else v_s[j, :, :]
                S.op("sp", lambda e: e.dma_start(out=vdst, in_=vft.ap[0:nv, :]), reads=[vft.b], dma="ldv%d" % (vf.i % 2))
                v_to_scratch(vbt, kcol(t), vb.i % 2)
                slg = slots.next()

                def f_g(e):
                    ins = None
                    for c in range(8):
                        for kc in range(8):
                            ins = e.matmul(slg.ap[:, c * 128:(c + 1) * 128], lhsT=w.ap[:, kc, 3 * D + c * 128:3 * D + (c + 1) * 128],
                                           rhs=hT.ap[:, kc * 128:(kc + 1) * 128], start=(kc == 0), stop=(kc == 7))
                    return ins
                S.op("pe", f_g, reads=[hT.b, w.b], writes=[slg.b], cost=4500)
                GTt = GTs.next()
                S.op("act", lambda e: e.activation(out=GTt.ap, in_=slg.ap, func=AF.Sigmoid), reads=[slg.b], writes=[GTt.b])
                S.op("sp", lambda e: e.dma_start(out=GT[:, qcol(t):qcol(t) + 128].rearrange("(c p) n -> p c n", p=128),
                                                 in_=GTt.ap.rearrange("p (c n) -> p c n", c=8)),
                     reads=[GTt.b], writes=[dbuf("GT", t)], dma="st_gt%d" % (t % 2))
                f1_ = f1[t % 2]
                slot_ = t if t < NT else NT + NP
                lfv = LF.ap[:, slot_ * NH:(slot_ + 1) * NH]
                S.op("dve", lambda e: e.tensor_tensor(out=f1_.ap[:, 0:NH], in0=fzb.ap[:, 0:NH], in1=bfb.ap, op=ALU.add),
                     reads=[fzb.b, bfb.b], writes=[f1_.b])
                S.op("act", lambda e: e.activation(out=f1_.ap[:, 16:32], in_=f1_.ap[:, 0:NH], func=AF.Exp, scale=-1.0), reads=[f1_.b], writes=[f1_.b])
                S.op("act", lambda e: e.activation(out=f1_.ap[:, 32:48], in_=f1_.ap[:, 16:32], func=AF.Ln, bias=1.0), reads=[f1_.b], writes=[f1_.b])
                S.op("dve", lambda e: e.tensor_scalar(out=lfv, in0=f1_.ap[:, 32:48], scalar1=-1.0, scalar2=None, op0=ALU.mult),
                     reads=[f1_.b], writes=[LF.b])
                ldst = lf_p[j, t * 128:(t + 1) * 128, :] if t < NT else lf_s[j, :, :]
                S.op("sp", lambda e: e.dma_start(out=ldst, in_=lfv[0:nv, :]), reads=[LF.b], dma="st_lf")
                QTt = QTs.next()
                transposes(qn, QTt, tb, "act")
                S.op("sp", lambda e: e.dma_start(out=QT[:, qcol(t):qcol(t) + 128].rearrange("(c p) n -> p c n", p=128),
                                                 in_=QTt.ap.rearrange("p (c n) -> p c n", c=8)),
                     reads=[QTt.b], writes=[dbuf("QT", t)], dma="st_qt%d" % (t % 2))
                k_to_scratch(kb, kcol(t))

            xt_next = load_x(0)
            for t in range(NTS):
                x = xt_next
                if t + 1 < NTS:
                    xt_next = load_x(t + 1)
                do_tile(t, x)
            S.barrier()
            A.off = after_lf
            return LF

        def phase_fox_a2(layer, LF):
            CT = A.alloc(KC, F32, "CT")
            psb = Ring([bank[1], bank[2], bank[3], bank[4]])
            ngrp = (NKS + 3) // 4

            def do_grp(g):
                s0 = g * 4
                ns = min(4, NKS - s0)
                pb = psb.next()

                def f(e):
                    ins = None
                    for i in range(ns):
                        s_ = s0 + i
                        ins = e.matmul(pb.ap[0:NH, i * 128:(i + 1) * 128], lhsT=LF.ap[:, s_ * NH:(s_ + 1) * NH], rhs=trif, start=True, stop=True)
                    return ins
                S.op("pe", f, reads=[LF.b] + CONSTB, writes=[pb.b])
                S.op("act", lambda e: e.activation(out=CT.ap[0:NH, s0 * 128:(s0 + ns) * 128], in_=pb.ap[0:NH, 0:ns * 128], func=AF.Copy),
                     reads=[pb.b], writes=[CT.b])
            for g in range(ngrp):
                do_grp(g)

            def fix(s_):
                S.op("dve", lambda e: e.tensor_scalar(out=CT.ap[0:NH, s_ * 128:(s_ + 1) * 128], in0=CT.ap[0:NH, s_ * 128:(s_ + 1) * 128],
                                                      scalar1=CT.ap[0:NH, s_ * 128 - 1:s_ * 128], scalar2=None, op0=ALU.add),
                     reads=[CT.b], writes=[CT.b])
            for s_ in range(1, NKS):
                if s_ == NT:
                    continue
                fix(s_)
            CW = 1024
            r1 = A.alloc(CW, F32, "r1")
            r2 = A.alloc(CW, F32, "r2")
            o6 = Ring([A.alloc(6 * CW, BF16, "o6_%d" % i) for i in range(2)])

            def do_chunk(c0):
                w_ = min(CW, KC - c0)
                o_ = o6.next()
                ov = o_.ap.rearrange("p (a n) -> p a n", a=6)
                cs = CT.ap[0:NH, c0:c0 + w_]
                S.op("act", lambda e: e.activation(out=ov[0:NH, 0, 0:w_], in_=cs, func=AF.Copy), reads=[CT.b], writes=[o_.b])
                S.op("dve", lambda e: e.tensor_tensor(out=r1.ap[0:NH, 0:w_], in0=cs, in1=ov[0:NH, 0, 0:w_], op=ALU.subtract),
                     reads=[CT.b, o_.b], writes=[r1.b])
                S.op("act", lambda e: e.activation(out=ov[0:NH, 1, 0:w_], in_=r1.ap[0:NH, 0:w_], func=AF.Copy), reads=[r1.b], writes=[o_.b])
                S.op("dve", lambda e: e.tensor_tensor(out=r2.ap[0:NH, 0:w_], in0=r1.ap[0:NH, 0:w_], in1=ov[0:NH, 1, 0:w_], op=ALU.subtract),
                     reads=[r1.b, o_.b], writes=[r2.b])
                S.op("act", lambda e: e.activation(out=ov[0:NH, 2, 0:w_], in_=r2.ap[0:NH, 0:w_], func=AF.Copy), reads=[r2.b], writes=[o_.b])
                S.op("pool", lambda e: e.tensor_scalar(out=ov[0:NH, 3:6, 0:w_], in0=ov[0:NH, 0:3, 0:w_], scalar1=-1.0, scalar2=None, op0=ALU.mult),
                     reads=[o_.b], writes=[o_.b])
                S.op("sp", lambda e: e.dma_start(out=CTs[:, :, c0:c0 + w_], in_=ov[0:NH, :, 0:w_]),
                     reads=[o_.b], writes=[dbuf("CTs", c0)], dma="st_ct%d" % (o6.i % 2))
            for c0 in range(0, KC, CW):
                do_chunk(c0)
            S.barrier()
            A.off = PERS

        def phase_fox_b(layer):
            NQB = T // 512
            sets = []
            for i in range(2):
                d = {}
                d["QA"] = A.alloc(TS, BF16, "QA%d" % i)
                d["KA"] = A.alloc(KC, BF16, "KA%d" % i)
                d["VA"] = A.alloc(NKS * 128, BF16, "VA%d" % i)
                d["G"] = A.alloc(TS, BF16, "G%d" % i)
                d["VAv"] = d["VA"].ap.rearrange("p (s n) -> p s n", n=128)
                S.op("pool", lambda e, d=d: e.memset(d["QA"].ap[64:70, :], 1.0), writes=[d["QA"].b])
                S.op("pool", lambda e, d=d: e.memset(d["KA"].ap[64:70, :], 1.0), writes=[d["KA"].b])
                S.op("pool", lambda e, d=d: e.memset(d["VA"].ap, 1.0), writes=[d["VA"].b])
                sets.append(d)
            PT = Ring([A.alloc(512, BF16, "PT%d" % i) for i in range(6)])
            rc = Ring([A.alloc(512, F32, "rc%d" % i) for i in range(2)])
            tmp = Ring([A.alloc(512, F32, "otmp%d" % i) for i in range(2)])
            og = Ring([A.alloc(512, BF16, "og%d" % i) for i in range(2)])
            psS = Ring([bank[0], bank[1], bank[2], bank[3], bank[6], bank[7]])
            psO = Ring([bank[4], bank[5]])
            allQT = [dbuf("QT", t) for t in range(NTS)]
            allGT = [dbuf("GT", t) for t in range(NTS)]
            allKT = [dbuf("KT", c) for c in [t * 128 for t in range(NT)] + [T + i * 128 for i in range(NP + 1)]]
            allVT = [dbuf("VT", c) for c in [t * 128 for t in range(NT)] + [T + i * 128 for i in range(NP + 1)]]
            allCT = [dbuf("CTs", c0) for c0 in range(0, KC, 1024)]

            def load_head(hd):
                d = sets[hd % 2]
                p = hd % 2
                r0 = hd * DH
                S.op("sp", lambda e: e.dma_start(out=d["QA"].ap[0:64, :], in_=QT[r0:r0 + DH, :]), reads=allQT, writes=[d["QA"].b], dma="la%d" % p)
                S.op("sp", lambda e: e.dma_start(out=d["QA"].ap[64:67, 0:T], in_=CTs[hd, 0:3, 0:T]), reads=allCT, writes=[d["QA"].b], dma="lb%d" % p)
                S.op("sp", lambda e: e.dma_start(out=d["QA"].ap[64:67, T:TS], in_=CTs[hd, 0:3, T + PAST:KC]), reads=allCT, writes=[d["QA"].b], dma="lc%d" % p)
                S.op("sp", lambda e: e.dma_start(out=d["KA"].ap[0:64, :], in_=KT[r0:r0 + DH, :]), reads=allKT, writes=[d["KA"].b], dma="ld%d" % p)
                S.op("sp", lambda e: e.dma_start(out=d["KA"].ap[67:70, :], in_=CTs[hd, 3:6, :]), reads=allCT, writes=[d["KA"].b], dma="le%d" % p)
                vo = 0 if p == 0 else 64
                step = 16
                for s0 in range(0, NKS, step):
                    s1 = min(NKS, s0 + step)
                    S.op("sp", lambda e, s0=s0, s1=s1: e.dma_start(out=d["VAv"][:, s0:s1, vo:vo + DH],
                                                                  in_=VT[s0 * 128:s1 * 128, r0:r0 + DH].rearrange("(s p) d -> p s d", p=128)),
                         reads=allVT, writes=[d["VA"].b], dma="lf%d" % p)
                S.op("sp", lambda e: e.dma_start(out=d["G"].ap[vo:vo + 64, :], in_=GT[r0:r0 + DH, :]), reads=allGT, writes=[d["G"].b], dma="lg%d" % p)

            def finish(hd, po, ncols, qc0):
                d = sets[hd % 2]
                p = hd % 2
                orow = slice(0, 64) if p == 0 else slice(64, 128)
                drow = slice(64, 128) if p == 0 else slice(0, 64)
                rc_ = rc.next()
                tmp_ = tmp.next()
                og_ = og.next()
                S.op("dve", lambda e: e.reciprocal(out=rc_.ap[drow, 0:ncols], in_=po.ap[drow, 0:ncols]), reads=[po.b], writes=[rc_.b])
                S.op("dve", lambda e: e.tensor_tensor(out=tmp_.ap[orow, 0:ncols], in0=po.ap[orow, 0:ncols], in1=rc_.ap[drow, 0:ncols], op=ALU.mult),
                     reads=[po.b, rc_.b], writes=[tmp_.b])
                S.op("pool", lambda e: e.tensor_tensor(out=og_.ap[orow, 0:ncols], in0=tmp_.ap[orow, 0:ncols], in1=d["G"].ap[orow, qc0:qc0 + ncols], op=ALU.mult),
                     reads=[tmp_.b, d["G"].b], writes=[og_.b])
                S.op("sp", lambda e: e.dma_start(out=OGT[hd * DH:(hd + 1) * DH, qc0:qc0 + ncols], in_=og_.ap[orow, 0:ncols]),
                     reads=[og_.b], writes=[dbuf("OGT", qc0 // 512)], dma="st_og%d" % (og.i % 2))

            def do_head(hd):
                d = sets[hd % 2]
                QA, KA, VA, VAv = d["QA"], d["KA"], d["VA"], d["VAv"]

                def s_step(kt, qb):
                    off = max(0, kt - 4 * qb) * 128
                    ps_ = psS.next()
                    q0 = qb * 512

                    def f(e):
                        if kt >= 4 * qb:
                            e.matmul(ps_.ap[:, off:off + 128], lhsT=idb, rhs=masknegb, start=True, stop=False)
                            ins = e.matmul(ps_.ap[:, off:off + 128], lhsT=KA.ap[0:70, kt * 128:(kt + 1) * 128],
                                           rhs=QA.ap[0:70, q0 + off:q0 + off + 128], start=False, stop=True)
                            if off + 128 < 512:
                                ins = e.matmul(ps_.ap[:, off + 128:512], lhsT=KA.ap[0:70, kt * 128:(kt + 1) * 128],
                                               rhs=QA.ap[0:70, q0 + off + 128:q0 + 512], start=True, stop=True)
                            return ins
                        return e.matmul(ps_.ap[:, 0:512], lhsT=KA.ap[0:70, kt * 128:(kt + 1) * 128], rhs=QA.ap[0:70, q0:q0 + 512],
                                        start=True, stop=True)
                    S.op("pe", f, reads=[KA.b, QA.b] + CONSTB, writes=[ps_.b], cost=280)
                    pt_ = PT.next()
                    S.op("act", lambda e: e.activation(out=pt_.ap[:, off:512], in_=ps_.ap[:, off:512], func=AF.Exp), reads=[ps_.b], writes=[pt_.b], cost=500)
                    return (kt, off, pt_)

                def pv_step(item, po, nkt):
                    kt, off, pt_ = item
                    S.op("pe", lambda e: e.matmul(po.ap[:, off:512], lhsT=VAv[:, kt, :], rhs=pt_.ap[:, off:512], start=(kt == 0), stop=(kt == nkt - 1),
                                                  skip_group_check=True),
                         reads=[VA.b, pt_.b], writes=[po.b], cost=280)

                for qb in range(NQB):
                    po = psO.next()
                    nkt = 4 * qb + 4
                    pend = []
                    for kt in range(nkt):
                        pend.append(s_step(kt, qb))
                        if len(pend) > 3:
                            pv_step(pend.pop(0), po, nkt)
                    while pend:
                        pv_step(pend.pop(0), po, nkt)
                    finish(hd, po, 512, qb * 512)

                po = psO.next()

                def samp_step(kt):
                    ps_ = psS.next()
                    kc0 = T + kt * 128
                    pt_ = PT.next()
                    if kt < NP:
                        S.op("pe", lambda e: e.matmul(ps_.ap[:, 0:128], lhsT=KA.ap[0:70, kc0:kc0 + 128], rhs=QA.ap[0:70, T:TS], start=True, stop=True),
                             reads=[KA.b, QA.b], writes=[ps_.b])
                        S.op("act", lambda e: e.activation(out=pt_.ap[:, 0:128], in_=ps_.ap[:, 0:128], func=AF.Exp), reads=[ps_.b], writes=[pt_.b])
                        S.op("pe", lambda e: e.matmul(po.ap[:, 0:128], lhsT=VAv[:, NT + kt, :], rhs=pt_.ap[:, 0:128], start=(kt == 0), stop=False,
                                                      skip_group_check=True),
                             reads=[VA.b, pt_.b], writes=[po.b])
                    else:
                        def f(e):
                            e.matmul(ps_.ap[0:DS, 0:128], lhsT=idb[0:DS, 0:DS], rhs=masknegb[0:DS, :], start=True, stop=False)
                            return e.matmul(ps_.ap[0:DS, 0:128], lhsT=KA.ap[0:70, kc0:kc0 + DS], rhs=QA.ap[0:70, T:TS], start=False, stop=True)
                        S.op("pe", f, reads=[KA.b, QA.b] + CONSTB, writes=[ps_.b])
                        S.op("act", lambda e: e.activation(out=pt_.ap[0:DS, 0:128], in_=ps_.ap[0:DS, 0:128], func=AF.Exp), reads=[ps_.b], writes=[pt_.b])
                        S.op("pe", lambda e: e.matmul(po.ap[:, 0:128], lhsT=VAv[0:DS, NT + kt, :], rhs=pt_.ap[0:DS, 0:128], start=(NP == 0), stop=True,
                                                      skip_group_check=True),
                             reads=[VA.b, pt_.b], writes=[po.b])
                for kt in range(NP + 1):
                    samp_step(kt)
                finish(hd, po, 128, T)

            load_head(0)
            for hd in range(NH):
                if hd + 1 < NH:
                    load_head(hd + 1)
                do_head(hd)
            S.barrier()
            A.off = PERS

        def phase_fox_c1(layer):
            j = layer // 2
            wo = A.alloc(8 * D, BF16, "wo")
            wo.ap = wo.ap.rearrange("p (c n) -> p c n", c=8)
            S.op("sp", lambda e: e.dma_start(out=wo.ap, in_=wb_fout[j].rearrange("(c p) n -> p c n", p=128)), reads=[dbuf("wb_fout", j)], writes=[wo.b], dma="wld0")
            gpost = A.alloc(D, F32, "gpost")
            load_bcast(gpost, post_mix[layer:layer + 1, :], "g0")
            gpre2 = A.alloc(D, F32, "gpre2")
            load_bcast(gpre2, pre_ffn[layer:layer + 1, :], "g1")
            tt = make_tail_tiles()
            xr = Ring([A.alloc(D, F32, "x%d" % i) for i in range(2)])
            ogr = Ring([A.alloc(D, BF16, "ogb%d" % i) for i in range(2)])
            slots = Ring([slot2(1), slot2(3), slot2(5)])
            allOG = [dbuf("OGT", q) for q in range(T // 512 + 1)]

            def load(t):
                xt = xr.next()
                src, rd = xsrc(layer, t)
                S.op("sp", lambda e: e.dma_start(out=xt.ap, in_=src), reads=rd, writes=[xt.b], dma="ldx%d" % (t % 2))
                ogt = ogr.next()
                S.op("sp", lambda e: e.dma_start(out=ogt.ap.rearrange("p (c n) -> p c n", c=8),
                                                 in_=OGT[:, qcol(t):qcol(t) + 128].rearrange("(c p) n -> p c n", p=128)),
                     reads=[dbuf("OGT", qcol(t) // 512)], writes=[ogt.b], dma="ldo%d" % (t % 2))
                return xt, ogt

            def do_tile(t, x, ogt):
                sl = slots.next()
                proj_tok(ogt, wo, 0, sl)
                tail(layer, t, sl, x, tt, gpost, gpre2, bank[0])

            nxt = load(0)
            for t in range(NTS):
                x, ogt = nxt
                if t + 1 < NTS:
                    nxt = load(t + 1)
                do_tile(t, x, ogt)
            S.barrier()
            A.off = PERS

        def phase_ffn(layer):
            last = (layer == NL - 1)
            wd = A.alloc(32 * D, BF16, "wd")
            wd.ap = wd.ap.rearrange("p (c n) -> p c n", c=32)
            for q in range(4):
                S.op("sp", lambda e, q=q: e.dma_start(out=wd.ap[:, q * 8:(q + 1) * 8, :],
                                                      in_=wb_down[layer, q * 1024:(q + 1) * 1024, :].rearrange("(c p) n -> p c n", p=128)),
                     reads=[dbuf("wb_down", layer)], writes=[wd.b], dma="wld%d" % (q % 2))
            gpost = A.alloc(D, F32, "gpostf")
            load_bcast(gpost, post_ffn[layer:layer + 1, :], "g0")
            NWR = 6
            wur = Ring([A.alloc(D, BF16, "wu%d" % i) for i in range(NWR)])
            hb = Ring([A.alloc(8 * 512, BF16, "hb%d" % i) for i in range(2)])
            xb = Ring([A.alloc(4 * D, F32, "xb%d" % i) for i in range(2)])
            u2 = A.alloc(32 * 512, BF16, "u2")
            u2v = u2.ap.rearrange("p (j n) -> p j n", j=32)
            rr = Ring([A.alloc(512, F32, "rr%d" % i) for i in range(3)])
            junk = A.alloc(D, BF16, "fjunk")
            ssr = Ring([A.alloc(8, F32, "fss%d" % i) for i in range(4)])
            tmp = A.alloc(D, F32, "ftmp")
            xo = Ring([A.alloc(D, F32, "xo%d" % i) for i in range(2)])
            psU = Ring([bank[0], bank[1], bank[2]])
            psD = Ring([slot2(3), slot2(5)])
            blocks = []
            t0 = 0
            while t0 < NTS:
                nt_ = min(4, NTS - t0)
                if t0 < NT and t0 + nt_ > NT:
                    nt_ = NT - t0
                blocks.append((t0, nt_))
                t0 += nt_
            wcount = [0]

            def load_w(jj):
                wt = wur.next()
                S.op("sp", lambda e: e.dma_start(out=wt.ap, in_=wb_up[layer, jj].rearrange("p c n -> p (c n)")),
                     reads=[dbuf("wb_up", layer)], writes=[wt.b], dma="lwu%d" % (wur.i % NWR))
                return wt

            def load_blk(bi):
                t0, nt_ = blocks[bi]
                ntok = nt_ * 128
                h_ = hb.next()
                S.op("sp", lambda e: e.dma_start(out=h_.ap.rearrange("p (c n) -> p c n", c=8)[:, :, 0:ntok],
                                                 in_=H2T[:, qcol(t0):qcol(t0) + ntok].rearrange("(c p) n -> p c n", p=128)),
                     reads=[dbuf("H2T", t) for t in range(t0, t0 + nt_)], writes=[h_.b], dma="lhb%d" % (bi % 2))
                x_ = xb.next()
                S.op("sp", lambda e: e.dma_start(out=x_.ap.rearrange("p (t d) -> p t d", t=4)[:, 0:nt_, :],
                                                 in_=xres[t0:t0 + nt_].rearrange("t p d -> p t d")),
                     reads=[dbuf("xres", t) for t in range(t0, t0 + nt_)], writes=[x_.b], dma="lxb%d" % (bi % 2))
                return h_, x_

            PRE = 4
            wq = []
            total_w = len(blocks) * 32
            for i in range(min(PRE, total_w)):
                wq.append(load_w(i % 32))
            wissued = [len(wq)]

            def do_up(jj, h_, ntok):
                hv = h_.ap.rearrange("p (c n) -> p c n", c=8)
                wt = wq.pop(0)
                if wissued[0] < total_w:
                    wq.append(load_w(wissued[0] % 32))
                    wissued[0] += 1
                wv = wt.ap.rearrange("p (c n) -> p c n", c=8)
                pu = psU.next()

                def f(e):
                    ins = None
                    for kc in range(8):
                        ins = e.matmul(pu.ap[:, 0:ntok], lhsT=wv[:, kc, :], rhs=hv[:, kc, 0:ntok], start=(kc == 0), stop=(kc == 7))
                    return ins
                S.op("pe", f, reads=[wt.b, h_.b], writes=[pu.b], cost=2150)
                r_ = rr.next()
                S.op("act", lambda e: e.activation(out=r_.ap[:, 0:ntok], in_=pu.ap[:, 0:ntok], func=AF.Relu), reads=[pu.b], writes=[r_.b], cost=600)
                S.op("pool", lambda e: e.tensor_tensor(out=u2v[:, jj, 0:ntok], in0=r_.ap[:, 0:ntok], in1=r_.ap[:, 0:ntok], op=ALU.mult),
                     reads=[r_.b], writes=[u2.b], cost=1200)

            def do_down(t, ti, x_):
                pd = psD.next()

                def f(e):
                    ins = None
                    for half in range(2):
                        for jj in range(32):
                            ins = e.matmul(pd.ap[:, half * 512:(half + 1) * 512], lhsT=u2v[:, jj, ti * 128:(ti + 1) * 128],
                                           rhs=wd.ap[:, jj, half * 512:(half + 1) * 512], start=(jj == 0), stop=(jj == 31))
                    return ins
                S.op("pe", f, reads=[u2.b, wd.b], writes=[pd.b], cost=17000)
                ss_ = ssr.next()
                stats(pd.ap, pd.b, D, junk, ss_)
                S.op("dve", lambda e: e.scalar_tensor_tensor(out=tmp.ap, in0=pd.ap, scalar=ss_.ap[:, 2:3], in1=gpost.ap, op0=ALU.mult, op1=ALU.mult),
                     reads=[pd.b, ss_.b, gpost.b], writes=[tmp.b])
                xo_ = xo.next()
                xin = x_.ap[:, ti * D:(ti + 1) * D]
                S.op("pool", lambda e: e.tensor_tensor(out=xo_.ap, in0=xin, in1=tmp.ap, op=ALU.add), reads=[x_.b, tmp.b], writes=[xo_.b])
                if last:
                    if t < NT:
                        S.op("sp", lambda e: e.dma_start(out=y_p[t * 128:(t + 1) * 128, :], in_=xo_.ap), reads=[xo_.b], dma="st_y%d" % (t % 2))
                    else:
                        S.op("sp", lambda e: e.dma_start(out=y_s[:, :], in_=xo_.ap[0:DS, :]), reads=[xo_.b], dma="st_y%d" % (t % 2))
                else:
                    nrow = 128 if t < NT else DS
                    S.op("sp", lambda e: e.dma_start(out=xres[t, 0:nrow, :], in_=xo_.ap[0:nrow, :]), reads=[xo_.b], writes=[dbuf("xres", t)], dma="st_y%d" % (t % 2))

            nxt = load_blk(0)
            for bi, (t0, nt_) in enumerate(blocks):
                h_, x_ = nxt
                if bi + 1 < len(blocks):
                    nxt = load_blk(bi + 1)
                for jj in range(32):
                    do_up(jj, h_, nt_ * 128)
                for ti in range(nt_):
                    do_down(t0 + ti, ti, x_)
            S.barrier()
            A.off = PERS

        def phase_hgrn(layer):
            j = layer // 2
            w = A.alloc(8 * 4 * D, BF16, "hw")
            w.ap = w.ap.rearrange("p (c n) -> p c n", c=8)

            def ld_w(c):
                S.op("sp", lambda e: e.dma_start(out=w.ap[:, c, :], in_=wb_hin[j, c * 128:(c + 1) * 128, :]),
                     reads=[dbuf("wb_hin", j)], writes=[w.b], dma="wld%d" % (c % 2))
            for c in range(8):
                ld_w(c)
            wo = A.alloc(8 * D, BF16, "hwo")
            wo.ap = wo.ap.rearrange("p (c n) -> p c n", c=8)
            S.op("sp", lambda e: e.dma_start(out=wo.ap, in_=wb_hout[j].rearrange("(c p) n -> p c n", p=128)), reads=[dbuf("wb_hout", j)], writes=[wo.b], dma="wld0")
            gpre = A.alloc(D, F32, "gpre")
            load_bcast(gpre, pre_mix[layer:layer + 1, :], "g0")
            gout = A.alloc(D, F32, "gout")
            load_bcast(gout, hgrn_out_norm[j:j + 1, :], "g1")
            gpost = A.alloc(D, F32, "gpost")
            load_bcast(gpost, post_mix[layer:layer + 1, :], "g2")
            gpre2 = A.alloc(D, F32, "gpre2")
            load_bcast(gpre2, pre_ffn[layer:layer + 1, :], "g3")
            om = oml[layer]
            tt = make_tail_tiles()
            xr = Ring([A.alloc(D, F32, "x%d" % i) for i in range(4)])
            otmp = A.alloc(D, F32, "otmp")
            junk = tt["junk"]
            ss = [A.alloc(8, F32, "ss%d" % i) for i in range(4)]
            h = A.alloc(D, BF16, "h")
            hT = A.alloc(D, BF16, "hT")
            qf = A.alloc(D, F32, "qf")
            kk = A.alloc(D, F32, "kk")
            lg = A.alloc(D, F32, "lg")
            dcl = A.alloc(D, F32, "dcl")
            E2 = A.alloc(D, F32, "E2")
            qt = A.alloc(D, BF16, "qt")
            vbr = [A.alloc(D, BF16, "vb%d" % i) for i in range(2)]
            gtr = [A.alloc(D, F32, "gt%d" % i) for i in range(2)]
            ktr = [A.alloc(D, BF16, "kt%d" % i) for i in range(2)]
            qTr = [A.alloc(D, BF16, "qT%d" % i) for i in range(2)]
            kTr = [A.alloc(D, BF16, "kT%d" % i) for i in range(2)]
            decr = [A.alloc(32, F32, "dec%d" % i) for i in range(2)]
            ATs = Ring([A.alloc(128, BF16, "ATs%d" % i) for i in range(2)])
            AT32 = Ring([A.alloc(128, F32, "AT32_%d" % i) for i in range(2)])
            Sst = A.alloc(HH * 128, F32, "Sst")
            Sb = Ring([A.alloc(128, BF16, "Sb%d" % i) for i in range(2)])
            T1 = Ring([A.alloc(128, F32, "T1_%d" % i) for i in range(2)])
            on2 = A.alloc(D, BF16, "on2")
            onT = A.alloc(D, BF16, "onT")
            Sh = [Tl(Sst.ap[:, hd * 128:(hd + 1) * 128], "S%d" % hd) for hd in range(HH)]
            slotsA = Ring([slot2(1), slot2(5)])
            slotB = slot2(3)
            tb = bank[0]
            decp = Tl(psum[:, 7, 0:32], "b7")
            psA0 = Tl(psum[:, 7, 64:192], "b7")
            psA0.b = decp.b
            psA = Ring([psA0])
            psK = Tl(psum[:, 7, 320:448], "b7")
            psK.b = decp.b

            def load_x(t):
                xt = xr.next()
                src, rd = xsrc(layer, t)
                S.op("sp", lambda e: e.dma_start(out=xt.ap, in_=src), reads=rd, writes=[xt.b], dma="ldx%d" % (t % 4))
                return xt

            def zero_state(hd):
                S.op("pool", lambda e: e.memset(Sh[hd].ap, 0.0), writes=[Sh[hd].b])
            for hd in range(HH):
                zero_state(hd)

            def do_head(hd, nv, dec_, so, qT, kT, kt_, vb):
                Sp = Sb.next()
                S_ = Sh[hd]
                S.op("dve", lambda e: e.tensor_scalar(out=Sp.ap, in0=S_.ap, scalar1=dec_.ap[:, hd * 4:hd * 4 + 1], scalar2=None, op0=ALU.mult),
                     reads=[S_.b, dec_.b], writes=[Sp.b], cost=220)
                pa = psA.next()
                S.op("pe", lambda e: e.matmul(pa.ap[0:nv, 0:nv], lhsT=kT.ap[:, hd * 128:hd * 128 + nv], rhs=qT.ap[:, hd * 128:hd * 128 + nv], start=True, stop=True),
                     reads=[kT.b, qT.b], writes=[pa.b], cost=120)
                at = ATs.next()
                at32 = AT32.next()
                S.op("dve", lambda e: e.tensor_tensor(out=at32.ap[0:nv, 0:nv], in0=pa.ap[0:nv, 0:nv], in1=himb[0:nv, 0:nv], op=ALU.min),
                     reads=[pa.b] + CONSTB, writes=[at32.b], cost=320)
                S.op("dve", lambda e: e.tensor_tensor(out=at.ap[0:nv, 0:nv], in0=at32.ap[0:nv, 0:nv], in1=lomb[0:nv, 0:nv], op=ALU.max),
                     reads=[at32.b] + CONSTB, writes=[at.b], cost=320)

                def f_o(e):
                    e.matmul(so.ap[0:nv, hd * 128:(hd + 1) * 128], lhsT=qT.ap[:, hd * 128:hd * 128 + nv], rhs=Sp.ap, start=True, stop=False)
                    return e.matmul(so.ap[0:nv, hd * 128:(hd + 1) * 128], lhsT=at.ap[0:nv, 0:nv], rhs=vb.ap[0:nv, hd * 128:(hd + 1) * 128], start=False, stop=True)
                S.op("pe", f_o, reads=[qT.b, Sp.b, at.b, vb.b], writes=[so.b], cost=250)
                S.op("pe", lambda e: e.matmul(psK.ap, lhsT=kt_.ap[0:nv, hd * 128:(hd + 1) * 128], rhs=vb.ap[0:nv, hd * 128:(hd + 1) * 128], start=True, stop=True),
                     reads=[kt_.b, vb.b], writes=[psK.b], cost=120)
                t1 = T1.next()
                S.op("dve", lambda e: e.tensor_scalar(out=t1.ap, in0=S_.ap, scalar1=dec_.ap[:, hd * 4 + 1:hd * 4 + 2], scalar2=None, op0=ALU.mult),
                     reads=[S_.b, dec_.b], writes=[t1.b], cost=220)
                S.op("dve", lambda e: e.scalar_tensor_tensor(out=S_.ap, in0=psK.ap, scalar=dec_.ap[:, hd * 4 + 2:hd * 4 + 3], in1=t1.ap,
                                                             op0=ALU.mult, op1=ALU.add),
                     reads=[psK.b, dec_.b, t1.b], writes=[S_.b], cost=380)

            def stage_a(t, x):
                samp = (t == NT)
                nv = DS if samp else 128
                d1 = d1s if samp else d1p
                sel = sels if samp else selp
                vb, gt, kt_, qT, kT, dec_ = vbr[t % 2], gtr[t % 2], ktr[t % 2], qTr[t % 2], kTr[t % 2], decr[t % 2]
                ss_ = ss[t % 4]
                stats(x.ap, x.b, D, junk, ss_)
                S.op("dve", lambda e: e.scalar_tensor_tensor(out=h.ap, in0=x.ap, scalar=ss_.ap[:, 2:3], in1=gpre.ap, op0=ALU.mult, op1=ALU.mult),
                     reads=[x.b, ss_.b, gpre.b], writes=[h.b])
                transposes(h, hT, tb, "act")
                sl1 = slotsA.next()
                proj_tok(hT, w, 0, sl1)
                S.op("act", lambda e: e.activation(out=qf.ap, in_=sl1.ap, func=AF.Silu), reads=[sl1.b], writes=[qf.b])
                yield
                sl2 = slotsA.next()
                proj_tok(hT, w, D, sl2)
                S.op("act", lambda e: e.activation(out=kk.ap, in_=sl2.ap, func=AF.Sigmoid, scale=-1.0), reads=[sl2.b], writes=[kk.b])
                S.op("dve", lambda e: e.tensor_tensor(out=kk.ap, in0=kk.ap, in1=om.ap, op=ALU.mult), reads=[kk.b, om.b], writes=[kk.b])
                S.op("act", lambda e: e.activation(out=lg.ap, in_=kk.ap, func=AF.Ln, scale=-1.0, bias=1.0), reads=[kk.b], writes=[lg.b])
                yield
                sl3 = slotsA.next()
                proj_tok(hT, w, 2 * D, sl3)
                S.op("act", lambda e: e.activation(out=vb.ap, in_=sl3.ap, func=AF.Copy), reads=[sl3.b], writes=[vb.b])
                yield
                sl4 = slotsA.next()
                proj_tok(hT, w, 3 * D, sl4)
                S.op("act", lambda e: e.activation(out=gt.ap, in_=sl4.ap, func=AF.Silu), reads=[sl4.b], writes=[gt.b])
                yield
                sl5 = slotsA.next()

                def f_d(e):
                    e.matmul(sl5.ap[:, 0:512], lhsT=d1[0:nv, :], rhs=lg.ap[0:nv, 0:512], start=True, stop=True)
                    return e.matmul(sl5.ap[:, 512:1024], lhsT=d1[0:nv, :], rhs=lg.ap[0:nv, 512:1024], start=True, stop=True)
                S.op("pe", f_d, reads=[lg.b] + CONSTB, writes=[sl5.b], cost=2200)

                def f_dec(e):
                    ins = None
                    for hd in range(HH):
                        ins = e.matmul(decp.ap[:, hd * 4:hd * 4 + 3], lhsT=lg.ap[0:nv, hd * 128:(hd + 1) * 128], rhs=sel[0:nv, :], start=True, stop=True)
                    return ins
                S.op("pe", f_dec, reads=[lg.b] + CONSTB, writes=[decp.b], cost=900)
                S.op("dve", lambda e: e.tensor_scalar(out=dcl.ap, in0=sl5.ap, scalar1=-80.0, scalar2=80.0, op0=ALU.max, op1=ALU.min), reads=[sl5.b], writes=[dcl.b])
                S.op("act", lambda e: e.activation(out=E2.ap, in_=dcl.ap, func=AF.Exp, scale=-1.0), reads=[dcl.b], writes=[E2.b])
                S.op("act", lambda e: e.activation(out=dcl.ap, in_=dcl.ap, func=AF.Exp), reads=[dcl.b, E2.b], writes=[dcl.b])
                S.op("act", lambda e: e.activation(out=dec_.ap, in_=decp.ap, func=AF.Exp), reads=[decp.b], writes=[dec_.b], cost=1400)
                S.op("dve", lambda e: e.tensor_tensor(out=qt.ap, in0=qf.ap, in1=dcl.ap, op=ALU.mult), reads=[qf.b, dcl.b], writes=[qt.b])
                S.op("pool", lambda e: e.tensor_tensor(out=kt_.ap, in0=kk.ap, in1=E2.ap, op=ALU.mult), reads=[kk.b, E2.b], writes=[kt_.b])
                yield
                transposes(qt, qT, tb, "dve")
                transposes(kt_, kT, tb, "act")
                yield

            def stage_b1(t):
                samp = (t == NT)
                nv = DS if samp else 128
                vb, gt, kt_, qT, kT, dec_ = vbr[t % 2], gtr[t % 2], ktr[t % 2], qTr[t % 2], kTr[t % 2], decr[t % 2]
                if samp:
                    S.op("sp", lambda e: e.dma_start(out=s_p[j].rearrange("h k v -> k h v"), in_=Sst.ap.rearrange("p (h v) -> p h v", h=HH)),
                         reads=[s_.b for s_ in Sh], dma="st_s")
                    S.op("sp", lambda e: e.dma_start(out=Sst.ap.rearrange("p (h v) -> p h v", h=HH), in_=st[j].rearrange("h k v -> k h v")),
                         reads=[s_.b for s_ in Sh], writes=[s_.b for s_ in Sh], dma="ld_s")
                so = slotB
                for hd in range(HH):
                    do_head(hd, nv, dec_, so, qT, kT, kt_, vb)
                    if hd % 2 == 1:
                        yield
                ss2 = ss[(t + 2) % 4]
                stats(so.ap, so.b, D, junk, ss2)
                S.op("dve", lambda e: e.scalar_tensor_tensor(out=otmp.ap, in0=so.ap, scalar=ss2.ap[:, 2:3], in1=gout.ap, op0=ALU.mult, op1=ALU.mult),
                     reads=[so.b, ss2.b, gout.b], writes=[otmp.b])
                yield

            def stage_b2(t, x):
                gt = gtr[t % 2]
                S.op("pool", lambda e: e.tensor_tensor(out=on2.ap, in0=otmp.ap, in1=gt.ap, op=ALU.mult), reads=[otmp.b, gt.b], writes=[on2.b])
                transposes(on2, onT, tb, "act")
                yield
                sl6 = slotsA.next()
                proj_tok(onT, wo, 0, sl6)
                tail(layer, t, sl6, x, tt, gpost, gpre2, tb)
                yield

            def drain(g):
                if g is None:
                    return None
                try:
                    next(g)
                    return g
                except StopIteration:
                    return None

            def pipeline(tiles):
                n = len(tiles)
                xs = {0: load_x(tiles[0])}
                for step in range(n + 2):
                    if step + 1 < n:
                        xs[step + 1] = load_x(tiles[step + 1])
                    gens = []
                    if step < n:
                        gens.append(stage_a(tiles[step], xs[step]))
                    if 0 <= step - 1 < n:
                        gens.append(stage_b1(tiles[step - 1]))
                    if 0 <= step - 2 < n:
                        gens.append(stage_b2(tiles[step - 2], xs[step - 2]))
                    while gens:
                        gens = [g for g in (drain(g) for g in gens) if g is not None]

            pipeline(list(range(NTS)))
            S.op("sp", lambda e: e.dma_start(out=s_s[j].rearrange("h k v -> k h v"), in_=Sst.ap.rearrange("p (h v) -> p h v", h=HH)),
                 reads=[s_.b for s_ in Sh], dma="st_s")
            S.barrier()
            A.off = PERS

        phase_wcast()
        for layer in range(NL):
            if layer % 2 == 0:
                LF = phase_fox_a(layer)
                phase_fox_a2(layer, LF)
                phase_fox_b(layer)
                phase_fox_c1(layer)
            else:
                phase_hgrn(layer)
            phase_ffn(layer)
        S.emit()
    return nc


_CACHE = {}


def get_nc(T, PAST, NL=4):
    key = (T, PAST, NL)
    if key not in _CACHE:
        _CACHE[key] = build(T, PAST, NL)
    return _CACHE[key]


def make_in_map(c, inp, T, PAST):
    f = lambda a: np.ascontiguousarray(np.asarray(a, dtype=np.float32))
    m = {
        "x_p": f(inp["x_prompt"][c]),
        "x_s": f(inp["x_sample"][c]),
        "ck": f(np.asarray(inp["cache_k"])[:, c].reshape(-1, PAST, D)),
        "cv": f(np.asarray(inp["cache_v"])[:, c].reshape(-1, PAST, D)),
        "clf": f(np.asarray(inp["cache_logf"])[:, c]),
        "st": f(np.asarray(inp["state_s"])[:, c]),
        "fox_w_in": f(inp["fox_w_in"]), "fox_b_f": f(inp["fox_b_f"]), "fox_q_norm": f(inp["fox_q_norm"]),
        "fox_k_norm": f(inp["fox_k_norm"]), "fox_w_out": f(inp["fox_w_out"]), "hgrn_w_in": f(inp["hgrn_w_in"]),
        "hgrn_lb": f(inp["hgrn_lb_logits"]), "hgrn_out_norm": f(inp["hgrn_out_norm"]), "hgrn_w_out": f(inp["hgrn_w_out"]),
        "pre_mix": f(inp["pre_mix_norm"]), "post_mix": f(inp["post_mix_norm"]), "pre_ffn": f(inp["pre_ffn_norm"]),
        "post_ffn": f(inp["post_ffn_norm"]), "ffn_up": f(inp["ffn_w_up"]), "ffn_down": f(inp["ffn_w_down"]),
        "consts": make_consts(),
    }
    return m


def assemble(results, B, T):
    def st(name, shape_tail=None):
        return np.stack([np.asarray(r[name], dtype=np.float32) for r in results], axis=0)
    y_p = st("y_p")
    y_s = st("y_s")
    k_p = np.moveaxis(st("k_p"), 0, 1).reshape(-1, B, T, NH, DH)
    v_p = np.moveaxis(st("v_p"), 0, 1).reshape(-1, B, T, NH, DH)
    lf_p = np.moveaxis(st("lf_p"), 0, 1)
    s_p = np.moveaxis(st("s_p"), 0, 1)
    k_s = np.moveaxis(st("k_s"), 0, 1).reshape(-1, B, DS, NH, DH)
    v_s = np.moveaxis(st("v_s"), 0, 1).reshape(-1, B, DS, NH, DH)
    lf_s = np.moveaxis(st("lf_s"), 0, 1)
    s_s = np.moveaxis(st("s_s"), 0, 1)
    return (y_p, y_s, k_p, v_p, lf_p, s_p, k_s, v_s, lf_s, s_s)


def kernel(**inputs):
    xp = np.asarray(inputs["x_prompt"])
    B, T = xp.shape[0], xp.shape[1]
    PAST = np.asarray(inputs["cache_k"]).shape[2]
    nc = get_nc(T, PAST)
    in_maps = [make_in_map(c, inputs, T, PAST) for c in range(B)]
    res = run_bass_kernel_spmd(nc, in_maps, core_ids=list(range(B)))
    return assemble(res.results, B, T)
```
